# Optimizing a Trainium2 kernel written in Bass

```python
import math
import jax, jax.numpy as jnp
from jax import lax
import numpy as np


D_MODEL = 2048
BATCH = 8
SEQ = 4096
DEPTH = 2

N_META = 16
HEAD_DIM = 128
A_HEADS = 4
A_WIDTH = A_HEADS * HEAD_DIM
HGRN_CHUNK = 64
B_WIDTH = 512
CONV_W = 3
C_HEADS = 4
C_HALF = HEAD_DIM // 2
C_WIDTH = C_HEADS * HEAD_DIM
D_HEADS = 4
D_WIDTH = D_HEADS * HEAD_DIM
IDX_HEADS = 16
IDX_DIM = 64
TOPK_MAX = 256
N_BRANCH = 4
BRANCH_WIDTH = 512
D_FF = 5632
N_BUCKETS = 32
MAX_DISTANCE = 128
Q_BLOCK = 128
EPS = 1e-6

IN_SPLITS = (A_WIDTH, A_WIDTH, A_WIDTH, A_WIDTH,
             B_WIDTH, B_WIDTH, B_WIDTH,
             C_WIDTH, C_WIDTH, C_WIDTH,
             D_WIDTH, HEAD_DIM, HEAD_DIM,
             IDX_HEADS * IDX_DIM, IDX_DIM, IDX_HEADS,
             N_BRANCH * D_MODEL)
IN_COLS = sum(IN_SPLITS)

kernel_name = "hybrid_gated_hgrn2_conv_diffattn_dsa_macaron"


def rmsnorm(x, g):
    xf = x.astype(jnp.float32)
    inv = lax.rsqrt(jnp.mean(xf * xf, axis=-1, keepdims=True) + EPS)
    return (xf * inv).astype(x.dtype) * g


def swiglu_ffn(h, w_gu, w_down):
    gate, up = jnp.split(h @ w_gu, 2, axis=-1)
    return (jax.nn.silu(gate) * up) @ w_down


def rel_bucket(q_pos, k_pos):
    n = jnp.maximum(q_pos - k_pos, 0)
    max_exact = N_BUCKETS // 2
    nf = jnp.maximum(n, 1).astype(jnp.float32)
    large = max_exact + (jnp.log(nf / max_exact) / math.log(MAX_DISTANCE / max_exact)
                         * (N_BUCKETS - max_exact)).astype(jnp.int32)
    large = jnp.minimum(large, N_BUCKETS - 1)
    return jnp.where(n < max_exact, n, large)


def map_query_blocks(fn, q_arrays, seq_len):
    pad = (-seq_len) % Q_BLOCK
    pos = jnp.concatenate([jnp.zeros((pad,), jnp.int32), jnp.arange(seq_len, dtype=jnp.int32)])
    n_blk = (seq_len + pad) // Q_BLOCK

    def to_blocks(a):
        a = jnp.pad(a, [(0, 0), (pad, 0)] + [(0, 0)] * (a.ndim - 2))
        a = a.reshape((a.shape[0], n_blk, Q_BLOCK) + a.shape[2:])
        return jnp.moveaxis(a, 1, 0)

    xs = tuple(to_blocks(a) for a in q_arrays) + (pos.reshape(n_blk, Q_BLOCK),)
    out = lax.map(lambda args: fn(*args), xs)
    out = jnp.moveaxis(out, 0, 1)
    out = out.reshape((out.shape[0], n_blk * Q_BLOCK) + out.shape[3:])
    return out[:, pad:]


def hgrn2_chunk(S, q, log_f, k, v):
    b = jnp.cumsum(log_f, axis=2)
    o_inter = jnp.einsum('bhtd,bhde->bhte', q * jnp.exp(b), S)
    c = q.shape[2]
    causal = jnp.tril(jnp.ones((c, c), bool))
    diff = b[:, :, :, None, :] - b[:, :, None, :, :]
    decay = jnp.exp(jnp.where(causal[:, :, None], diff, -jnp.inf))
    scores = jnp.einsum('bhtd,bhsd,bhtsd->bhts', q, k, decay)
    o = o_inter + jnp.einsum('bhts,bhse->bhte', scores, v)
    b_last = b[:, :, -1:, :]
    k_dec = k * jnp.exp(b_last - b)
    S_new = jnp.exp(b_last[:, :, 0, :])[..., None] * S + jnp.einsum('bhsd,bhse->bhde', k_dec, v)
    return S_new, o


def hgrn2_mixer(q_raw, f_raw, i_raw, g_raw, lb, gnorm):
    Bb, L, _ = q_raw.shape

    def heads(t):
        return t.astype(jnp.float32).reshape(Bb, L, A_HEADS, HEAD_DIM).transpose(0, 2, 1, 3)

    lb = lb.astype(jnp.float32).reshape(1, A_HEADS, 1, HEAD_DIM)
    z = heads(f_raw)
    log_f = jnp.logaddexp(jnp.log(lb), jnp.log1p(-lb) + jax.nn.log_sigmoid(z))
    k = (1.0 - lb) * jax.nn.sigmoid(-z)
    q = jax.nn.silu(heads(q_raw))
    v = heads(i_raw)
    S0 = jnp.zeros((Bb, A_HEADS, HEAD_DIM, HEAD_DIM), jnp.float32)
    m = N_META
    S_meta, o_meta = hgrn2_chunk(S0, q[:, :, :m], log_f[:, :, :m], k[:, :, :m], v[:, :, :m])

    def to_chunks(t):
        r = t[:, :, m:]
        n = r.shape[2] // HGRN_CHUNK
        return jnp.moveaxis(r.reshape(Bb, A_HEADS, n, HGRN_CHUNK, HEAD_DIM), 2, 0)

    _, o_real = lax.scan(lambda S, xs: hgrn2_chunk(S, *xs), S_meta,
                         (to_chunks(q), to_chunks(log_f), to_chunks(k), to_chunks(v)))
    o_real = jnp.moveaxis(o_real, 0, 2).reshape(Bb, A_HEADS, L - m, HEAD_DIM)
    o = jnp.concatenate([o_meta, o_real], axis=2).transpose(0, 2, 1, 3)
    o = rmsnorm(o, gnorm.astype(jnp.float32).reshape(A_HEADS, HEAD_DIM)).reshape(Bb, L, A_WIDTH)
    return o.astype(g_raw.dtype) * jax.nn.silu(g_raw)


def short_conv_mixer(b_gate, c_gate, u, conv_w):
    zc = c_gate * u
    L = zc.shape[1]
    zp = jnp.pad(zc, ((0, 0), (CONV_W - 1, 0), (0, 0)))
    y = conv_w[0] * zp[:, CONV_W - 1:CONV_W - 1 + L]
    for j in range(1, CONV_W):
        y = y + conv_w[j] * zp[:, CONV_W - 1 - j:CONV_W - 1 - j + L]
    return b_gate * y


def diff_attention(c_q, c_k, c_v, q_norm, k_norm, lam_params, subln, table, layer):
    Bb, L, _ = c_q.shape
    q = rmsnorm(c_q.reshape(Bb, L, C_HEADS, 2, C_HALF), q_norm) * (C_HALF ** -0.5)
    k = rmsnorm(c_k.reshape(Bb, L, C_HEADS, 2, C_HALF), k_norm)
    v = c_v.reshape(Bb, L, C_HEADS, HEAD_DIM)
    lp = lam_params.astype(jnp.float32)
    lam_init = 0.8 - 0.6 * math.exp(-0.3 * layer)
    lam = jnp.exp(jnp.sum(lp[0] * lp[1])) - jnp.exp(jnp.sum(lp[2] * lp[3])) + lam_init
    k_pos = jnp.arange(L, dtype=jnp.int32)

    def block(q_blk, q_pos):
        bias = jnp.take(table, rel_bucket(q_pos[:, None], k_pos[None, :]), axis=0)
        logits = (jnp.einsum('bqhcd,bkhcd->bchqk', q_blk, k).astype(jnp.float32)
                  + jnp.transpose(bias, (2, 0, 1)).astype(jnp.float32))
        logits = jnp.where(k_pos[None, :] <= q_pos[:, None], logits, -jnp.inf)
        p = jax.nn.softmax(logits, axis=-1)
        p_diff = p[:, 0] - lam * p[:, 1]
        return jnp.einsum('bhqk,bkhe->bqhe', p_diff.astype(v.dtype), v)

    o = map_query_blocks(block, (q,), L)
    o = rmsnorm(o, subln) * (1.0 - lam_init)
    return o.reshape(Bb, L, C_WIDTH)


def dsa_attention(d_q, d_k, d_v, d_qi, d_ki, d_w, q_norm, k_norm, table, top_k):
    Bb, L, _ = d_q.shape
    q = rmsnorm(d_q.reshape(Bb, L, D_HEADS, HEAD_DIM), q_norm) * (HEAD_DIM ** -0.5)
    k = rmsnorm(d_k, k_norm)
    v = d_v
    qi = d_qi.reshape(Bb, L, IDX_HEADS, IDX_DIM).astype(jnp.float32) * (IDX_DIM ** -0.5)
    ki = d_ki.astype(jnp.float32)
    wi = d_w.astype(jnp.float32) * (IDX_HEADS ** -0.5)
    k_pos = jnp.arange(L, dtype=jnp.int32)
    gather = jax.vmap(lambda t, i: t[i])

    def block(q_blk, qi_blk, wi_blk, q_pos):
        score = jax.nn.relu(jnp.einsum('bqhd,bkd->bqhk', qi_blk, ki))
        index = jnp.einsum('bqhk,bqh->bqk', score, wi_blk)
        index = jnp.where(k_pos[None, None, :] <= q_pos[None, :, None], index, -jnp.inf)
        _, sel = lax.top_k(index, top_k)
        k_sel = gather(k, sel)
        v_sel = gather(v, sel)
        bias = jnp.take(table, rel_bucket(q_pos[None, :, None], sel), axis=0)
        logits = (jnp.einsum('bqhd,bqkd->bhqk', q_blk, k_sel).astype(jnp.float32)
                  + jnp.transpose(bias, (0, 3, 1, 2)).astype(jnp.float32))
        logits = jnp.where((sel <= q_pos[None, :, None])[:, None], logits, -jnp.inf)
        p = jax.nn.softmax(logits, axis=-1)
        return jnp.einsum('bhqk,bqkd->bqhd', p.astype(v.dtype), v_sel)

    o = map_query_blocks(block, (q, qi, wi), L)
    return o.reshape(Bb, L, D_WIDTH)


def hybrid_mixer(h, layer, top_k, rel_bias, w_in, lb, hgrn_gnorm, conv_w, diff_q_norm,
                 diff_k_norm, diff_lambda, diff_subln, dsa_q_norm, dsa_k_norm, w_branch, w_out):
    Bb, L, _ = h.shape
    offsets = np.cumsum(np.array(IN_SPLITS))[:-1].tolist()
    (a_q, a_f, a_i, a_g, b_b, b_c, b_u, c_q, c_k, c_v,
     d_q, d_k, d_v, d_qi, d_ki, d_w, gate_logits) = jnp.split(h @ w_in, offsets, axis=-1)
    branches = (
        hgrn2_mixer(a_q, a_f, a_i, a_g, lb, hgrn_gnorm),
        short_conv_mixer(b_b, b_c, b_u, conv_w),
        diff_attention(c_q, c_k, c_v, diff_q_norm, diff_k_norm, diff_lambda, diff_subln,
                       rel_bias[:, :C_HEADS], layer),
        dsa_attention(d_q, d_k, d_v, d_qi, d_ki, d_w, dsa_q_norm, dsa_k_norm,
                      rel_bias[:, C_HEADS:], top_k),
    )
    gates = jax.nn.sigmoid(gate_logits.reshape(Bb, L, N_BRANCH, D_MODEL))
    merged = gates[:, :, 0] * (branches[0] @ w_branch[0])
    for m in range(1, N_BRANCH):
        merged = merged + gates[:, :, m] * (branches[m] @ w_branch[m])
    return merged @ w_out


def setup_inputs(seed: int = 0) -> dict:
    key = jax.random.key(seed)
    ks = jax.random.split(key, 24)
    f32 = jnp.float32

    def nrm(k, shape, scale):
        return jax.random.normal(k, shape, f32) * scale

    def gain(k, shape):
        return 1.0 + 0.05 * jax.random.normal(k, shape, f32)

    return {
        "x": nrm(ks[0], (BATCH, SEQ, D_MODEL), 1.0),
        "meta_tokens": nrm(ks[1], (N_META, D_MODEL), 1.0),
        "rel_bias": nrm(ks[2], (N_BUCKETS, C_HEADS + D_HEADS), 0.2),
        "ffn1_norm": gain(ks[3], (DEPTH, D_MODEL)),
        "ffn1_w_gu": nrm(ks[4], (DEPTH, D_MODEL, 2 * D_FF), D_MODEL ** -0.5),
        "ffn1_w_down": nrm(ks[5], (DEPTH, D_FF, D_MODEL), D_FF ** -0.5),
        "mix_norm": gain(ks[6], (DEPTH, D_MODEL)),
        "w_in": nrm(ks[7], (DEPTH, D_MODEL, IN_COLS), D_MODEL ** -0.5),
        "hgrn_lb": nrm(ks[8], (DEPTH, A_WIDTH), 1.0),
        "hgrn_gnorm": gain(ks[9], (DEPTH, A_WIDTH)),
        "conv_w": nrm(ks[10], (DEPTH, CONV_W, B_WIDTH), CONV_W ** -0.5),
        "diff_q_norm": gain(ks[11], (DEPTH, C_HALF)),
        "diff_k_norm": gain(ks[12], (DEPTH, C_HALF)),
        "diff_lambda": nrm(ks[13], (DEPTH, 4, C_HALF), 0.1),
        "diff_subln": gain(ks[14], (DEPTH, HEAD_DIM)),
        "dsa_q_norm": gain(ks[15], (DEPTH, HEAD_DIM)),
        "dsa_k_norm": gain(ks[16], (DEPTH, HEAD_DIM)),
        "w_branch": nrm(ks[17], (DEPTH, N_BRANCH, BRANCH_WIDTH, D_MODEL), BRANCH_WIDTH ** -0.5),
        "w_out": nrm(ks[18], (DEPTH, D_MODEL, D_MODEL), D_MODEL ** -0.5),
        "ffn2_norm": gain(ks[19], (DEPTH, D_MODEL)),
        "ffn2_w_gu": nrm(ks[20], (DEPTH, D_MODEL, 2 * D_FF), D_MODEL ** -0.5),
        "ffn2_w_down": nrm(ks[21], (DEPTH, D_FF, D_MODEL), D_FF ** -0.5),
    }


def reference(x, meta_tokens, rel_bias, ffn1_norm, ffn1_w_gu, ffn1_w_down, mix_norm, w_in,
              hgrn_lb, hgrn_gnorm, conv_w, diff_q_norm, diff_k_norm, diff_lambda, diff_subln,
              dsa_q_norm, dsa_k_norm, w_branch, w_out, ffn2_norm, ffn2_w_gu, ffn2_w_down):
    Bb, seq, _ = x.shape
    top_k = min(TOPK_MAX, seq // 4)
    meta = jnp.broadcast_to(meta_tokens.astype(x.dtype)[None], (Bb, N_META, D_MODEL))
    h = jnp.concatenate([meta, x], axis=1)
    lbs = jnp.cumsum(jax.nn.softmax(hgrn_lb.astype(jnp.float32), axis=0), axis=0)
    lbs = lbs - lbs[0:1]
    for l in range(DEPTH):
        h = h + 0.5 * swiglu_ffn(rmsnorm(h, ffn1_norm[l]), ffn1_w_gu[l], ffn1_w_down[l])
        h = h + hybrid_mixer(rmsnorm(h, mix_norm[l]), l, top_k, rel_bias, w_in[l], lbs[l],
                             hgrn_gnorm[l], conv_w[l], diff_q_norm[l], diff_k_norm[l],
                             diff_lambda[l], diff_subln[l], dsa_q_norm[l], dsa_k_norm[l],
                             w_branch[l], w_out[l])
        h = h + 0.5 * swiglu_ffn(rmsnorm(h, ffn2_norm[l]), ffn2_w_gu[l], ffn2_w_down[l])
    return h[:, N_META:]
```

```python
import numpy as np
from contextlib import ExitStack
import concourse.bass as bass
import concourse.mybir as mybir
from concourse.bass_utils import run_bass_kernel_spmd

F32 = mybir.dt.float32
BF16 = mybir.dt.bfloat16
I32 = mybir.dt.int32
ALU = mybir.AluOpType
AF = mybir.ActivationFunctionType
AX = mybir.AxisListType

ENGS = ("pe", "act", "dve", "pool", "sp")


class Prog:
    def __init__(self, nc):
        self.nc = nc
        self.es = ExitStack()
        self.eng_h = {"pe": nc.tensor, "act": nc.scalar, "dve": nc.vector,
                      "pool": nc.gpsimd, "sp": nc.sync}
        self.sem = {e: self.es.enter_context(nc.semaphore("s_" + e)) for e in ENGS}
        self.sig_cnt = {e: 0 for e in ENGS}
        self.waited = {e: {} for e in ENGS}
        self.dma_sems = {}
        self.semobj = {("eng", e): self.sem[e] for e in ENGS}
        self.ops = []
        self.lastw = {}
        self.readers = {}
        self.n_inst = 0
        self.phase_es = None

    def add(self, eng, fn, reads=(), writes=(), dma_key=None):
        idx = len(self.ops)
        is_dma = dma_key is not None
        deps = {}
        for r in reads:
            w = self.lastw.get(r)
            if w is not None:
                deps[w] = "raw"
        for wk in writes:
            w = self.lastw.get(wk)
            if w is not None and w not in deps:
                deps[w] = "waw"
            for r in self.readers.get(wk, {}).values():
                if r not in deps:
                    deps[r] = "war"
        op = dict(eng=eng, fn=fn, deps=deps, dma_key=dma_key, sig=False, dma_val=None)
        if is_dma:
            if dma_key not in self.dma_sems:
                s = self.es.enter_context(self.nc.semaphore("d_%d" % len(self.dma_sems)))
                self.dma_sems[dma_key] = [s, 0]
            self.dma_sems[dma_key][1] += 16
            op["dma_val"] = self.dma_sems[dma_key][1]
        self.ops.append(op)
        rk = ("dma", dma_key) if is_dma else eng
        for r in reads:
            self.readers.setdefault(r, {})[rk] = idx
        for wk in writes:
            self.lastw[wk] = idx
            self.readers[wk] = {}
        return idx

    def dma(self, q, out, in_, reads, writes, key):
        self.add(q, lambda e: e.dma_start(out=out, in_=in_), reads, writes, dma_key=key)

    def flush(self):
        ops = self.ops
        if not ops:
            return
        for i, op in enumerate(ops):
            waits = []
            for d, kind in op["deps"].items():
                od = ops[d]
                if od["dma_key"] is not None:
                    waits.append(("dma", od["dma_key"], od["dma_val"]))
                    continue
                if od["eng"] == op["eng"] and op["dma_key"] is None:
                    if op["eng"] == "pe" or kind != "raw":
                        continue
                od["sig"] = True
                waits.append(("eng", od["eng"], d))
            op["waits"] = waits
        last = {}
        for i, op in enumerate(ops):
            if op["dma_key"] is None:
                last[op["eng"]] = i
        for e, i in last.items():
            ops[i]["sig"] = True
        cnt = dict(self.sig_cnt)
        for op in ops:
            if op["dma_key"] is None and op["sig"]:
                cnt[op["eng"]] += 1
                op["sig_val"] = cnt[op["eng"]]
        end_cnt = cnt

        def emit_engine(ename):
            def body(eh):
                waited = self.waited[ename]
                for op in ops:
                    if op["eng"] != ename:
                        continue
                    for w in op["waits"]:
                        if w[0] == "dma":
                            semk = ("dma", w[1]); val = w[2]
                            s = self.dma_sems[w[1]][0]
                        else:
                            semk = ("eng", w[1]); val = ops[w[2]]["sig_val"]
                            s = self.sem[w[1]]
                        if waited.get(semk, 0) >= val:
                            continue
                        waited[semk] = val
                        eh.wait_ge(s, val)
                        self.n_inst += 1
                    ins = op["fn"](eh)
                    self.n_inst += 1
                    if op["dma_key"] is not None:
                        ins.then_inc(self.dma_sems[op["dma_key"]][0], 16)
                    elif op["sig"]:
                        ins.then_inc(self.sem[ename], 1)
                for e2 in ENGS:
                    if e2 == ename:
                        continue
                    v = end_cnt[e2]
                    if v > 0 and waited.get(("eng", e2), 0) < v:
                        waited[("eng", e2)] = v
                        eh.wait_ge(self.sem[e2], v)
                for k, (s, c) in self.dma_sems.items():
                    if c > 0 and waited.get(("dma", k), 0) < c:
                        waited[("dma", k)] = c
                        eh.wait_ge(s, c)
            return body

        with self.nc.Block() as block:
            block.tensor(emit_engine("pe"))
            block.scalar(emit_engine("act"))
            block.vector(emit_engine("dve"))
            block.gpsimd(emit_engine("pool"))
            block.sync(emit_engine("sp"))
        self.sig_cnt = end_cnt
        self.ops = []
        self.lastw = {}
        self.readers = {}

    def close(self):
        self.flush()
        self.es.close()
import math


D = 2048
NCH = 16
DFF = 5632
NJ = 44
EPS = 1e-6
OFF = dict(a_q=0, a_f=512, a_i=1024, a_g=1536, b_b=2048, b_c=2560, b_u=3072, c_q=3584, c_k=4096,
           c_v=4608, d_q=5120, d_k=5632, d_v=5760, d_qi=5888, d_ki=6912, d_w=6976, gate=6992)
NFM = 6976
NEG = -30000.0


class Rot:
    def __init__(self, name, tensors):
        self.name = name
        self.t = tensors
        self.i = 0

    def next(self):
        k = self.i % len(self.t)
        self.i += 1
        return self.t[k], (self.name, k)


def split_cols(t0, n, step):
    out = []
    o = 0
    while o < n:
        m = min(step, n - o)
        out.append((t0 + o, m))
        o += m
    return out


class Ctx:
    pass


def token_tiles(L):
    return [(0, 16)] + [(16 + 128 * i, 128) for i in range((L - 16) // 128)]


def make_groups(L, G):
    tl = token_tiles(L)
    nt = len(tl) - 1
    per = (nt + G - 1) // G
    groups = []
    i = 1
    first = True
    while i <= nt:
        j = min(nt, i + per - 1)
        g0 = 0 if first else tl[i][0]
        g1 = tl[j][0] + tl[j][1]
        subs = []
        if first:
            subs.append((0, 16))
            subs += split_cols(16, g1 - 16, 512)
        else:
            subs += split_cols(g0, g1 - g0, 512)
        tiles = ([0] if first else []) + list(range(i, j + 1))
        groups.append((g0, g1 - g0, subs, tiles))
        first = False
        i = j + 1
    return groups


def fm(ap2d):
    return ap2d.rearrange("(c p) t -> p c t", p=128)


_UID = [0]


def alloc(c, es, name, shape, dt):
    _UID[0] += 1
    return es.enter_context(c.nc.sbuf_tensor("s%d_%s" % (_UID[0], name), list(shape), dt))


def palloc(c, es, name, shape, dt=None):
    _UID[0] += 1
    return es.enter_context(c.nc.psum_tensor("p%d_%s" % (_UID[0], name), list(shape), dt or F32))


def rsqrt_op(c, out, in_, scale, reads, wkey, np_=128):
    P = c.P
    P.add("act", lambda e: e.activation(out=out, in_=in_, func=AF.Sqrt, bias=c.epsc[0:np_, 0:1], scale=scale),
          list(reads) + ["const"], [wkey])
    P.add("dve", lambda e: e.reciprocal(out=out, in_=out), [wkey], [wkey])


def norm_stage(c, es_bufs, src, gcol, hbuf, hkey, g0, ng):
    for _ in norm_gen(c, es_bufs, src, gcol, hbuf, hkey, g0, ng):
        pass


def norm_gen(c, es_bufs, src, gcol, hbuf, hkey, g0, ng):
    P = c.P
    xts, sqs, rstd, psn = es_bufs
    srcv = fm(src)
    for (t0, n) in split_cols(g0, ng, 128):
        off = t0 - g0
        xt, xk = xts.next()
        P.dma("sp", xt[:, :, 0:n], srcv[:, :, t0:t0 + n], [], [xk], xk)
        for ch in range(NCH):
            sq, sk = sqs.next()
            P.add("act", lambda e, sq=sq, xt=xt, ch=ch, n=n: e.activation(out=sq[:, 0:n], in_=xt[:, ch, 0:n], func=AF.Square),
                  [xk], [sk])
            P.add("pe", lambda e, sq=sq, ch=ch, n=n: e.matmul(psn[:, 0:n], lhsT=c.ones_bf[:, :], rhs=sq[:, 0:n],
                                                              start=(ch == 0), stop=(ch == NCH - 1)),
                  [sk, "const"], ["psn"])
        rsqrt_op(c, rstd[:, 0:n], psn[:, 0:n], 1.0 / D, ["psn"], "rstd")
        for ch in range(NCH):
            P.add("dve", lambda e, xt=xt, ch=ch, n=n, off=off: e.scalar_tensor_tensor(
                out=hbuf[:, ch, off:off + n], in0=xt[:, ch, 0:n], scalar=gcol[:, ch:ch + 1], in1=rstd[:, 0:n],
                op0=ALU.mult, op1=ALU.mult), [xk, "rstd", "params"], [hkey])
        yield


def norm_bufs(c, es):
    xts = Rot("xt", [alloc(c, es, "xt%d" % i, [128, NCH, 128], F32) for i in range(2)])
    sqs = Rot("sq", [alloc(c, es, "sq%d" % i, [128, 256], BF16) for i in range(3)])
    rstd = alloc(c, es, "rstd", [128, 256], F32)
    psn = palloc(c, es, "psn", [128, 512])
    return (xts, sqs, rstd, psn)


def ffn_phase(c, src, dst, gcol, w_gu, w_down):
    nc, P = c.nc, c.P
    es = ExitStack()
    groups = make_groups(c.L, c.G_ffn)
    NG = max(g[1] for g in groups)
    nb = norm_bufs(c, es)
    hbuf = alloc(c, es, "hbuf", [128, NCH, NG], BF16)
    act = alloc(c, es, "actb", [128, NJ, NG], BF16)
    wgu = Rot("wgu", [alloc(c, es, "wgu%d" % i, [128, NCH, 256], BF16) for i in range(2)])
    wd = Rot("wd", [alloc(c, es, "wd%d" % i, [128, NJ, 128], BF16) for i in range(2)])
    sgs = Rot("sg", [alloc(c, es, "sg%d" % i, [128, 512], F32) for i in range(2)])
    xrs = Rot("xr", [alloc(c, es, "xr%d" % i, [128, 512], F32) for i in range(3)])
    xos = Rot("xo", [alloc(c, es, "xo%d" % i, [128, 512], F32) for i in range(3)])
    psg = Rot("psg", [palloc(c, es, "psg%d" % i, [128, 512]) for i in range(2)])
    psu = Rot("psu", [palloc(c, es, "psu%d" % i, [128, 512]) for i in range(2)])
    pso = Rot("pso", [palloc(c, es, "pso%d" % i, [128, 512]) for i in range(2)])
    wguv = fm(w_gu)
    wdv = fm(w_down)
    srcv, dstv = fm(src), fm(dst)
    for gi_, (g0, ng, subs, _tiles) in enumerate(groups):
        if gi_ == 0:
            norm_stage(c, nb, src, gcol, hbuf, "hbuf", g0, ng)
        if gi_ + 1 < len(groups):
            ngen = norm_gen(c, nb, src, gcol, hbuf, "hbuf", groups[gi_ + 1][0], groups[gi_ + 1][1])
        else:
            ngen = iter(())
        for j in range(NJ):
            w, wk = wgu.next()
            P.dma("pool", w[:, :, 0:128], wguv[:, :, j * 128:(j + 1) * 128], [], [wk], wk)
            P.dma("pool", w[:, :, 128:256], wguv[:, :, DFF + j * 128:DFF + (j + 1) * 128], [], [wk], wk)
            for (t0, n) in subs:
                off = t0 - g0
                pg, pgk = psg.next()
                pu, puk = psu.next()
                for ch in range(NCH):
                    P.add("pe", lambda e, pg=pg, w=w, ch=ch, off=off, n=n: e.matmul(
                        pg[:, 0:n], lhsT=w[:, ch, 0:128], rhs=hbuf[:, ch, off:off + n], start=(ch == 0), stop=(ch == NCH - 1)),
                        [wk, "hbuf"], [pgk])
                for ch in range(NCH):
                    P.add("pe", lambda e, pu=pu, w=w, ch=ch, off=off, n=n: e.matmul(
                        pu[:, 0:n], lhsT=w[:, ch, 128:256], rhs=hbuf[:, ch, off:off + n], start=(ch == 0), stop=(ch == NCH - 1)),
                        [wk, "hbuf"], [puk])
                sg, sgk = sgs.next()
                P.add("act", lambda e, sg=sg, pg=pg, n=n: e.activation(out=sg[:, 0:n], in_=pg[:, 0:n], func=AF.Silu),
                      [pgk], [sgk])
                P.add("dve", lambda e, sg=sg, pu=pu, j=j, off=off, n=n: e.tensor_tensor(
                    out=act[:, j, off:off + n], in0=sg[:, 0:n], in1=pu[:, 0:n], op=ALU.mult), [sgk, puk], [("act", j)])
        for m in range(NCH):
            w, wk = wd.next()
            P.dma("pool", w[:, :, :], wdv[:, :, m * 128:(m + 1) * 128], [], [wk], wk)
            for (t0, n) in subs:
                off = t0 - g0
                xr, xrk = xrs.next()
                P.dma("sp", xr[:, 0:n], srcv[:, m, t0:t0 + n], [], [xrk], xrk)
                po, pok = pso.next()
                for j in range(NJ):
                    P.add("pe", lambda e, po=po, w=w, j=j, off=off, n=n: e.matmul(
                        po[:, 0:n], lhsT=w[:, j, :], rhs=act[:, j, off:off + n], start=(j == 0), stop=(j == NJ - 1)),
                        [wk, ("act", j)], [pok])
                xo, xok = xos.next()
                P.add("dve", lambda e, xo=xo, po=po, xr=xr, n=n: e.scalar_tensor_tensor(
                    out=xo[:, 0:n], in0=po[:, 0:n], scalar=0.5, in1=xr[:, 0:n], op0=ALU.mult, op1=ALU.add),
                    [pok, xrk], [xok])
                P.dma("act", dstv[:, m, t0:t0 + n], xo[:, 0:n], [xok], [], xok)
            if m >= 2:
                next(ngen, None)
        for _ in ngen:
            pass
    P.flush()
    es.close()


PL = 80
PC = dict(ffn1=0, mix=16, ffn2=32, gnorm=48, conv=52, dqn=64, dkn=65, subln=66, dsaq=67, dsak=68, lb=69, lam=73)


def pack_params(inp):
    p = np.zeros((128, 2 * PL), np.float32)
    for l in range(2):
        b = l * PL
        p[:, b + PC["ffn1"]:b + PC["ffn1"] + 16] = inp["ffn1_norm"][l].reshape(16, 128).T
        p[:, b + PC["mix"]:b + PC["mix"] + 16] = inp["mix_norm"][l].reshape(16, 128).T
        p[:, b + PC["ffn2"]:b + PC["ffn2"] + 16] = inp["ffn2_norm"][l].reshape(16, 128).T
        p[:, b + PC["gnorm"]:b + PC["gnorm"] + 4] = inp["hgrn_gnorm"][l].reshape(4, 128).T
        for j in range(3):
            p[:, b + PC["conv"] + j * 4:b + PC["conv"] + j * 4 + 4] = inp["conv_w"][l, j].reshape(4, 128).T
        p[:, b + PC["dqn"]] = np.tile(inp["diff_q_norm"][l], 2)
        p[:, b + PC["dkn"]] = np.tile(inp["diff_k_norm"][l], 2)
        p[:, b + PC["subln"]] = inp["diff_subln"][l]
        p[:, b + PC["dsaq"]] = inp["dsa_q_norm"][l]
        p[:, b + PC["dsak"]] = inp["dsa_k_norm"][l]
        p[:, b + PC["lb"]:b + PC["lb"] + 4] = inp["hgrn_lb"][l].reshape(4, 128).T
        p[0:64, b + PC["lam"]:b + PC["lam"] + 4] = inp["diff_lambda"][l].T
    return p


def rel_bucket_np(n):
    n = np.maximum(n, 0)
    nf = np.maximum(n, 1).astype(np.float32)
    large = 16 + (np.log(nf / np.float32(16)) / np.float32(math.log(128 / 16)) * np.float32(16)).astype(np.int32)
    large = np.minimum(large, 31)
    return np.where(n < 16, n, large)


def make_consts():
    cst = {}
    cst["ident"] = np.eye(128, dtype=np.float32)
    sel = np.zeros((33, 3, 256), np.float32)
    for ci, delta in enumerate((0, 128, 16)):
        m = np.arange(255)
        n = m - 127 + delta
        bk = rel_bucket_np(n)
        for mm in range(255):
            if n[mm] < 0:
                sel[32, ci, mm] = 1.0
            else:
                sel[bk[mm], ci, mm] = 1.0
    cst["sel"] = sel
    e31 = np.zeros((32, 128), np.float32)
    e31[31, :] = 1.0
    cst["e31"] = e31
    o64 = np.zeros((128, 128), np.float32)
    o64[:64, :64] = 1.0
    o64[64:, 64:] = 1.0
    cst["ones64"] = o64
    q = np.arange(128)[:, None]
    k = np.arange(128)[None, :]
    cst["fut"] = np.where(k > q, -1e30, 0.0).astype(np.float32)
    cst["causT"] = (q <= k).astype(np.int32)
    return cst


WNAMES = ("ffn1_w_gu", "ffn1_w_down", "w_in", "w_branch", "w_out", "ffn2_w_gu", "ffn2_w_down")


def build(S, topk, G=4, phases=None, dbg=False):
    L = S + 16
    nc = bass.Bass("TRN2", target_bir_lowering=False)
    c = Ctx()
    c.nc = nc
    c.L, c.S, c.topk = L, S, topk
    c.tiles = token_tiles(L)
    c.G_ffn, c.G_proj, c.G_merge = (5, 2, 5) if S >= 2048 else (2, 2, 2)
    c.dbg = dbg

    def din(name, shape, dt=F32):
        return nc.dram_tensor(name, list(shape), dt, kind="ExternalInput").ap()

    def dscr(name, shape, dt=F32):
        kind = "ExternalOutput" if dbg else "Internal"
        return nc.dram_tensor(name, list(shape), dt, kind=kind).ap()

    c.xin = din("xin", [D, L])
    c.params_d = din("params", [128, 2 * PL])
    c.relb_d = din("relb", [32, 8])
    c.ident_d = din("ident", [128, 128])
    c.sel_d = din("sel", [33, 3, 256])
    c.e31_d = din("e31", [32, 128])
    c.ones64_d = din("ones64", [128, 128])
    c.fut_d = din("fut", [128, 128])
    c.causT_d = din("causT", [128, 128], I32)
    c.w = {}
    c.w["ffn1_w_gu"] = din("ffn1_w_gu", [2, D, 2 * DFF])
    c.w["ffn1_w_down"] = din("ffn1_w_down", [2, DFF, D])
    c.w["ffn2_w_gu"] = din("ffn2_w_gu", [2, D, 2 * DFF])
    c.w["ffn2_w_down"] = din("ffn2_w_down", [2, DFF, D])
    c.w["w_in"] = din("w_in", [2, D, 15184])
    c.w["w_dw_rep"] = din("w_dw_rep", [2, D, 1024])
    c.w["w_branch"] = din("w_branch", [2, 4, 512, D])
    c.w["w_out"] = din("w_out", [2, D, D])
    c.yout = nc.dram_tensor("yout", [D, L], F32, kind="ExternalOutput").ap()
    c.xres = dscr("xres", [D, L])
    c.proj = dscr("proj", [NFM, L])
    c.wabs = dscr("wabs", [1024, L], BF16)
    c.vA = dscr("vA", [L, 512], BF16)
    c.vC = dscr("vC", [L, 512], BF16)
    c.vD = dscr("vD", [L, 128], BF16)
    c.wT = dscr("wT", [L, 16])
    c.br = [dscr("br%d" % i, [512, L], BF16) for i in range(4)]
    c.gsc = dscr("gsc", [24, 128, 255])

    es = ExitStack()
    P = Prog(nc)
    c.P = P
    c.ident_f = alloc(c, es, "ident_f", [128, 128], F32)
    c.ident_bf = alloc(c, es, "ident_bf", [128, 128], BF16)
    c.ones_bf = alloc(c, es, "ones_bf", [128, 128], BF16)
    c.ones_f = alloc(c, es, "ones_f", [128, 128], F32)
    c.epsc = alloc(c, es, "epsc", [128, 1], F32)
    c.ones64_bf = alloc(c, es, "ones64_bf", [128, 128], BF16)
    c.params = alloc(c, es, "params_sb", [128, 2 * PL], F32)
    c.BT = alloc(c, es, "BT", [128, 24, 128], BF16)
    c.cbias = alloc(c, es, "cbias", [128, 8], F32)
    c.fut = alloc(c, es, "fut", [128, 128], F32)
    c.causT = alloc(c, es, "causT", [128, 128], I32)
    c.lbs = alloc(c, es, "lbs", [128, 2, 4], F32)
    c.oml = alloc(c, es, "oml", [128, 2, 4], F32)
    c.nlam = alloc(c, es, "nlam", [128, 2], F32)
    setup_phase(c)

    ph = phases
    for l in range(2):
        pb = l * PL
        first = (l == 0)
        if ph is None or ("ffn1_%d" % l) in ph:
            ffn_phase(c, c.xin if first else c.xres, c.xres, c.params[:, pb + PC["ffn1"]:pb + PC["ffn1"] + 16],
                      c.w["ffn1_w_gu"][l], c.w["ffn1_w_down"][l])
        if ph is None or ("proj_%d" % l) in ph:
            proj_phase(c, l)
        if ph is None or ("conv_%d" % l) in ph:
            conv_phase(c, l)
        if ph is None or ("hgrn_%d" % l) in ph:
            hgrn_phase(c, l)
        if ph is None or ("diff_%d" % l) in ph:
            diff_phase(c, l)
        if ph is None or ("dsa_%d" % l) in ph:
            dsa_phase(c, l)
        if ph is None or ("merge_%d" % l) in ph:
            merge_phase(c, l)
        if ph is None or ("ffn2_%d" % l) in ph:
            ffn_phase(c, c.xres, c.yout if l == 1 else c.xres, c.params[:, pb + PC["ffn2"]:pb + PC["ffn2"] + 16],
                      c.w["ffn2_w_gu"][l], c.w["ffn2_w_down"][l])
    P.close()
    es.close()
    return nc, c


def setup_phase(c):
    nc, P = c.nc, c.P
    es = ExitStack()
    P.dma("sp", c.ident_f[:], c.ident_d, [], ["const_i"], "ident_f")
    P.dma("sp", c.params[:], c.params_d, [], ["params"], "params")
    P.dma("sp", c.fut[:], c.fut_d, [], ["const_f"], "fut")
    P.dma("sp", c.causT[:], c.causT_d, [], ["const_c"], "causT")
    o64 = alloc(c, es, "o64f", [128, 128], F32)
    P.dma("sp", o64[:], c.ones64_d, [], ["o64f"], "o64f")
    P.add("dve", lambda e: e.tensor_copy(out=c.ident_bf[:], in_=c.ident_f[:]), ["const_i"], ["const"])
    P.add("dve", lambda e: e.tensor_copy(out=c.ones64_bf[:], in_=o64[:]), ["o64f"], ["const"])
    P.add("dve", lambda e: e.memset(c.ones_bf[:], 1.0), [], ["const"])
    P.add("dve", lambda e: e.memset(c.ones_f[:], 1.0), [], ["const"])
    P.add("dve", lambda e: e.memset(c.epsc[:], EPS), [], ["const"])
    tab = alloc(c, es, "tab", [32, 8], F32)
    sel = alloc(c, es, "selsb", [33, 3, 256], F32)
    e31 = alloc(c, es, "e31sb", [32, 128], F32)
    P.dma("sp", tab[:], c.relb_d, [], ["tab"], "tab")
    P.dma("sp", sel[:], c.sel_d, [], ["sel"], "sel")
    P.dma("sp", e31[:], c.e31_d, [], ["e31"], "e31")
    tabB = Rot("tabB", [alloc(c, es, "tabB%d" % i, [33, 128], F32) for i in range(2)])
    gsb = Rot("gsb", [alloc(c, es, "gsb%d" % i, [128, 255], F32) for i in range(2)])
    btf = Rot("btf", [alloc(c, es, "btf%d" % i, [128, 128], F32) for i in range(2)])
    psG = Rot("psG", [palloc(c, es, "psG%d" % i, [128, 256]) for i in range(2)])
    psc = palloc(c, es, "psc", [128, 8])
    P.add("pe", lambda e: e.matmul(psc[:, :], lhsT=e31[:, :], rhs=tab[:, :], start=True, stop=True), ["e31", "tab"], ["psc"])
    P.add("dve", lambda e: e.tensor_copy(out=c.cbias[:], in_=psc[:]), ["psc"], ["const"])
    for hh in range(8):
        tb, tbk = tabB.next()
        P.add("dve", lambda e, tb=tb: e.memset(tb[:, :], NEG), [], [tbk])
        P.add("dve", lambda e, tb=tb, hh=hh: e.tensor_scalar(out=tb[0:32, :], in0=c.ones_f[0:32, :], scalar1=tab[0:32, hh:hh + 1],
                                                             scalar2=None, op0=ALU.mult), ["tab", "const", tbk], [tbk])
        for ci in range(3):
            pg, pgk = psG.next()
            P.add("pe", lambda e, pg=pg, tb=tb, ci=ci: e.matmul(pg[:, 0:256], lhsT=tb[:, :], rhs=sel[:, ci, :], start=True, stop=True),
                  [tbk, "sel"], [pgk])
            gs, gsk = gsb.next()
            P.add("act", lambda e, gs=gs, pg=pg: e.activation(out=gs[:, :], in_=pg[:, 0:255], func=AF.Copy), [pgk], [gsk])
            idx = hh * 3 + ci
            P.dma("sp", c.gsc[idx], gs[:, :], [gsk], [("gsc", idx)], gsk)
            bt, btk = btf.next()
            skew = bass.AP(tensor=c.gsc.tensor, offset=idx * 128 * 255 + 127, ap=[[254, 128], [1, 128]])
            P.dma("sp", bt[:, :], skew, [("gsc", idx)], [btk], btk)
            P.add("dve", lambda e, bt=bt, idx=idx, hh=hh: e.tensor_scalar(out=c.BT[:, idx, :], in0=bt[:, :], scalar1=c.cbias[:, hh:hh + 1],
                                                                          scalar2=None, op0=ALU.subtract), [btk, "const"], ["const"])
    P.add("dve", lambda e: e.memset(c.lbs[:, 0, :], 0.0), [], ["const"])
    P.add("dve", lambda e: e.memset(c.oml[:, 0, :], 1.0), [], ["const"])
    dl = alloc(c, es, "dl", [128, 4], F32)
    P.add("dve", lambda e: e.tensor_tensor(out=dl[:, :], in0=c.params[:, PL + PC["lb"]:PL + PC["lb"] + 4],
                                           in1=c.params[:, PC["lb"]:PC["lb"] + 4], op=ALU.subtract), ["params"], ["dl"])
    P.add("act", lambda e: e.activation(out=c.lbs[:, 1, :], in_=dl[:, :], func=AF.Sigmoid), ["dl"], ["const"])
    P.add("dve", lambda e: e.tensor_scalar(out=c.oml[:, 1, :], in0=c.lbs[:, 1, :], scalar1=-1.0, scalar2=1.0,
                                           op0=ALU.mult, op1=ALU.add), ["const"], ["const"])
    pr = alloc(c, es, "pr", [128, 4], F32)
    psl = palloc(c, es, "psl", [128, 4])
    el = alloc(c, es, "el", [128, 4], F32)
    for l in range(2):
        b = l * PL + PC["lam"]
        P.add("dve", lambda e, l=l, b=b: e.tensor_tensor(out=pr[:, 2 * l:2 * l + 1], in0=c.params[:, b:b + 1], in1=c.params[:, b + 1:b + 2],
                                                         op=ALU.mult), ["params"], ["pr"])
        P.add("dve", lambda e, l=l, b=b: e.tensor_tensor(out=pr[:, 2 * l + 1:2 * l + 2], in0=c.params[:, b + 2:b + 3], in1=c.params[:, b + 3:b + 4],
                                                         op=ALU.mult), ["params"], ["pr"])
    P.add("pe", lambda e: e.matmul(psl[:, :], lhsT=c.ones_f[:, :], rhs=pr[:, :], start=True, stop=True), ["pr", "const"], ["psl"])
    P.add("act", lambda e: e.activation(out=el[:, :], in_=psl[:, :], func=AF.Exp), ["psl"], ["el"])
    for l in range(2):
        lam_init = 0.8 - 0.6 * math.exp(-0.3 * l)
        P.add("dve", lambda e, l=l, li=lam_init: e.scalar_tensor_tensor(
            out=c.nlam[:, l:l + 1], in0=el[:, 2 * l + 1:2 * l + 2], scalar=-li, in1=el[:, 2 * l:2 * l + 1],
            op0=ALU.add, op1=ALU.subtract), ["el"], ["const"])
    P.flush()
    es.close()


_CACHE = {}


def host_inputs(inp, b, consts, params, w_dw_rep):
    x = inp["x"]
    xin = np.ascontiguousarray(np.concatenate([inp["meta_tokens"].T, x[b].T], axis=1), dtype=np.float32)
    m = {"xin": xin, "params": params, "relb": np.ascontiguousarray(inp["rel_bias"], dtype=np.float32),
         "ident": consts["ident"], "sel": consts["sel"], "e31": consts["e31"], "ones64": consts["ones64"],
         "fut": consts["fut"], "causT": consts["causT"], "w_dw_rep": w_dw_rep}
    for k in WNAMES:
        m[k] = np.ascontiguousarray(inp[k], dtype=np.float32)
    return m


def run(inp, phases=None, dbg=False, G=4, trace=False):
    inp = {k: np.asarray(v) for k, v in inp.items()}
    B, S, _ = inp["x"].shape
    topk = min(256, S // 4)
    nc, c = build(S, topk, G=G, phases=phases, dbg=dbg)
    consts = make_consts()
    params = pack_params(inp)
    w16 = inp["w_in"][:, :, OFF["d_w"]:OFF["d_w"] + 16]
    w_dw_rep = np.ascontiguousarray(np.repeat(w16, 64, axis=2), dtype=np.float32)
    in_maps = [host_inputs(inp, b, consts, params, w_dw_rep) for b in range(B)]
    res = run_bass_kernel_spmd(nc, in_maps, core_ids=list(range(B)), trace=trace)
    return res, c


def kernel(**inputs):
    res, c = run(inputs)
    outs = [r["yout"] for r in res.results]
    y = np.stack([np.ascontiguousarray(o[:, 16:].T) for o in outs], axis=0)
    return y.astype(np.float32)


def fm_chunks():
    skip = [(OFF["a_i"], OFF["a_g"]), (OFF["c_v"], OFF["d_q"]), (OFF["d_v"], OFF["d_qi"])]
    out = []
    c0 = 0
    while c0 < NFM:
        n = min(128, NFM - c0)
        if not any(a <= c0 < b for a, b in skip):
            out.append((c0, n))
        c0 += n
    return out


def proj_phase(c, l):
    nc, P = c.nc, c.P
    es = ExitStack()
    groups = make_groups(c.L, c.G_proj)
    NG = max(g[1] for g in groups)
    pb = l * PL
    nb = norm_bufs(c, es)
    hbuf = alloc(c, es, "hbuf", [128, NCH, NG], BF16)
    wfm = Rot("wfm", [alloc(c, es, "wfm%d" % i, [128, NCH, 128], BF16) for i in range(3)])
    wtm = Rot("wtm", [alloc(c, es, "wtm%d" % i, [128, NCH, 512], BF16) for i in range(2)])
    evs = Rot("ev", [alloc(c, es, "ev%d" % i, [128, 512], F32) for i in range(3)])
    evb = Rot("evb", [alloc(c, es, "evb%d" % i, [128, 512], BF16) for i in range(3)])
    ps = Rot("ps", [palloc(c, es, "ps%d" % i, [128, 512]) for i in range(4)])
    win = fm(c.w["w_in"][l])
    wrep = fm(c.w["w_dw_rep"][l])
    gcol = c.params[:, pb + PC["mix"]:pb + PC["mix"] + 16]
    for (g0, ng, subs, tiles) in groups:
        norm_stage(c, nb, c.xres, gcol, hbuf, "hbuf", g0, ng)
        for (c0, mc) in fm_chunks():
            w, wk = wfm.next()
            P.dma("pool", w[:, :, 0:mc], win[:, :, c0:c0 + mc], [], [wk], wk)
            for (t0, n) in subs:
                off = t0 - g0
                p, pk = ps.next()
                for ch in range(NCH):
                    P.add("pe", lambda e, p=p, w=w, ch=ch, off=off, n=n, mc=mc: e.matmul(
                        p[0:mc, 0:n], lhsT=w[:, ch, 0:mc], rhs=hbuf[:, ch, off:off + n], start=(ch == 0), stop=(ch == NCH - 1)),
                        [wk, "hbuf"], [pk])
                ev, ek = evs.next()
                P.add("act", lambda e, ev=ev, p=p, n=n, mc=mc: e.activation(out=ev[0:mc, 0:n], in_=p[0:mc, 0:n], func=AF.Copy),
                      [pk], [ek])
                P.dma("sp", c.proj[c0:c0 + mc, t0:t0 + n], ev[0:mc, 0:n], [ek], [], ek)
        for r in range(8):
            w, wk = wfm.next()
            P.dma("pool", w[:, :, :], wrep[:, :, r * 128:(r + 1) * 128], [], [wk], wk)
            for (t0, n) in subs:
                off = t0 - g0
                p, pk = ps.next()
                for ch in range(NCH):
                    P.add("pe", lambda e, p=p, w=w, ch=ch, off=off, n=n: e.matmul(
                        p[:, 0:n], lhsT=w[:, ch, :], rhs=hbuf[:, ch, off:off + n], start=(ch == 0), stop=(ch == NCH - 1)),
                        [wk, "hbuf"], [pk])
                ev, ek = evb.next()
                P.add("act", lambda e, ev=ev, p=p, n=n: e.activation(out=ev[:, 0:n], in_=p[:, 0:n], func=AF.Abs), [pk], [ek])
                P.dma("sp", c.wabs[r * 128:(r + 1) * 128, t0:t0 + n], ev[:, 0:n], [ek], [], ek)
        for (c0, ncol, dst, isf32) in ((OFF["a_i"], 512, c.vA, False), (OFF["c_v"], 512, c.vC, False),
                                       (OFF["d_v"], 128, c.vD, False), (OFF["d_w"], 16, c.wT, True)):
            w, wk = wtm.next()
            P.dma("pool", w[:, :, 0:ncol], win[:, :, c0:c0 + ncol], [], [wk], wk)
            for ti in tiles:
                t0, nt = c.tiles[ti]
                off = t0 - g0
                p, pk = ps.next()
                for ch in range(NCH):
                    P.add("pe", lambda e, p=p, w=w, ch=ch, off=off, nt=nt, ncol=ncol: e.matmul(
                        p[0:nt, 0:ncol], lhsT=hbuf[:, ch, off:off + nt], rhs=w[:, ch, 0:ncol], start=(ch == 0), stop=(ch == NCH - 1)),
                        [wk, "hbuf"], [pk])
                if isf32:
                    ev, ek = evs.next()
                else:
                    ev, ek = evb.next()
                P.add("act", lambda e, ev=ev, p=p, nt=nt, ncol=ncol: e.activation(out=ev[0:nt, 0:ncol], in_=p[0:nt, 0:ncol], func=AF.Copy),
                      [pk], [ek])
                P.dma("sp", dst[t0:t0 + nt, 0:ncol], ev[0:nt, 0:ncol], [ek], [], ek)
    P.flush()
    es.close()


def conv_phase(c, l):
    nc, P = c.nc, c.P
    es = ExitStack()
    L = c.L
    pb = l * PL + PC["conv"]
    bb = Rot("bb", [alloc(c, es, "bb%d" % i, [128, L], F32) for i in range(2)])
    bc = Rot("bc", [alloc(c, es, "bc%d" % i, [128, L], F32) for i in range(2)])
    bu = Rot("bu", [alloc(c, es, "bu%d" % i, [128, L], F32) for i in range(2)])
    zc = alloc(c, es, "zc", [128, L + 2], F32)
    yb = alloc(c, es, "yb", [128, L], F32)
    ob = Rot("ob", [alloc(c, es, "ob%d" % i, [128, L], BF16) for i in range(2)])
    P.add("dve", lambda e: e.memset(zc[:, 0:2], 0.0), [], ["zc0"])
    for ch in range(4):
        tb, tbk = bb.next()
        tc_, tck = bc.next()
        tu, tuk = bu.next()
        P.dma("sp", tb[:, :], c.proj[OFF["b_b"] + ch * 128:OFF["b_b"] + (ch + 1) * 128, :], [], [tbk], tbk)
        P.dma("sp", tc_[:, :], c.proj[OFF["b_c"] + ch * 128:OFF["b_c"] + (ch + 1) * 128, :], [], [tck], tck)
        P.dma("sp", tu[:, :], c.proj[OFF["b_u"] + ch * 128:OFF["b_u"] + (ch + 1) * 128, :], [], [tuk], tuk)
        P.add("pool", lambda e, tc_=tc_, tu=tu: e.tensor_tensor(out=zc[:, 2:L + 2], in0=tc_[:, :], in1=tu[:, :], op=ALU.mult),
              [tck, tuk], ["zc"])
        w0 = c.params[:, pb + ch:pb + ch + 1]
        w1 = c.params[:, pb + 4 + ch:pb + 4 + ch + 1]
        w2 = c.params[:, pb + 8 + ch:pb + 8 + ch + 1]
        P.add("dve", lambda e, w0=w0: e.tensor_scalar(out=yb[:, :], in0=zc[:, 2:L + 2], scalar1=w0, scalar2=None, op0=ALU.mult),
              ["zc", "zc0", "params"], ["yb"])
        P.add("dve", lambda e, w1=w1: e.scalar_tensor_tensor(out=yb[:, :], in0=zc[:, 1:L + 1], scalar=w1, in1=yb[:, :],
                                                             op0=ALU.mult, op1=ALU.add), ["zc", "zc0", "yb", "params"], ["yb"])
        P.add("dve", lambda e, w2=w2: e.scalar_tensor_tensor(out=yb[:, :], in0=zc[:, 0:L], scalar=w2, in1=yb[:, :],
                                                             op0=ALU.mult, op1=ALU.add), ["zc", "zc0", "yb", "params"], ["yb"])
        o, ok = ob.next()
        P.add("dve", lambda e, o=o, tb=tb: e.tensor_tensor(out=o[:, :], in0=yb[:, :], in1=tb[:, :], op=ALU.mult), ["yb", tbk], [ok])
        P.dma("sp", c.br[1][ch * 128:(ch + 1) * 128, :], o[:, :], [ok], [], ok)
    P.flush()
    es.close()


def merge_phase(c, l):
    nc, P = c.nc, c.P
    es = ExitStack()
    groups = make_groups(c.L, c.G_merge)
    NG = max(g[1] for g in groups)
    pb = l * PL
    nb = norm_bufs(c, es)
    hbuf = alloc(c, es, "hbuf", [128, NCH, NG], BF16)
    brb = alloc(c, es, "brb", [128, 16, NG], BF16)
    mrg = alloc(c, es, "mrg", [128, NCH, NG], BF16)
    wg = Rot("wg", [alloc(c, es, "wg%d" % i, [128, NCH, 128], BF16) for i in range(3)])
    wb = Rot("wb", [alloc(c, es, "wb%d" % i, [128, 4, 128], BF16) for i in range(3)])
    wo = Rot("wo", [alloc(c, es, "wo%d" % i, [128, NCH, 128], BF16) for i in range(2)])
    sgs = Rot("sg", [alloc(c, es, "sg%d" % i, [128, 512], F32) for i in range(2)])
    macc = Rot("macc", [alloc(c, es, "macc%d" % i, [128, 512], F32) for i in range(3)])
    tmps = Rot("tmp", [alloc(c, es, "tmp%d" % i, [128, 512], F32) for i in range(2)])
    xrs = Rot("xr", [alloc(c, es, "xr%d" % i, [128, 512], F32) for i in range(3)])
    xos = Rot("xo", [alloc(c, es, "xo%d" % i, [128, 512], F32) for i in range(3)])
    psg = Rot("psg", [palloc(c, es, "psg%d" % i, [128, 512]) for i in range(2)])
    psb = Rot("psb", [palloc(c, es, "psb%d" % i, [128, 512]) for i in range(2)])
    pso = Rot("pso", [palloc(c, es, "pso%d" % i, [128, 512]) for i in range(2)])
    win = fm(c.w["w_in"][l])
    wout = fm(c.w["w_out"][l])
    gcol = c.params[:, pb + PC["mix"]:pb + PC["mix"] + 16]
    xv = fm(c.xres)
    for gi_, (g0, ng, subs, tiles) in enumerate(groups):
        if gi_ == 0:
            norm_stage(c, nb, c.xres, gcol, hbuf, "hbuf", g0, ng)
        if gi_ + 1 < len(groups):
            ngen = norm_gen(c, nb, c.xres, gcol, hbuf, "hbuf", groups[gi_ + 1][0], groups[gi_ + 1][1])
        else:
            ngen = iter(())
        for br in range(4):
            P.dma("sp", brb[:, br * 4:(br + 1) * 4, 0:ng], fm(c.br[br])[:, :, g0:g0 + ng], [], [("brb", br)], ("brb", br))
        for fc in range(NCH):
            accs = {}
            for br in range(4):
                w, wk = wg.next()
                gc0 = OFF["gate"] + br * D + fc * 128
                P.dma("pool", w[:, :, :], win[:, :, gc0:gc0 + 128], [], [wk], wk)
                w2, w2k = wb.next()
                P.dma("pool", w2[:, :, :], c.w["w_branch"][l, br].rearrange("(kc p) m -> p kc m", p=128)[:, :, fc * 128:(fc + 1) * 128],
                      [], [w2k], w2k)
                for si, (t0, n) in enumerate(subs):
                    off = t0 - g0
                    pg, pgk = psg.next()
                    pbr, pbk = psb.next()
                    for ch in range(NCH):
                        P.add("pe", lambda e, pg=pg, w=w, ch=ch, off=off, n=n: e.matmul(
                            pg[:, 0:n], lhsT=w[:, ch, :], rhs=hbuf[:, ch, off:off + n], start=(ch == 0), stop=(ch == NCH - 1)),
                            [wk, "hbuf"], [pgk])
                    for kc in range(4):
                        P.add("pe", lambda e, pbr=pbr, w2=w2, kc=kc, br=br, off=off, n=n: e.matmul(
                            pbr[:, 0:n], lhsT=w2[:, kc, :], rhs=brb[:, br * 4 + kc, off:off + n], start=(kc == 0), stop=(kc == 3)),
                            [w2k, ("brb", br)], [pbk])
                    sg, sgk = sgs.next()
                    P.add("act", lambda e, sg=sg, pg=pg, n=n: e.activation(out=sg[:, 0:n], in_=pg[:, 0:n], func=AF.Sigmoid),
                          [pgk], [sgk])
                    if br == 0:
                        accs[si] = macc.next()
                        ma, mak = accs[si]
                        P.add("dve", lambda e, ma=ma, sg=sg, pbr=pbr, n=n: e.tensor_tensor(
                            out=ma[:, 0:n], in0=sg[:, 0:n], in1=pbr[:, 0:n], op=ALU.mult), [sgk, pbk], [mak])
                    else:
                        ma, mak = accs[si]
                        tm, tmk = tmps.next()
                        P.add("dve", lambda e, tm=tm, sg=sg, pbr=pbr, n=n: e.tensor_tensor(
                            out=tm[:, 0:n], in0=sg[:, 0:n], in1=pbr[:, 0:n], op=ALU.mult), [sgk, pbk], [tmk])
                        if br < 3:
                            P.add("dve", lambda e, ma=ma, tm=tm, n=n: e.tensor_tensor(
                                out=ma[:, 0:n], in0=ma[:, 0:n], in1=tm[:, 0:n], op=ALU.add), [mak, tmk], [mak])
                        else:
                            P.add("dve", lambda e, ma=ma, tm=tm, n=n, fc=fc, off=off: e.tensor_tensor(
                                out=mrg[:, fc, off:off + n], in0=ma[:, 0:n], in1=tm[:, 0:n], op=ALU.add), [mak, tmk], [("mrg", fc)])
        for m in range(NCH):
            w, wk = wo.next()
            P.dma("pool", w[:, :, :], wout[:, :, m * 128:(m + 1) * 128], [], [wk], wk)
            for (t0, n) in subs:
                off = t0 - g0
                xr, xrk = xrs.next()
                P.dma("sp", xr[:, 0:n], xv[:, m, t0:t0 + n], [], [xrk], xrk)
                po, pok = pso.next()
                for ch in range(NCH):
                    P.add("pe", lambda e, po=po, w=w, ch=ch, off=off, n=n: e.matmul(
                        po[:, 0:n], lhsT=w[:, ch, :], rhs=mrg[:, ch, off:off + n], start=(ch == 0), stop=(ch == NCH - 1)),
                        [wk, ("mrg", ch)], [pok])
                xo, xok = xos.next()
                P.add("dve", lambda e, xo=xo, po=po, xr=xr, n=n: e.tensor_tensor(
                    out=xo[:, 0:n], in0=po[:, 0:n], in1=xr[:, 0:n], op=ALU.add), [pok, xrk], [xok])
                P.dma("act", xv[:, m, t0:t0 + n], xo[:, 0:n], [xok], [], xok)
            if m >= 2:
                next(ngen, None)
        for _ in ngen:
            pass
    P.flush()
    es.close()


def load_rows_norm(c, bufs, row0, nrows, out_fn, gsc_col, ones_m, gsize, key_out, dup64=False):
    P = c.P
    st, sq, rs, psn = bufs
    for (t0, n) in split_cols(0, c.L, 512):
        s, sk = st.next()
        if dup64:
            P.dma("sp", s[0:64, 0:n], c.proj[row0:row0 + 64, t0:t0 + n], [], [sk], sk)
            P.dma("sp", s[64:128, 0:n], c.proj[row0:row0 + 64, t0:t0 + n], [], [sk], sk)
        else:
            P.dma("sp", s[0:nrows, 0:n], c.proj[row0:row0 + nrows, t0:t0 + n], [], [sk], sk)
        if gsize is None:
            P.add("act", lambda e, s=s, t0=t0, n=n: e.activation(out=out_fn(t0, n), in_=s[:, 0:n], func=AF.Copy), [sk], [key_out])
            continue
        q, qk = sq.next()
        P.add("act", lambda e, q=q, s=s, n=n: e.activation(out=q[:, 0:n], in_=s[:, 0:n], func=AF.Square), [sk], [qk])
        P.add("pe", lambda e, q=q, n=n: e.matmul(psn[:, 0:n], lhsT=ones_m[:, :], rhs=q[:, 0:n], start=True, stop=True),
              [qk, "const"], ["psn"])
        r, rk = rs.next()
        rsqrt_op(c, r[:, 0:n], psn[:, 0:n], 1.0 / gsize, ["psn"], rk)
        P.add("dve", lambda e, s=s, r=r, t0=t0, n=n: e.scalar_tensor_tensor(
            out=out_fn(t0, n), in0=s[:, 0:n], scalar=gsc_col, in1=r[:, 0:n], op0=ALU.mult, op1=ALU.mult),
            [sk, rk, "gsc"], [key_out])


def rows_bufs(c, es):
    st = Rot("st", [alloc(c, es, "st%d" % i, [128, 512], F32) for i in range(2)])
    sq = Rot("sqq", [alloc(c, es, "sqq%d" % i, [128, 512], BF16) for i in range(2)])
    rs = Rot("rs", [alloc(c, es, "rs%d" % i, [128, 512], F32) for i in range(2)])
    psn = palloc(c, es, "psn", [128, 512])
    return (st, sq, rs, psn)


def near_case(i, j):
    if j == i:
        return 0
    if j >= 1 and j == i - 1:
        return 1
    if j == 0 and i == 1:
        return 2
    return None


def load_vtm(c, vsb, src, col0, ncol, key, pitch_view):
    P = c.P
    nt_full = len(c.tiles) - 1
    P.dma("sp", pitch_view(0, 16), src[0:16, col0:col0 + ncol], [], [key], key)
    for ti in range(1, nt_full + 1):
        t0, nt = c.tiles[ti]
        P.dma("sp", pitch_view(ti, nt), src[t0:t0 + nt, col0:col0 + ncol], [], [key], key)


def diff_phase(c, l):
    nc, P = c.nc, c.P
    es = ExitStack()
    L = c.L
    NT = len(c.tiles)
    pb = l * PL
    lam_init = 0.8 - 0.6 * math.exp(-0.3 * l)
    rb = rows_bufs(c, es)
    qC = alloc(c, es, "qC", [128, 4, L], BF16)
    kC = alloc(c, es, "kC", [128, 4, L], BF16)
    vC = alloc(c, es, "vCs", [128, NT, 4, 129], BF16)
    ob = alloc(c, es, "obC", [128, 4, L], BF16)
    gs = alloc(c, es, "gsC", [128, 4], F32)
    zc = alloc(c, es, "zcol", [128, 1], F32)
    pTs = Rot("pT", [alloc(c, es, "pT%d" % i, [128, 512], BF16) for i in range(3)])
    rr = Rot("rr", [alloc(c, es, "rr%d" % i, [128, 4], F32) for i in range(2)])
    ods = Rot("od", [alloc(c, es, "od%d" % i, [128, 128], F32) for i in range(2)])
    junk = alloc(c, es, "junk", [128, 128], F32)
    ons = Rot("on", [alloc(c, es, "on%d" % i, [128, 128], BF16) for i in range(2)])
    pss = Rot("pss", [palloc(c, es, "pss%d" % i, [128, 512]) for i in range(4)])
    pso = Rot("pso", [palloc(c, es, "pso%d" % i, [128, 512]) for i in range(2)])
    pst = palloc(c, es, "pst", [128, 512])
    posb = Rot("posb", [alloc(c, es, "posb%d" % i, [128, 264], F32) for i in range(2)])
    P.add("dve", lambda e: e.tensor_scalar(out=gs[:, 0:1], in0=c.params[:, pb + PC["dqn"]:pb + PC["dqn"] + 1], scalar1=0.125,
                                           scalar2=None, op0=ALU.mult), ["params"], ["gsc"])
    P.add("dve", lambda e: e.tensor_copy(out=gs[:, 1:2], in_=c.params[:, pb + PC["dkn"]:pb + PC["dkn"] + 1]), ["params"], ["gsc"])
    P.add("dve", lambda e: e.tensor_scalar(out=gs[:, 2:3], in0=c.params[:, pb + PC["subln"]:pb + PC["subln"] + 1],
                                           scalar1=1.0 - lam_init, scalar2=None, op0=ALU.mult), ["params"], ["gsc"])
    P.add("dve", lambda e: e.memset(zc[:, :], 0.0), [], ["gsc"])
    P.add("dve", lambda e: e.memset(vC[:, :, :, 128:129], 1.0), [], ["vC1"])
    for h in range(4):
        load_rows_norm(c, rb, OFF["c_q"] + h * 128, 128, lambda t0, n, h=h: qC[:, h, t0:t0 + n], gs[:, 0:1], c.ones64_bf, 64, ("qC", h))
        load_rows_norm(c, rb, OFF["c_k"] + h * 128, 128, lambda t0, n, h=h: kC[:, h, t0:t0 + n], gs[:, 1:2], c.ones64_bf, 64, ("kC", h))
    for ti in range(NT):
        t0, nt = c.tiles[ti]
        P.dma("sp", vC[0:nt, ti, :, 0:128], c.vC[t0:t0 + nt, :].rearrange("t (h e) -> t h e", h=4), [], ["vC"], "vC")
    for i_ in range(NT):
      for h_ in range(4):
        def do_block(i, h, q0, nq):
            pos = [pso.next(), pso.next()]
            groups_ = [(cc, jg) for cc in range(2) for jg in range(0, i + 1, 4)]
            psl = {}

            def emit_s(gi):
                cc, jg = groups_[gi]
                grp = list(range(jg, min(i + 1, jg + 4)))
                ps, psk = pss.next()
                psl[gi] = (ps, psk, grp)
                for sl, j in enumerate(grp):
                    k0, nk = c.tiles[j]
                    case = near_case(i, j)
                    P.add("pe", lambda e, ps=ps, cc=cc, k0=k0, nk=nk, case=case, sl=sl: e.matmul(
                        ps[0:nk, sl * 128:sl * 128 + nq], lhsT=kC[cc * 64:(cc + 1) * 64, h, k0:k0 + nk],
                        rhs=qC[cc * 64:(cc + 1) * 64, h, q0:q0 + nq], start=True, stop=(case is None)), [("kC", h), ("qC", h)], [psk])
                    if case is not None:
                        P.add("pe", lambda e, ps=ps, nk=nk, case=case, sl=sl: e.matmul(
                            ps[0:nk, sl * 128:sl * 128 + nq], lhsT=c.ident_bf[0:nk, 0:nk], rhs=c.BT[0:nk, h * 3 + case, 0:nq],
                            start=False, stop=True), ["const"], [psk])

            def emit_pv(gi):
                cc, jg = groups_[gi]
                ps, psk, grp = psl[gi]
                po, pok = pos[cc]
                W = (len(grp) - 1) * 128 + nq
                pT, pTk = pTs.next()
                P.add("act", lambda e, pT=pT, ps=ps, W=W: e.activation(out=pT[:, 0:W], in_=ps[:, 0:W], func=AF.Exp), [psk], [pTk])
                for sl, j in enumerate(grp):
                    k0, nk = c.tiles[j]
                    P.add("pe", lambda e, po=po, pT=pT, nk=nk, j=j, sl=sl: e.matmul(
                        po[0:nq, 0:129], lhsT=pT[0:nk, sl * 128:sl * 128 + nq], rhs=vC[0:nk, j, h, 0:129], start=(j == 0), stop=(j == i)),
                        [pTk, "vC", "vC1"], [pok])

            emit_s(0)
            for gi in range(len(groups_)):
                if gi + 1 < len(groups_):
                    emit_s(gi + 1)
                emit_pv(gi)
            ob2, ob2k = posb.next()
            P.add("act", lambda e, ob2=ob2: e.activation(out=ob2[0:nq, 0:129], in_=pos[0][0][0:nq, 0:129], func=AF.Copy), [pos[0][1]], [ob2k])
            P.add("act", lambda e, ob2=ob2: e.activation(out=ob2[0:nq, 132:261], in_=pos[1][0][0:nq, 0:129], func=AF.Copy), [pos[1][1]], [ob2k])
            p0, p0k = ob2[:, 0:132], ob2k
            p1, p1k = ob2[:, 132:264], ob2k
            r, rk = rr.next()
            P.add("dve", lambda e, r=r, p0=p0, nq=nq: e.reciprocal(out=r[0:nq, 0:1], in_=p0[0:nq, 128:129]), [p0k], [rk])
            P.add("dve", lambda e, r=r, p1=p1, nq=nq: e.reciprocal(out=r[0:nq, 1:2], in_=p1[0:nq, 128:129]), [p1k], [rk])
            P.add("dve", lambda e, r=r, nq=nq: e.tensor_tensor(out=r[0:nq, 2:3], in0=r[0:nq, 1:2], in1=c.nlam[0:nq, l:l + 1], op=ALU.mult),
                  [rk, "const"], [rk])
            od, odk = ods.next()
            P.add("dve", lambda e, od=od, p0=p0, r=r, nq=nq: e.tensor_scalar(out=od[0:nq, :], in0=p0[0:nq, 0:128], scalar1=r[0:nq, 0:1],
                                                                              scalar2=None, op0=ALU.mult), [p0k, rk], [odk])
            P.add("dve", lambda e, od=od, p1=p1, r=r, nq=nq: e.scalar_tensor_tensor(
                out=od[0:nq, :], in0=p1[0:nq, 0:128], scalar=r[0:nq, 2:3], in1=od[0:nq, :], op0=ALU.mult, op1=ALU.add),
                [p1k, rk, odk], [odk])
            P.add("dve", lambda e, od=od, nq=nq: e.tensor_tensor(out=junk[0:nq, :], in0=od[0:nq, :], in1=od[0:nq, :], op=ALU.mult),
                  [odk], ["junk"])
            P.add("dve", lambda e, r=r, nq=nq: e.tensor_reduce(out=r[0:nq, 3:4], in_=junk[0:nq, :], axis=AX.X, op=ALU.add),
                  ["junk", rk], [rk])
            rsqrt_op(c, r[0:nq, 3:4], r[0:nq, 3:4], 1.0 / 128, [rk], rk, np_=nq)
            on, onk = ons.next()
            P.add("dve", lambda e, on=on, od=od, r=r, nq=nq: e.tensor_scalar(out=on[0:nq, :], in0=od[0:nq, :], scalar1=r[0:nq, 3:4],
                                                                              scalar2=None, op0=ALU.mult), [odk, rk], [onk])
            P.add("pe", lambda e, on=on, nq=nq: e.matmul(pst[:, 0:nq], lhsT=on[0:nq, :], rhs=c.ident_bf[0:nq, 0:nq], start=True, stop=True),
                  [onk, "const"], ["pst"])
            P.add("act", lambda e, h=h, q0=q0, nq=nq: e.activation(out=ob[:, h, q0:q0 + nq], in_=pst[:, 0:nq], func=AF.Copy,
                                                                    scale=gs[:, 2:3]), ["pst", "gsc"], ["obC"])
        do_block(i_, h_, c.tiles[i_][0], c.tiles[i_][1])
    P.dma("sp", fm(c.br[2]), ob[:, :, :], ["obC"], [], "obC")
    P.flush()
    es.close()


def hgrn_phase(c, l):
    nc, P = c.nc, c.P
    es = ExitStack()
    L = c.L
    NT = len(c.tiles)
    pb = l * PL
    zf = alloc(c, es, "zf", [128, L], F32)
    bb = alloc(c, es, "bb", [128, L], F32)
    kf = alloc(c, es, "kf", [128, L], F32)
    qf = alloc(c, es, "qf", [128, L], F32)
    tmp = alloc(c, es, "tmpE", [128, L], F32)
    qt = alloc(c, es, "qt", [128, L], BF16)
    qh = alloc(c, es, "qh", [128, L], BF16)
    kh = alloc(c, es, "kh", [128, L], BF16)
    khT = alloc(c, es, "khT", [128, NT, 128], BF16)
    vA = alloc(c, es, "vAs", [128, NT, 128], BF16)
    osb = alloc(c, es, "osb", [128, L], BF16)
    obr = alloc(c, es, "obr", [128, L], BF16)
    e1 = alloc(c, es, "e1", [128, NT], F32)
    e2 = alloc(c, es, "e2", [128, NT], F32)
    Sf = alloc(c, es, "Sf", [128, 128], F32)
    St = alloc(c, es, "St", [128, 128], F32)
    Sb = Rot("Sb", [alloc(c, es, "Sb%d" % i, [128, 128], BF16) for i in range(2)])
    Pm = Rot("Pm", [alloc(c, es, "Pm%d" % i, [128, 128], BF16) for i in range(2)])
    sq = Rot("sqh", [alloc(c, es, "sqh%d" % i, [128, 512], BF16) for i in range(2)])
    rs = Rot("rsh", [alloc(c, es, "rsh%d" % i, [128, 512], F32) for i in range(2)])
    pss = Rot("pss", [palloc(c, es, "pss%d" % i, [128, 512]) for i in range(2)])
    pso = Rot("pso", [palloc(c, es, "pso%d" % i, [128, 512]) for i in range(2)])
    psd = Rot("psd", [palloc(c, es, "psd%d" % i, [128, 512]) for i in range(2)])
    pstb = palloc(c, es, "pstb", [128, 128], BF16)
    psn = palloc(c, es, "psnh", [128, 512])
    for k in range(2):
        P.add("pool", lambda e, k=k: e.memset(Pm.t[k][:, :], 0.0), [], [("Pm", k)])
    for h in range(4):
        lbc = c.lbs[:, l, h:h + 1]
        omc = c.oml[:, l, h:h + 1]
        P.dma("sp", zf[:, :], c.proj[OFF["a_f"] + h * 128:OFF["a_f"] + (h + 1) * 128, :], [], ["zf"], "zf")
        P.dma("sp", qf[:, :], c.proj[OFF["a_q"] + h * 128:OFF["a_q"] + (h + 1) * 128, :], [], ["qf"], "qf")
        load_vtm(c, vA, c.vA, h * 128, 128, "vA", lambda ti, nt: vA[0:nt, ti, :])
        P.add("act", lambda e: e.activation(out=zf[:, :], in_=zf[:, :], func=AF.Sigmoid), ["zf"], ["zf"])
        P.add("dve", lambda e, lbc=lbc, omc=omc: e.tensor_scalar(out=zf[:, :], in0=zf[:, :], scalar1=omc, scalar2=lbc,
                                                                 op0=ALU.mult, op1=ALU.add), ["zf", "const"], ["zf"])
        P.add("dve", lambda e: e.tensor_scalar(out=kf[:, :], in0=zf[:, :], scalar1=-1.0, scalar2=1.0, op0=ALU.mult, op1=ALU.add),
              ["zf"], ["kf"])
        P.add("act", lambda e: e.activation(out=zf[:, :], in_=zf[:, :], func=AF.Ln), ["zf", "kf"], ["zf"])
        for ti in range(NT):
            t0, nt = c.tiles[ti]
            P.add("dve", lambda e, t0=t0, nt=nt: e.tensor_tensor_scan(out=bb[:, t0:t0 + nt], data0=c.ones_f[:, 0:nt], data1=zf[:, t0:t0 + nt],
                                                                      initial=0.0, op0=ALU.mult, op1=ALU.add), ["zf", "const"], ["bb"])
        for ti in range(NT):
            t0, nt = c.tiles[ti]
            mid = t0 + nt // 2
            P.add("dve", lambda e, t0=t0, nt=nt, mid=mid: e.tensor_scalar(out=zf[:, t0:t0 + nt], in0=bb[:, t0:t0 + nt], scalar1=bb[:, mid:mid + 1],
                                                                          scalar2=None, op0=ALU.subtract), ["bb", "zf"], ["zf"])
        for ti in range(NT):
            t0, nt = c.tiles[ti]
            P.add("act", lambda e, ti=ti, t0=t0, nt=nt: e.activation(out=e1[:, ti:ti + 1], in_=bb[:, t0 + nt - 1:t0 + nt], func=AF.Exp),
                  ["bb"], ["e1"])
            P.add("act", lambda e, ti=ti, t0=t0, nt=nt: e.activation(out=e2[:, ti:ti + 1], in_=zf[:, t0 + nt - 1:t0 + nt], func=AF.Exp),
                  ["zf"], ["e2"])
        P.add("act", lambda e: e.activation(out=qf[:, :], in_=qf[:, :], func=AF.Silu), ["qf"], ["qf"])
        P.add("act", lambda e: e.activation(out=tmp[:, :], in_=bb[:, :], func=AF.Exp), ["bb"], ["tmp"])
        P.add("dve", lambda e: e.tensor_tensor(out=qt[:, :], in0=qf[:, :], in1=tmp[:, :], op=ALU.mult), ["qf", "tmp"], ["qt"])
        P.add("act", lambda e: e.activation(out=tmp[:, :], in_=zf[:, :], func=AF.Exp), ["zf", "qt"], ["tmp"])
        P.add("dve", lambda e: e.tensor_tensor(out=qh[:, :], in0=qf[:, :], in1=tmp[:, :], op=ALU.mult), ["qf", "tmp"], ["qh"])
        P.add("act", lambda e: e.activation(out=tmp[:, :], in_=zf[:, :], func=AF.Exp, scale=-1.0), ["zf", "qh"], ["tmp"])
        P.add("dve", lambda e: e.tensor_tensor(out=kh[:, :], in0=kf[:, :], in1=tmp[:, :], op=ALU.mult), ["kf", "tmp"], ["kh"])
        for ti in range(NT):
            t0, nt = c.tiles[ti]
            P.add("pe", lambda e, t0=t0, nt=nt: e.transpose(out=pstb[0:nt, :], in_=kh[:, t0:t0 + nt], identity=c.ident_bf[:, :]),
                  ["kh", "const"], ["pstb"])
            P.add("act", lambda e, ti=ti, nt=nt: e.activation(out=khT[0:nt, ti, :], in_=pstb[0:nt, :], func=AF.Copy), ["pstb"], ["khT"])
        P.add("dve", lambda e: e.memset(Sf[:, :], 0.0), [], ["Sf"])
        sb, sbk = Sb.next()
        P.add("pool", lambda e, sb=sb: e.memset(sb[:, :], 0.0), [], [sbk])
        for ti in range(NT):
            t0, nt = c.tiles[ti]
            ps, psk = pss.next()
            P.add("pe", lambda e, ps=ps, t0=t0, nt=nt: e.matmul(ps[0:nt, 0:nt], lhsT=kh[:, t0:t0 + nt], rhs=qh[:, t0:t0 + nt], start=True, stop=True),
                  ["kh", "qh"], [psk])
            pm, pmk = Pm.next()
            P.add("dve", lambda e, pm=pm, ps=ps, nt=nt: e.copy_predicated(out=pm[0:nt, 0:nt], mask=c.causT[0:nt, 0:nt], data=ps[0:nt, 0:nt]),
                  [psk, "const_c"], [pmk])
            po, pok = pso.next()
            P.add("pe", lambda e, po=po, pm=pm, ti=ti, nt=nt: e.matmul(po[:, 0:nt], lhsT=vA[0:nt, ti, :], rhs=pm[0:nt, 0:nt], start=True, stop=False),
                  ["vA", pmk], [pok])
            P.add("pe", lambda e, po=po, sb=sb, t0=t0, nt=nt: e.matmul(po[:, 0:nt], lhsT=sb[:, :], rhs=qt[:, t0:t0 + nt], start=False, stop=True),
                  [sbk, "qt"], [pok])
            P.add("act", lambda e, po=po, t0=t0, nt=nt: e.activation(out=osb[:, t0:t0 + nt], in_=po[:, 0:nt], func=AF.Copy), [pok], ["osb"])
            pd, pdk = psd.next()
            P.add("pe", lambda e, pd=pd, ti=ti, nt=nt: e.matmul(pd[:, 0:128], lhsT=khT[0:nt, ti, :], rhs=vA[0:nt, ti, :], start=True, stop=True),
                  ["khT", "vA"], [pdk])
            P.add("dve", lambda e, ti=ti: e.tensor_scalar(out=St[:, :], in0=Sf[:, :], scalar1=e1[:, ti:ti + 1], scalar2=None, op0=ALU.mult),
                  ["Sf", "e1"], ["St"])
            P.add("dve", lambda e, pd=pd, ti=ti: e.scalar_tensor_tensor(out=Sf[:, :], in0=pd[:, 0:128], scalar=e2[:, ti:ti + 1], in1=St[:, :],
                                                                        op0=ALU.mult, op1=ALU.add), [pdk, "St", "e2"], ["Sf"])
            sb, sbk = Sb.next()
            P.add("act", lambda e, sb=sb: e.activation(out=sb[:, :], in_=Sf[:, :], func=AF.Copy), ["Sf"], [sbk])
        P.dma("sp", qf[:, :], c.proj[OFF["a_g"] + h * 128:OFF["a_g"] + (h + 1) * 128, :], [], ["qf"], "qf")
        P.add("act", lambda e: e.activation(out=qf[:, :], in_=qf[:, :], func=AF.Silu), ["qf"], ["qf"])
        gcol = c.params[:, pb + PC["gnorm"] + h:pb + PC["gnorm"] + h + 1]
        for (t0, n) in split_cols(0, L, 512):
            q, qk = sq.next()
            P.add("act", lambda e, q=q, t0=t0, n=n: e.activation(out=q[:, 0:n], in_=osb[:, t0:t0 + n], func=AF.Square), ["osb"], [qk])
            P.add("pe", lambda e, q=q, n=n: e.matmul(psn[:, 0:n], lhsT=c.ones_bf[:, :], rhs=q[:, 0:n], start=True, stop=True),
                  [qk, "const"], ["psn"])
            r, rk = rs.next()
            rsqrt_op(c, r[:, 0:n], psn[:, 0:n], 1.0 / 128, ["psn"], rk)
            P.add("dve", lambda e, r=r, t0=t0, n=n, gcol=gcol: e.scalar_tensor_tensor(
                out=r[:, 0:n], in0=osb[:, t0:t0 + n], scalar=gcol, in1=r[:, 0:n], op0=ALU.mult, op1=ALU.mult),
                ["osb", rk, "params"], [rk])
            P.add("dve", lambda e, r=r, t0=t0, n=n: e.tensor_tensor(out=obr[:, t0:t0 + n], in0=r[:, 0:n], in1=qf[:, t0:t0 + n], op=ALU.mult),
                  [rk, "qf"], ["obr"])
        P.dma("sp", c.br[0][h * 128:(h + 1) * 128, :], obr[:, :], ["obr"], [], "obr")
    P.flush()
    es.close()


def dsa_phase(c, l):
    nc, P = c.nc, c.P
    es = ExitStack()
    L = c.L
    NT = len(c.tiles)
    pb = l * PL
    topk = c.topk
    rb = rows_bufs(c, es)
    st, _sq, _rs, _psn = rb
    qD = alloc(c, es, "qD", [128, 4, L], BF16)
    kD = alloc(c, es, "kD", [128, L], BF16)
    vD = alloc(c, es, "vDs", [128, NT, 129], BF16)
    kiD = alloc(c, es, "kiD", [128, L], BF16)
    sgn = alloc(c, es, "sgn", [128, NT, 16], F32)
    idx = alloc(c, es, "idx", [128, L], F32)
    work = alloc(c, es, "work", [128, L], F32)
    maskb = alloc(c, es, "maskb", [128, L], BF16)
    mT = alloc(c, es, "mT", [128, NT, 128], BF16)
    gs = alloc(c, es, "gsD", [128, 2], F32)
    zc = alloc(c, es, "zcolD", [128, 1], F32)
    m8 = alloc(c, es, "m8", [128, 8], F32)
    th = alloc(c, es, "th", [128, 1], F32)
    rls = Rot("rl", [alloc(c, es, "rl%d" % i, [128, 512], F32) for i in range(2)])
    eTs = Rot("eT", [alloc(c, es, "eT%d" % i, [128, 512], BF16) for i in range(2)])
    pTs = Rot("pTD", [alloc(c, es, "pTD%d" % i, [128, 512], BF16) for i in range(2)])
    rr = Rot("rrD", [alloc(c, es, "rrD%d" % i, [128, 1], F32) for i in range(2)])
    ons = Rot("onD", [alloc(c, es, "onD%d" % i, [128, 128], BF16) for i in range(2)])
    ocs = Rot("ocD", [alloc(c, es, "ocD%d" % i, [128, 128], BF16) for i in range(3)])
    psi = Rot("psi", [palloc(c, es, "psi%d" % i, [128, 512]) for i in range(2)])
    pss = Rot("pssD", [palloc(c, es, "pssD%d" % i, [128, 512]) for i in range(2)])
    pso = Rot("psoD", [palloc(c, es, "psoD%d" % i, [128, 512]) for i in range(2)])
    pstb = palloc(c, es, "pstbD", [128, 128], BF16)
    P.add("dve", lambda e: e.tensor_scalar(out=gs[:, 0:1], in0=c.params[:, pb + PC["dsaq"]:pb + PC["dsaq"] + 1], scalar1=128 ** -0.5,
                                           scalar2=None, op0=ALU.mult), ["params"], ["gsc"])
    P.add("dve", lambda e: e.tensor_copy(out=gs[:, 1:2], in_=c.params[:, pb + PC["dsak"]:pb + PC["dsak"] + 1]), ["params"], ["gsc"])
    P.add("dve", lambda e: e.memset(zc[:, :], 0.0), [], ["gsc"])
    P.add("dve", lambda e: e.memset(vD[:, :, 128:129], 1.0), [], ["vD1"])
    for h in range(4):
        load_rows_norm(c, rb, OFF["d_q"] + h * 128, 128, lambda t0, n, h=h: qD[:, h, t0:t0 + n], gs[:, 0:1], c.ones_bf, 128, ("qD", h))
    load_rows_norm(c, rb, OFF["d_k"], 128, lambda t0, n: kD[:, t0:t0 + n], gs[:, 1:2], c.ones_bf, 128, "kD")
    load_rows_norm(c, rb, OFF["d_ki"], 64, lambda t0, n: kiD[:, t0:t0 + n], None, None, None, "kiD", dup64=True)
    load_vtm(c, vD, c.vD, 0, 128, "vD", lambda ti, nt: vD[0:nt, ti, 0:128])
    load_vtm(c, sgn, c.wT, 0, 16, "sgn", lambda ti, nt: sgn[0:nt, ti, :])
    P.add("act", lambda e: e.activation(out=sgn[:, :, :], in_=sgn[:, :, :], func=AF.Sign), ["sgn"], ["sgn"])
    qsts = Rot("qst", [alloc(c, es, "qst%d" % i, [128, 8, 128], F32) for i in range(2)])
    wsts = Rot("wst", [alloc(c, es, "wst%d" % i, [128, 8, 128], BF16) for i in range(2)])
    qits = Rot("qit", [alloc(c, es, "qit%d" % i, [128, 8, 128], BF16) for i in range(2)])
    mTs = [mT, alloc(c, es, "mT2", [128, NT, 128], BF16)]
    osb = [alloc(c, es, "osbD%d" % i, [128, 132], F32) for i in range(4)]
    qiv = c.proj[OFF["d_qi"]:OFF["d_qi"] + 1024, :].rearrange("(r p) t -> p r t", p=128)
    wav = c.wabs.rearrange("(r p) t -> p r t", p=128)

    def stage_a1(i):
        q0, nq = c.tiles[i]
        K = q0 + nq
        qs, qsk = qsts.next()
        ws, wsk = wsts.next()
        P.dma("sp", qs[:, :, 0:nq], qiv[:, :, q0:q0 + nq], [], [qsk], qsk)
        P.dma("sp", ws[:, :, 0:nq], wav[:, :, q0:q0 + nq], [], [wsk], wsk)
        qit, qitk = qits.next()
        P.add("pool", lambda e: e.tensor_tensor(out=qit[:, :, 0:nq], in0=qs[:, :, 0:nq], in1=ws[:, :, 0:nq], op=ALU.mult),
              [qsk, wsk], [qitk])
        for (kb0, kn) in split_cols(0, K, 512):
            for hi in range(16):
                r, half = hi // 2, hi % 2
                p, pk = psi.next()
                P.add("pe", lambda e, p=p, r=r, half=half, kb0=kb0, kn=kn: e.matmul(
                    p[0:nq, 0:kn], lhsT=qit[half * 64:(half + 1) * 64, r, 0:nq], rhs=kiD[half * 64:(half + 1) * 64, kb0:kb0 + kn],
                    start=True, stop=True), [qitk, "kiD"], [pk])
                rl, rlk = rls.next()
                P.add("act", lambda e, rl=rl, p=p, kn=kn: e.activation(out=rl[0:nq, 0:kn], in_=p[0:nq, 0:kn], func=AF.Relu), [pk], [rlk])
                acc, acck = (idx, "idx") if hi % 2 == 0 else (work, "work")
                if hi < 2:
                    P.add("dve", lambda e, rl=rl, hi=hi, kb0=kb0, kn=kn, acc=acc: e.tensor_scalar(
                        out=acc[0:nq, kb0:kb0 + kn], in0=rl[0:nq, 0:kn], scalar1=sgn[0:nq, i, hi:hi + 1], scalar2=None, op0=ALU.mult),
                        [rlk, "sgn"], [acck])
                else:
                    P.add("dve", lambda e, rl=rl, hi=hi, kb0=kb0, kn=kn, acc=acc: e.scalar_tensor_tensor(
                        out=acc[0:nq, kb0:kb0 + kn], in0=rl[0:nq, 0:kn], scalar=sgn[0:nq, i, hi:hi + 1], in1=acc[0:nq, kb0:kb0 + kn],
                        op0=ALU.mult, op1=ALU.add), [rlk, "sgn", acck], [acck])
        P.add("dve", lambda e: e.tensor_tensor(out=idx[0:nq, 0:K], in0=idx[0:nq, 0:K], in1=work[0:nq, 0:K], op=ALU.add),
              ["idx", "work"], ["idx"])
        P.add("dve", lambda e: e.tensor_tensor(out=idx[0:nq, q0:q0 + nq], in0=idx[0:nq, q0:q0 + nq], in1=c.fut[0:nq, 0:nq], op=ALU.add),
              ["idx", "const_f"], ["idx"])

    def stage_a2(i):
        q0, nq = c.tiles[i]
        K = q0 + nq
        mTc = mTs[i % 2]
        if K > topk:
            for rd in range(topk // 8):
                src = idx if rd == 0 else work
                P.add("dve", lambda e, src=src: e.max(out=m8[0:nq, :], in_=src[0:nq, 0:K]), ["idx", "work"], ["m8"])
                if rd < topk // 8 - 1:
                    P.add("dve", lambda e, src=src: e.match_replace(out=work[0:nq, 0:K], in_to_replace=m8[0:nq, :],
                                                                    in_values=src[0:nq, 0:K], imm_value=-1e30),
                          ["idx", "work", "m8"], ["work"])
            P.add("dve", lambda e: e.tensor_copy(out=th[0:nq, :], in_=m8[0:nq, 7:8]), ["m8"], ["th"])
        else:
            P.add("dve", lambda e: e.memset(th[0:nq, :], -1e29), ["m8"], ["th"])
        P.add("dve", lambda e: e.tensor_scalar(out=maskb[0:nq, 0:K], in0=idx[0:nq, 0:K], scalar1=th[0:nq, 0:1], scalar2=None,
                                               op0=ALU.is_ge), ["idx", "th"], ["maskb"])
        for j in range(i + 1):
            k0, nk = c.tiles[j]
            P.add("pe", lambda e, k0=k0, nk=nk: e.transpose(out=pstb[0:nk, 0:nq], in_=maskb[0:nq, k0:k0 + nk], identity=c.ident_bf[0:nq, 0:nq]),
                  ["maskb", "const"], ["pstbD"])
            P.add("act", lambda e, j=j, nk=nk: e.activation(out=mTc[0:nk, j, 0:nq], in_=pstb[0:nk, 0:nq], func=AF.Copy), ["pstbD"], [("mT", i % 2, j)])

    def stage_b_main(i):
        q0, nq = c.tiles[i]
        mTc = mTs[i % 2]
        for h in range(4):
            po, pok = pso.next()
            for jg in range(0, i + 1, 4):
                grp = list(range(jg, min(i + 1, jg + 4)))
                ps, psk = pss.next()
                for sl, j in enumerate(grp):
                    k0, nk = c.tiles[j]
                    case = near_case(i, j)
                    P.add("pe", lambda e, ps=ps, h=h, k0=k0, nk=nk, case=case, sl=sl: e.matmul(
                        ps[0:nk, sl * 128:sl * 128 + nq], lhsT=kD[:, k0:k0 + nk], rhs=qD[:, h, q0:q0 + nq], start=True, stop=(case is None)),
                        ["kD", ("qD", h)], [psk])
                    if case is not None:
                        P.add("pe", lambda e, ps=ps, nk=nk, h=h, case=case, sl=sl: e.matmul(
                            ps[0:nk, sl * 128:sl * 128 + nq], lhsT=c.ident_bf[0:nk, 0:nk], rhs=c.BT[0:nk, (4 + h) * 3 + case, 0:nq],
                            start=False, stop=True), ["const"], [psk])
                W = (len(grp) - 1) * 128 + nq
                eT, eTk = eTs.next()
                P.add("act", lambda e, eT=eT, ps=ps, W=W: e.activation(out=eT[:, 0:W], in_=ps[:, 0:W], func=AF.Exp), [psk], [eTk])
                pT, pTk = pTs.next()
                if nq == 128:
                    ng_ = len(grp)
                    P.add("pool", lambda e, pT=pT, eT=eT, jg=jg, ng_=ng_: e.tensor_tensor(
                        out=pT[:, 0:ng_ * 128].rearrange("p (s q) -> p s q", q=128), in0=eT[:, 0:ng_ * 128].rearrange("p (s q) -> p s q", q=128),
                        in1=mTc[:, jg:jg + ng_, :], op=ALU.mult), [eTk] + [("mT", i % 2, j) for j in grp], [pTk])
                else:
                    P.add("pool", lambda e, pT=pT, eT=eT: e.tensor_tensor(
                        out=pT[0:16, 0:nq], in0=eT[0:16, 0:nq], in1=mTc[0:16, 0, 0:nq], op=ALU.mult), [eTk, ("mT", i % 2, 0)], [pTk])
                for sl, j in enumerate(grp):
                    k0, nk = c.tiles[j]
                    P.add("pe", lambda e, po=po, pT=pT, nk=nk, j=j, sl=sl: e.matmul(
                        po[0:nq, 0:129], lhsT=pT[0:nk, sl * 128:sl * 128 + nq], rhs=vD[0:nk, j, 0:129], start=(j == 0), stop=(j == i)),
                        [pTk, "vD", "vD1"], [pok])
            P.add("act", lambda e, po=po, h=h: e.activation(out=osb[h][0:nq, 0:129], in_=po[0:nq, 0:129], func=AF.Copy), [pok], [("osbD", h)])

    def stage_b_fin(i):
        q0, nq = c.tiles[i]
        for h in range(4):
            r, rk = rr.next()
            P.add("dve", lambda e, r=r, h=h: e.reciprocal(out=r[0:nq, 0:1], in_=osb[h][0:nq, 128:129]), [("osbD", h)], [rk])
            on, onk = ons.next()
            P.add("dve", lambda e, on=on, r=r, h=h: e.tensor_scalar(out=on[0:nq, :], in0=osb[h][0:nq, 0:128], scalar1=r[0:nq, 0:1],
                                                                     scalar2=None, op0=ALU.mult), [("osbD", h), rk], [onk])
            ps2, ps2k = pss.next()
            P.add("pe", lambda e, ps2=ps2, on=on: e.matmul(ps2[:, 0:nq], lhsT=on[0:nq, :], rhs=c.ident_bf[0:nq, 0:nq], start=True, stop=True),
                  [onk, "const"], [ps2k])
            oc, ock = ocs.next()
            P.add("act", lambda e, oc=oc, ps2=ps2: e.activation(out=oc[:, 0:nq], in_=ps2[:, 0:nq], func=AF.Copy), [ps2k], [ock])
            P.dma("sp", c.br[3][h * 128:(h + 1) * 128, q0:q0 + nq], oc[:, 0:nq], [ock], [], ock)

    stage_a1(0)
    stage_a2(0)
    for i in range(NT):
        if i + 1 < NT:
            stage_a1(i + 1)
        stage_b_main(i)
        if i + 1 < NT:
            stage_a2(i + 1)
        stage_b_fin(i)
    P.flush()
    es.close()
```

```python
import numpy as np
from contextlib import ExitStack
import concourse.bass as bass
import concourse.mybir as mybir
from concourse.bass_utils import run_bass_kernel_spmd

F32 = mybir.dt.float32
BF16 = mybir.dt.bfloat16
I32 = mybir.dt.int32
ALU = mybir.AluOpType
AF = mybir.ActivationFunctionType
AX = mybir.AxisListType

ENGS = ("pe", "act", "dve", "pool", "sp")


class Prog:
    def __init__(self, nc):
        self.nc = nc
        self.es = ExitStack()
        self.eng_h = {"pe": nc.tensor, "act": nc.scalar, "dve": nc.vector,
                      "pool": nc.gpsimd, "sp": nc.sync}
        self.sem = {e: self.es.enter_context(nc.semaphore("s_" + e)) for e in ENGS}
        self.sig_cnt = {e: 0 for e in ENGS}
        self.waited = {e: {} for e in ENGS}
        self.dma_sems = {}
        self.semobj = {("eng", e): self.sem[e] for e in ENGS}
        self.ops = []
        self.lastw = {}
        self.readers = {}
        self.n_inst = 0
        self.phase_es = None

    def add(self, eng, fn, reads=(), writes=(), dma_key=None):
        idx = len(self.ops)
        is_dma = dma_key is not None
        deps = {}
        for r in reads:
            w = self.lastw.get(r)
            if w is not None:
                deps[w] = "raw"
        for wk in writes:
            w = self.lastw.get(wk)
            if w is not None and w not in deps:
                deps[w] = "waw"
            for r in self.readers.get(wk, {}).values():
                if r not in deps:
                    deps[r] = "war"
        op = dict(eng=eng, fn=fn, deps=deps, dma_key=dma_key, sig=False, dma_val=None)
        if is_dma:
            if dma_key not in self.dma_sems:
                s = self.es.enter_context(self.nc.semaphore("d_%d" % len(self.dma_sems)))
                self.dma_sems[dma_key] = [s, 0]
            self.dma_sems[dma_key][1] += 16
            op["dma_val"] = self.dma_sems[dma_key][1]
        self.ops.append(op)
        rk = ("dma", dma_key) if is_dma else eng
        for r in reads:
            self.readers.setdefault(r, {})[rk] = idx
        for wk in writes:
            self.lastw[wk] = idx
            self.readers[wk] = {}
        return idx

    def dma(self, q, out, in_, reads, writes, key):
        self.add(q, lambda e: e.dma_start(out=out, in_=in_), reads, writes, dma_key=key)

    def flush(self):
        ops = self.ops
        if not ops:
            return
        for i, op in enumerate(ops):
            waits = []
            for d, kind in op["deps"].items():
                od = ops[d]
                if od["dma_key"] is not None:
                    waits.append(("dma", od["dma_key"], od["dma_val"]))
                    continue
                if od["eng"] == op["eng"] and op["dma_key"] is None:
                    if op["eng"] == "pe" or kind != "raw":
                        continue
                od["sig"] = True
                waits.append(("eng", od["eng"], d))
            op["waits"] = waits
        last = {}
        for i, op in enumerate(ops):
            if op["dma_key"] is None:
                last[op["eng"]] = i
        for e, i in last.items():
            ops[i]["sig"] = True
        cnt = dict(self.sig_cnt)
        for op in ops:
            if op["dma_key"] is None and op["sig"]:
                cnt[op["eng"]] += 1
                op["sig_val"] = cnt[op["eng"]]
        end_cnt = cnt

        def emit_engine(ename):
            def body(eh):
                waited = self.waited[ename]
                for op in ops:
                    if op["eng"] != ename:
                        continue
                    for w in op["waits"]:
                        if w[0] == "dma":
                            semk = ("dma", w[1]); val = w[2]
                            s = self.dma_sems[w[1]][0]
                        else:
                            semk = ("eng", w[1]); val = ops[w[2]]["sig_val"]
                            s = self.sem[w[1]]
                        if waited.get(semk, 0) >= val:
                            continue
                        waited[semk] = val
                        eh.wait_ge(s, val)
                        self.n_inst += 1
                    ins = op["fn"](eh)
                    self.n_inst += 1
                    if op["dma_key"] is not None:
                        ins.then_inc(self.dma_sems[op["dma_key"]][0], 16)
                    elif op["sig"]:
                        ins.then_inc(self.sem[ename], 1)
                for e2 in ENGS:
                    if e2 == ename:
                        continue
                    v = end_cnt[e2]
                    if v > 0 and waited.get(("eng", e2), 0) < v:
                        waited[("eng", e2)] = v
                        eh.wait_ge(self.sem[e2], v)
                for k, (s, c) in self.dma_sems.items():
                    if c > 0 and waited.get(("dma", k), 0) < c:
                        waited[("dma", k)] = c
                        eh.wait_ge(s, c)
            return body

        with self.nc.Block() as block:
            block.tensor(emit_engine("pe"))
            block.scalar(emit_engine("act"))
            block.vector(emit_engine("dve"))
            block.gpsimd(emit_engine("pool"))
            block.sync(emit_engine("sp"))
        self.sig_cnt = end_cnt
        self.ops = []
        self.lastw = {}
        self.readers = {}

    def close(self):
        self.flush()
        self.es.close()
import math


D = 2048
NCH = 16
DFF = 5632
NJ = 44
EPS = 1e-6
OFF = dict(a_q=0, a_f=512, a_i=1024, a_g=1536, b_b=2048, b_c=2560, b_u=3072, c_q=3584, c_k=4096,
           c_v=4608, d_q=5120, d_k=5632, d_v=5760, d_qi=5888, d_ki=6912, d_w=6976, gate=6992)
NFM = 6976
NEG = -30000.0


class Rot:
    def __init__(self, name, tensors):
        self.name = name
        self.t = tensors
        self.i = 0

    def next(self):
        k = self.i % len(self.t)
        self.i += 1
        return self.t[k], (self.name, k)


def split_cols(t0, n, step):
    out = []
    o = 0
    while o < n:
        m = min(step, n - o)
        out.append((t0 + o, m))
        o += m
    return out


class Ctx:
    pass


def token_tiles(L):
    return [(0, 16)] + [(16 + 128 * i, 128) for i in range((L - 16) // 128)]


def make_groups(L, G):
    tl = token_tiles(L)
    nt = len(tl) - 1
    per = (nt + G - 1) // G
    groups = []
    i = 1
    first = True
    while i <= nt:
        j = min(nt, i + per - 1)
        g0 = 0 if first else tl[i][0]
        g1 = tl[j][0] + tl[j][1]
        subs = []
        if first:
            subs.append((0, 16))
            subs += split_cols(16, g1 - 16, 512)
        else:
            subs += split_cols(g0, g1 - g0, 512)
        tiles = ([0] if first else []) + list(range(i, j + 1))
        groups.append((g0, g1 - g0, subs, tiles))
        first = False
        i = j + 1
    return groups


def fm(ap2d):
    return ap2d.rearrange("(c p) t -> p c t", p=128)


_UID = [0]


def alloc(c, es, name, shape, dt):
    _UID[0] += 1
    return es.enter_context(c.nc.sbuf_tensor("s%d_%s" % (_UID[0], name), list(shape), dt))


def palloc(c, es, name, shape, dt=None):
    _UID[0] += 1
    return es.enter_context(c.nc.psum_tensor("p%d_%s" % (_UID[0], name), list(shape), dt or F32))


def rsqrt_op(c, out, in_, scale, reads, wkey, np_=128):
    P = c.P
    P.add("act", lambda e: e.activation(out=out, in_=in_, func=AF.Sqrt, bias=c.epsc[0:np_, 0:1], scale=scale),
          list(reads) + ["const"], [wkey])
    P.add("dve", lambda e: e.reciprocal(out=out, in_=out), [wkey], [wkey])


def norm_stage(c, es_bufs, src, gcol, hbuf, hkey, g0, ng):
    for _ in norm_gen(c, es_bufs, src, gcol, hbuf, hkey, g0, ng):
        pass


def norm_gen(c, es_bufs, src, gcol, hbuf, hkey, g0, ng):
    P = c.P
    xts, sqs, rstd, psn = es_bufs
    srcv = fm(src)
    for (t0, n) in split_cols(g0, ng, 128):
        off = t0 - g0
        xt, xk = xts.next()
        P.dma("sp", xt[:, :, 0:n], srcv[:, :, t0:t0 + n], [], [xk], xk)
        for ch in range(NCH):
            sq, sk = sqs.next()
            P.add("act", lambda e, sq=sq, xt=xt, ch=ch, n=n: e.activation(out=sq[:, 0:n], in_=xt[:, ch, 0:n], func=AF.Square),
                  [xk], [sk])
            P.add("pe", lambda e, sq=sq, ch=ch, n=n: e.matmul(psn[:, 0:n], lhsT=c.ones_bf[:, :], rhs=sq[:, 0:n],
                                                              start=(ch == 0), stop=(ch == NCH - 1)),
                  [sk, "const"], ["psn"])
        rsqrt_op(c, rstd[:, 0:n], psn[:, 0:n], 1.0 / D, ["psn"], "rstd")
        for ch in range(NCH):
            P.add("dve", lambda e, xt=xt, ch=ch, n=n, off=off: e.scalar_tensor_tensor(
                out=hbuf[:, ch, off:off + n], in0=xt[:, ch, 0:n], scalar=gcol[:, ch:ch + 1], in1=rstd[:, 0:n],
                op0=ALU.mult, op1=ALU.mult), [xk, "rstd", "params"], [hkey])
        yield


def norm_bufs(c, es):
    xts = Rot("xt", [alloc(c, es, "xt%d" % i, [128, NCH, 128], F32) for i in range(2)])
    sqs = Rot("sq", [alloc(c, es, "sq%d" % i, [128, 256], BF16) for i in range(3)])
    rstd = alloc(c, es, "rstd", [128, 256], F32)
    psn = palloc(c, es, "psn", [128, 512])
    return (xts, sqs, rstd, psn)


def ffn_phase(c, src, dst, gcol, w_gu, w_down):
    nc, P = c.nc, c.P
    es = ExitStack()
    groups = make_groups(c.L, c.G_ffn)
    NG = max(g[1] for g in groups)
    nb = norm_bufs(c, es)
    hbuf = alloc(c, es, "hbuf", [128, NCH, NG], BF16)
    act = alloc(c, es, "actb", [128, NJ, NG], BF16)
    wgu = Rot("wgu", [alloc(c, es, "wgu%d" % i, [128, NCH, 256], BF16) for i in range(2)])
    wd = Rot("wd", [alloc(c, es, "wd%d" % i, [128, NJ, 128], BF16) for i in range(2)])
    sgs = Rot("sg", [alloc(c, es, "sg%d" % i, [128, 512], F32) for i in range(2)])
    xrs = Rot("xr", [alloc(c, es, "xr%d" % i, [128, 512], F32) for i in range(3)])
    xos = Rot("xo", [alloc(c, es, "xo%d" % i, [128, 512], F32) for i in range(3)])
    psg = Rot("psg", [palloc(c, es, "psg%d" % i, [128, 512]) for i in range(2)])
    psu = Rot("psu", [palloc(c, es, "psu%d" % i, [128, 512]) for i in range(2)])
    pso = Rot("pso", [palloc(c, es, "pso%d" % i, [128, 512]) for i in range(2)])
    wguv = fm(w_gu)
    wdv = fm(w_down)
    srcv, dstv = fm(src), fm(dst)
    for gi_, (g0, ng, subs, _tiles) in enumerate(groups):
        if gi_ == 0:
            norm_stage(c, nb, src, gcol, hbuf, "hbuf", g0, ng)
        if gi_ + 1 < len(groups):
            ngen = norm_gen(c, nb, src, gcol, hbuf, "hbuf", groups[gi_ + 1][0], groups[gi_ + 1][1])
        else:
            ngen = iter(())
        for j in range(NJ):
            w, wk = wgu.next()
            P.dma("pool", w[:, :, 0:128], wguv[:, :, j * 128:(j + 1) * 128], [], [wk], wk)
            P.dma("pool", w[:, :, 128:256], wguv[:, :, DFF + j * 128:DFF + (j + 1) * 128], [], [wk], wk)
            for (t0, n) in subs:
                off = t0 - g0
                pg, pgk = psg.next()
                pu, puk = psu.next()
                for ch in range(NCH):
                    P.add("pe", lambda e, pg=pg, w=w, ch=ch, off=off, n=n: e.matmul(
                        pg[:, 0:n], lhsT=w[:, ch, 0:128], rhs=hbuf[:, ch, off:off + n], start=(ch == 0), stop=(ch == NCH - 1)),
                        [wk, "hbuf"], [pgk])
                for ch in range(NCH):
                    P.add("pe", lambda e, pu=pu, w=w, ch=ch, off=off, n=n: e.matmul(
                        pu[:, 0:n], lhsT=w[:, ch, 128:256], rhs=hbuf[:, ch, off:off + n], start=(ch == 0), stop=(ch == NCH - 1)),
                        [wk, "hbuf"], [puk])
                sg, sgk = sgs.next()
                P.add("act", lambda e, sg=sg, pg=pg, n=n: e.activation(out=sg[:, 0:n], in_=pg[:, 0:n], func=AF.Silu),
                      [pgk], [sgk])
                P.add("dve", lambda e, sg=sg, pu=pu, j=j, off=off, n=n: e.tensor_tensor(
                    out=act[:, j, off:off + n], in0=sg[:, 0:n], in1=pu[:, 0:n], op=ALU.mult), [sgk, puk], [("act", j)])
        for m in range(NCH):
            w, wk = wd.next()
            P.dma("pool", w[:, :, :], wdv[:, :, m * 128:(m + 1) * 128], [], [wk], wk)
            for (t0, n) in subs:
                off = t0 - g0
                xr, xrk = xrs.next()
                P.dma("sp", xr[:, 0:n], srcv[:, m, t0:t0 + n], [], [xrk], xrk)
                po, pok = pso.next()
                for j in range(NJ):
                    P.add("pe", lambda e, po=po, w=w, j=j, off=off, n=n: e.matmul(
                        po[:, 0:n], lhsT=w[:, j, :], rhs=act[:, j, off:off + n], start=(j == 0), stop=(j == NJ - 1)),
                        [wk, ("act", j)], [pok])
                xo, xok = xos.next()
                P.add("dve", lambda e, xo=xo, po=po, xr=xr, n=n: e.scalar_tensor_tensor(
                    out=xo[:, 0:n], in0=po[:, 0:n], scalar=0.5, in1=xr[:, 0:n], op0=ALU.mult, op1=ALU.add),
                    [pok, xrk], [xok])
                P.dma("act", dstv[:, m, t0:t0 + n], xo[:, 0:n], [xok], [], xok)
            if m >= 2:
                next(ngen, None)
        for _ in ngen:
            pass
    P.flush()
    es.close()


PL = 80
PC = dict(ffn1=0, mix=16, ffn2=32, gnorm=48, conv=52, dqn=64, dkn=65, subln=66, dsaq=67, dsak=68, lb=69, lam=73)


def pack_params(inp):
    p = np.zeros((128, 2 * PL), np.float32)
    for l in range(2):
        b = l * PL
        p[:, b + PC["ffn1"]:b + PC["ffn1"] + 16] = inp["ffn1_norm"][l].reshape(16, 128).T
        p[:, b + PC["mix"]:b + PC["mix"] + 16] = inp["mix_norm"][l].reshape(16, 128).T
        p[:, b + PC["ffn2"]:b + PC["ffn2"] + 16] = inp["ffn2_norm"][l].reshape(16, 128).T
        p[:, b + PC["gnorm"]:b + PC["gnorm"] + 4] = inp["hgrn_gnorm"][l].reshape(4, 128).T
        for j in range(3):
            p[:, b + PC["conv"] + j * 4:b + PC["conv"] + j * 4 + 4] = inp["conv_w"][l, j].reshape(4, 128).T
        p[:, b + PC["dqn"]] = np.tile(inp["diff_q_norm"][l], 2)
        p[:, b + PC["dkn"]] = np.tile(inp["diff_k_norm"][l], 2)
        p[:, b + PC["subln"]] = inp["diff_subln"][l]
        p[:, b + PC["dsaq"]] = inp["dsa_q_norm"][l]
        p[:, b + PC["dsak"]] = inp["dsa_k_norm"][l]
        p[:, b + PC["lb"]:b + PC["lb"] + 4] = inp["hgrn_lb"][l].reshape(4, 128).T
        p[0:64, b + PC["lam"]:b + PC["lam"] + 4] = inp["diff_lambda"][l].T
    return p


def rel_bucket_np(n):
    n = np.maximum(n, 0)
    nf = np.maximum(n, 1).astype(np.float32)
    large = 16 + (np.log(nf / np.float32(16)) / np.float32(math.log(128 / 16)) * np.float32(16)).astype(np.int32)
    large = np.minimum(large, 31)
    return np.where(n < 16, n, large)


def make_consts():
    cst = {}
    cst["ident"] = np.eye(128, dtype=np.float32)
    sel = np.zeros((33, 3, 256), np.float32)
    for ci, delta in enumerate((0, 128, 16)):
        m = np.arange(255)
        n = m - 127 + delta
        bk = rel_bucket_np(n)
        for mm in range(255):
            if n[mm] < 0:
                sel[32, ci, mm] = 1.0
            else:
                sel[bk[mm], ci, mm] = 1.0
    cst["sel"] = sel
    e31 = np.zeros((32, 128), np.float32)
    e31[31, :] = 1.0
    cst["e31"] = e31
    o64 = np.zeros((128, 128), np.float32)
    o64[:64, :64] = 1.0
    o64[64:, 64:] = 1.0
    cst["ones64"] = o64
    q = np.arange(128)[:, None]
    k = np.arange(128)[None, :]
    cst["fut"] = np.where(k > q, -1e30, 0.0).astype(np.float32)
    cst["causT"] = (q <= k).astype(np.int32)
    return cst


WNAMES = ("ffn1_w_gu", "ffn1_w_down", "w_in", "w_branch", "w_out", "ffn2_w_gu", "ffn2_w_down")


def build(S, topk, G=4, phases=None, dbg=False):
    L = S + 16
    nc = bass.Bass("TRN2", target_bir_lowering=False)
    c = Ctx()
    c.nc = nc
    c.L, c.S, c.topk = L, S, topk
    c.tiles = token_tiles(L)
    c.G_ffn, c.G_proj, c.G_merge = (5, 2, 5) if S >= 2048 else (2, 2, 2)
    c.dbg = dbg

    def din(name, shape, dt=F32):
        return nc.dram_tensor(name, list(shape), dt, kind="ExternalInput").ap()

    def dscr(name, shape, dt=F32):
        kind = "ExternalOutput" if dbg else "Internal"
        return nc.dram_tensor(name, list(shape), dt, kind=kind).ap()

    c.xin = din("xin", [D, L])
    c.params_d = din("params", [128, 2 * PL])
    c.relb_d = din("relb", [32, 8])
    c.ident_d = din("ident", [128, 128])
    c.sel_d = din("sel", [33, 3, 256])
    c.e31_d = din("e31", [32, 128])
    c.ones64_d = din("ones64", [128, 128])
    c.fut_d = din("fut", [128, 128])
    c.causT_d = din("causT", [128, 128], I32)
    c.w = {}
    c.w["ffn1_w_gu"] = din("ffn1_w_gu", [2, D, 2 * DFF])
    c.w["ffn1_w_down"] = din("ffn1_w_down", [2, DFF, D])
    c.w["ffn2_w_gu"] = din("ffn2_w_gu", [2, D, 2 * DFF])
    c.w["ffn2_w_down"] = din("ffn2_w_down", [2, DFF, D])
    c.w["w_in"] = din("w_in", [2, D, 15184])
    c.w["w_dw_rep"] = din("w_dw_rep", [2, D, 1024])
    c.w["w_branch"] = din("w_branch", [2, 4, 512, D])
    c.w["w_out"] = din("w_out", [2, D, D])
    c.yout = nc.dram_tensor("yout", [D, L], F32, kind="ExternalOutput").ap()
    c.xres = dscr("xres", [D, L])
    c.proj = dscr("proj", [NFM, L])
    c.wabs = dscr("wabs", [1024, L], BF16)
    c.vA = dscr("vA", [L, 512], BF16)
    c.vC = dscr("vC", [L, 512], BF16)
    c.vD = dscr("vD", [L, 128], BF16)
    c.wT = dscr("wT", [L, 16])
    c.br = [dscr("br%d" % i, [512, L], BF16) for i in range(4)]
    c.gsc = dscr("gsc", [24, 128, 255])

    es = ExitStack()
    P = Prog(nc)
    c.P = P
    c.ident_f = alloc(c, es, "ident_f", [128, 128], F32)
    c.ident_bf = alloc(c, es, "ident_bf", [128, 128], BF16)
    c.ones_bf = alloc(c, es, "ones_bf", [128, 128], BF16)
    c.ones_f = alloc(c, es, "ones_f", [128, 128], F32)
    c.epsc = alloc(c, es, "epsc", [128, 1], F32)
    c.ones64_bf = alloc(c, es, "ones64_bf", [128, 128], BF16)
    c.params = alloc(c, es, "params_sb", [128, 2 * PL], F32)
    c.BT = alloc(c, es, "BT", [128, 24, 128], BF16)
    c.cbias = alloc(c, es, "cbias", [128, 8], F32)
    c.fut = alloc(c, es, "fut", [128, 128], F32)
    c.causT = alloc(c, es, "causT", [128, 128], I32)
    c.lbs = alloc(c, es, "lbs", [128, 2, 4], F32)
    c.oml = alloc(c, es, "oml", [128, 2, 4], F32)
    c.nlam = alloc(c, es, "nlam", [128, 2], F32)
    setup_phase(c)

    ph = phases
    for l in range(2):
        pb = l * PL
        first = (l == 0)
        if ph is None or ("ffn1_%d" % l) in ph:
            ffn_phase(c, c.xin if first else c.xres, c.xres, c.params[:, pb + PC["ffn1"]:pb + PC["ffn1"] + 16],
                      c.w["ffn1_w_gu"][l], c.w["ffn1_w_down"][l])
        if ph is None or ("proj_%d" % l) in ph:
            proj_phase(c, l)
        if ph is None or ("conv_%d" % l) in ph:
            conv_phase(c, l)
        if ph is None or ("hgrn_%d" % l) in ph:
            hgrn_phase(c, l)
        if ph is None or ("diff_%d" % l) in ph:
            diff_phase(c, l)
        if ph is None or ("dsa_%d" % l) in ph:
            dsa_phase(c, l)
        if ph is None or ("merge_%d" % l) in ph:
            merge_phase(c, l)
        if ph is None or ("ffn2_%d" % l) in ph:
            ffn_phase(c, c.xres, c.yout if l == 1 else c.xres, c.params[:, pb + PC["ffn2"]:pb + PC["ffn2"] + 16],
                      c.w["ffn2_w_gu"][l], c.w["ffn2_w_down"][l])
    P.close()
    es.close()
    return nc, c


def setup_phase(c):
    nc, P = c.nc, c.P
    es = ExitStack()
    P.dma("sp", c.ident_f[:], c.ident_d, [], ["const_i"], "ident_f")
    P.dma("sp", c.params[:], c.params_d, [], ["params"], "params")
    P.dma("sp", c.fut[:], c.fut_d, [], ["const_f"], "fut")
    P.dma("sp", c.causT[:], c.causT_d, [], ["const_c"], "causT")
    o64 = alloc(c, es, "o64f", [128, 128], F32)
    P.dma("sp", o64[:], c.ones64_d, [], ["o64f"], "o64f")
    P.add("dve", lambda e: e.tensor_copy(out=c.ident_bf[:], in_=c.ident_f[:]), ["const_i"], ["const"])
    P.add("dve", lambda e: e.tensor_copy(out=c.ones64_bf[:], in_=o64[:]), ["o64f"], ["const"])
    P.add("dve", lambda e: e.memset(c.ones_bf[:], 1.0), [], ["const"])
    P.add("dve", lambda e: e.memset(c.ones_f[:], 1.0), [], ["const"])
    P.add("dve", lambda e: e.memset(c.epsc[:], EPS), [], ["const"])
    tab = alloc(c, es, "tab", [32, 8], F32)
    sel = alloc(c, es, "selsb", [33, 3, 256], F32)
    e31 = alloc(c, es, "e31sb", [32, 128], F32)
    P.dma("sp", tab[:], c.relb_d, [], ["tab"], "tab")
    P.dma("sp", sel[:], c.sel_d, [], ["sel"], "sel")
    P.dma("sp", e31[:], c.e31_d, [], ["e31"], "e31")
    tabB = Rot("tabB", [alloc(c, es, "tabB%d" % i, [33, 128], F32) for i in range(2)])
    gsb = Rot("gsb", [alloc(c, es, "gsb%d" % i, [128, 255], F32) for i in range(2)])
    btf = Rot("btf", [alloc(c, es, "btf%d" % i, [128, 128], F32) for i in range(2)])
    psG = Rot("psG", [palloc(c, es, "psG%d" % i, [128, 256]) for i in range(2)])
    psc = palloc(c, es, "psc", [128, 8])
    P.add("pe", lambda e: e.matmul(psc[:, :], lhsT=e31[:, :], rhs=tab[:, :], start=True, stop=True), ["e31", "tab"], ["psc"])
    P.add("dve", lambda e: e.tensor_copy(out=c.cbias[:], in_=psc[:]), ["psc"], ["const"])
    for hh in range(8):
        tb, tbk = tabB.next()
        P.add("dve", lambda e, tb=tb: e.memset(tb[:, :], NEG), [], [tbk])
        P.add("dve", lambda e, tb=tb, hh=hh: e.tensor_scalar(out=tb[0:32, :], in0=c.ones_f[0:32, :], scalar1=tab[0:32, hh:hh + 1],
                                                             scalar2=None, op0=ALU.mult), ["tab", "const", tbk], [tbk])
        for ci in range(3):
            pg, pgk = psG.next()
            P.add("pe", lambda e, pg=pg, tb=tb, ci=ci: e.matmul(pg[:, 0:256], lhsT=tb[:, :], rhs=sel[:, ci, :], start=True, stop=True),
                  [tbk, "sel"], [pgk])
            gs, gsk = gsb.next()
            P.add("act", lambda e, gs=gs, pg=pg: e.activation(out=gs[:, :], in_=pg[:, 0:255], func=AF.Copy), [pgk], [gsk])
            idx = hh * 3 + ci
            P.dma("sp", c.gsc[idx], gs[:, :], [gsk], [("gsc", idx)], gsk)
            bt, btk = btf.next()
            skew = bass.AP(tensor=c.gsc.tensor, offset=idx * 128 * 255 + 127, ap=[[254, 128], [1, 128]])
            P.dma("sp", bt[:, :], skew, [("gsc", idx)], [btk], btk)
            P.add("dve", lambda e, bt=bt, idx=idx, hh=hh: e.tensor_scalar(out=c.BT[:, idx, :], in0=bt[:, :], scalar1=c.cbias[:, hh:hh + 1],
                                                                          scalar2=None, op0=ALU.subtract), [btk, "const"], ["const"])
    P.add("dve", lambda e: e.memset(c.lbs[:, 0, :], 0.0), [], ["const"])
    P.add("dve", lambda e: e.memset(c.oml[:, 0, :], 1.0), [], ["const"])
    dl = alloc(c, es, "dl", [128, 4], F32)
    P.add("dve", lambda e: e.tensor_tensor(out=dl[:, :], in0=c.params[:, PL + PC["lb"]:PL + PC["lb"] + 4],
                                           in1=c.params[:, PC["lb"]:PC["lb"] + 4], op=ALU.subtract), ["params"], ["dl"])
    P.add("act", lambda e: e.activation(out=c.lbs[:, 1, :], in_=dl[:, :], func=AF.Sigmoid), ["dl"], ["const"])
    P.add("dve", lambda e: e.tensor_scalar(out=c.oml[:, 1, :], in0=c.lbs[:, 1, :], scalar1=-1.0, scalar2=1.0,
                                           op0=ALU.mult, op1=ALU.add), ["const"], ["const"])
    pr = alloc(c, es, "pr", [128, 4], F32)
    psl = palloc(c, es, "psl", [128, 4])
    el = alloc(c, es, "el", [128, 4], F32)
    for l in range(2):
        b = l * PL + PC["lam"]
        P.add("dve", lambda e, l=l, b=b: e.tensor_tensor(out=pr[:, 2 * l:2 * l + 1], in0=c.params[:, b:b + 1], in1=c.params[:, b + 1:b + 2],
                                                         op=ALU.mult), ["params"], ["pr"])
        P.add("dve", lambda e, l=l, b=b: e.tensor_tensor(out=pr[:, 2 * l + 1:2 * l + 2], in0=c.params[:, b + 2:b + 3], in1=c.params[:, b + 3:b + 4],
                                                         op=ALU.mult), ["params"], ["pr"])
    P.add("pe", lambda e: e.matmul(psl[:, :], lhsT=c.ones_f[:, :], rhs=pr[:, :], start=True, stop=True), ["pr", "const"], ["psl"])
    P.add("act", lambda e: e.activation(out=el[:, :], in_=psl[:, :], func=AF.Exp), ["psl"], ["el"])
    for l in range(2):
        lam_init = 0.8 - 0.6 * math.exp(-0.3 * l)
        P.add("dve", lambda e, l=l, li=lam_init: e.scalar_tensor_tensor(
            out=c.nlam[:, l:l + 1], in0=el[:, 2 * l + 1:2 * l + 2], scalar=-li, in1=el[:, 2 * l:2 * l + 1],
            op0=ALU.add, op1=ALU.subtract), ["el"], ["const"])
    P.flush()
    es.close()


_CACHE = {}


def host_inputs(inp, b, consts, params, w_dw_rep):
    x = inp["x"]
    xin = np.ascontiguousarray(np.concatenate([inp["meta_tokens"].T, x[b].T], axis=1), dtype=np.float32)
    m = {"xin": xin, "params": params, "relb": np.ascontiguousarray(inp["rel_bias"], dtype=np.float32),
         "ident": consts["ident"], "sel": consts["sel"], "e31": consts["e31"], "ones64": consts["ones64"],
         "fut": consts["fut"], "causT": consts["causT"], "w_dw_rep": w_dw_rep}
    for k in WNAMES:
        m[k] = np.ascontiguousarray(inp[k], dtype=np.float32)
    return m


def run(inp, phases=None, dbg=False, G=4, trace=False):
    inp = {k: np.asarray(v) for k, v in inp.items()}
    B, S, _ = inp["x"].shape
    topk = min(256, S // 4)
    nc, c = build(S, topk, G=G, phases=phases, dbg=dbg)
    consts = make_consts()
    params = pack_params(inp)
    w16 = inp["w_in"][:, :, OFF["d_w"]:OFF["d_w"] + 16]
    w_dw_rep = np.ascontiguousarray(np.repeat(w16, 64, axis=2), dtype=np.float32)
    in_maps = [host_inputs(inp, b, consts, params, w_dw_rep) for b in range(B)]
    res = run_bass_kernel_spmd(nc, in_maps, core_ids=list(range(B)), trace=trace)
    return res, c


def kernel(**inputs):
    res, c = run(inputs)
    outs = [r["yout"] for r in res.results]
    y = np.stack([np.ascontiguousarray(o[:, 16:].T) for o in outs], axis=0)
    return y.astype(np.float32)


def fm_chunks():
    skip = [(OFF["a_i"], OFF["a_g"]), (OFF["c_v"], OFF["d_q"]), (OFF["d_v"], OFF["d_qi"])]
    out = []
    c0 = 0
    while c0 < NFM:
        n = min(128, NFM - c0)
        if not any(a <= c0 < b for a, b in skip):
            out.append((c0, n))
        c0 += n
    return out


def proj_phase(c, l):
    nc, P = c.nc, c.P
    es = ExitStack()
    groups = make_groups(c.L, c.G_proj)
    NG = max(g[1] for g in groups)
    pb = l * PL
    nb = norm_bufs(c, es)
    hbuf = alloc(c, es, "hbuf", [128, NCH, NG], BF16)
    wfm = Rot("wfm", [alloc(c, es, "wfm%d" % i, [128, NCH, 128], BF16) for i in range(3)])
    wtm = Rot("wtm", [alloc(c, es, "wtm%d" % i, [128, NCH, 512], BF16) for i in range(2)])
    evs = Rot("ev", [alloc(c, es, "ev%d" % i, [128, 512], F32) for i in range(3)])
    evb = Rot("evb", [alloc(c, es, "evb%d" % i, [128, 512], BF16) for i in range(3)])
    ps = Rot("ps", [palloc(c, es, "ps%d" % i, [128, 512]) for i in range(4)])
    win = fm(c.w["w_in"][l])
    wrep = fm(c.w["w_dw_rep"][l])
    gcol = c.params[:, pb + PC["mix"]:pb + PC["mix"] + 16]
    for (g0, ng, subs, tiles) in groups:
        norm_stage(c, nb, c.xres, gcol, hbuf, "hbuf", g0, ng)
        for (c0, mc) in fm_chunks():
            w, wk = wfm.next()
            P.dma("pool", w[:, :, 0:mc], win[:, :, c0:c0 + mc], [], [wk], wk)
            for (t0, n) in subs:
                off = t0 - g0
                p, pk = ps.next()
                for ch in range(NCH):
                    P.add("pe", lambda e, p=p, w=w, ch=ch, off=off, n=n, mc=mc: e.matmul(
                        p[0:mc, 0:n], lhsT=w[:, ch, 0:mc], rhs=hbuf[:, ch, off:off + n], start=(ch == 0), stop=(ch == NCH - 1)),
                        [wk, "hbuf"], [pk])
                ev, ek = evs.next()
                P.add("act", lambda e, ev=ev, p=p, n=n, mc=mc: e.activation(out=ev[0:mc, 0:n], in_=p[0:mc, 0:n], func=AF.Copy),
                      [pk], [ek])
                P.dma("sp", c.proj[c0:c0 + mc, t0:t0 + n], ev[0:mc, 0:n], [ek], [], ek)
        for r in range(8):
            w, wk = wfm.next()
            P.dma("pool", w[:, :, :], wrep[:, :, r * 128:(r + 1) * 128], [], [wk], wk)
            for (t0, n) in subs:
                off = t0 - g0
                p, pk = ps.next()
                for ch in range(NCH):
                    P.add("pe", lambda e, p=p, w=w, ch=ch, off=off, n=n: e.matmul(
                        p[:, 0:n], lhsT=w[:, ch, :], rhs=hbuf[:, ch, off:off + n], start=(ch == 0), stop=(ch == NCH - 1)),
                        [wk, "hbuf"], [pk])
                ev, ek = evb.next()
                P.add("act", lambda e, ev=ev, p=p, n=n: e.activation(out=ev[:, 0:n], in_=p[:, 0:n], func=AF.Abs), [pk], [ek])
                P.dma("sp", c.wabs[r * 128:(r + 1) * 128, t0:t0 + n], ev[:, 0:n], [ek], [], ek)
        for (c0, ncol, dst, isf32) in ((OFF["a_i"], 512, c.vA, False), (OFF["c_v"], 512, c.vC, False),
                                       (OFF["d_v"], 128, c.vD, False), (OFF["d_w"], 16, c.wT, True)):
            w, wk = wtm.next()
            P.dma("pool", w[:, :, 0:ncol], win[:, :, c0:c0 + ncol], [], [wk], wk)
            for ti in tiles:
                t0, nt = c.tiles[ti]
                off = t0 - g0
                p, pk = ps.next()
                for ch in range(NCH):
                    P.add("pe", lambda e, p=p, w=w, ch=ch, off=off, nt=nt, ncol=ncol: e.matmul(
                        p[0:nt, 0:ncol], lhsT=hbuf[:, ch, off:off + nt], rhs=w[:, ch, 0:ncol], start=(ch == 0), stop=(ch == NCH - 1)),
                        [wk, "hbuf"], [pk])
                if isf32:
                    ev, ek = evs.next()
                else:
                    ev, ek = evb.next()
                P.add("act", lambda e, ev=ev, p=p, nt=nt, ncol=ncol: e.activation(out=ev[0:nt, 0:ncol], in_=p[0:nt, 0:ncol], func=AF.Copy),
                      [pk], [ek])
                P.dma("sp", dst[t0:t0 + nt, 0:ncol], ev[0:nt, 0:ncol], [ek], [], ek)
    P.flush()
    es.close()


def conv_phase(c, l):
    nc, P = c.nc, c.P
    es = ExitStack()
    L = c.L
    pb = l * PL + PC["conv"]
    bb = Rot("bb", [alloc(c, es, "bb%d" % i, [128, L], F32) for i in range(2)])
    bc = Rot("bc", [alloc(c, es, "bc%d" % i, [128, L], F32) for i in range(2)])
    bu = Rot("bu", [alloc(c, es, "bu%d" % i, [128, L], F32) for i in range(2)])
    zc = alloc(c, es, "zc", [128, L + 2], F32)
    yb = alloc(c, es, "yb", [128, L], F32)
    ob = Rot("ob", [alloc(c, es, "ob%d" % i, [128, L], BF16) for i in range(2)])
    P.add("dve", lambda e: e.memset(zc[:, 0:2], 0.0), [], ["zc0"])
    for ch in range(4):
        tb, tbk = bb.next()
        tc_, tck = bc.next()
        tu, tuk = bu.next()
        P.dma("sp", tb[:, :], c.proj[OFF["b_b"] + ch * 128:OFF["b_b"] + (ch + 1) * 128, :], [], [tbk], tbk)
        P.dma("sp", tc_[:, :], c.proj[OFF["b_c"] + ch * 128:OFF["b_c"] + (ch + 1) * 128, :], [], [tck], tck)
        P.dma("sp", tu[:, :], c.proj[OFF["b_u"] + ch * 128:OFF["b_u"] + (ch + 1) * 128, :], [], [tuk], tuk)
        P.add("pool", lambda e, tc_=tc_, tu=tu: e.tensor_tensor(out=zc[:, 2:L + 2], in0=tc_[:, :], in1=tu[:, :], op=ALU.mult),
              [tck, tuk], ["zc"])
        w0 = c.params[:, pb + ch:pb + ch + 1]
        w1 = c.params[:, pb + 4 + ch:pb + 4 + ch + 1]
        w2 = c.params[:, pb + 8 + ch:pb + 8 + ch + 1]
        P.add("dve", lambda e, w0=w0: e.tensor_scalar(out=yb[:, :], in0=zc[:, 2:L + 2], scalar1=w0, scalar2=None, op0=ALU.mult),
              ["zc", "zc0", "params"], ["yb"])
        P.add("dve", lambda e, w1=w1: e.scalar_tensor_tensor(out=yb[:, :], in0=zc[:, 1:L + 1], scalar=w1, in1=yb[:, :],
                                                             op0=ALU.mult, op1=ALU.add), ["zc", "zc0", "yb", "params"], ["yb"])
        P.add("dve", lambda e, w2=w2: e.scalar_tensor_tensor(out=yb[:, :], in0=zc[:, 0:L], scalar=w2, in1=yb[:, :],
                                                             op0=ALU.mult, op1=ALU.add), ["zc", "zc0", "yb", "params"], ["yb"])
        o, ok = ob.next()
        P.add("dve", lambda e, o=o, tb=tb: e.tensor_tensor(out=o[:, :], in0=yb[:, :], in1=tb[:, :], op=ALU.mult), ["yb", tbk], [ok])
        P.dma("sp", c.br[1][ch * 128:(ch + 1) * 128, :], o[:, :], [ok], [], ok)
    P.flush()
    es.close()


def merge_phase(c, l):
    nc, P = c.nc, c.P
    es = ExitStack()
    groups = make_groups(c.L, c.G_merge)
    NG = max(g[1] for g in groups)
    pb = l * PL
    nb = norm_bufs(c, es)
    hbuf = alloc(c, es, "hbuf", [128, NCH, NG], BF16)
    brb = alloc(c, es, "brb", [128, 16, NG], BF16)
    mrg = alloc(c, es, "mrg", [128, NCH, NG], BF16)
    wg = Rot("wg", [alloc(c, es, "wg%d" % i, [128, NCH, 128], BF16) for i in range(3)])
    wb = Rot("wb", [alloc(c, es, "wb%d" % i, [128, 4, 128], BF16) for i in range(3)])
    wo = Rot("wo", [alloc(c, es, "wo%d" % i, [128, NCH, 128], BF16) for i in range(2)])
    sgs = Rot("sg", [alloc(c, es, "sg%d" % i, [128, 512], F32) for i in range(2)])
    macc = Rot("macc", [alloc(c, es, "macc%d" % i, [128, 512], F32) for i in range(3)])
    tmps = Rot("tmp", [alloc(c, es, "tmp%d" % i, [128, 512], F32) for i in range(2)])
    xrs = Rot("xr", [alloc(c, es, "xr%d" % i, [128, 512], F32) for i in range(3)])
    xos = Rot("xo", [alloc(c, es, "xo%d" % i, [128, 512], F32) for i in range(3)])
    psg = Rot("psg", [palloc(c, es, "psg%d" % i, [128, 512]) for i in range(2)])
    psb = Rot("psb", [palloc(c, es, "psb%d" % i, [128, 512]) for i in range(2)])
    pso = Rot("pso", [palloc(c, es, "pso%d" % i, [128, 512]) for i in range(2)])
    win = fm(c.w["w_in"][l])
    wout = fm(c.w["w_out"][l])
    gcol = c.params[:, pb + PC["mix"]:pb + PC["mix"] + 16]
    xv = fm(c.xres)
    for gi_, (g0, ng, subs, tiles) in enumerate(groups):
        if gi_ == 0:
            norm_stage(c, nb, c.xres, gcol, hbuf, "hbuf", g0, ng)
        if gi_ + 1 < len(groups):
            ngen = norm_gen(c, nb, c.xres, gcol, hbuf, "hbuf", groups[gi_ + 1][0], groups[gi_ + 1][1])
        else:
            ngen = iter(())
        for br in range(4):
            P.dma("sp", brb[:, br * 4:(br + 1) * 4, 0:ng], fm(c.br[br])[:, :, g0:g0 + ng], [], [("brb", br)], ("brb", br))
        for fc in range(NCH):
            accs = {}
            for br in range(4):
                w, wk = wg.next()
                gc0 = OFF["gate"] + br * D + fc * 128
                P.dma("pool", w[:, :, :], win[:, :, gc0:gc0 + 128], [], [wk], wk)
                w2, w2k = wb.next()
                P.dma("pool", w2[:, :, :], c.w["w_branch"][l, br].rearrange("(kc p) m -> p kc m", p=128)[:, :, fc * 128:(fc + 1) * 128],
                      [], [w2k], w2k)
                for si, (t0, n) in enumerate(subs):
                    off = t0 - g0
                    pg, pgk = psg.next()
                    pbr, pbk = psb.next()
                    for ch in range(NCH):
                        P.add("pe", lambda e, pg=pg, w=w, ch=ch, off=off, n=n: e.matmul(
                            pg[:, 0:n], lhsT=w[:, ch, :], rhs=hbuf[:, ch, off:off + n], start=(ch == 0), stop=(ch == NCH - 1)),
                            [wk, "hbuf"], [pgk])
                    for kc in range(4):
                        P.add("pe", lambda e, pbr=pbr, w2=w2, kc=kc, br=br, off=off, n=n: e.matmul(
                            pbr[:, 0:n], lhsT=w2[:, kc, :], rhs=brb[:, br * 4 + kc, off:off + n], start=(kc == 0), stop=(kc == 3)),
                            [w2k, ("brb", br)], [pbk])
                    sg, sgk = sgs.next()
                    P.add("act", lambda e, sg=sg, pg=pg, n=n: e.activation(out=sg[:, 0:n], in_=pg[:, 0:n], func=AF.Sigmoid),
                          [pgk], [sgk])
                    if br == 0:
                        accs[si] = macc.next()
                        ma, mak = accs[si]
                        P.add("dve", lambda e, ma=ma, sg=sg, pbr=pbr, n=n: e.tensor_tensor(
                            out=ma[:, 0:n], in0=sg[:, 0:n], in1=pbr[:, 0:n], op=ALU.mult), [sgk, pbk], [mak])
                    else:
                        ma, mak = accs[si]
                        tm, tmk = tmps.next()
                        P.add("dve", lambda e, tm=tm, sg=sg, pbr=pbr, n=n: e.tensor_tensor(
                            out=tm[:, 0:n], in0=sg[:, 0:n], in1=pbr[:, 0:n], op=ALU.mult), [sgk, pbk], [tmk])
                        if br < 3:
                            P.add("dve", lambda e, ma=ma, tm=tm, n=n: e.tensor_tensor(
                                out=ma[:, 0:n], in0=ma[:, 0:n], in1=tm[:, 0:n], op=ALU.add), [mak, tmk], [mak])
                        else:
                            P.add("dve", lambda e, ma=ma, tm=tm, n=n, fc=fc, off=off: e.tensor_tensor(
                                out=mrg[:, fc, off:off + n], in0=ma[:, 0:n], in1=tm[:, 0:n], op=ALU.add), [mak, tmk], [("mrg", fc)])
        for m in range(NCH):
            w, wk = wo.next()
            P.dma("pool", w[:, :, :], wout[:, :, m * 128:(m + 1) * 128], [], [wk], wk)
            for (t0, n) in subs:
                off = t0 - g0
                xr, xrk = xrs.next()
                P.dma("sp", xr[:, 0:n], xv[:, m, t0:t0 + n], [], [xrk], xrk)
                po, pok = pso.next()
                for ch in range(NCH):
                    P.add("pe", lambda e, po=po, w=w, ch=ch, off=off, n=n: e.matmul(
                        po[:, 0:n], lhsT=w[:, ch, :], rhs=mrg[:, ch, off:off + n], start=(ch == 0), stop=(ch == NCH - 1)),
                        [wk, ("mrg", ch)], [pok])
                xo, xok = xos.next()
                P.add("dve", lambda e, xo=xo, po=po, xr=xr, n=n: e.tensor_tensor(
                    out=xo[:, 0:n], in0=po[:, 0:n], in1=xr[:, 0:n], op=ALU.add), [pok, xrk], [xok])
                P.dma("act", xv[:, m, t0:t0 + n], xo[:, 0:n], [xok], [], xok)
            if m >= 2:
                next(ngen, None)
        for _ in ngen:
            pass
    P.flush()
    es.close()


def load_rows_norm(c, bufs, row0, nrows, out_fn, gsc_col, ones_m, gsize, key_out, dup64=False):
    P = c.P
    st, sq, rs, psn = bufs
    for (t0, n) in split_cols(0, c.L, 512):
        s, sk = st.next()
        if dup64:
            P.dma("sp", s[0:64, 0:n], c.proj[row0:row0 + 64, t0:t0 + n], [], [sk], sk)
            P.dma("sp", s[64:128, 0:n], c.proj[row0:row0 + 64, t0:t0 + n], [], [sk], sk)
        else:
            P.dma("sp", s[0:nrows, 0:n], c.proj[row0:row0 + nrows, t0:t0 + n], [], [sk], sk)
        if gsize is None:
            P.add("act", lambda e, s=s, t0=t0, n=n: e.activation(out=out_fn(t0, n), in_=s[:, 0:n], func=AF.Copy), [sk], [key_out])
            continue
        q, qk = sq.next()
        P.add("act", lambda e, q=q, s=s, n=n: e.activation(out=q[:, 0:n], in_=s[:, 0:n], func=AF.Square), [sk], [qk])
        P.add("pe", lambda e, q=q, n=n: e.matmul(psn[:, 0:n], lhsT=ones_m[:, :], rhs=q[:, 0:n], start=True, stop=True),
              [qk, "const"], ["psn"])
        r, rk = rs.next()
        rsqrt_op(c, r[:, 0:n], psn[:, 0:n], 1.0 / gsize, ["psn"], rk)
        P.add("dve", lambda e, s=s, r=r, t0=t0, n=n: e.scalar_tensor_tensor(
            out=out_fn(t0, n), in0=s[:, 0:n], scalar=gsc_col, in1=r[:, 0:n], op0=ALU.mult, op1=ALU.mult),
            [sk, rk, "gsc"], [key_out])


def rows_bufs(c, es):
    st = Rot("st", [alloc(c, es, "st%d" % i, [128, 512], F32) for i in range(2)])
    sq = Rot("sqq", [alloc(c, es, "sqq%d" % i, [128, 512], BF16) for i in range(2)])
    rs = Rot("rs", [alloc(c, es, "rs%d" % i, [128, 512], F32) for i in range(2)])
    psn = palloc(c, es, "psn", [128, 512])
    return (st, sq, rs, psn)


def near_case(i, j):
    if j == i:
        return 0
    if j >= 1 and j == i - 1:
        return 1
    if j == 0 and i == 1:
        return 2
    return None


def load_vtm(c, vsb, src, col0, ncol, key, pitch_view):
    P = c.P
    nt_full = len(c.tiles) - 1
    P.dma("sp", pitch_view(0, 16), src[0:16, col0:col0 + ncol], [], [key], key)
    for ti in range(1, nt_full + 1):
        t0, nt = c.tiles[ti]
        P.dma("sp", pitch_view(ti, nt), src[t0:t0 + nt, col0:col0 + ncol], [], [key], key)


def diff_phase(c, l):
    nc, P = c.nc, c.P
    es = ExitStack()
    L = c.L
    NT = len(c.tiles)
    pb = l * PL
    lam_init = 0.8 - 0.6 * math.exp(-0.3 * l)
    rb = rows_bufs(c, es)
    qC = alloc(c, es, "qC", [128, 4, L], BF16)
    kC = alloc(c, es, "kC", [128, 4, L], BF16)
    vC = alloc(c, es, "vCs", [128, NT, 4, 129], BF16)
    ob = alloc(c, es, "obC", [128, 4, L], BF16)
    gs = alloc(c, es, "gsC", [128, 4], F32)
    zc = alloc(c, es, "zcol", [128, 1], F32)
    pTs = Rot("pT", [alloc(c, es, "pT%d" % i, [128, 512], BF16) for i in range(3)])
    rr = Rot("rr", [alloc(c, es, "rr%d" % i, [128, 4], F32) for i in range(2)])
    ods = Rot("od", [alloc(c, es, "od%d" % i, [128, 128], F32) for i in range(2)])
    junk = alloc(c, es, "junk", [128, 128], F32)
    ons = Rot("on", [alloc(c, es, "on%d" % i, [128, 128], BF16) for i in range(2)])
    pss = Rot("pss", [palloc(c, es, "pss%d" % i, [128, 512]) for i in range(4)])
    pso = Rot("pso", [palloc(c, es, "pso%d" % i, [128, 512]) for i in range(2)])
    pst = palloc(c, es, "pst", [128, 512])
    posb = Rot("posb", [alloc(c, es, "posb%d" % i, [128, 264], F32) for i in range(2)])
    P.add("dve", lambda e: e.tensor_scalar(out=gs[:, 0:1], in0=c.params[:, pb + PC["dqn"]:pb + PC["dqn"] + 1], scalar1=0.125,
                                           scalar2=None, op0=ALU.mult), ["params"], ["gsc"])
    P.add("dve", lambda e: e.tensor_copy(out=gs[:, 1:2], in_=c.params[:, pb + PC["dkn"]:pb + PC["dkn"] + 1]), ["params"], ["gsc"])
    P.add("dve", lambda e: e.tensor_scalar(out=gs[:, 2:3], in0=c.params[:, pb + PC["subln"]:pb + PC["subln"] + 1],
                                           scalar1=1.0 - lam_init, scalar2=None, op0=ALU.mult), ["params"], ["gsc"])
    P.add("dve", lambda e: e.memset(zc[:, :], 0.0), [], ["gsc"])
    P.add("dve", lambda e: e.memset(vC[:, :, :, 128:129], 1.0), [], ["vC1"])
    for h in range(4):
        load_rows_norm(c, rb, OFF["c_q"] + h * 128, 128, lambda t0, n, h=h: qC[:, h, t0:t0 + n], gs[:, 0:1], c.ones64_bf, 64, ("qC", h))
        load_rows_norm(c, rb, OFF["c_k"] + h * 128, 128, lambda t0, n, h=h: kC[:, h, t0:t0 + n], gs[:, 1:2], c.ones64_bf, 64, ("kC", h))
    for ti in range(NT):
        t0, nt = c.tiles[ti]
        P.dma("sp", vC[0:nt, ti, :, 0:128], c.vC[t0:t0 + nt, :].rearrange("t (h e) -> t h e", h=4), [], ["vC"], "vC")
    c.prev_tail = None
    for i_ in range(NT):
      for h_ in range(4):
        def do_block(i, h, q0, nq):
            pos = [pso.next(), pso.next()]
            groups_ = [(cc, jg) for cc in range(2) for jg in range(0, i + 1, 4)]
            psl = {}

            def emit_s(gi):
                cc, jg = groups_[gi]
                grp = list(range(jg, min(i + 1, jg + 4)))
                ps, psk = pss.next()
                psl[gi] = (ps, psk, grp)
                for sl, j in enumerate(grp):
                    k0, nk = c.tiles[j]
                    case = near_case(i, j)
                    P.add("pe", lambda e, ps=ps, cc=cc, k0=k0, nk=nk, case=case, sl=sl: e.matmul(
                        ps[0:nk, sl * 128:sl * 128 + nq], lhsT=kC[cc * 64:(cc + 1) * 64, h, k0:k0 + nk],
                        rhs=qC[cc * 64:(cc + 1) * 64, h, q0:q0 + nq], start=True, stop=(case is None)), [("kC", h), ("qC", h)], [psk])
                    if case is not None:
                        P.add("pe", lambda e, ps=ps, nk=nk, case=case, sl=sl: e.matmul(
                            ps[0:nk, sl * 128:sl * 128 + nq], lhsT=c.ident_bf[0:nk, 0:nk], rhs=c.BT[0:nk, h * 3 + case, 0:nq],
                            start=False, stop=True), ["const"], [psk])

            def emit_pv(gi):
                cc, jg = groups_[gi]
                ps, psk, grp = psl[gi]
                po, pok = pos[cc]
                W = (len(grp) - 1) * 128 + nq
                pT, pTk = pTs.next()
                P.add("act", lambda e, pT=pT, ps=ps, W=W: e.activation(out=pT[:, 0:W], in_=ps[:, 0:W], func=AF.Exp), [psk], [pTk])
                for sl, j in enumerate(grp):
                    k0, nk = c.tiles[j]
                    P.add("pe", lambda e, po=po, pT=pT, nk=nk, j=j, sl=sl: e.matmul(
                        po[0:nq, 0:129], lhsT=pT[0:nk, sl * 128:sl * 128 + nq], rhs=vC[0:nk, j, h, 0:129], start=(j == 0), stop=(j == i)),
                        [pTk, "vC", "vC1"], [pok])

            emit_s(0)
            for gi in range(len(groups_)):
                if gi + 1 < len(groups_):
                    emit_s(gi + 1)
                emit_pv(gi)
            ob2, ob2k = posb.next()
            P.add("act", lambda e, ob2=ob2: e.activation(out=ob2[0:nq, 0:129], in_=pos[0][0][0:nq, 0:129], func=AF.Copy), [pos[0][1]], [ob2k])
            P.add("act", lambda e, ob2=ob2: e.activation(out=ob2[0:nq, 132:261], in_=pos[1][0][0:nq, 0:129], func=AF.Copy), [pos[1][1]], [ob2k])
            p0, p0k = ob2[:, 0:132], ob2k
            p1, p1k = ob2[:, 132:264], ob2k
            r, rk = rr.next()
            P.add("dve", lambda e, r=r, p0=p0, nq=nq: e.reciprocal(out=r[0:nq, 0:1], in_=p0[0:nq, 128:129]), [p0k], [rk])
            P.add("dve", lambda e, r=r, p1=p1, nq=nq: e.reciprocal(out=r[0:nq, 1:2], in_=p1[0:nq, 128:129]), [p1k], [rk])
            P.add("dve", lambda e, r=r, nq=nq: e.tensor_tensor(out=r[0:nq, 2:3], in0=r[0:nq, 1:2], in1=c.nlam[0:nq, l:l + 1], op=ALU.mult),
                  [rk, "const"], [rk])
            od, odk = ods.next()
            P.add("dve", lambda e, od=od, p0=p0, r=r, nq=nq: e.tensor_scalar(out=od[0:nq, :], in0=p0[0:nq, 0:128], scalar1=r[0:nq, 0:1],
                                                                              scalar2=None, op0=ALU.mult), [p0k, rk], [odk])
            P.add("dve", lambda e, od=od, p1=p1, r=r, nq=nq: e.scalar_tensor_tensor(
                out=od[0:nq, :], in0=p1[0:nq, 0:128], scalar=r[0:nq, 2:3], in1=od[0:nq, :], op0=ALU.mult, op1=ALU.add),
                [p1k, rk, odk], [odk])
            P.add("dve", lambda e, od=od, nq=nq: e.tensor_tensor(out=junk[0:nq, :], in0=od[0:nq, :], in1=od[0:nq, :], op=ALU.mult),
                  [odk], ["junk"])
            P.add("dve", lambda e, r=r, nq=nq: e.tensor_reduce(out=r[0:nq, 3:4], in_=junk[0:nq, :], axis=AX.X, op=ALU.add),
                  ["junk", rk], [rk])
            rsqrt_op(c, r[0:nq, 3:4], r[0:nq, 3:4], 1.0 / 128, [rk], rk, np_=nq)
            on, onk = ons.next()
            P.add("dve", lambda e, on=on, od=od, r=r, nq=nq: e.tensor_scalar(out=on[0:nq, :], in0=od[0:nq, :], scalar1=r[0:nq, 3:4],
                                                                              scalar2=None, op0=ALU.mult), [odk, rk], [onk])
            def tail():
                P.add("pe", lambda e, on=on, nq=nq: e.matmul(pst[:, 0:nq], lhsT=on[0:nq, :], rhs=c.ident_bf[0:nq, 0:nq], start=True, stop=True),
                      [onk, "const"], ["pst"])
                P.add("act", lambda e, h=h, q0=q0, nq=nq: e.activation(out=ob[:, h, q0:q0 + nq], in_=pst[:, 0:nq], func=AF.Copy,
                                                                        scale=gs[:, 2:3]), ["pst", "gsc"], ["obC"])
            return tail
        tl_ = do_block(i_, h_, c.tiles[i_][0], c.tiles[i_][1])
        if c.prev_tail is not None:
            c.prev_tail()
        c.prev_tail = tl_
    c.prev_tail()
    P.dma("sp", fm(c.br[2]), ob[:, :, :], ["obC"], [], "obC")
    P.flush()
    es.close()


def hgrn_phase(c, l):
    nc, P = c.nc, c.P
    es = ExitStack()
    L = c.L
    NT = len(c.tiles)
    pb = l * PL
    zf = alloc(c, es, "zf", [128, L], F32)
    bb = alloc(c, es, "bb", [128, L], F32)
    kf = alloc(c, es, "kf", [128, L], F32)
    qf = alloc(c, es, "qf", [128, L], F32)
    tmp = alloc(c, es, "tmpE", [128, L], F32)
    qt = alloc(c, es, "qt", [128, L], BF16)
    qh = alloc(c, es, "qh", [128, L], BF16)
    kh = alloc(c, es, "kh", [128, L], BF16)
    khT = alloc(c, es, "khT", [128, NT, 128], BF16)
    vA = alloc(c, es, "vAs", [128, NT, 128], BF16)
    osb = alloc(c, es, "osb", [128, L], BF16)
    obr = alloc(c, es, "obr", [128, L], BF16)
    e1 = alloc(c, es, "e1", [128, NT], F32)
    e2 = alloc(c, es, "e2", [128, NT], F32)
    Sf = alloc(c, es, "Sf", [128, 128], F32)
    St = alloc(c, es, "St", [128, 128], F32)
    Sb = Rot("Sb", [alloc(c, es, "Sb%d" % i, [128, 128], BF16) for i in range(2)])
    Pm = Rot("Pm", [alloc(c, es, "Pm%d" % i, [128, 128], BF16) for i in range(2)])
    sq = Rot("sqh", [alloc(c, es, "sqh%d" % i, [128, 512], BF16) for i in range(2)])
    rs = Rot("rsh", [alloc(c, es, "rsh%d" % i, [128, 512], F32) for i in range(2)])
    pss = Rot("pss", [palloc(c, es, "pss%d" % i, [128, 512]) for i in range(2)])
    pso = Rot("pso", [palloc(c, es, "pso%d" % i, [128, 512]) for i in range(2)])
    psd = Rot("psd", [palloc(c, es, "psd%d" % i, [128, 512]) for i in range(2)])
    pstb = palloc(c, es, "pstb", [128, 128], BF16)
    psn = palloc(c, es, "psnh", [128, 512])
    for k in range(2):
        P.add("pool", lambda e, k=k: e.memset(Pm.t[k][:, :], 0.0), [], [("Pm", k)])
    for h in range(4):
        lbc = c.lbs[:, l, h:h + 1]
        omc = c.oml[:, l, h:h + 1]
        P.dma("sp", zf[:, :], c.proj[OFF["a_f"] + h * 128:OFF["a_f"] + (h + 1) * 128, :], [], ["zf"], "zf")
        P.dma("sp", qf[:, :], c.proj[OFF["a_q"] + h * 128:OFF["a_q"] + (h + 1) * 128, :], [], ["qf"], "qf")
        load_vtm(c, vA, c.vA, h * 128, 128, "vA", lambda ti, nt: vA[0:nt, ti, :])
        P.add("act", lambda e: e.activation(out=zf[:, :], in_=zf[:, :], func=AF.Sigmoid), ["zf"], ["zf"])
        P.add("dve", lambda e, lbc=lbc, omc=omc: e.tensor_scalar(out=zf[:, :], in0=zf[:, :], scalar1=omc, scalar2=lbc,
                                                                 op0=ALU.mult, op1=ALU.add), ["zf", "const"], ["zf"])
        P.add("dve", lambda e: e.tensor_scalar(out=kf[:, :], in0=zf[:, :], scalar1=-1.0, scalar2=1.0, op0=ALU.mult, op1=ALU.add),
              ["zf"], ["kf"])
        P.add("act", lambda e: e.activation(out=zf[:, :], in_=zf[:, :], func=AF.Ln), ["zf", "kf"], ["zf"])
        for ti in range(NT):
            t0, nt = c.tiles[ti]
            P.add("dve", lambda e, t0=t0, nt=nt: e.tensor_tensor_scan(out=bb[:, t0:t0 + nt], data0=c.ones_f[:, 0:nt], data1=zf[:, t0:t0 + nt],
                                                                      initial=0.0, op0=ALU.mult, op1=ALU.add), ["zf", "const"], ["bb"])
        for ti in range(NT):
            t0, nt = c.tiles[ti]
            mid = t0 + nt // 2
            P.add("dve", lambda e, t0=t0, nt=nt, mid=mid: e.tensor_scalar(out=zf[:, t0:t0 + nt], in0=bb[:, t0:t0 + nt], scalar1=bb[:, mid:mid + 1],
                                                                          scalar2=None, op0=ALU.subtract), ["bb", "zf"], ["zf"])
        for ti in range(NT):
            t0, nt = c.tiles[ti]
            P.add("act", lambda e, ti=ti, t0=t0, nt=nt: e.activation(out=e1[:, ti:ti + 1], in_=bb[:, t0 + nt - 1:t0 + nt], func=AF.Exp),
                  ["bb"], ["e1"])
            P.add("act", lambda e, ti=ti, t0=t0, nt=nt: e.activation(out=e2[:, ti:ti + 1], in_=zf[:, t0 + nt - 1:t0 + nt], func=AF.Exp),
                  ["zf"], ["e2"])
        P.add("act", lambda e: e.activation(out=qf[:, :], in_=qf[:, :], func=AF.Silu), ["qf"], ["qf"])
        P.add("act", lambda e: e.activation(out=tmp[:, :], in_=bb[:, :], func=AF.Exp), ["bb"], ["tmp"])
        P.add("dve", lambda e: e.tensor_tensor(out=qt[:, :], in0=qf[:, :], in1=tmp[:, :], op=ALU.mult), ["qf", "tmp"], ["qt"])
        P.add("act", lambda e: e.activation(out=tmp[:, :], in_=zf[:, :], func=AF.Exp), ["zf", "qt"], ["tmp"])
        P.add("dve", lambda e: e.tensor_tensor(out=qh[:, :], in0=qf[:, :], in1=tmp[:, :], op=ALU.mult), ["qf", "tmp"], ["qh"])
        P.add("act", lambda e: e.activation(out=tmp[:, :], in_=zf[:, :], func=AF.Exp, scale=-1.0), ["zf", "qh"], ["tmp"])
        P.add("dve", lambda e: e.tensor_tensor(out=kh[:, :], in0=kf[:, :], in1=tmp[:, :], op=ALU.mult), ["kf", "tmp"], ["kh"])
        for ti in range(NT):
            t0, nt = c.tiles[ti]
            P.add("pe", lambda e, t0=t0, nt=nt: e.transpose(out=pstb[0:nt, :], in_=kh[:, t0:t0 + nt], identity=c.ident_bf[:, :]),
                  ["kh", "const"], ["pstb"])
            P.add("act", lambda e, ti=ti, nt=nt: e.activation(out=khT[0:nt, ti, :], in_=pstb[0:nt, :], func=AF.Copy), ["pstb"], ["khT"])
        P.add("dve", lambda e: e.memset(Sf[:, :], 0.0), [], ["Sf"])
        sb, sbk = Sb.next()
        P.add("pool", lambda e, sb=sb: e.memset(sb[:, :], 0.0), [], [sbk])
        for ti in range(NT):
            t0, nt = c.tiles[ti]
            ps, psk = pss.next()
            P.add("pe", lambda e, ps=ps, t0=t0, nt=nt: e.matmul(ps[0:nt, 0:nt], lhsT=kh[:, t0:t0 + nt], rhs=qh[:, t0:t0 + nt], start=True, stop=True),
                  ["kh", "qh"], [psk])
            pm, pmk = Pm.next()
            P.add("dve", lambda e, pm=pm, ps=ps, nt=nt: e.copy_predicated(out=pm[0:nt, 0:nt], mask=c.causT[0:nt, 0:nt], data=ps[0:nt, 0:nt]),
                  [psk, "const_c"], [pmk])
            po, pok = pso.next()
            P.add("pe", lambda e, po=po, pm=pm, ti=ti, nt=nt: e.matmul(po[:, 0:nt], lhsT=vA[0:nt, ti, :], rhs=pm[0:nt, 0:nt], start=True, stop=False),
                  ["vA", pmk], [pok])
            P.add("pe", lambda e, po=po, sb=sb, t0=t0, nt=nt: e.matmul(po[:, 0:nt], lhsT=sb[:, :], rhs=qt[:, t0:t0 + nt], start=False, stop=True),
                  [sbk, "qt"], [pok])
            P.add("act", lambda e, po=po, t0=t0, nt=nt: e.activation(out=osb[:, t0:t0 + nt], in_=po[:, 0:nt], func=AF.Copy), [pok], ["osb"])
            pd, pdk = psd.next()
            P.add("pe", lambda e, pd=pd, ti=ti, nt=nt: e.matmul(pd[:, 0:128], lhsT=khT[0:nt, ti, :], rhs=vA[0:nt, ti, :], start=True, stop=True),
                  ["khT", "vA"], [pdk])
            P.add("dve", lambda e, ti=ti: e.tensor_scalar(out=St[:, :], in0=Sf[:, :], scalar1=e1[:, ti:ti + 1], scalar2=None, op0=ALU.mult),
                  ["Sf", "e1"], ["St"])
            P.add("dve", lambda e, pd=pd, ti=ti: e.scalar_tensor_tensor(out=Sf[:, :], in0=pd[:, 0:128], scalar=e2[:, ti:ti + 1], in1=St[:, :],
                                                                        op0=ALU.mult, op1=ALU.add), [pdk, "St", "e2"], ["Sf"])
            sb, sbk = Sb.next()
            P.add("act", lambda e, sb=sb: e.activation(out=sb[:, :], in_=Sf[:, :], func=AF.Copy), ["Sf"], [sbk])
        P.dma("sp", qf[:, :], c.proj[OFF["a_g"] + h * 128:OFF["a_g"] + (h + 1) * 128, :], [], ["qf"], "qf")
        P.add("act", lambda e: e.activation(out=qf[:, :], in_=qf[:, :], func=AF.Silu), ["qf"], ["qf"])
        gcol = c.params[:, pb + PC["gnorm"] + h:pb + PC["gnorm"] + h + 1]
        for (t0, n) in split_cols(0, L, 512):
            q, qk = sq.next()
            P.add("act", lambda e, q=q, t0=t0, n=n: e.activation(out=q[:, 0:n], in_=osb[:, t0:t0 + n], func=AF.Square), ["osb"], [qk])
            P.add("pe", lambda e, q=q, n=n: e.matmul(psn[:, 0:n], lhsT=c.ones_bf[:, :], rhs=q[:, 0:n], start=True, stop=True),
                  [qk, "const"], ["psn"])
            r, rk = rs.next()
            rsqrt_op(c, r[:, 0:n], psn[:, 0:n], 1.0 / 128, ["psn"], rk)
            P.add("dve", lambda e, r=r, t0=t0, n=n, gcol=gcol: e.scalar_tensor_tensor(
                out=r[:, 0:n], in0=osb[:, t0:t0 + n], scalar=gcol, in1=r[:, 0:n], op0=ALU.mult, op1=ALU.mult),
                ["osb", rk, "params"], [rk])
            P.add("dve", lambda e, r=r, t0=t0, n=n: e.tensor_tensor(out=obr[:, t0:t0 + n], in0=r[:, 0:n], in1=qf[:, t0:t0 + n], op=ALU.mult),
                  [rk, "qf"], ["obr"])
        P.dma("sp", c.br[0][h * 128:(h + 1) * 128, :], obr[:, :], ["obr"], [], "obr")
    P.flush()
    es.close()


def dsa_phase(c, l):
    nc, P = c.nc, c.P
    es = ExitStack()
    L = c.L
    NT = len(c.tiles)
    pb = l * PL
    topk = c.topk
    rb = rows_bufs(c, es)
    st, _sq, _rs, _psn = rb
    qD = alloc(c, es, "qD", [128, 4, L], BF16)
    kD = alloc(c, es, "kD", [128, L], BF16)
    vD = alloc(c, es, "vDs", [128, NT, 129], BF16)
    kiD = alloc(c, es, "kiD", [128, L], BF16)
    sgn = alloc(c, es, "sgn", [128, NT, 16], F32)
    idx = alloc(c, es, "idx", [128, L], F32)
    work = alloc(c, es, "work", [128, L], F32)
    maskb = alloc(c, es, "maskb", [128, L], BF16)
    mT = alloc(c, es, "mT", [128, NT, 128], BF16)
    gs = alloc(c, es, "gsD", [128, 2], F32)
    zc = alloc(c, es, "zcolD", [128, 1], F32)
    m8 = alloc(c, es, "m8", [128, 8], F32)
    th = alloc(c, es, "th", [128, 1], F32)
    rls = Rot("rl", [alloc(c, es, "rl%d" % i, [128, 512], BF16) for i in range(3)])
    idxs = [idx, alloc(c, es, "idx2", [128, L], F32)]
    dgs = Rot("dg", [alloc(c, es, "dg%d" % i, [128, 16, 128], BF16) for i in range(2)])
    eTs = Rot("eT", [alloc(c, es, "eT%d" % i, [128, 512], BF16) for i in range(2)])
    pTs = Rot("pTD", [alloc(c, es, "pTD%d" % i, [128, 512], BF16) for i in range(2)])
    rr = Rot("rrD", [alloc(c, es, "rrD%d" % i, [128, 1], F32) for i in range(2)])
    ons = Rot("onD", [alloc(c, es, "onD%d" % i, [128, 128], BF16) for i in range(2)])
    ocs = Rot("ocD", [alloc(c, es, "ocD%d" % i, [128, 128], BF16) for i in range(3)])
    psi = Rot("psi", [palloc(c, es, "psi%d" % i, [128, 512]) for i in range(2)])
    pss = Rot("pssD", [palloc(c, es, "pssD%d" % i, [128, 512]) for i in range(2)])
    pso = Rot("psoD", [palloc(c, es, "psoD%d" % i, [128, 512]) for i in range(2)])
    pstb = palloc(c, es, "pstbD", [128, 128], BF16)
    P.add("dve", lambda e: e.tensor_scalar(out=gs[:, 0:1], in0=c.params[:, pb + PC["dsaq"]:pb + PC["dsaq"] + 1], scalar1=128 ** -0.5,
                                           scalar2=None, op0=ALU.mult), ["params"], ["gsc"])
    P.add("dve", lambda e: e.tensor_copy(out=gs[:, 1:2], in_=c.params[:, pb + PC["dsak"]:pb + PC["dsak"] + 1]), ["params"], ["gsc"])
    P.add("dve", lambda e: e.memset(zc[:, :], 0.0), [], ["gsc"])
    P.add("dve", lambda e: e.memset(vD[:, :, 128:129], 1.0), [], ["vD1"])
    for h in range(4):
        load_rows_norm(c, rb, OFF["d_q"] + h * 128, 128, lambda t0, n, h=h: qD[:, h, t0:t0 + n], gs[:, 0:1], c.ones_bf, 128, ("qD", h))
    load_rows_norm(c, rb, OFF["d_k"], 128, lambda t0, n: kD[:, t0:t0 + n], gs[:, 1:2], c.ones_bf, 128, "kD")
    load_rows_norm(c, rb, OFF["d_ki"], 64, lambda t0, n: kiD[:, t0:t0 + n], None, None, None, "kiD", dup64=True)
    load_vtm(c, vD, c.vD, 0, 128, "vD", lambda ti, nt: vD[0:nt, ti, 0:128])
    load_vtm(c, sgn, c.wT, 0, 16, "sgn", lambda ti, nt: sgn[0:nt, ti, :])
    P.add("act", lambda e: e.activation(out=sgn[:, :, :], in_=sgn[:, :, :], func=AF.Sign), ["sgn"], ["sgn"])
    qsts = Rot("qst", [alloc(c, es, "qst%d" % i, [128, 8, 128], F32) for i in range(2)])
    wsts = Rot("wst", [alloc(c, es, "wst%d" % i, [128, 8, 128], BF16) for i in range(2)])
    qits = Rot("qit", [alloc(c, es, "qit%d" % i, [128, 8, 128], BF16) for i in range(2)])
    mTs = [mT, alloc(c, es, "mT2", [128, NT, 128], BF16)]
    osb = [alloc(c, es, "osbD%d" % i, [128, 132], F32) for i in range(4)]
    qiv = c.proj[OFF["d_qi"]:OFF["d_qi"] + 1024, :].rearrange("(r p) t -> p r t", p=128)
    wav = c.wabs.rearrange("(r p) t -> p r t", p=128)

    def stage_a1(i):
        q0, nq = c.tiles[i]
        K = q0 + nq
        qs, qsk = qsts.next()
        ws, wsk = wsts.next()
        P.dma("sp", qs[:, :, 0:nq], qiv[:, :, q0:q0 + nq], [], [qsk], qsk)
        P.dma("sp", ws[:, :, 0:nq], wav[:, :, q0:q0 + nq], [], [wsk], wsk)
        qit, qitk = qits.next()
        P.add("pool", lambda e: e.tensor_tensor(out=qit[:, :, 0:nq], in0=qs[:, :, 0:nq], in1=ws[:, :, 0:nq], op=ALU.mult),
              [qsk, wsk], [qitk])
        idxc, idxk = idxs[i % 2], ("idx", i % 2)
        dg, dgk = dgs.next()
        for hi in range(16):
            P.add("pool", lambda e, hi=hi: e.tensor_scalar(out=dg[0:nq, hi, 0:nq], in0=c.ident_bf[0:nq, 0:nq], scalar1=sgn[0:nq, i, hi:hi + 1],
                                                           scalar2=None, op0=ALU.mult), ["const", "sgn"], [dgk])
        for (kb0, kn) in split_cols(0, K, 512):
            dots = {}

            def emit_dot(hi):
                r, half = hi // 2, hi % 2
                p, pk = psi.next()
                dots[hi] = (p, pk)
                P.add("pe", lambda e, p=p, r=r, half=half, kb0=kb0, kn=kn: e.matmul(
                    p[0:nq, 0:kn], lhsT=qit[half * 64:(half + 1) * 64, r, 0:nq], rhs=kiD[half * 64:(half + 1) * 64, kb0:kb0 + kn],
                    start=True, stop=True), [qitk, "kiD"], [pk])

            def emit_acc(hi):
                p, pk = dots[hi]
                rl, rlk = rls.next()
                P.add("act", lambda e, rl=rl, p=p, kn=kn: e.activation(out=rl[0:nq, 0:kn], in_=p[0:nq, 0:kn], func=AF.Relu), [pk], [rlk])
                P.add("pe", lambda e, rl=rl, hi=hi, kn=kn: e.matmul(_psn[0:nq, 0:kn], lhsT=dg[0:nq, hi, 0:nq], rhs=rl[0:nq, 0:kn],
                                                                   start=(hi == 0), stop=(hi == 15)), [rlk, dgk], ["psn"])

            emit_dot(0)
            for hi in range(16):
                if hi + 1 < 16:
                    emit_dot(hi + 1)
                emit_acc(hi)
            P.add("act", lambda e, kb0=kb0, kn=kn: e.activation(out=idxc[0:nq, kb0:kb0 + kn], in_=_psn[0:nq, 0:kn], func=AF.Copy),
                  ["psn"], [idxk])

    def stage_a2(i):
        q0, nq = c.tiles[i]
        K = q0 + nq
        mTc = mTs[i % 2]
        idxc, idxk = idxs[i % 2], ("idx", i % 2)
        P.add("dve", lambda e: e.tensor_tensor(out=idxc[0:nq, q0:q0 + nq], in0=idxc[0:nq, q0:q0 + nq], in1=c.fut[0:nq, 0:nq], op=ALU.add),
              [idxk, "const_f"], [idxk])
        if K > topk:
            for rd in range(topk // 8):
                src = idxc if rd == 0 else work
                P.add("dve", lambda e, src=src: e.max(out=m8[0:nq, :], in_=src[0:nq, 0:K]), [idxk, "work"], ["m8"])
                if rd < topk // 8 - 1:
                    P.add("dve", lambda e, src=src: e.match_replace(out=work[0:nq, 0:K], in_to_replace=m8[0:nq, :],
                                                                    in_values=src[0:nq, 0:K], imm_value=-1e30),
                          [idxk, "work", "m8"], ["work"])
            P.add("dve", lambda e: e.tensor_copy(out=th[0:nq, :], in_=m8[0:nq, 7:8]), ["m8"], ["th"])
        else:
            P.add("dve", lambda e: e.memset(th[0:nq, :], -1e29), ["m8"], ["th"])
        P.add("dve", lambda e: e.tensor_scalar(out=maskb[0:nq, 0:K], in0=idxc[0:nq, 0:K], scalar1=th[0:nq, 0:1], scalar2=None,
                                               op0=ALU.is_ge), [idxk, "th"], ["maskb"])
        for j in range(i + 1):
            k0, nk = c.tiles[j]
            P.add("pe", lambda e, k0=k0, nk=nk: e.transpose(out=pstb[0:nk, 0:nq], in_=maskb[0:nq, k0:k0 + nk], identity=c.ident_bf[0:nq, 0:nq]),
                  ["maskb", "const"], ["pstbD"])
            P.add("act", lambda e, j=j, nk=nk: e.activation(out=mTc[0:nk, j, 0:nq], in_=pstb[0:nk, 0:nq], func=AF.Copy), ["pstbD"], [("mT", i % 2, j)])

    def stage_b_main(i):
        q0, nq = c.tiles[i]
        mTc = mTs[i % 2]
        for h in range(4):
            po, pok = pso.next()
            for jg in range(0, i + 1, 4):
                grp = list(range(jg, min(i + 1, jg + 4)))
                ps, psk = pss.next()
                for sl, j in enumerate(grp):
                    k0, nk = c.tiles[j]
                    case = near_case(i, j)
                    P.add("pe", lambda e, ps=ps, h=h, k0=k0, nk=nk, case=case, sl=sl: e.matmul(
                        ps[0:nk, sl * 128:sl * 128 + nq], lhsT=kD[:, k0:k0 + nk], rhs=qD[:, h, q0:q0 + nq], start=True, stop=(case is None)),
                        ["kD", ("qD", h)], [psk])
                    if case is not None:
                        P.add("pe", lambda e, ps=ps, nk=nk, h=h, case=case, sl=sl: e.matmul(
                            ps[0:nk, sl * 128:sl * 128 + nq], lhsT=c.ident_bf[0:nk, 0:nk], rhs=c.BT[0:nk, (4 + h) * 3 + case, 0:nq],
                            start=False, stop=True), ["const"], [psk])
                W = (len(grp) - 1) * 128 + nq
                eT, eTk = eTs.next()
                P.add("act", lambda e, eT=eT, ps=ps, W=W: e.activation(out=eT[:, 0:W], in_=ps[:, 0:W], func=AF.Exp), [psk], [eTk])
                pT, pTk = pTs.next()
                if nq == 128:
                    ng_ = len(grp)
                    P.add("pool", lambda e, pT=pT, eT=eT, jg=jg, ng_=ng_: e.tensor_tensor(
                        out=pT[:, 0:ng_ * 128].rearrange("p (s q) -> p s q", q=128), in0=eT[:, 0:ng_ * 128].rearrange("p (s q) -> p s q", q=128),
                        in1=mTc[:, jg:jg + ng_, :], op=ALU.mult), [eTk] + [("mT", i % 2, j) for j in grp], [pTk])
                else:
                    P.add("pool", lambda e, pT=pT, eT=eT: e.tensor_tensor(
                        out=pT[0:16, 0:nq], in0=eT[0:16, 0:nq], in1=mTc[0:16, 0, 0:nq], op=ALU.mult), [eTk, ("mT", i % 2, 0)], [pTk])
                for sl, j in enumerate(grp):
                    k0, nk = c.tiles[j]
                    P.add("pe", lambda e, po=po, pT=pT, nk=nk, j=j, sl=sl: e.matmul(
                        po[0:nq, 0:129], lhsT=pT[0:nk, sl * 128:sl * 128 + nq], rhs=vD[0:nk, j, 0:129], start=(j == 0), stop=(j == i)),
                        [pTk, "vD", "vD1"], [pok])
            P.add("act", lambda e, po=po, h=h: e.activation(out=osb[h][0:nq, 0:129], in_=po[0:nq, 0:129], func=AF.Copy), [pok], [("osbD", h)])

    def stage_b_fin(i):
        q0, nq = c.tiles[i]
        for h in range(4):
            r, rk = rr.next()
            P.add("dve", lambda e, r=r, h=h: e.reciprocal(out=r[0:nq, 0:1], in_=osb[h][0:nq, 128:129]), [("osbD", h)], [rk])
            on, onk = ons.next()
            P.add("dve", lambda e, on=on, r=r, h=h: e.tensor_scalar(out=on[0:nq, :], in0=osb[h][0:nq, 0:128], scalar1=r[0:nq, 0:1],
                                                                     scalar2=None, op0=ALU.mult), [("osbD", h), rk], [onk])
            ps2, ps2k = pss.next()
            P.add("pe", lambda e, ps2=ps2, on=on: e.matmul(ps2[:, 0:nq], lhsT=on[0:nq, :], rhs=c.ident_bf[0:nq, 0:nq], start=True, stop=True),
                  [onk, "const"], [ps2k])
            oc, ock = ocs.next()
            P.add("act", lambda e, oc=oc, ps2=ps2: e.activation(out=oc[:, 0:nq], in_=ps2[:, 0:nq], func=AF.Copy), [ps2k], [ock])
            P.dma("sp", c.br[3][h * 128:(h + 1) * 128, q0:q0 + nq], oc[:, 0:nq], [ock], [], ock)

    stage_a1(0)
    if NT > 1:
        stage_a1(1)
    stage_a2(0)
    for i in range(NT):
        if i + 2 < NT:
            stage_a1(i + 2)
        stage_b_main(i)
        if i + 1 < NT:
            stage_a2(i + 1)
        stage_b_fin(i)
    P.flush()
    es.close()
```

```python
import numpy as np
from contextlib import ExitStack
import concourse.bass as bass
import concourse.mybir as mybir
from concourse.bass_utils import run_bass_kernel_spmd

F32 = mybir.dt.float32
BF16 = mybir.dt.bfloat16
I32 = mybir.dt.int32
ALU = mybir.AluOpType
AF = mybir.ActivationFunctionType
AX = mybir.AxisListType

ENGS = ("pe", "act", "dve", "pool", "sp")


class Prog:
    def __init__(self, nc):
        self.nc = nc
        self.es = ExitStack()
        self.eng_h = {"pe": nc.tensor, "act": nc.scalar, "dve": nc.vector,
                      "pool": nc.gpsimd, "sp": nc.sync}
        self.sem = {e: self.es.enter_context(nc.semaphore("s_" + e)) for e in ENGS}
        self.sig_cnt = {e: 0 for e in ENGS}
        self.waited = {e: {} for e in ENGS}
        self.dma_sems = {}
        self.semobj = {("eng", e): self.sem[e] for e in ENGS}
        self.ops = []
        self.lastw = {}
        self.readers = {}
        self.n_inst = 0
        self.phase_es = None

    def add(self, eng, fn, reads=(), writes=(), dma_key=None):
        idx = len(self.ops)
        is_dma = dma_key is not None
        deps = {}
        for r in reads:
            w = self.lastw.get(r)
            if w is not None:
                deps[w] = "raw"
        for wk in writes:
            w = self.lastw.get(wk)
            if w is not None and w not in deps:
                deps[w] = "waw"
            for r in self.readers.get(wk, {}).values():
                if r not in deps:
                    deps[r] = "war"
        op = dict(eng=eng, fn=fn, deps=deps, dma_key=dma_key, sig=False, dma_val=None)
        if is_dma:
            if dma_key not in self.dma_sems:
                s = self.es.enter_context(self.nc.semaphore("d_%d" % len(self.dma_sems)))
                self.dma_sems[dma_key] = [s, 0]
            self.dma_sems[dma_key][1] += 16
            op["dma_val"] = self.dma_sems[dma_key][1]
        self.ops.append(op)
        rk = ("dma", dma_key) if is_dma else eng
        for r in reads:
            self.readers.setdefault(r, {})[rk] = idx
        for wk in writes:
            self.lastw[wk] = idx
            self.readers[wk] = {}
        return idx

    def dma(self, q, out, in_, reads, writes, key):
        self.add(q, lambda e: e.dma_start(out=out, in_=in_), reads, writes, dma_key=key)

    def flush(self):
        ops = self.ops
        if not ops:
            return
        for i, op in enumerate(ops):
            waits = []
            for d, kind in op["deps"].items():
                od = ops[d]
                if od["dma_key"] is not None:
                    waits.append(("dma", od["dma_key"], od["dma_val"]))
                    continue
                if od["eng"] == op["eng"] and op["dma_key"] is None:
                    if op["eng"] == "pe" or kind != "raw":
                        continue
                od["sig"] = True
                waits.append(("eng", od["eng"], d))
            op["waits"] = waits
        last = {}
        for i, op in enumerate(ops):
            if op["dma_key"] is None:
                last[op["eng"]] = i
        for e, i in last.items():
            ops[i]["sig"] = True
        cnt = dict(self.sig_cnt)
        for op in ops:
            if op["dma_key"] is None and op["sig"]:
                cnt[op["eng"]] += 1
                op["sig_val"] = cnt[op["eng"]]
        end_cnt = cnt

        def emit_engine(ename):
            def body(eh):
                waited = self.waited[ename]
                for op in ops:
                    if op["eng"] != ename:
                        continue
                    for w in op["waits"]:
                        if w[0] == "dma":
                            semk = ("dma", w[1]); val = w[2]
                            s = self.dma_sems[w[1]][0]
                        else:
                            semk = ("eng", w[1]); val = ops[w[2]]["sig_val"]
                            s = self.sem[w[1]]
                        if waited.get(semk, 0) >= val:
                            continue
                        waited[semk] = val
                        eh.wait_ge(s, val)
                        self.n_inst += 1
                    ins = op["fn"](eh)
                    self.n_inst += 1
                    if op["dma_key"] is not None:
                        ins.then_inc(self.dma_sems[op["dma_key"]][0], 16)
                    elif op["sig"]:
                        ins.then_inc(self.sem[ename], 1)
                for e2 in ENGS:
                    if e2 == ename:
                        continue
                    v = end_cnt[e2]
                    if v > 0 and waited.get(("eng", e2), 0) < v:
                        waited[("eng", e2)] = v
                        eh.wait_ge(self.sem[e2], v)
                for k, (s, c) in self.dma_sems.items():
                    if c > 0 and waited.get(("dma", k), 0) < c:
                        waited[("dma", k)] = c
                        eh.wait_ge(s, c)
            return body

        with self.nc.Block() as block:
            block.tensor(emit_engine("pe"))
            block.scalar(emit_engine("act"))
            block.vector(emit_engine("dve"))
            block.gpsimd(emit_engine("pool"))
            block.sync(emit_engine("sp"))
        self.sig_cnt = end_cnt
        self.ops = []
        self.lastw = {}
        self.readers = {}

    def close(self):
        self.flush()
        self.es.close()
import math


D = 2048
NCH = 16
DFF = 5632
NJ = 44
EPS = 1e-6
OFF = dict(a_q=0, a_f=512, a_i=1024, a_g=1536, b_b=2048, b_c=2560, b_u=3072, c_q=3584, c_k=4096,
           c_v=4608, d_q=5120, d_k=5632, d_v=5760, d_qi=5888, d_ki=6912, d_w=6976, gate=6992)
NFM = 6976
NEG = -30000.0


class Rot:
    def __init__(self, name, tensors):
        self.name = name
        self.t = tensors
        self.i = 0

    def next(self):
        k = self.i % len(self.t)
        self.i += 1
        return self.t[k], (self.name, k)


def split_cols(t0, n, step):
    out = []
    o = 0
    while o < n:
        m = min(step, n - o)
        out.append((t0 + o, m))
        o += m
    return out


class Ctx:
    pass


def token_tiles(L):
    return [(0, 16)] + [(16 + 128 * i, 128) for i in range((L - 16) // 128)]


def make_groups(L, G):
    tl = token_tiles(L)
    nt = len(tl) - 1
    G = min(G, nt)
    sizes = [nt // G + (1 if k < nt % G else 0) for k in range(G)]
    groups = []
    i = 1
    first = True
    while i <= nt:
        j = min(nt, i + sizes[len(groups)] - 1)
        g0 = 0 if first else tl[i][0]
        g1 = tl[j][0] + tl[j][1]
        subs = []
        if first:
            subs.append((0, 16))
            subs += split_cols(16, g1 - 16, 512)
        else:
            subs += split_cols(g0, g1 - g0, 512)
        tiles = ([0] if first else []) + list(range(i, j + 1))
        groups.append((g0, g1 - g0, subs, tiles))
        first = False
        i = j + 1
    return groups


def fm(ap2d):
    return ap2d.rearrange("(c p) t -> p c t", p=128)


_UID = [0]


def alloc(c, es, name, shape, dt):
    _UID[0] += 1
    return es.enter_context(c.nc.sbuf_tensor("s%d_%s" % (_UID[0], name), list(shape), dt))


def palloc(c, es, name, shape, dt=None):
    _UID[0] += 1
    return es.enter_context(c.nc.psum_tensor("p%d_%s" % (_UID[0], name), list(shape), dt or F32))


def rsqrt_op(c, out, in_, scale, reads, wkey, np_=128):
    P = c.P
    P.add("act", lambda e: e.activation(out=out, in_=in_, func=AF.Sqrt, bias=c.epsc[0:np_, 0:1], scale=scale),
          list(reads) + ["const"], [wkey])
    P.add("dve", lambda e: e.reciprocal(out=out, in_=out), [wkey], [wkey])


def norm_stage(c, es_bufs, src, gcol, hbuf, hkey, g0, ng):
    for _ in norm_gen(c, es_bufs, src, gcol, hbuf, hkey, g0, ng):
        pass


def norm_gen(c, es_bufs, src, gcol, hbuf, hkey, g0, ng):
    P = c.P
    xts, sqs, rstd, psn = es_bufs
    srcv = fm(src)
    for (t0, n) in split_cols(g0, ng, 128):
        off = t0 - g0
        xt, xk = xts.next()
        P.dma("sp", xt[:, :, 0:n], srcv[:, :, t0:t0 + n], [], [xk], xk)
        for ch in range(NCH):
            sq, sk = sqs.next()
            P.add("act", lambda e, sq=sq, xt=xt, ch=ch, n=n: e.activation(out=sq[:, 0:n], in_=xt[:, ch, 0:n], func=AF.Square),
                  [xk], [sk])
            P.add("pe", lambda e, sq=sq, ch=ch, n=n: e.matmul(psn[:, 0:n], lhsT=c.ones_bf[:, :], rhs=sq[:, 0:n],
                                                              start=(ch == 0), stop=(ch == NCH - 1)),
                  [sk, "const"], ["psn"])
        rsqrt_op(c, rstd[:, 0:n], psn[:, 0:n], 1.0 / D, ["psn"], "rstd")
        for ch in range(NCH):
            P.add("dve", lambda e, xt=xt, ch=ch, n=n, off=off: e.scalar_tensor_tensor(
                out=hbuf[:, ch, off:off + n], in0=xt[:, ch, 0:n], scalar=gcol[:, ch:ch + 1], in1=rstd[:, 0:n],
                op0=ALU.mult, op1=ALU.mult), [xk, "rstd", "params"], [hkey])
        yield


def norm_bufs(c, es):
    xts = Rot("xt", [alloc(c, es, "xt%d" % i, [128, NCH, 128], F32) for i in range(2)])
    sqs = Rot("sq", [alloc(c, es, "sq%d" % i, [128, 256], BF16) for i in range(3)])
    rstd = alloc(c, es, "rstd", [128, 256], F32)
    psn = palloc(c, es, "psn", [128, 512])
    return (xts, sqs, rstd, psn)


def ffn_phase(c, src, dst, gcol, w_gu, w_down):
    nc, P = c.nc, c.P
    es = ExitStack()
    groups = make_groups(c.L, c.G_ffn)
    NG = max(g[1] for g in groups)
    nb = norm_bufs(c, es)
    hbuf = alloc(c, es, "hbuf", [128, NCH, NG], BF16)
    act = alloc(c, es, "actb", [128, NJ, NG], BF16)
    wgu = Rot("wgu", [alloc(c, es, "wgu%d" % i, [128, NCH, 256], BF16) for i in range(2)])
    wd = Rot("wd", [alloc(c, es, "wd%d" % i, [128, NJ, 128], BF16) for i in range(2)])
    sgs = Rot("sg", [alloc(c, es, "sg%d" % i, [128, 512], F32) for i in range(2)])
    xrs = Rot("xr", [alloc(c, es, "xr%d" % i, [128, 512], F32) for i in range(2)])
    xos = Rot("xo", [alloc(c, es, "xo%d" % i, [128, 512], F32) for i in range(2)])
    psg = Rot("psg", [palloc(c, es, "psg%d" % i, [128, 512]) for i in range(2)])
    psu = Rot("psu", [palloc(c, es, "psu%d" % i, [128, 512]) for i in range(2)])
    pso = Rot("pso", [palloc(c, es, "pso%d" % i, [128, 512]) for i in range(2)])
    wguv = fm(w_gu)
    wdv = fm(w_down)
    srcv, dstv = fm(src), fm(dst)
    for gi_, (g0, ng, subs, _tiles) in enumerate(groups):
        if gi_ == 0:
            norm_stage(c, nb, src, gcol, hbuf, "hbuf", g0, ng)
        if gi_ + 1 < len(groups):
            ngen = norm_gen(c, nb, src, gcol, hbuf, "hbuf", groups[gi_ + 1][0], groups[gi_ + 1][1])
        else:
            ngen = iter(())
        for j in range(NJ):
            w, wk = wgu.next()
            P.dma("pool", w[:, :, 0:128], wguv[:, :, j * 128:(j + 1) * 128], [], [wk], wk)
            P.dma("pool", w[:, :, 128:256], wguv[:, :, DFF + j * 128:DFF + (j + 1) * 128], [], [wk], wk)
            for (t0, n) in subs:
                off = t0 - g0
                pg, pgk = psg.next()
                pu, puk = psu.next()
                for ch in range(NCH):
                    P.add("pe", lambda e, pg=pg, w=w, ch=ch, off=off, n=n: e.matmul(
                        pg[:, 0:n], lhsT=w[:, ch, 0:128], rhs=hbuf[:, ch, off:off + n], start=(ch == 0), stop=(ch == NCH - 1)),
                        [wk, "hbuf"], [pgk])
                for ch in range(NCH):
                    P.add("pe", lambda e, pu=pu, w=w, ch=ch, off=off, n=n: e.matmul(
                        pu[:, 0:n], lhsT=w[:, ch, 128:256], rhs=hbuf[:, ch, off:off + n], start=(ch == 0), stop=(ch == NCH - 1)),
                        [wk, "hbuf"], [puk])
                sg, sgk = sgs.next()
                P.add("act", lambda e, sg=sg, pg=pg, n=n: e.activation(out=sg[:, 0:n], in_=pg[:, 0:n], func=AF.Silu),
                      [pgk], [sgk])
                P.add("dve", lambda e, sg=sg, pu=pu, j=j, off=off, n=n: e.tensor_tensor(
                    out=act[:, j, off:off + n], in0=sg[:, 0:n], in1=pu[:, 0:n], op=ALU.mult), [sgk, puk], [("act", j)])
        for m in range(NCH):
            w, wk = wd.next()
            P.dma("pool", w[:, :, :], wdv[:, :, m * 128:(m + 1) * 128], [], [wk], wk)
            for (t0, n) in subs:
                off = t0 - g0
                xr, xrk = xrs.next()
                P.dma("sp", xr[:, 0:n], srcv[:, m, t0:t0 + n], [], [xrk], xrk)
                po, pok = pso.next()
                for j in range(NJ):
                    P.add("pe", lambda e, po=po, w=w, j=j, off=off, n=n: e.matmul(
                        po[:, 0:n], lhsT=w[:, j, :], rhs=act[:, j, off:off + n], start=(j == 0), stop=(j == NJ - 1)),
                        [wk, ("act", j)], [pok])
                xo, xok = xos.next()
                P.add("dve", lambda e, xo=xo, po=po, xr=xr, n=n: e.scalar_tensor_tensor(
                    out=xo[:, 0:n], in0=po[:, 0:n], scalar=0.5, in1=xr[:, 0:n], op0=ALU.mult, op1=ALU.add),
                    [pok, xrk], [xok])
                P.dma("act", dstv[:, m, t0:t0 + n], xo[:, 0:n], [xok], [], xok)
            if m >= 2:
                next(ngen, None)
        for _ in ngen:
            pass
    P.flush()
    es.close()


PL = 80
PC = dict(ffn1=0, mix=16, ffn2=32, gnorm=48, conv=52, dqn=64, dkn=65, subln=66, dsaq=67, dsak=68, lb=69, lam=73)


def pack_params(inp):
    p = np.zeros((128, 2 * PL), np.float32)
    for l in range(2):
        b = l * PL
        p[:, b + PC["ffn1"]:b + PC["ffn1"] + 16] = inp["ffn1_norm"][l].reshape(16, 128).T
        p[:, b + PC["mix"]:b + PC["mix"] + 16] = inp["mix_norm"][l].reshape(16, 128).T
        p[:, b + PC["ffn2"]:b + PC["ffn2"] + 16] = inp["ffn2_norm"][l].reshape(16, 128).T
        p[:, b + PC["gnorm"]:b + PC["gnorm"] + 4] = inp["hgrn_gnorm"][l].reshape(4, 128).T
        for j in range(3):
            p[:, b + PC["conv"] + j * 4:b + PC["conv"] + j * 4 + 4] = inp["conv_w"][l, j].reshape(4, 128).T
        p[:, b + PC["dqn"]] = np.tile(inp["diff_q_norm"][l], 2)
        p[:, b + PC["dkn"]] = np.tile(inp["diff_k_norm"][l], 2)
        p[:, b + PC["subln"]] = inp["diff_subln"][l]
        p[:, b + PC["dsaq"]] = inp["dsa_q_norm"][l]
        p[:, b + PC["dsak"]] = inp["dsa_k_norm"][l]
        p[:, b + PC["lb"]:b + PC["lb"] + 4] = inp["hgrn_lb"][l].reshape(4, 128).T
        p[0:64, b + PC["lam"]:b + PC["lam"] + 4] = inp["diff_lambda"][l].T
    return p


def rel_bucket_np(n):
    n = np.maximum(n, 0)
    nf = np.maximum(n, 1).astype(np.float32)
    large = 16 + (np.log(nf / np.float32(16)) / np.float32(math.log(128 / 16)) * np.float32(16)).astype(np.int32)
    large = np.minimum(large, 31)
    return np.where(n < 16, n, large)


def make_consts():
    cst = {}
    cst["ident"] = np.eye(128, dtype=np.float32)
    sel = np.zeros((33, 3, 256), np.float32)
    for ci, delta in enumerate((0, 128, 16)):
        m = np.arange(255)
        n = m - 127 + delta
        bk = rel_bucket_np(n)
        for mm in range(255):
            if n[mm] < 0:
                sel[32, ci, mm] = 1.0
            else:
                sel[bk[mm], ci, mm] = 1.0
    cst["sel"] = sel
    e31 = np.zeros((32, 128), np.float32)
    e31[31, :] = 1.0
    cst["e31"] = e31
    o64 = np.zeros((128, 128), np.float32)
    o64[:64, :64] = 1.0
    o64[64:, 64:] = 1.0
    cst["ones64"] = o64
    q = np.arange(128)[:, None]
    k = np.arange(128)[None, :]
    cst["fut"] = np.where(k > q, -1e30, 0.0).astype(np.float32)
    cst["causT"] = (q <= k).astype(np.int32)
    return cst


WNAMES = ("ffn1_w_gu", "ffn1_w_down", "w_in", "w_branch", "w_out", "ffn2_w_gu", "ffn2_w_down")


def build(S, topk, G=4, phases=None, dbg=False):
    L = S + 16
    nc = bass.Bass("TRN2", target_bir_lowering=False)
    c = Ctx()
    c.nc = nc
    c.L, c.S, c.topk = L, S, topk
    c.tiles = token_tiles(L)
    c.G_ffn, c.G_proj, c.G_merge = (4, 2, 4) if S >= 2048 else (2, 2, 2)
    c.dbg = dbg

    def din(name, shape, dt=F32):
        return nc.dram_tensor(name, list(shape), dt, kind="ExternalInput").ap()

    def dscr(name, shape, dt=F32):
        kind = "ExternalOutput" if dbg else "Internal"
        return nc.dram_tensor(name, list(shape), dt, kind=kind).ap()

    c.xin = din("xin", [D, L])
    c.params_d = din("params", [128, 2 * PL])
    c.relb_d = din("relb", [32, 8])
    c.ident_d = din("ident", [128, 128])
    c.sel_d = din("sel", [33, 3, 256])
    c.e31_d = din("e31", [32, 128])
    c.ones64_d = din("ones64", [128, 128])
    c.fut_d = din("fut", [128, 128])
    c.causT_d = din("causT", [128, 128], I32)
    c.w = {}
    c.w["ffn1_w_gu"] = din("ffn1_w_gu", [2, D, 2 * DFF])
    c.w["ffn1_w_down"] = din("ffn1_w_down", [2, DFF, D])
    c.w["ffn2_w_gu"] = din("ffn2_w_gu", [2, D, 2 * DFF])
    c.w["ffn2_w_down"] = din("ffn2_w_down", [2, DFF, D])
    c.w["w_in"] = din("w_in", [2, D, 15184])
    c.w["w_dw_rep"] = din("w_dw_rep", [2, D, 1024])
    c.w["w_branch"] = din("w_branch", [2, 4, 512, D])
    c.w["w_out"] = din("w_out", [2, D, D])
    c.yout = nc.dram_tensor("yout", [D, L], F32, kind="ExternalOutput").ap()
    c.xres = dscr("xres", [D, L])
    c.proj = dscr("proj", [NFM, L])
    c.wabs = dscr("wabs", [1024, L], BF16)
    c.vA = dscr("vA", [L, 512], BF16)
    c.vC = dscr("vC", [L, 512], BF16)
    c.vD = dscr("vD", [L, 128], BF16)
    c.wT = dscr("wT", [L, 16])
    c.br = [dscr("br%d" % i, [512, L], BF16) for i in range(4)]
    c.gsc = dscr("gsc", [24, 128, 255])

    es = ExitStack()
    P = Prog(nc)
    c.P = P
    c.ident_f = alloc(c, es, "ident_f", [128, 128], F32)
    c.ident_bf = alloc(c, es, "ident_bf", [128, 128], BF16)
    c.ones_bf = alloc(c, es, "ones_bf", [128, 128], BF16)
    c.ones_f = alloc(c, es, "ones_f", [128, 128], F32)
    c.epsc = alloc(c, es, "epsc", [128, 1], F32)
    c.ones64_bf = alloc(c, es, "ones64_bf", [128, 128], BF16)
    c.params = alloc(c, es, "params_sb", [128, 2 * PL], F32)
    c.BT = alloc(c, es, "BT", [128, 24, 128], BF16)
    c.cbias = alloc(c, es, "cbias", [128, 8], F32)
    c.fut = alloc(c, es, "fut", [128, 128], F32)
    c.causT = alloc(c, es, "causT", [128, 128], I32)
    c.lbs = alloc(c, es, "lbs", [128, 2, 4], F32)
    c.oml = alloc(c, es, "oml", [128, 2, 4], F32)
    c.nlam = alloc(c, es, "nlam", [128, 2], F32)
    setup_phase(c)

    ph = phases
    for l in range(2):
        pb = l * PL
        first = (l == 0)
        if ph is None or ("ffn1_%d" % l) in ph:
            ffn_phase(c, c.xin if first else c.xres, c.xres, c.params[:, pb + PC["ffn1"]:pb + PC["ffn1"] + 16],
                      c.w["ffn1_w_gu"][l], c.w["ffn1_w_down"][l])
        if ph is None or ("proj_%d" % l) in ph:
            proj_phase(c, l)
        if ph is None or ("conv_%d" % l) in ph:
            conv_phase(c, l)
        if ph is None or ("hgrn_%d" % l) in ph:
            hgrn_phase(c, l)
        if ph is None or ("diff_%d" % l) in ph:
            diff_phase(c, l)
        if ph is None or ("dsa_%d" % l) in ph:
            dsa_phase(c, l)
        if ph is None or ("merge_%d" % l) in ph:
            merge_phase(c, l)
        if ph is None or ("ffn2_%d" % l) in ph:
            ffn_phase(c, c.xres, c.yout if l == 1 else c.xres, c.params[:, pb + PC["ffn2"]:pb + PC["ffn2"] + 16],
                      c.w["ffn2_w_gu"][l], c.w["ffn2_w_down"][l])
    P.close()
    es.close()
    return nc, c


def setup_phase(c):
    nc, P = c.nc, c.P
    es = ExitStack()
    P.dma("sp", c.ident_f[:], c.ident_d, [], ["const_i"], "ident_f")
    P.dma("sp", c.params[:], c.params_d, [], ["params"], "params")
    P.dma("sp", c.fut[:], c.fut_d, [], ["const_f"], "fut")
    P.dma("sp", c.causT[:], c.causT_d, [], ["const_c"], "causT")
    o64 = alloc(c, es, "o64f", [128, 128], F32)
    P.dma("sp", o64[:], c.ones64_d, [], ["o64f"], "o64f")
    P.add("dve", lambda e: e.tensor_copy(out=c.ident_bf[:], in_=c.ident_f[:]), ["const_i"], ["const"])
    P.add("dve", lambda e: e.tensor_copy(out=c.ones64_bf[:], in_=o64[:]), ["o64f"], ["const"])
    P.add("dve", lambda e: e.memset(c.ones_bf[:], 1.0), [], ["const"])
    P.add("dve", lambda e: e.memset(c.ones_f[:], 1.0), [], ["const"])
    P.add("dve", lambda e: e.memset(c.epsc[:], EPS), [], ["const"])
    tab = alloc(c, es, "tab", [32, 8], F32)
    sel = alloc(c, es, "selsb", [33, 3, 256], F32)
    e31 = alloc(c, es, "e31sb", [32, 128], F32)
    P.dma("sp", tab[:], c.relb_d, [], ["tab"], "tab")
    P.dma("sp", sel[:], c.sel_d, [], ["sel"], "sel")
    P.dma("sp", e31[:], c.e31_d, [], ["e31"], "e31")
    tabB = Rot("tabB", [alloc(c, es, "tabB%d" % i, [33, 128], F32) for i in range(2)])
    gsb = Rot("gsb", [alloc(c, es, "gsb%d" % i, [128, 255], F32) for i in range(2)])
    btf = Rot("btf", [alloc(c, es, "btf%d" % i, [128, 128], F32) for i in range(2)])
    psG = Rot("psG", [palloc(c, es, "psG%d" % i, [128, 256]) for i in range(2)])
    psc = palloc(c, es, "psc", [128, 8])
    P.add("pe", lambda e: e.matmul(psc[:, :], lhsT=e31[:, :], rhs=tab[:, :], start=True, stop=True), ["e31", "tab"], ["psc"])
    P.add("dve", lambda e: e.tensor_copy(out=c.cbias[:], in_=psc[:]), ["psc"], ["const"])
    for hh in range(8):
        tb, tbk = tabB.next()
        P.add("dve", lambda e, tb=tb: e.memset(tb[:, :], NEG), [], [tbk])
        P.add("dve", lambda e, tb=tb, hh=hh: e.tensor_scalar(out=tb[0:32, :], in0=c.ones_f[0:32, :], scalar1=tab[0:32, hh:hh + 1],
                                                             scalar2=None, op0=ALU.mult), ["tab", "const", tbk], [tbk])
        for ci in range(3):
            pg, pgk = psG.next()
            P.add("pe", lambda e, pg=pg, tb=tb, ci=ci: e.matmul(pg[:, 0:256], lhsT=tb[:, :], rhs=sel[:, ci, :], start=True, stop=True),
                  [tbk, "sel"], [pgk])
            gs, gsk = gsb.next()
            P.add("act", lambda e, gs=gs, pg=pg: e.activation(out=gs[:, :], in_=pg[:, 0:255], func=AF.Copy), [pgk], [gsk])
            idx = hh * 3 + ci
            P.dma("sp", c.gsc[idx], gs[:, :], [gsk], [("gsc", idx)], gsk)
            bt, btk = btf.next()
            skew = bass.AP(tensor=c.gsc.tensor, offset=idx * 128 * 255 + 127, ap=[[254, 128], [1, 128]])
            P.dma("sp", bt[:, :], skew, [("gsc", idx)], [btk], btk)
            P.add("dve", lambda e, bt=bt, idx=idx, hh=hh: e.tensor_scalar(out=c.BT[:, idx, :], in0=bt[:, :], scalar1=c.cbias[:, hh:hh + 1],
                                                                          scalar2=None, op0=ALU.subtract), [btk, "const"], ["const"])
    P.add("dve", lambda e: e.memset(c.lbs[:, 0, :], 0.0), [], ["const"])
    P.add("dve", lambda e: e.memset(c.oml[:, 0, :], 1.0), [], ["const"])
    dl = alloc(c, es, "dl", [128, 4], F32)
    P.add("dve", lambda e: e.tensor_tensor(out=dl[:, :], in0=c.params[:, PL + PC["lb"]:PL + PC["lb"] + 4],
                                           in1=c.params[:, PC["lb"]:PC["lb"] + 4], op=ALU.subtract), ["params"], ["dl"])
    P.add("act", lambda e: e.activation(out=c.lbs[:, 1, :], in_=dl[:, :], func=AF.Sigmoid), ["dl"], ["const"])
    P.add("dve", lambda e: e.tensor_scalar(out=c.oml[:, 1, :], in0=c.lbs[:, 1, :], scalar1=-1.0, scalar2=1.0,
                                           op0=ALU.mult, op1=ALU.add), ["const"], ["const"])
    pr = alloc(c, es, "pr", [128, 4], F32)
    psl = palloc(c, es, "psl", [128, 4])
    el = alloc(c, es, "el", [128, 4], F32)
    for l in range(2):
        b = l * PL + PC["lam"]
        P.add("dve", lambda e, l=l, b=b: e.tensor_tensor(out=pr[:, 2 * l:2 * l + 1], in0=c.params[:, b:b + 1], in1=c.params[:, b + 1:b + 2],
                                                         op=ALU.mult), ["params"], ["pr"])
        P.add("dve", lambda e, l=l, b=b: e.tensor_tensor(out=pr[:, 2 * l + 1:2 * l + 2], in0=c.params[:, b + 2:b + 3], in1=c.params[:, b + 3:b + 4],
                                                         op=ALU.mult), ["params"], ["pr"])
    P.add("pe", lambda e: e.matmul(psl[:, :], lhsT=c.ones_f[:, :], rhs=pr[:, :], start=True, stop=True), ["pr", "const"], ["psl"])
    P.add("act", lambda e: e.activation(out=el[:, :], in_=psl[:, :], func=AF.Exp), ["psl"], ["el"])
    for l in range(2):
        lam_init = 0.8 - 0.6 * math.exp(-0.3 * l)
        P.add("dve", lambda e, l=l, li=lam_init: e.scalar_tensor_tensor(
            out=c.nlam[:, l:l + 1], in0=el[:, 2 * l + 1:2 * l + 2], scalar=-li, in1=el[:, 2 * l:2 * l + 1],
            op0=ALU.add, op1=ALU.subtract), ["el"], ["const"])
    P.flush()
    es.close()


_CACHE = {}


def host_inputs(inp, b, consts, params, w_dw_rep):
    x = inp["x"]
    xin = np.ascontiguousarray(np.concatenate([inp["meta_tokens"].T, x[b].T], axis=1), dtype=np.float32)
    m = {"xin": xin, "params": params, "relb": np.ascontiguousarray(inp["rel_bias"], dtype=np.float32),
         "ident": consts["ident"], "sel": consts["sel"], "e31": consts["e31"], "ones64": consts["ones64"],
         "fut": consts["fut"], "causT": consts["causT"], "w_dw_rep": w_dw_rep}
    for k in WNAMES:
        m[k] = np.ascontiguousarray(inp[k], dtype=np.float32)
    return m


def run(inp, phases=None, dbg=False, G=4, trace=False):
    inp = {k: np.asarray(v) for k, v in inp.items()}
    B, S, _ = inp["x"].shape
    topk = min(256, S // 4)
    nc, c = build(S, topk, G=G, phases=phases, dbg=dbg)
    consts = make_consts()
    params = pack_params(inp)
    w16 = inp["w_in"][:, :, OFF["d_w"]:OFF["d_w"] + 16]
    w_dw_rep = np.ascontiguousarray(np.repeat(w16, 64, axis=2), dtype=np.float32)
    in_maps = [host_inputs(inp, b, consts, params, w_dw_rep) for b in range(B)]
    res = run_bass_kernel_spmd(nc, in_maps, core_ids=list(range(B)), trace=trace)
    return res, c


def kernel(**inputs):
    res, c = run(inputs)
    outs = [r["yout"] for r in res.results]
    y = np.stack([np.ascontiguousarray(o[:, 16:].T) for o in outs], axis=0)
    return y.astype(np.float32)


def fm_chunks():
    skip = [(OFF["a_i"], OFF["a_g"]), (OFF["c_v"], OFF["d_q"]), (OFF["d_v"], OFF["d_qi"])]
    out = []
    c0 = 0
    while c0 < NFM:
        n = min(128, NFM - c0)
        if not any(a <= c0 < b for a, b in skip):
            out.append((c0, n))
        c0 += n
    return out


def proj_phase(c, l):
    nc, P = c.nc, c.P
    es = ExitStack()
    groups = make_groups(c.L, c.G_proj)
    NG = max(g[1] for g in groups)
    pb = l * PL
    nb = norm_bufs(c, es)
    hbuf = alloc(c, es, "hbuf", [128, NCH, NG], BF16)
    wfm = Rot("wfm", [alloc(c, es, "wfm%d" % i, [128, NCH, 128], BF16) for i in range(3)])
    wtm = Rot("wtm", [alloc(c, es, "wtm%d" % i, [128, NCH, 512], BF16) for i in range(2)])
    evs = Rot("ev", [alloc(c, es, "ev%d" % i, [128, 512], F32) for i in range(2)])
    evb = Rot("evb", [alloc(c, es, "evb%d" % i, [128, 512], BF16) for i in range(2)])
    ps = Rot("ps", [palloc(c, es, "ps%d" % i, [128, 512]) for i in range(4)])
    win = fm(c.w["w_in"][l])
    wrep = fm(c.w["w_dw_rep"][l])
    gcol = c.params[:, pb + PC["mix"]:pb + PC["mix"] + 16]
    hbufs = [hbuf, alloc(c, es, "hbuf2", [128, NCH, NG], BF16)]
    tmc_ = [0]
    for gi_, (g0, ng, subs, tiles) in enumerate(groups):
        hbuf = hbufs[gi_ % 2]
        hk = ("hbuf", gi_ % 2)
        if gi_ == 0:
            norm_stage(c, nb, c.xres, gcol, hbuf, hk, g0, ng)
        if gi_ + 1 < len(groups):
            ngen = norm_gen(c, nb, c.xres, gcol, hbufs[(gi_ + 1) % 2], ("hbuf", (gi_ + 1) % 2), groups[gi_ + 1][0], groups[gi_ + 1][1])
        else:
            ngen = iter(())
        for (c0, mc) in fm_chunks():
            w, wk = wfm.next()
            P.dma("pool", w[:, :, 0:mc], win[:, :, c0:c0 + mc], [], [wk], wk)
            for (t0, n) in subs:
                off = t0 - g0
                p, pk = ps.next()
                for ch in range(NCH):
                    P.add("pe", lambda e, p=p, w=w, ch=ch, off=off, n=n, mc=mc, hbuf=hbuf: e.matmul(
                        p[0:mc, 0:n], lhsT=w[:, ch, 0:mc], rhs=hbuf[:, ch, off:off + n], start=(ch == 0), stop=(ch == NCH - 1)),
                        [wk, hk], [pk])
                ev, ek = evs.next()
                P.add("act", lambda e, ev=ev, p=p, n=n, mc=mc: e.activation(out=ev[0:mc, 0:n], in_=p[0:mc, 0:n], func=AF.Copy),
                      [pk], [ek])
                P.dma("sp", c.proj[c0:c0 + mc, t0:t0 + n], ev[0:mc, 0:n], [ek], [], ek)
        for r in range(8):
            w, wk = wfm.next()
            P.dma("pool", w[:, :, :], wrep[:, :, r * 128:(r + 1) * 128], [], [wk], wk)
            for (t0, n) in subs:
                off = t0 - g0
                p, pk = ps.next()
                for ch in range(NCH):
                    P.add("pe", lambda e, p=p, w=w, ch=ch, off=off, n=n, hbuf=hbuf: e.matmul(
                        p[:, 0:n], lhsT=w[:, ch, :], rhs=hbuf[:, ch, off:off + n], start=(ch == 0), stop=(ch == NCH - 1)),
                        [wk, hk], [pk])
                ev, ek = evb.next()
                P.add("act", lambda e, ev=ev, p=p, n=n: e.activation(out=ev[:, 0:n], in_=p[:, 0:n], func=AF.Abs), [pk], [ek])
                P.dma("sp", c.wabs[r * 128:(r + 1) * 128, t0:t0 + n], ev[:, 0:n], [ek], [], ek)
        for (c0, ncol, dst, isf32) in ((OFF["a_i"], 512, c.vA, False), (OFF["c_v"], 512, c.vC, False),
                                       (OFF["d_v"], 128, c.vD, False), (OFF["d_w"], 16, c.wT, True)):
            w, wk = wtm.next()
            P.dma("pool", w[:, :, 0:ncol], win[:, :, c0:c0 + ncol], [], [wk], wk)
            for ti in tiles:
                t0, nt = c.tiles[ti]
                off = t0 - g0
                p, pk = ps.next()
                for ch in range(NCH):
                    P.add("pe", lambda e, p=p, w=w, ch=ch, off=off, nt=nt, ncol=ncol, hbuf=hbuf: e.matmul(
                        p[0:nt, 0:ncol], lhsT=hbuf[:, ch, off:off + nt], rhs=w[:, ch, 0:ncol], start=(ch == 0), stop=(ch == NCH - 1)),
                        [wk, hk], [pk])
                if isf32:
                    ev, ek = evs.next()
                else:
                    ev, ek = evb.next()
                P.add("act", lambda e, ev=ev, p=p, nt=nt, ncol=ncol: e.activation(out=ev[0:nt, 0:ncol], in_=p[0:nt, 0:ncol], func=AF.Copy),
                      [pk], [ek])
                P.dma("sp", dst[t0:t0 + nt, 0:ncol], ev[0:nt, 0:ncol], [ek], [], ek)
                tmc_[0] += 1
                if tmc_[0] % 3 == 0:
                    next(ngen, None)
        for _ in ngen:
            pass
    P.flush()
    es.close()


def conv_phase(c, l):
    nc, P = c.nc, c.P
    es = ExitStack()
    L = c.L
    pb = l * PL + PC["conv"]
    bb = Rot("bb", [alloc(c, es, "bb%d" % i, [128, L], F32) for i in range(2)])
    bc = Rot("bc", [alloc(c, es, "bc%d" % i, [128, L], F32) for i in range(2)])
    bu = Rot("bu", [alloc(c, es, "bu%d" % i, [128, L], F32) for i in range(2)])
    zc = alloc(c, es, "zc", [128, L + 2], F32)
    yb = alloc(c, es, "yb", [128, L], F32)
    ob = Rot("ob", [alloc(c, es, "ob%d" % i, [128, L], BF16) for i in range(2)])
    P.add("dve", lambda e: e.memset(zc[:, 0:2], 0.0), [], ["zc0"])
    for ch in range(4):
        tb, tbk = bb.next()
        tc_, tck = bc.next()
        tu, tuk = bu.next()
        P.dma("sp", tb[:, :], c.proj[OFF["b_b"] + ch * 128:OFF["b_b"] + (ch + 1) * 128, :], [], [tbk], tbk)
        P.dma("sp", tc_[:, :], c.proj[OFF["b_c"] + ch * 128:OFF["b_c"] + (ch + 1) * 128, :], [], [tck], tck)
        P.dma("sp", tu[:, :], c.proj[OFF["b_u"] + ch * 128:OFF["b_u"] + (ch + 1) * 128, :], [], [tuk], tuk)
        P.add("pool", lambda e, tc_=tc_, tu=tu: e.tensor_tensor(out=zc[:, 2:L + 2], in0=tc_[:, :], in1=tu[:, :], op=ALU.mult),
              [tck, tuk], ["zc"])
        w0 = c.params[:, pb + ch:pb + ch + 1]
        w1 = c.params[:, pb + 4 + ch:pb + 4 + ch + 1]
        w2 = c.params[:, pb + 8 + ch:pb + 8 + ch + 1]
        P.add("dve", lambda e, w0=w0: e.tensor_scalar(out=yb[:, :], in0=zc[:, 2:L + 2], scalar1=w0, scalar2=None, op0=ALU.mult),
              ["zc", "zc0", "params"], ["yb"])
        P.add("dve", lambda e, w1=w1: e.scalar_tensor_tensor(out=yb[:, :], in0=zc[:, 1:L + 1], scalar=w1, in1=yb[:, :],
                                                             op0=ALU.mult, op1=ALU.add), ["zc", "zc0", "yb", "params"], ["yb"])
        P.add("dve", lambda e, w2=w2: e.scalar_tensor_tensor(out=yb[:, :], in0=zc[:, 0:L], scalar=w2, in1=yb[:, :],
                                                             op0=ALU.mult, op1=ALU.add), ["zc", "zc0", "yb", "params"], ["yb"])
        o, ok = ob.next()
        P.add("dve", lambda e, o=o, tb=tb: e.tensor_tensor(out=o[:, :], in0=yb[:, :], in1=tb[:, :], op=ALU.mult), ["yb", tbk], [ok])
        P.dma("sp", c.br[1][ch * 128:(ch + 1) * 128, :], o[:, :], [ok], [], ok)
    P.flush()
    es.close()


def merge_phase(c, l):
    nc, P = c.nc, c.P
    es = ExitStack()
    groups = make_groups(c.L, c.G_merge)
    NG = max(g[1] for g in groups)
    pb = l * PL
    nb = norm_bufs(c, es)
    hbuf = alloc(c, es, "hbuf", [128, NCH, NG], BF16)
    brb = alloc(c, es, "brb", [128, 16, NG], BF16)
    mrg = alloc(c, es, "mrg", [128, NCH, NG], BF16)
    wg = Rot("wg", [alloc(c, es, "wg%d" % i, [128, NCH, 128], BF16) for i in range(3)])
    wb = Rot("wb", [alloc(c, es, "wb%d" % i, [128, 4, 128], BF16) for i in range(3)])
    wo = Rot("wo", [alloc(c, es, "wo%d" % i, [128, NCH, 128], BF16) for i in range(2)])
    sgs = Rot("sg", [alloc(c, es, "sg%d" % i, [128, 512], F32) for i in range(2)])
    macc = Rot("macc", [alloc(c, es, "macc%d" % i, [128, 512], F32) for i in range(3)])
    tmps = Rot("tmp", [alloc(c, es, "tmp%d" % i, [128, 512], F32) for i in range(2)])
    xrs = Rot("xr", [alloc(c, es, "xr%d" % i, [128, 512], F32) for i in range(3)])
    xos = Rot("xo", [alloc(c, es, "xo%d" % i, [128, 512], F32) for i in range(3)])
    psg = Rot("psg", [palloc(c, es, "psg%d" % i, [128, 512]) for i in range(2)])
    psb = Rot("psb", [palloc(c, es, "psb%d" % i, [128, 512]) for i in range(2)])
    pso = Rot("pso", [palloc(c, es, "pso%d" % i, [128, 512]) for i in range(2)])
    win = fm(c.w["w_in"][l])
    wout = fm(c.w["w_out"][l])
    gcol = c.params[:, pb + PC["mix"]:pb + PC["mix"] + 16]
    xv = fm(c.xres)
    for gi_, (g0, ng, subs, tiles) in enumerate(groups):
        if gi_ == 0:
            norm_stage(c, nb, c.xres, gcol, hbuf, "hbuf", g0, ng)
        if gi_ + 1 < len(groups):
            ngen = norm_gen(c, nb, c.xres, gcol, hbuf, "hbuf", groups[gi_ + 1][0], groups[gi_ + 1][1])
        else:
            ngen = iter(())
        for br in range(4):
            P.dma("sp", brb[:, br * 4:(br + 1) * 4, 0:ng], fm(c.br[br])[:, :, g0:g0 + ng], [], [("brb", br)], ("brb", br))
        for fc in range(NCH):
            accs = {}
            for br in range(4):
                w, wk = wg.next()
                gc0 = OFF["gate"] + br * D + fc * 128
                P.dma("pool", w[:, :, :], win[:, :, gc0:gc0 + 128], [], [wk], wk)
                w2, w2k = wb.next()
                P.dma("pool", w2[:, :, :], c.w["w_branch"][l, br].rearrange("(kc p) m -> p kc m", p=128)[:, :, fc * 128:(fc + 1) * 128],
                      [], [w2k], w2k)
                for si, (t0, n) in enumerate(subs):
                    off = t0 - g0
                    pg, pgk = psg.next()
                    pbr, pbk = psb.next()
                    for ch in range(NCH):
                        P.add("pe", lambda e, pg=pg, w=w, ch=ch, off=off, n=n: e.matmul(
                            pg[:, 0:n], lhsT=w[:, ch, :], rhs=hbuf[:, ch, off:off + n], start=(ch == 0), stop=(ch == NCH - 1)),
                            [wk, "hbuf"], [pgk])
                    for kc in range(4):
                        P.add("pe", lambda e, pbr=pbr, w2=w2, kc=kc, br=br, off=off, n=n: e.matmul(
                            pbr[:, 0:n], lhsT=w2[:, kc, :], rhs=brb[:, br * 4 + kc, off:off + n], start=(kc == 0), stop=(kc == 3)),
                            [w2k, ("brb", br)], [pbk])
                    sg, sgk = sgs.next()
                    P.add("act", lambda e, sg=sg, pg=pg, n=n: e.activation(out=sg[:, 0:n], in_=pg[:, 0:n], func=AF.Sigmoid),
                          [pgk], [sgk])
                    if br == 0:
                        accs[si] = macc.next()
                        ma, mak = accs[si]
                        P.add("dve", lambda e, ma=ma, sg=sg, pbr=pbr, n=n: e.tensor_tensor(
                            out=ma[:, 0:n], in0=sg[:, 0:n], in1=pbr[:, 0:n], op=ALU.mult), [sgk, pbk], [mak])
                    else:
                        ma, mak = accs[si]
                        tm, tmk = tmps.next()
                        P.add("dve", lambda e, tm=tm, sg=sg, pbr=pbr, n=n: e.tensor_tensor(
                            out=tm[:, 0:n], in0=sg[:, 0:n], in1=pbr[:, 0:n], op=ALU.mult), [sgk, pbk], [tmk])
                        if br < 3:
                            P.add("dve", lambda e, ma=ma, tm=tm, n=n: e.tensor_tensor(
                                out=ma[:, 0:n], in0=ma[:, 0:n], in1=tm[:, 0:n], op=ALU.add), [mak, tmk], [mak])
                        else:
                            P.add("dve", lambda e, ma=ma, tm=tm, n=n, fc=fc, off=off: e.tensor_tensor(
                                out=mrg[:, fc, off:off + n], in0=ma[:, 0:n], in1=tm[:, 0:n], op=ALU.add), [mak, tmk], [("mrg", fc)])
        for m in range(NCH):
            w, wk = wo.next()
            P.dma("pool", w[:, :, :], wout[:, :, m * 128:(m + 1) * 128], [], [wk], wk)
            for (t0, n) in subs:
                off = t0 - g0
                xr, xrk = xrs.next()
                P.dma("sp", xr[:, 0:n], xv[:, m, t0:t0 + n], [], [xrk], xrk)
                po, pok = pso.next()
                for ch in range(NCH):
                    P.add("pe", lambda e, po=po, w=w, ch=ch, off=off, n=n: e.matmul(
                        po[:, 0:n], lhsT=w[:, ch, :], rhs=mrg[:, ch, off:off + n], start=(ch == 0), stop=(ch == NCH - 1)),
                        [wk, ("mrg", ch)], [pok])
                xo, xok = xos.next()
                P.add("dve", lambda e, xo=xo, po=po, xr=xr, n=n: e.tensor_tensor(
                    out=xo[:, 0:n], in0=po[:, 0:n], in1=xr[:, 0:n], op=ALU.add), [pok, xrk], [xok])
                P.dma("act", xv[:, m, t0:t0 + n], xo[:, 0:n], [xok], [], xok)
            if m >= 2:
                next(ngen, None)
        for _ in ngen:
            pass
    P.flush()
    es.close()


def load_rows_norm(c, bufs, row0, nrows, out_fn, gsc_col, ones_m, gsize, key_out, dup64=False):
    P = c.P
    st, sq, rs, psn = bufs
    for (t0, n) in split_cols(0, c.L, 512):
        s, sk = st.next()
        if dup64:
            P.dma("sp", s[0:64, 0:n], c.proj[row0:row0 + 64, t0:t0 + n], [], [sk], sk)
            P.dma("sp", s[64:128, 0:n], c.proj[row0:row0 + 64, t0:t0 + n], [], [sk], sk)
        else:
            P.dma("sp", s[0:nrows, 0:n], c.proj[row0:row0 + nrows, t0:t0 + n], [], [sk], sk)
        if gsize is None:
            P.add("act", lambda e, s=s, t0=t0, n=n: e.activation(out=out_fn(t0, n), in_=s[:, 0:n], func=AF.Copy), [sk], [key_out])
            continue
        q, qk = sq.next()
        P.add("act", lambda e, q=q, s=s, n=n: e.activation(out=q[:, 0:n], in_=s[:, 0:n], func=AF.Square), [sk], [qk])
        P.add("pe", lambda e, q=q, n=n: e.matmul(psn[:, 0:n], lhsT=ones_m[:, :], rhs=q[:, 0:n], start=True, stop=True),
              [qk, "const"], ["psn"])
        r, rk = rs.next()
        rsqrt_op(c, r[:, 0:n], psn[:, 0:n], 1.0 / gsize, ["psn"], rk)
        P.add("dve", lambda e, s=s, r=r, t0=t0, n=n: e.scalar_tensor_tensor(
            out=out_fn(t0, n), in0=s[:, 0:n], scalar=gsc_col, in1=r[:, 0:n], op0=ALU.mult, op1=ALU.mult),
            [sk, rk, "gsc"], [key_out])


def rows_bufs(c, es):
    st = Rot("st", [alloc(c, es, "st%d" % i, [128, 512], F32) for i in range(2)])
    sq = Rot("sqq", [alloc(c, es, "sqq%d" % i, [128, 512], BF16) for i in range(2)])
    rs = Rot("rs", [alloc(c, es, "rs%d" % i, [128, 512], F32) for i in range(2)])
    psn = palloc(c, es, "psn", [128, 512])
    return (st, sq, rs, psn)


def near_case(i, j):
    if j == i:
        return 0
    if j >= 1 and j == i - 1:
        return 1
    if j == 0 and i == 1:
        return 2
    return None


def load_vtm(c, vsb, src, col0, ncol, key, pitch_view):
    P = c.P
    nt_full = len(c.tiles) - 1
    P.dma("sp", pitch_view(0, 16), src[0:16, col0:col0 + ncol], [], [key], key)
    for ti in range(1, nt_full + 1):
        t0, nt = c.tiles[ti]
        P.dma("sp", pitch_view(ti, nt), src[t0:t0 + nt, col0:col0 + ncol], [], [key], key)


def diff_phase(c, l):
    nc, P = c.nc, c.P
    es = ExitStack()
    L = c.L
    NT = len(c.tiles)
    pb = l * PL
    lam_init = 0.8 - 0.6 * math.exp(-0.3 * l)
    rb = rows_bufs(c, es)
    qC = alloc(c, es, "qC", [128, 4, L], BF16)
    kC = alloc(c, es, "kC", [128, 4, L], BF16)
    vC = alloc(c, es, "vCs", [128, NT, 4, 129], BF16)
    ob = alloc(c, es, "obC", [128, 4, L], BF16)
    gs = alloc(c, es, "gsC", [128, 4], F32)
    zc = alloc(c, es, "zcol", [128, 1], F32)
    pTs = Rot("pT", [alloc(c, es, "pT%d" % i, [128, 512], BF16) for i in range(3)])
    rr = Rot("rr", [alloc(c, es, "rr%d" % i, [128, 4], F32) for i in range(2)])
    ods = Rot("od", [alloc(c, es, "od%d" % i, [128, 128], F32) for i in range(2)])
    junk = alloc(c, es, "junk", [128, 128], F32)
    ons = Rot("on", [alloc(c, es, "on%d" % i, [128, 128], BF16) for i in range(2)])
    pss = Rot("pss", [palloc(c, es, "pss%d" % i, [128, 512]) for i in range(4)])
    pso = Rot("pso", [palloc(c, es, "pso%d" % i, [128, 512]) for i in range(2)])
    pst = palloc(c, es, "pst", [128, 512])
    posb = Rot("posb", [alloc(c, es, "posb%d" % i, [128, 264], F32) for i in range(2)])
    P.add("dve", lambda e: e.tensor_scalar(out=gs[:, 0:1], in0=c.params[:, pb + PC["dqn"]:pb + PC["dqn"] + 1], scalar1=0.125,
                                           scalar2=None, op0=ALU.mult), ["params"], ["gsc"])
    P.add("dve", lambda e: e.tensor_copy(out=gs[:, 1:2], in_=c.params[:, pb + PC["dkn"]:pb + PC["dkn"] + 1]), ["params"], ["gsc"])
    P.add("dve", lambda e: e.tensor_scalar(out=gs[:, 2:3], in0=c.params[:, pb + PC["subln"]:pb + PC["subln"] + 1],
                                           scalar1=1.0 - lam_init, scalar2=None, op0=ALU.mult), ["params"], ["gsc"])
    P.add("dve", lambda e: e.memset(zc[:, :], 0.0), [], ["gsc"])
    P.add("dve", lambda e: e.memset(vC[:, :, :, 128:129], 1.0), [], ["vC1"])
    for h in range(4):
        load_rows_norm(c, rb, OFF["c_q"] + h * 128, 128, lambda t0, n, h=h: qC[:, h, t0:t0 + n], gs[:, 0:1], c.ones64_bf, 64, ("qC", h))
        load_rows_norm(c, rb, OFF["c_k"] + h * 128, 128, lambda t0, n, h=h: kC[:, h, t0:t0 + n], gs[:, 1:2], c.ones64_bf, 64, ("kC", h))
    for ti in range(NT):
        t0, nt = c.tiles[ti]
        P.dma("sp", vC[0:nt, ti, :, 0:128], c.vC[t0:t0 + nt, :].rearrange("t (h e) -> t h e", h=4), [], ["vC"], "vC")
    c.prev_tail = None
    for i_ in range(NT):
      for h_ in range(4):
        def do_block(i, h, q0, nq):
            pos = [pso.next(), pso.next()]
            groups_ = [(cc, jg) for cc in range(2) for jg in range(0, i + 1, 4)]
            psl = {}

            def emit_s(gi):
                cc, jg = groups_[gi]
                grp = list(range(jg, min(i + 1, jg + 4)))
                ps, psk = pss.next()
                psl[gi] = (ps, psk, grp)
                for sl, j in enumerate(grp):
                    k0, nk = c.tiles[j]
                    case = near_case(i, j)
                    P.add("pe", lambda e, ps=ps, cc=cc, k0=k0, nk=nk, case=case, sl=sl: e.matmul(
                        ps[0:nk, sl * 128:sl * 128 + nq], lhsT=kC[cc * 64:(cc + 1) * 64, h, k0:k0 + nk],
                        rhs=qC[cc * 64:(cc + 1) * 64, h, q0:q0 + nq], start=True, stop=(case is None)), [("kC", h), ("qC", h)], [psk])
                    if case is not None:
                        P.add("pe", lambda e, ps=ps, nk=nk, case=case, sl=sl: e.matmul(
                            ps[0:nk, sl * 128:sl * 128 + nq], lhsT=c.ident_bf[0:nk, 0:nk], rhs=c.BT[0:nk, h * 3 + case, 0:nq],
                            start=False, stop=True), ["const"], [psk])

            def emit_pv(gi):
                cc, jg = groups_[gi]
                ps, psk, grp = psl[gi]
                po, pok = pos[cc]
                W = (len(grp) - 1) * 128 + nq
                pT, pTk = pTs.next()
                P.add("act", lambda e, pT=pT, ps=ps, W=W: e.activation(out=pT[:, 0:W], in_=ps[:, 0:W], func=AF.Exp), [psk], [pTk])
                for sl, j in enumerate(grp):
                    k0, nk = c.tiles[j]
                    P.add("pe", lambda e, po=po, pT=pT, nk=nk, j=j, sl=sl: e.matmul(
                        po[0:nq, 0:129], lhsT=pT[0:nk, sl * 128:sl * 128 + nq], rhs=vC[0:nk, j, h, 0:129], start=(j == 0), stop=(j == i)),
                        [pTk, "vC", "vC1"], [pok])

            emit_s(0)
            for gi in range(len(groups_)):
                if gi + 1 < len(groups_):
                    emit_s(gi + 1)
                emit_pv(gi)
            ob2, ob2k = posb.next()
            P.add("act", lambda e, ob2=ob2: e.activation(out=ob2[0:nq, 0:129], in_=pos[0][0][0:nq, 0:129], func=AF.Copy), [pos[0][1]], [ob2k])
            P.add("act", lambda e, ob2=ob2: e.activation(out=ob2[0:nq, 132:261], in_=pos[1][0][0:nq, 0:129], func=AF.Copy), [pos[1][1]], [ob2k])
            p0, p0k = ob2[:, 0:132], ob2k
            p1, p1k = ob2[:, 132:264], ob2k
            r, rk = rr.next()
            P.add("dve", lambda e, r=r, p0=p0, nq=nq: e.reciprocal(out=r[0:nq, 0:1], in_=p0[0:nq, 128:129]), [p0k], [rk])
            P.add("dve", lambda e, r=r, p1=p1, nq=nq: e.reciprocal(out=r[0:nq, 1:2], in_=p1[0:nq, 128:129]), [p1k], [rk])
            P.add("dve", lambda e, r=r, nq=nq: e.tensor_tensor(out=r[0:nq, 2:3], in0=r[0:nq, 1:2], in1=c.nlam[0:nq, l:l + 1], op=ALU.mult),
                  [rk, "const"], [rk])
            od, odk = ods.next()
            P.add("dve", lambda e, od=od, p0=p0, r=r, nq=nq: e.tensor_scalar(out=od[0:nq, :], in0=p0[0:nq, 0:128], scalar1=r[0:nq, 0:1],
                                                                              scalar2=None, op0=ALU.mult), [p0k, rk], [odk])
            P.add("dve", lambda e, od=od, p1=p1, r=r, nq=nq: e.scalar_tensor_tensor(
                out=od[0:nq, :], in0=p1[0:nq, 0:128], scalar=r[0:nq, 2:3], in1=od[0:nq, :], op0=ALU.mult, op1=ALU.add),
                [p1k, rk, odk], [odk])
            P.add("dve", lambda e, od=od, nq=nq: e.tensor_tensor(out=junk[0:nq, :], in0=od[0:nq, :], in1=od[0:nq, :], op=ALU.mult),
                  [odk], ["junk"])
            P.add("dve", lambda e, r=r, nq=nq: e.tensor_reduce(out=r[0:nq, 3:4], in_=junk[0:nq, :], axis=AX.X, op=ALU.add),
                  ["junk", rk], [rk])
            rsqrt_op(c, r[0:nq, 3:4], r[0:nq, 3:4], 1.0 / 128, [rk], rk, np_=nq)
            on, onk = ons.next()
            P.add("dve", lambda e, on=on, od=od, r=r, nq=nq: e.tensor_scalar(out=on[0:nq, :], in0=od[0:nq, :], scalar1=r[0:nq, 3:4],
                                                                              scalar2=None, op0=ALU.mult), [odk, rk], [onk])
            def tail():
                P.add("pe", lambda e, on=on, nq=nq: e.matmul(pst[:, 0:nq], lhsT=on[0:nq, :], rhs=c.ident_bf[0:nq, 0:nq], start=True, stop=True),
                      [onk, "const"], ["pst"])
                P.add("act", lambda e, h=h, q0=q0, nq=nq: e.activation(out=ob[:, h, q0:q0 + nq], in_=pst[:, 0:nq], func=AF.Copy,
                                                                        scale=gs[:, 2:3]), ["pst", "gsc"], ["obC"])
            return tail
        tl_ = do_block(i_, h_, c.tiles[i_][0], c.tiles[i_][1])
        if c.prev_tail is not None:
            c.prev_tail()
        c.prev_tail = tl_
    c.prev_tail()
    P.dma("sp", fm(c.br[2]), ob[:, :, :], ["obC"], [], "obC")
    P.flush()
    es.close()


def hgrn_phase(c, l):
    nc, P = c.nc, c.P
    es = ExitStack()
    L = c.L
    NT = len(c.tiles)
    pb = l * PL
    zf = alloc(c, es, "zf", [128, L], F32)
    bb = alloc(c, es, "bb", [128, L], F32)
    kf = alloc(c, es, "kf", [128, L], F32)
    qf = alloc(c, es, "qf", [128, L], F32)
    tmp = alloc(c, es, "tmpE", [128, L], F32)
    qt = alloc(c, es, "qt", [128, L], BF16)
    qh = alloc(c, es, "qh", [128, L], BF16)
    kh = alloc(c, es, "kh", [128, L], BF16)
    khT = alloc(c, es, "khT", [128, NT, 128], BF16)
    vA = alloc(c, es, "vAs", [128, NT, 128], BF16)
    osb = alloc(c, es, "osb", [128, L], BF16)
    obr = alloc(c, es, "obr", [128, L], BF16)
    e1 = alloc(c, es, "e1", [128, NT], F32)
    e2 = alloc(c, es, "e2", [128, NT], F32)
    Sf = alloc(c, es, "Sf", [128, 128], F32)
    St = alloc(c, es, "St", [128, 128], F32)
    Sb = Rot("Sb", [alloc(c, es, "Sb%d" % i, [128, 128], BF16) for i in range(2)])
    Pm = Rot("Pm", [alloc(c, es, "Pm%d" % i, [128, 128], BF16) for i in range(2)])
    sq = Rot("sqh", [alloc(c, es, "sqh%d" % i, [128, 512], BF16) for i in range(2)])
    rs = Rot("rsh", [alloc(c, es, "rsh%d" % i, [128, 512], F32) for i in range(2)])
    pss = Rot("pss", [palloc(c, es, "pss%d" % i, [128, 512]) for i in range(2)])
    pso = Rot("pso", [palloc(c, es, "pso%d" % i, [128, 512]) for i in range(2)])
    psd = Rot("psd", [palloc(c, es, "psd%d" % i, [128, 512]) for i in range(2)])
    pstb = palloc(c, es, "pstb", [128, 128], BF16)
    psn = palloc(c, es, "psnh", [128, 512])
    for k in range(2):
        P.add("pool", lambda e, k=k: e.memset(Pm.t[k][:, :], 0.0), [], [("Pm", k)])
    for h in range(4):
        lbc = c.lbs[:, l, h:h + 1]
        omc = c.oml[:, l, h:h + 1]
        P.dma("sp", zf[:, :], c.proj[OFF["a_f"] + h * 128:OFF["a_f"] + (h + 1) * 128, :], [], ["zf"], "zf")
        P.dma("sp", qf[:, :], c.proj[OFF["a_q"] + h * 128:OFF["a_q"] + (h + 1) * 128, :], [], ["qf"], "qf")
        load_vtm(c, vA, c.vA, h * 128, 128, "vA", lambda ti, nt: vA[0:nt, ti, :])
        P.add("act", lambda e: e.activation(out=zf[:, :], in_=zf[:, :], func=AF.Sigmoid), ["zf"], ["zf"])
        P.add("dve", lambda e, lbc=lbc, omc=omc: e.tensor_scalar(out=zf[:, :], in0=zf[:, :], scalar1=omc, scalar2=lbc,
                                                                 op0=ALU.mult, op1=ALU.add), ["zf", "const"], ["zf"])
        P.add("dve", lambda e: e.tensor_scalar(out=kf[:, :], in0=zf[:, :], scalar1=-1.0, scalar2=1.0, op0=ALU.mult, op1=ALU.add),
              ["zf"], ["kf"])
        P.add("act", lambda e: e.activation(out=zf[:, :], in_=zf[:, :], func=AF.Ln), ["zf", "kf"], ["zf"])
        for ti in range(NT):
            t0, nt = c.tiles[ti]
            P.add("dve", lambda e, t0=t0, nt=nt: e.tensor_tensor_scan(out=bb[:, t0:t0 + nt], data0=c.ones_f[:, 0:nt], data1=zf[:, t0:t0 + nt],
                                                                      initial=0.0, op0=ALU.mult, op1=ALU.add), ["zf", "const"], ["bb"])
        for ti in range(NT):
            t0, nt = c.tiles[ti]
            mid = t0 + nt // 2
            P.add("dve", lambda e, t0=t0, nt=nt, mid=mid: e.tensor_scalar(out=zf[:, t0:t0 + nt], in0=bb[:, t0:t0 + nt], scalar1=bb[:, mid:mid + 1],
                                                                          scalar2=None, op0=ALU.subtract), ["bb", "zf"], ["zf"])
        for ti in range(NT):
            t0, nt = c.tiles[ti]
            P.add("act", lambda e, ti=ti, t0=t0, nt=nt: e.activation(out=e1[:, ti:ti + 1], in_=bb[:, t0 + nt - 1:t0 + nt], func=AF.Exp),
                  ["bb"], ["e1"])
            P.add("act", lambda e, ti=ti, t0=t0, nt=nt: e.activation(out=e2[:, ti:ti + 1], in_=zf[:, t0 + nt - 1:t0 + nt], func=AF.Exp),
                  ["zf"], ["e2"])
        P.add("act", lambda e: e.activation(out=qf[:, :], in_=qf[:, :], func=AF.Silu), ["qf"], ["qf"])
        P.add("act", lambda e: e.activation(out=tmp[:, :], in_=bb[:, :], func=AF.Exp), ["bb"], ["tmp"])
        P.add("dve", lambda e: e.tensor_tensor(out=qt[:, :], in0=qf[:, :], in1=tmp[:, :], op=ALU.mult), ["qf", "tmp"], ["qt"])
        P.add("act", lambda e: e.activation(out=tmp[:, :], in_=zf[:, :], func=AF.Exp), ["zf", "qt"], ["tmp"])
        P.add("dve", lambda e: e.tensor_tensor(out=qh[:, :], in0=qf[:, :], in1=tmp[:, :], op=ALU.mult), ["qf", "tmp"], ["qh"])
        P.add("act", lambda e: e.activation(out=tmp[:, :], in_=zf[:, :], func=AF.Exp, scale=-1.0), ["zf", "qh"], ["tmp"])
        P.add("dve", lambda e: e.tensor_tensor(out=kh[:, :], in0=kf[:, :], in1=tmp[:, :], op=ALU.mult), ["kf", "tmp"], ["kh"])
        for ti in range(NT):
            t0, nt = c.tiles[ti]
            P.add("pe", lambda e, t0=t0, nt=nt: e.transpose(out=pstb[0:nt, :], in_=kh[:, t0:t0 + nt], identity=c.ident_bf[:, :]),
                  ["kh", "const"], ["pstb"])
            P.add("act", lambda e, ti=ti, nt=nt: e.activation(out=khT[0:nt, ti, :], in_=pstb[0:nt, :], func=AF.Copy), ["pstb"], ["khT"])
        P.add("dve", lambda e: e.memset(Sf[:, :], 0.0), [], ["Sf"])
        sb, sbk = Sb.next()
        P.add("pool", lambda e, sb=sb: e.memset(sb[:, :], 0.0), [], [sbk])
        for ti in range(NT):
            t0, nt = c.tiles[ti]
            ps, psk = pss.next()
            P.add("pe", lambda e, ps=ps, t0=t0, nt=nt: e.matmul(ps[0:nt, 0:nt], lhsT=kh[:, t0:t0 + nt], rhs=qh[:, t0:t0 + nt], start=True, stop=True),
                  ["kh", "qh"], [psk])
            pm, pmk = Pm.next()
            P.add("dve", lambda e, pm=pm, ps=ps, nt=nt: e.copy_predicated(out=pm[0:nt, 0:nt], mask=c.causT[0:nt, 0:nt], data=ps[0:nt, 0:nt]),
                  [psk, "const_c"], [pmk])
            po, pok = pso.next()
            P.add("pe", lambda e, po=po, pm=pm, ti=ti, nt=nt: e.matmul(po[:, 0:nt], lhsT=vA[0:nt, ti, :], rhs=pm[0:nt, 0:nt], start=True, stop=False),
                  ["vA", pmk], [pok])
            P.add("pe", lambda e, po=po, sb=sb, t0=t0, nt=nt: e.matmul(po[:, 0:nt], lhsT=sb[:, :], rhs=qt[:, t0:t0 + nt], start=False, stop=True),
                  [sbk, "qt"], [pok])
            P.add("act", lambda e, po=po, t0=t0, nt=nt: e.activation(out=osb[:, t0:t0 + nt], in_=po[:, 0:nt], func=AF.Copy), [pok], ["osb"])
            pd, pdk = psd.next()
            P.add("pe", lambda e, pd=pd, ti=ti, nt=nt: e.matmul(pd[:, 0:128], lhsT=khT[0:nt, ti, :], rhs=vA[0:nt, ti, :], start=True, stop=True),
                  ["khT", "vA"], [pdk])
            P.add("dve", lambda e, ti=ti: e.tensor_scalar(out=St[:, :], in0=Sf[:, :], scalar1=e1[:, ti:ti + 1], scalar2=None, op0=ALU.mult),
                  ["Sf", "e1"], ["St"])
            P.add("dve", lambda e, pd=pd, ti=ti: e.scalar_tensor_tensor(out=Sf[:, :], in0=pd[:, 0:128], scalar=e2[:, ti:ti + 1], in1=St[:, :],
                                                                        op0=ALU.mult, op1=ALU.add), [pdk, "St", "e2"], ["Sf"])
            sb, sbk = Sb.next()
            P.add("act", lambda e, sb=sb: e.activation(out=sb[:, :], in_=Sf[:, :], func=AF.Copy), ["Sf"], [sbk])
        P.dma("sp", qf[:, :], c.proj[OFF["a_g"] + h * 128:OFF["a_g"] + (h + 1) * 128, :], [], ["qf"], "qf")
        P.add("act", lambda e: e.activation(out=qf[:, :], in_=qf[:, :], func=AF.Silu), ["qf"], ["qf"])
        gcol = c.params[:, pb + PC["gnorm"] + h:pb + PC["gnorm"] + h + 1]
        for (t0, n) in split_cols(0, L, 512):
            q, qk = sq.next()
            P.add("act", lambda e, q=q, t0=t0, n=n: e.activation(out=q[:, 0:n], in_=osb[:, t0:t0 + n], func=AF.Square), ["osb"], [qk])
            P.add("pe", lambda e, q=q, n=n: e.matmul(psn[:, 0:n], lhsT=c.ones_bf[:, :], rhs=q[:, 0:n], start=True, stop=True),
                  [qk, "const"], ["psn"])
            r, rk = rs.next()
            rsqrt_op(c, r[:, 0:n], psn[:, 0:n], 1.0 / 128, ["psn"], rk)
            P.add("dve", lambda e, r=r, t0=t0, n=n, gcol=gcol: e.scalar_tensor_tensor(
                out=r[:, 0:n], in0=osb[:, t0:t0 + n], scalar=gcol, in1=r[:, 0:n], op0=ALU.mult, op1=ALU.mult),
                ["osb", rk, "params"], [rk])
            P.add("dve", lambda e, r=r, t0=t0, n=n: e.tensor_tensor(out=obr[:, t0:t0 + n], in0=r[:, 0:n], in1=qf[:, t0:t0 + n], op=ALU.mult),
                  [rk, "qf"], ["obr"])
        P.dma("sp", c.br[0][h * 128:(h + 1) * 128, :], obr[:, :], ["obr"], [], "obr")
    P.flush()
    es.close()


def dsa_phase(c, l):
    nc, P = c.nc, c.P
    es = ExitStack()
    L = c.L
    NT = len(c.tiles)
    pb = l * PL
    topk = c.topk
    rb = rows_bufs(c, es)
    st, _sq, _rs, _psn = rb
    qD = alloc(c, es, "qD", [128, 4, L], BF16)
    kD = alloc(c, es, "kD", [128, L], BF16)
    vD = alloc(c, es, "vDs", [128, NT, 129], BF16)
    kiD = alloc(c, es, "kiD", [128, L], BF16)
    sgn = alloc(c, es, "sgn", [128, NT, 16], F32)
    idx = alloc(c, es, "idx", [128, L], F32)
    work = alloc(c, es, "work", [128, L], F32)
    maskb = alloc(c, es, "maskb", [128, L], BF16)
    mT = alloc(c, es, "mT", [128, NT, 128], BF16)
    gs = alloc(c, es, "gsD", [128, 2], F32)
    zc = alloc(c, es, "zcolD", [128, 1], F32)
    m8 = alloc(c, es, "m8", [128, 8], F32)
    th = alloc(c, es, "th", [128, 1], F32)
    rls = Rot("rl", [alloc(c, es, "rl%d" % i, [128, 512], BF16) for i in range(3)])
    idxs = [idx, alloc(c, es, "idx2", [128, L], F32)]
    dgs = Rot("dg", [alloc(c, es, "dg%d" % i, [128, 16, 128], BF16) for i in range(2)])
    eTs = Rot("eT", [alloc(c, es, "eT%d" % i, [128, 512], BF16) for i in range(2)])
    pTs = Rot("pTD", [alloc(c, es, "pTD%d" % i, [128, 512], BF16) for i in range(2)])
    rr = Rot("rrD", [alloc(c, es, "rrD%d" % i, [128, 1], F32) for i in range(2)])
    ons = Rot("onD", [alloc(c, es, "onD%d" % i, [128, 128], BF16) for i in range(2)])
    ocs = Rot("ocD", [alloc(c, es, "ocD%d" % i, [128, 128], BF16) for i in range(3)])
    psi = Rot("psi", [palloc(c, es, "psi%d" % i, [128, 512]) for i in range(2)])
    pss = Rot("pssD", [palloc(c, es, "pssD%d" % i, [128, 512]) for i in range(2)])
    pso = Rot("psoD", [palloc(c, es, "psoD%d" % i, [128, 512]) for i in range(2)])
    pstb = palloc(c, es, "pstbD", [128, 128], BF16)
    P.add("dve", lambda e: e.tensor_scalar(out=gs[:, 0:1], in0=c.params[:, pb + PC["dsaq"]:pb + PC["dsaq"] + 1], scalar1=128 ** -0.5,
                                           scalar2=None, op0=ALU.mult), ["params"], ["gsc"])
    P.add("dve", lambda e: e.tensor_copy(out=gs[:, 1:2], in_=c.params[:, pb + PC["dsak"]:pb + PC["dsak"] + 1]), ["params"], ["gsc"])
    P.add("dve", lambda e: e.memset(zc[:, :], 0.0), [], ["gsc"])
    P.add("dve", lambda e: e.memset(vD[:, :, 128:129], 1.0), [], ["vD1"])
    for h in range(4):
        load_rows_norm(c, rb, OFF["d_q"] + h * 128, 128, lambda t0, n, h=h: qD[:, h, t0:t0 + n], gs[:, 0:1], c.ones_bf, 128, ("qD", h))
    load_rows_norm(c, rb, OFF["d_k"], 128, lambda t0, n: kD[:, t0:t0 + n], gs[:, 1:2], c.ones_bf, 128, "kD")
    load_rows_norm(c, rb, OFF["d_ki"], 64, lambda t0, n: kiD[:, t0:t0 + n], None, None, None, "kiD", dup64=True)
    load_vtm(c, vD, c.vD, 0, 128, "vD", lambda ti, nt: vD[0:nt, ti, 0:128])
    load_vtm(c, sgn, c.wT, 0, 16, "sgn", lambda ti, nt: sgn[0:nt, ti, :])
    P.add("act", lambda e: e.activation(out=sgn[:, :, :], in_=sgn[:, :, :], func=AF.Sign), ["sgn"], ["sgn"])
    qsts = Rot("qst", [alloc(c, es, "qst%d" % i, [128, 8, 128], F32) for i in range(2)])
    wsts = Rot("wst", [alloc(c, es, "wst%d" % i, [128, 8, 128], BF16) for i in range(2)])
    qits = Rot("qit", [alloc(c, es, "qit%d" % i, [128, 8, 128], BF16) for i in range(2)])
    mTs = [mT, alloc(c, es, "mT2", [128, NT, 128], BF16)]
    osb = [alloc(c, es, "osbD%d" % i, [128, 132], F32) for i in range(4)]
    qiv = c.proj[OFF["d_qi"]:OFF["d_qi"] + 1024, :].rearrange("(r p) t -> p r t", p=128)
    wav = c.wabs.rearrange("(r p) t -> p r t", p=128)

    def stage_a1(i):
        q0, nq = c.tiles[i]
        K = q0 + nq
        qs, qsk = qsts.next()
        ws, wsk = wsts.next()
        P.dma("sp", qs[:, :, 0:nq], qiv[:, :, q0:q0 + nq], [], [qsk], qsk)
        P.dma("sp", ws[:, :, 0:nq], wav[:, :, q0:q0 + nq], [], [wsk], wsk)
        qit, qitk = qits.next()
        P.add("pool", lambda e: e.tensor_tensor(out=qit[:, :, 0:nq], in0=qs[:, :, 0:nq], in1=ws[:, :, 0:nq], op=ALU.mult),
              [qsk, wsk], [qitk])
        idxc, idxk = idxs[i % 2], ("idx", i % 2)
        dg, dgk = dgs.next()
        for hi in range(16):
            P.add("pool", lambda e, hi=hi: e.tensor_scalar(out=dg[0:nq, hi, 0:nq], in0=c.ident_bf[0:nq, 0:nq], scalar1=sgn[0:nq, i, hi:hi + 1],
                                                           scalar2=None, op0=ALU.mult), ["const", "sgn"], [dgk])
        for (kb0, kn) in split_cols(0, K, 512):
            dots = {}

            def emit_dot(hi):
                r, half = hi // 2, hi % 2
                p, pk = psi.next()
                dots[hi] = (p, pk)
                P.add("pe", lambda e, p=p, r=r, half=half, kb0=kb0, kn=kn: e.matmul(
                    p[0:nq, 0:kn], lhsT=qit[half * 64:(half + 1) * 64, r, 0:nq], rhs=kiD[half * 64:(half + 1) * 64, kb0:kb0 + kn],
                    start=True, stop=True), [qitk, "kiD"], [pk])

            def emit_acc(hi):
                p, pk = dots[hi]
                rl, rlk = rls.next()
                P.add("act", lambda e, rl=rl, p=p, kn=kn: e.activation(out=rl[0:nq, 0:kn], in_=p[0:nq, 0:kn], func=AF.Relu), [pk], [rlk])
                P.add("pe", lambda e, rl=rl, hi=hi, kn=kn: e.matmul(_psn[0:nq, 0:kn], lhsT=dg[0:nq, hi, 0:nq], rhs=rl[0:nq, 0:kn],
                                                                   start=(hi == 0), stop=(hi == 15)), [rlk, dgk], ["psn"])

            emit_dot(0)
            for hi in range(16):
                if hi + 1 < 16:
                    emit_dot(hi + 1)
                emit_acc(hi)
            P.add("act", lambda e, kb0=kb0, kn=kn: e.activation(out=idxc[0:nq, kb0:kb0 + kn], in_=_psn[0:nq, 0:kn], func=AF.Copy),
                  ["psn"], [idxk])

    def stage_a2(i):
        q0, nq = c.tiles[i]
        K = q0 + nq
        mTc = mTs[i % 2]
        idxc, idxk = idxs[i % 2], ("idx", i % 2)
        P.add("dve", lambda e: e.tensor_tensor(out=idxc[0:nq, q0:q0 + nq], in0=idxc[0:nq, q0:q0 + nq], in1=c.fut[0:nq, 0:nq], op=ALU.add),
              [idxk, "const_f"], [idxk])
        if K > topk:
            for rd in range(topk // 8):
                src = idxc if rd == 0 else work
                P.add("dve", lambda e, src=src: e.max(out=m8[0:nq, :], in_=src[0:nq, 0:K]), [idxk, "work"], ["m8"])
                if rd < topk // 8 - 1:
                    P.add("dve", lambda e, src=src: e.match_replace(out=work[0:nq, 0:K], in_to_replace=m8[0:nq, :],
                                                                    in_values=src[0:nq, 0:K], imm_value=-1e30),
                          [idxk, "work", "m8"], ["work"])
            P.add("dve", lambda e: e.tensor_copy(out=th[0:nq, :], in_=m8[0:nq, 7:8]), ["m8"], ["th"])
        else:
            P.add("dve", lambda e: e.memset(th[0:nq, :], -1e29), ["m8"], ["th"])
        P.add("dve", lambda e: e.tensor_scalar(out=maskb[0:nq, 0:K], in0=idxc[0:nq, 0:K], scalar1=th[0:nq, 0:1], scalar2=None,
                                               op0=ALU.is_ge), [idxk, "th"], ["maskb"])
        for j in range(i + 1):
            k0, nk = c.tiles[j]
            P.add("pe", lambda e, k0=k0, nk=nk: e.transpose(out=pstb[0:nk, 0:nq], in_=maskb[0:nq, k0:k0 + nk], identity=c.ident_bf[0:nq, 0:nq]),
                  ["maskb", "const"], ["pstbD"])
            P.add("act", lambda e, j=j, nk=nk: e.activation(out=mTc[0:nk, j, 0:nq], in_=pstb[0:nk, 0:nq], func=AF.Copy), ["pstbD"], [("mT", i % 2, j)])

    def stage_b_main(i):
        q0, nq = c.tiles[i]
        mTc = mTs[i % 2]
        for h in range(4):
            po, pok = pso.next()
            for jg in range(0, i + 1, 4):
                grp = list(range(jg, min(i + 1, jg + 4)))
                ps, psk = pss.next()
                for sl, j in enumerate(grp):
                    k0, nk = c.tiles[j]
                    case = near_case(i, j)
                    P.add("pe", lambda e, ps=ps, h=h, k0=k0, nk=nk, case=case, sl=sl: e.matmul(
                        ps[0:nk, sl * 128:sl * 128 + nq], lhsT=kD[:, k0:k0 + nk], rhs=qD[:, h, q0:q0 + nq], start=True, stop=(case is None)),
                        ["kD", ("qD", h)], [psk])
                    if case is not None:
                        P.add("pe", lambda e, ps=ps, nk=nk, h=h, case=case, sl=sl: e.matmul(
                            ps[0:nk, sl * 128:sl * 128 + nq], lhsT=c.ident_bf[0:nk, 0:nk], rhs=c.BT[0:nk, (4 + h) * 3 + case, 0:nq],
                            start=False, stop=True), ["const"], [psk])
                W = (len(grp) - 1) * 128 + nq
                eT, eTk = eTs.next()
                P.add("act", lambda e, eT=eT, ps=ps, W=W: e.activation(out=eT[:, 0:W], in_=ps[:, 0:W], func=AF.Exp), [psk], [eTk])
                pT, pTk = pTs.next()
                if nq == 128:
                    ng_ = len(grp)
                    P.add("pool", lambda e, pT=pT, eT=eT, jg=jg, ng_=ng_: e.tensor_tensor(
                        out=pT[:, 0:ng_ * 128].rearrange("p (s q) -> p s q", q=128), in0=eT[:, 0:ng_ * 128].rearrange("p (s q) -> p s q", q=128),
                        in1=mTc[:, jg:jg + ng_, :], op=ALU.mult), [eTk] + [("mT", i % 2, j) for j in grp], [pTk])
                else:
                    P.add("pool", lambda e, pT=pT, eT=eT: e.tensor_tensor(
                        out=pT[0:16, 0:nq], in0=eT[0:16, 0:nq], in1=mTc[0:16, 0, 0:nq], op=ALU.mult), [eTk, ("mT", i % 2, 0)], [pTk])
                for sl, j in enumerate(grp):
                    k0, nk = c.tiles[j]
                    P.add("pe", lambda e, po=po, pT=pT, nk=nk, j=j, sl=sl: e.matmul(
                        po[0:nq, 0:129], lhsT=pT[0:nk, sl * 128:sl * 128 + nq], rhs=vD[0:nk, j, 0:129], start=(j == 0), stop=(j == i)),
                        [pTk, "vD", "vD1"], [pok])
            P.add("act", lambda e, po=po, h=h: e.activation(out=osb[h][0:nq, 0:129], in_=po[0:nq, 0:129], func=AF.Copy), [pok], [("osbD", h)])

    def stage_b_fin(i):
        q0, nq = c.tiles[i]
        for h in range(4):
            r, rk = rr.next()
            P.add("dve", lambda e, r=r, h=h: e.reciprocal(out=r[0:nq, 0:1], in_=osb[h][0:nq, 128:129]), [("osbD", h)], [rk])
            on, onk = ons.next()
            P.add("dve", lambda e, on=on, r=r, h=h: e.tensor_scalar(out=on[0:nq, :], in0=osb[h][0:nq, 0:128], scalar1=r[0:nq, 0:1],
                                                                     scalar2=None, op0=ALU.mult), [("osbD", h), rk], [onk])
            ps2, ps2k = pss.next()
            P.add("pe", lambda e, ps2=ps2, on=on: e.matmul(ps2[:, 0:nq], lhsT=on[0:nq, :], rhs=c.ident_bf[0:nq, 0:nq], start=True, stop=True),
                  [onk, "const"], [ps2k])
            oc, ock = ocs.next()
            P.add("act", lambda e, oc=oc, ps2=ps2: e.activation(out=oc[:, 0:nq], in_=ps2[:, 0:nq], func=AF.Copy), [ps2k], [ock])
            P.dma("sp", c.br[3][h * 128:(h + 1) * 128, q0:q0 + nq], oc[:, 0:nq], [ock], [], ock)

    stage_a1(0)
    if NT > 1:
        stage_a1(1)
    stage_a2(0)
    for i in range(NT):
        if i + 2 < NT:
            stage_a1(i + 2)
        stage_b_main(i)
        if i + 1 < NT:
            stage_a2(i + 1)
        stage_b_fin(i)
    P.flush()
    es.close()
```

```python
import numpy as np
from contextlib import ExitStack
import concourse.bass as bass
import concourse.mybir as mybir
from concourse.bass_utils import run_bass_kernel_spmd

F32 = mybir.dt.float32
BF16 = mybir.dt.bfloat16
I32 = mybir.dt.int32
ALU = mybir.AluOpType
AF = mybir.ActivationFunctionType
AX = mybir.AxisListType

ENGS = ("pe", "act", "dve", "pool", "sp")


class Prog:
    def __init__(self, nc):
        self.nc = nc
        self.es = ExitStack()
        self.eng_h = {"pe": nc.tensor, "act": nc.scalar, "dve": nc.vector,
                      "pool": nc.gpsimd, "sp": nc.sync}
        self.sem = {e: self.es.enter_context(nc.semaphore("s_" + e)) for e in ENGS}
        self.sig_cnt = {e: 0 for e in ENGS}
        self.waited = {e: {} for e in ENGS}
        self.dma_sems = {}
        self.semobj = {("eng", e): self.sem[e] for e in ENGS}
        self.ops = []
        self.lastw = {}
        self.readers = {}
        self.n_inst = 0
        self.phase_es = None

    def add(self, eng, fn, reads=(), writes=(), dma_key=None):
        idx = len(self.ops)
        is_dma = dma_key is not None
        deps = {}
        for r in reads:
            w = self.lastw.get(r)
            if w is not None:
                deps[w] = "raw"
        for wk in writes:
            w = self.lastw.get(wk)
            if w is not None and w not in deps:
                deps[w] = "waw"
            for r in self.readers.get(wk, {}).values():
                if r not in deps:
                    deps[r] = "war"
        op = dict(eng=eng, fn=fn, deps=deps, dma_key=dma_key, sig=False, dma_val=None)
        if is_dma:
            if dma_key not in self.dma_sems:
                s = self.es.enter_context(self.nc.semaphore("d_%d" % len(self.dma_sems)))
                self.dma_sems[dma_key] = [s, 0]
            self.dma_sems[dma_key][1] += 16
            op["dma_val"] = self.dma_sems[dma_key][1]
        self.ops.append(op)
        rk = ("dma", dma_key) if is_dma else eng
        for r in reads:
            self.readers.setdefault(r, {})[rk] = idx
        for wk in writes:
            self.lastw[wk] = idx
            self.readers[wk] = {}
        return idx

    def dma(self, q, out, in_, reads, writes, key):
        self.add(q, lambda e: e.dma_start(out=out, in_=in_), reads, writes, dma_key=key)

    def flush(self):
        ops = self.ops
        if not ops:
            return
        for i, op in enumerate(ops):
            waits = []
            for d, kind in op["deps"].items():
                od = ops[d]
                if od["dma_key"] is not None:
                    waits.append(("dma", od["dma_key"], od["dma_val"]))
                    continue
                if od["eng"] == op["eng"] and op["dma_key"] is None:
                    if op["eng"] == "pe" or kind != "raw":
                        continue
                od["sig"] = True
                waits.append(("eng", od["eng"], d))
            op["waits"] = waits
        last = {}
        for i, op in enumerate(ops):
            if op["dma_key"] is None:
                last[op["eng"]] = i
        for e, i in last.items():
            ops[i]["sig"] = True
        cnt = dict(self.sig_cnt)
        for op in ops:
            if op["dma_key"] is None and op["sig"]:
                cnt[op["eng"]] += 1
                op["sig_val"] = cnt[op["eng"]]
        end_cnt = cnt

        def emit_engine(ename):
            def body(eh):
                waited = self.waited[ename]
                for op in ops:
                    if op["eng"] != ename:
                        continue
                    for w in op["waits"]:
                        if w[0] == "dma":
                            semk = ("dma", w[1]); val = w[2]
                            s = self.dma_sems[w[1]][0]
                        else:
                            semk = ("eng", w[1]); val = ops[w[2]]["sig_val"]
                            s = self.sem[w[1]]
                        if waited.get(semk, 0) >= val:
                            continue
                        waited[semk] = val
                        eh.wait_ge(s, val)
                        self.n_inst += 1
                    ins = op["fn"](eh)
                    self.n_inst += 1
                    if op["dma_key"] is not None:
                        ins.then_inc(self.dma_sems[op["dma_key"]][0], 16)
                    elif op["sig"]:
                        ins.then_inc(self.sem[ename], 1)
                for e2 in ENGS:
                    if e2 == ename:
                        continue
                    v = end_cnt[e2]
                    if v > 0 and waited.get(("eng", e2), 0) < v:
                        waited[("eng", e2)] = v
                        eh.wait_ge(self.sem[e2], v)
                for k, (s, c) in self.dma_sems.items():
                    if c > 0 and waited.get(("dma", k), 0) < c:
                        waited[("dma", k)] = c
                        eh.wait_ge(s, c)
            return body

        with self.nc.Block() as block:
            block.tensor(emit_engine("pe"))
            block.scalar(emit_engine("act"))
            block.vector(emit_engine("dve"))
            block.gpsimd(emit_engine("pool"))
            block.sync(emit_engine("sp"))
        self.sig_cnt = end_cnt
        self.ops = []
        self.lastw = {}
        self.readers = {}

    def close(self):
        self.flush()
        self.es.close()
import math


D = 2048
NCH = 16
DFF = 5632
NJ = 44
EPS = 1e-6
OFF = dict(a_q=0, a_f=512, a_i=1024, a_g=1536, b_b=2048, b_c=2560, b_u=3072, c_q=3584, c_k=4096,
           c_v=4608, d_q=5120, d_k=5632, d_v=5760, d_qi=5888, d_ki=6912, d_w=6976, gate=6992)
NFM = 6976
NEG = -30000.0


class Rot:
    def __init__(self, name, tensors):
        self.name = name
        self.t = tensors
        self.i = 0

    def next(self):
        k = self.i % len(self.t)
        self.i += 1
        return self.t[k], (self.name, k)


def split_cols(t0, n, step):
    out = []
    o = 0
    while o < n:
        m = min(step, n - o)
        out.append((t0 + o, m))
        o += m
    return out


class Ctx:
    pass


def token_tiles(L):
    return [(0, 16)] + [(16 + 128 * i, 128) for i in range((L - 16) // 128)]


def make_groups(L, G):
    tl = token_tiles(L)
    nt = len(tl) - 1
    G = min(G, nt)
    sizes = [nt // G + (1 if k < nt % G else 0) for k in range(G)]
    groups = []
    i = 1
    first = True
    while i <= nt:
        j = min(nt, i + sizes[len(groups)] - 1)
        g0 = 0 if first else tl[i][0]
        g1 = tl[j][0] + tl[j][1]
        subs = []
        if first:
            subs.append((0, 16))
            subs += split_cols(16, g1 - 16, 512)
        else:
            subs += split_cols(g0, g1 - g0, 512)
        tiles = ([0] if first else []) + list(range(i, j + 1))
        groups.append((g0, g1 - g0, subs, tiles))
        first = False
        i = j + 1
    return groups


def fm(ap2d):
    return ap2d.rearrange("(c p) t -> p c t", p=128)


_UID = [0]


def alloc(c, es, name, shape, dt):
    _UID[0] += 1
    return es.enter_context(c.nc.sbuf_tensor("s%d_%s" % (_UID[0], name), list(shape), dt))


def palloc(c, es, name, shape, dt=None):
    _UID[0] += 1
    return es.enter_context(c.nc.psum_tensor("p%d_%s" % (_UID[0], name), list(shape), dt or F32))


def rsqrt_op(c, out, in_, scale, reads, wkey, np_=128):
    P = c.P
    P.add("act", lambda e: e.activation(out=out, in_=in_, func=AF.Sqrt, bias=c.epsc[0:np_, 0:1], scale=scale),
          list(reads) + ["const"], [wkey])
    P.add("dve", lambda e: e.reciprocal(out=out, in_=out), [wkey], [wkey])


def norm_stage(c, es_bufs, src, gcol, hbuf, hkey, g0, ng):
    for _ in norm_gen(c, es_bufs, src, gcol, hbuf, hkey, g0, ng):
        pass


def norm_gen(c, es_bufs, src, gcol, hbuf, hkey, g0, ng):
    P = c.P
    xts, sqs, rstd, psn = es_bufs
    srcv = fm(src)
    for (t0, n) in split_cols(g0, ng, 128):
        off = t0 - g0
        xt, xk = xts.next()
        P.dma("sp", xt[:, :, 0:n], srcv[:, :, t0:t0 + n], [], [xk], xk)
        for ch in range(NCH):
            sq, sk = sqs.next()
            P.add("act", lambda e, sq=sq, xt=xt, ch=ch, n=n: e.activation(out=sq[:, 0:n], in_=xt[:, ch, 0:n], func=AF.Square),
                  [xk], [sk])
            P.add("pe", lambda e, sq=sq, ch=ch, n=n: e.matmul(psn[:, 0:n], lhsT=c.ones_bf[:, :], rhs=sq[:, 0:n],
                                                              start=(ch == 0), stop=(ch == NCH - 1)),
                  [sk, "const"], ["psn"])
        rsqrt_op(c, rstd[:, 0:n], psn[:, 0:n], 1.0 / D, ["psn"], "rstd")
        for ch in range(NCH):
            P.add("dve", lambda e, xt=xt, ch=ch, n=n, off=off: e.scalar_tensor_tensor(
                out=hbuf[:, ch, off:off + n], in0=xt[:, ch, 0:n], scalar=gcol[:, ch:ch + 1], in1=rstd[:, 0:n],
                op0=ALU.mult, op1=ALU.mult), [xk, "rstd", "params"], [hkey])
        yield


def norm_bufs(c, es):
    xts = Rot("xt", [alloc(c, es, "xt%d" % i, [128, NCH, 128], F32) for i in range(2)])
    sqs = Rot("sq", [alloc(c, es, "sq%d" % i, [128, 128], BF16) for i in range(3)])
    rstd = alloc(c, es, "rstd", [128, 256], F32)
    psn = palloc(c, es, "psn", [128, 512])
    return (xts, sqs, rstd, psn)


def ffn_phase(c, src, dst, gcol, w_gu, w_down):
    nc, P = c.nc, c.P
    es = ExitStack()
    groups = make_groups(c.L, c.G_ffn)
    NG = max(g[1] for g in groups)
    nb = norm_bufs(c, es)
    hbuf = alloc(c, es, "hbuf", [128, NCH, NG], BF16)
    act = alloc(c, es, "actb", [128, NJ, NG], BF16)
    wgu = Rot("wgu", [alloc(c, es, "wgu%d" % i, [128, NCH, 256], BF16) for i in range(3)])
    wd = Rot("wd", [alloc(c, es, "wd%d" % i, [128, NJ, 128], BF16) for i in range(2)])
    sgs = Rot("sg", [alloc(c, es, "sg%d" % i, [128, 512], F32) for i in range(2)])
    xrs = Rot("xr", [alloc(c, es, "xr%d" % i, [128, 512], F32) for i in range(2)])
    xos = Rot("xo", [alloc(c, es, "xo%d" % i, [128, 512], F32) for i in range(2)])
    psg = Rot("psg", [palloc(c, es, "psg%d" % i, [128, 512]) for i in range(2)])
    psu = Rot("psu", [palloc(c, es, "psu%d" % i, [128, 512]) for i in range(2)])
    pso = Rot("pso", [palloc(c, es, "pso%d" % i, [128, 512]) for i in range(2)])
    wguv = fm(w_gu)
    wdv = fm(w_down)
    srcv, dstv = fm(src), fm(dst)
    for gi_, (g0, ng, subs, _tiles) in enumerate(groups):
        if gi_ == 0:
            norm_stage(c, nb, src, gcol, hbuf, "hbuf", g0, ng)
        if gi_ + 1 < len(groups):
            ngen = norm_gen(c, nb, src, gcol, hbuf, "hbuf", groups[gi_ + 1][0], groups[gi_ + 1][1])
        else:
            ngen = iter(())
        for j in range(NJ):
            w, wk = wgu.next()
            P.dma("pool", w[:, :, 0:128], wguv[:, :, j * 128:(j + 1) * 128], [], [wk], wk)
            P.dma("pool", w[:, :, 128:256], wguv[:, :, DFF + j * 128:DFF + (j + 1) * 128], [], [wk], wk)
            for (t0, n) in subs:
                off = t0 - g0
                pg, pgk = psg.next()
                pu, puk = psu.next()
                for ch in range(NCH):
                    P.add("pe", lambda e, pg=pg, w=w, ch=ch, off=off, n=n: e.matmul(
                        pg[:, 0:n], lhsT=w[:, ch, 0:128], rhs=hbuf[:, ch, off:off + n], start=(ch == 0), stop=(ch == NCH - 1)),
                        [wk, "hbuf"], [pgk])
                for ch in range(NCH):
                    P.add("pe", lambda e, pu=pu, w=w, ch=ch, off=off, n=n: e.matmul(
                        pu[:, 0:n], lhsT=w[:, ch, 128:256], rhs=hbuf[:, ch, off:off + n], start=(ch == 0), stop=(ch == NCH - 1)),
                        [wk, "hbuf"], [puk])
                sg, sgk = sgs.next()
                P.add("act", lambda e, sg=sg, pg=pg, n=n: e.activation(out=sg[:, 0:n], in_=pg[:, 0:n], func=AF.Silu),
                      [pgk], [sgk])
                P.add("dve", lambda e, sg=sg, pu=pu, j=j, off=off, n=n: e.tensor_tensor(
                    out=act[:, j, off:off + n], in0=sg[:, 0:n], in1=pu[:, 0:n], op=ALU.mult), [sgk, puk], [("act", j)])
        for m in range(NCH):
            w, wk = wd.next()
            P.dma("pool", w[:, :, :], wdv[:, :, m * 128:(m + 1) * 128], [], [wk], wk)
            for (t0, n) in subs:
                off = t0 - g0
                xr, xrk = xrs.next()
                P.dma("sp", xr[:, 0:n], srcv[:, m, t0:t0 + n], [], [xrk], xrk)
                po, pok = pso.next()
                for j in range(NJ):
                    P.add("pe", lambda e, po=po, w=w, j=j, off=off, n=n: e.matmul(
                        po[:, 0:n], lhsT=w[:, j, :], rhs=act[:, j, off:off + n], start=(j == 0), stop=(j == NJ - 1)),
                        [wk, ("act", j)], [pok])
                xo, xok = xos.next()
                P.add("dve", lambda e, xo=xo, po=po, xr=xr, n=n: e.scalar_tensor_tensor(
                    out=xo[:, 0:n], in0=po[:, 0:n], scalar=0.5, in1=xr[:, 0:n], op0=ALU.mult, op1=ALU.add),
                    [pok, xrk], [xok])
                P.dma("act", dstv[:, m, t0:t0 + n], xo[:, 0:n], [xok], [], xok)
            if m >= 2:
                next(ngen, None)
        for _ in ngen:
            pass
    P.flush()
    es.close()


PL = 80
PC = dict(ffn1=0, mix=16, ffn2=32, gnorm=48, conv=52, dqn=64, dkn=65, subln=66, dsaq=67, dsak=68, lb=69, lam=73)


def pack_params(inp):
    p = np.zeros((128, 2 * PL), np.float32)
    for l in range(2):
        b = l * PL
        p[:, b + PC["ffn1"]:b + PC["ffn1"] + 16] = inp["ffn1_norm"][l].reshape(16, 128).T
        p[:, b + PC["mix"]:b + PC["mix"] + 16] = inp["mix_norm"][l].reshape(16, 128).T
        p[:, b + PC["ffn2"]:b + PC["ffn2"] + 16] = inp["ffn2_norm"][l].reshape(16, 128).T
        p[:, b + PC["gnorm"]:b + PC["gnorm"] + 4] = inp["hgrn_gnorm"][l].reshape(4, 128).T
        for j in range(3):
            p[:, b + PC["conv"] + j * 4:b + PC["conv"] + j * 4 + 4] = inp["conv_w"][l, j].reshape(4, 128).T
        p[:, b + PC["dqn"]] = np.tile(inp["diff_q_norm"][l], 2)
        p[:, b + PC["dkn"]] = np.tile(inp["diff_k_norm"][l], 2)
        p[:, b + PC["subln"]] = inp["diff_subln"][l]
        p[:, b + PC["dsaq"]] = inp["dsa_q_norm"][l]
        p[:, b + PC["dsak"]] = inp["dsa_k_norm"][l]
        p[:, b + PC["lb"]:b + PC["lb"] + 4] = inp["hgrn_lb"][l].reshape(4, 128).T
        p[0:64, b + PC["lam"]:b + PC["lam"] + 4] = inp["diff_lambda"][l].T
    return p


def rel_bucket_np(n):
    n = np.maximum(n, 0)
    nf = np.maximum(n, 1).astype(np.float32)
    large = 16 + (np.log(nf / np.float32(16)) / np.float32(math.log(128 / 16)) * np.float32(16)).astype(np.int32)
    large = np.minimum(large, 31)
    return np.where(n < 16, n, large)


def make_consts():
    cst = {}
    cst["ident"] = np.eye(128, dtype=np.float32)
    sel = np.zeros((33, 3, 256), np.float32)
    for ci, delta in enumerate((0, 128, 16)):
        m = np.arange(255)
        n = m - 127 + delta
        bk = rel_bucket_np(n)
        for mm in range(255):
            if n[mm] < 0:
                sel[32, ci, mm] = 1.0
            else:
                sel[bk[mm], ci, mm] = 1.0
    cst["sel"] = sel
    e31 = np.zeros((32, 128), np.float32)
    e31[31, :] = 1.0
    cst["e31"] = e31
    o64 = np.zeros((128, 128), np.float32)
    o64[:64, :64] = 1.0
    o64[64:, 64:] = 1.0
    cst["ones64"] = o64
    q = np.arange(128)[:, None]
    k = np.arange(128)[None, :]
    cst["fut"] = np.where(k > q, -1e30, 0.0).astype(np.float32)
    cst["causT"] = (q <= k).astype(np.int32)
    return cst


WNAMES = ("ffn1_w_gu", "ffn1_w_down", "w_in", "w_branch", "w_out", "ffn2_w_gu", "ffn2_w_down")


def build(S, topk, G=4, phases=None, dbg=False):
    L = S + 16
    nc = bass.Bass("TRN2", target_bir_lowering=False)
    c = Ctx()
    c.nc = nc
    c.L, c.S, c.topk = L, S, topk
    c.tiles = token_tiles(L)
    c.G_ffn, c.G_proj, c.G_merge = (4, 2, 4) if S >= 2048 else (2, 2, 2)
    c.dbg = dbg

    def din(name, shape, dt=F32):
        return nc.dram_tensor(name, list(shape), dt, kind="ExternalInput").ap()

    def dscr(name, shape, dt=F32):
        kind = "ExternalOutput" if dbg else "Internal"
        return nc.dram_tensor(name, list(shape), dt, kind=kind).ap()

    c.xin = din("xin", [D, L])
    c.params_d = din("params", [128, 2 * PL])
    c.relb_d = din("relb", [32, 8])
    c.ident_d = din("ident", [128, 128])
    c.sel_d = din("sel", [33, 3, 256])
    c.e31_d = din("e31", [32, 128])
    c.ones64_d = din("ones64", [128, 128])
    c.fut_d = din("fut", [128, 128])
    c.causT_d = din("causT", [128, 128], I32)
    c.w = {}
    c.w["ffn1_w_gu"] = din("ffn1_w_gu", [2, D, 2 * DFF])
    c.w["ffn1_w_down"] = din("ffn1_w_down", [2, DFF, D])
    c.w["ffn2_w_gu"] = din("ffn2_w_gu", [2, D, 2 * DFF])
    c.w["ffn2_w_down"] = din("ffn2_w_down", [2, DFF, D])
    c.w["w_in"] = din("w_in", [2, D, 15184])
    c.w["w_dw_rep"] = din("w_dw_rep", [2, D, 1024])
    c.w["w_branch"] = din("w_branch", [2, 4, 512, D])
    c.w["w_out"] = din("w_out", [2, D, D])
    c.yout = nc.dram_tensor("yout", [D, L], F32, kind="ExternalOutput").ap()
    c.xres = dscr("xres", [D, L])
    c.proj = dscr("proj", [NFM, L])
    c.wabs = dscr("wabs", [1024, L], BF16)
    c.vA = dscr("vA", [L, 512], BF16)
    c.vC = dscr("vC", [L, 512], BF16)
    c.vD = dscr("vD", [L, 128], BF16)
    c.wT = dscr("wT", [L, 16])
    c.br = [dscr("br%d" % i, [512, L], BF16) for i in range(4)]
    c.gsc = dscr("gsc", [24, 128, 255])

    es = ExitStack()
    P = Prog(nc)
    c.P = P
    c.ident_f = alloc(c, es, "ident_f", [128, 128], F32)
    c.ident_bf = alloc(c, es, "ident_bf", [128, 128], BF16)
    c.ones_bf = alloc(c, es, "ones_bf", [128, 128], BF16)
    c.ones_f = alloc(c, es, "ones_f", [128, 128], F32)
    c.epsc = alloc(c, es, "epsc", [128, 1], F32)
    c.ones64_bf = alloc(c, es, "ones64_bf", [128, 128], BF16)
    c.params = alloc(c, es, "params_sb", [128, 2 * PL], F32)
    c.BT = alloc(c, es, "BT", [128, 24, 128], BF16)
    c.cbias = alloc(c, es, "cbias", [128, 8], F32)
    c.fut = alloc(c, es, "fut", [128, 128], F32)
    c.causT = alloc(c, es, "causT", [128, 128], I32)
    c.lbs = alloc(c, es, "lbs", [128, 2, 4], F32)
    c.oml = alloc(c, es, "oml", [128, 2, 4], F32)
    c.nlam = alloc(c, es, "nlam", [128, 2], F32)
    setup_phase(c)

    ph = phases
    for l in range(2):
        pb = l * PL
        first = (l == 0)
        if ph is None or ("ffn1_%d" % l) in ph:
            ffn_phase(c, c.xin if first else c.xres, c.xres, c.params[:, pb + PC["ffn1"]:pb + PC["ffn1"] + 16],
                      c.w["ffn1_w_gu"][l], c.w["ffn1_w_down"][l])
        if ph is None or ("proj_%d" % l) in ph:
            proj_phase(c, l)
        if ph is None or ("conv_%d" % l) in ph:
            conv_phase(c, l)
        if ph is None or ("hgrn_%d" % l) in ph:
            hgrn_phase(c, l)
        if ph is None or ("diff_%d" % l) in ph:
            diff_phase(c, l)
        if ph is None or ("dsa_%d" % l) in ph:
            dsa_phase(c, l)
        if ph is None or ("merge_%d" % l) in ph:
            merge_phase(c, l)
        if ph is None or ("ffn2_%d" % l) in ph:
            ffn_phase(c, c.xres, c.yout if l == 1 else c.xres, c.params[:, pb + PC["ffn2"]:pb + PC["ffn2"] + 16],
                      c.w["ffn2_w_gu"][l], c.w["ffn2_w_down"][l])
    P.close()
    es.close()
    return nc, c


def setup_phase(c):
    nc, P = c.nc, c.P
    es = ExitStack()
    P.dma("sp", c.ident_f[:], c.ident_d, [], ["const_i"], "ident_f")
    P.dma("sp", c.params[:], c.params_d, [], ["params"], "params")
    P.dma("sp", c.fut[:], c.fut_d, [], ["const_f"], "fut")
    P.dma("sp", c.causT[:], c.causT_d, [], ["const_c"], "causT")
    o64 = alloc(c, es, "o64f", [128, 128], F32)
    P.dma("sp", o64[:], c.ones64_d, [], ["o64f"], "o64f")
    P.add("dve", lambda e: e.tensor_copy(out=c.ident_bf[:], in_=c.ident_f[:]), ["const_i"], ["const"])
    P.add("dve", lambda e: e.tensor_copy(out=c.ones64_bf[:], in_=o64[:]), ["o64f"], ["const"])
    P.add("dve", lambda e: e.memset(c.ones_bf[:], 1.0), [], ["const"])
    P.add("dve", lambda e: e.memset(c.ones_f[:], 1.0), [], ["const"])
    P.add("dve", lambda e: e.memset(c.epsc[:], EPS), [], ["const"])
    tab = alloc(c, es, "tab", [32, 8], F32)
    sel = alloc(c, es, "selsb", [33, 3, 256], F32)
    e31 = alloc(c, es, "e31sb", [32, 128], F32)
    P.dma("sp", tab[:], c.relb_d, [], ["tab"], "tab")
    P.dma("sp", sel[:], c.sel_d, [], ["sel"], "sel")
    P.dma("sp", e31[:], c.e31_d, [], ["e31"], "e31")
    tabB = Rot("tabB", [alloc(c, es, "tabB%d" % i, [33, 128], F32) for i in range(2)])
    gsb = Rot("gsb", [alloc(c, es, "gsb%d" % i, [128, 255], F32) for i in range(2)])
    btf = Rot("btf", [alloc(c, es, "btf%d" % i, [128, 128], F32) for i in range(2)])
    psG = Rot("psG", [palloc(c, es, "psG%d" % i, [128, 256]) for i in range(2)])
    psc = palloc(c, es, "psc", [128, 8])
    P.add("pe", lambda e: e.matmul(psc[:, :], lhsT=e31[:, :], rhs=tab[:, :], start=True, stop=True), ["e31", "tab"], ["psc"])
    P.add("dve", lambda e: e.tensor_copy(out=c.cbias[:], in_=psc[:]), ["psc"], ["const"])
    for hh in range(8):
        tb, tbk = tabB.next()
        P.add("dve", lambda e, tb=tb: e.memset(tb[:, :], NEG), [], [tbk])
        P.add("dve", lambda e, tb=tb, hh=hh: e.tensor_scalar(out=tb[0:32, :], in0=c.ones_f[0:32, :], scalar1=tab[0:32, hh:hh + 1],
                                                             scalar2=None, op0=ALU.mult), ["tab", "const", tbk], [tbk])
        for ci in range(3):
            pg, pgk = psG.next()
            P.add("pe", lambda e, pg=pg, tb=tb, ci=ci: e.matmul(pg[:, 0:256], lhsT=tb[:, :], rhs=sel[:, ci, :], start=True, stop=True),
                  [tbk, "sel"], [pgk])
            gs, gsk = gsb.next()
            P.add("act", lambda e, gs=gs, pg=pg: e.activation(out=gs[:, :], in_=pg[:, 0:255], func=AF.Copy), [pgk], [gsk])
            idx = hh * 3 + ci
            P.dma("sp", c.gsc[idx], gs[:, :], [gsk], [("gsc", idx)], gsk)
            bt, btk = btf.next()
            skew = bass.AP(tensor=c.gsc.tensor, offset=idx * 128 * 255 + 127, ap=[[254, 128], [1, 128]])
            P.dma("sp", bt[:, :], skew, [("gsc", idx)], [btk], btk)
            P.add("dve", lambda e, bt=bt, idx=idx, hh=hh: e.tensor_scalar(out=c.BT[:, idx, :], in0=bt[:, :], scalar1=c.cbias[:, hh:hh + 1],
                                                                          scalar2=None, op0=ALU.subtract), [btk, "const"], ["const"])
    P.add("dve", lambda e: e.memset(c.lbs[:, 0, :], 0.0), [], ["const"])
    P.add("dve", lambda e: e.memset(c.oml[:, 0, :], 1.0), [], ["const"])
    dl = alloc(c, es, "dl", [128, 4], F32)
    P.add("dve", lambda e: e.tensor_tensor(out=dl[:, :], in0=c.params[:, PL + PC["lb"]:PL + PC["lb"] + 4],
                                           in1=c.params[:, PC["lb"]:PC["lb"] + 4], op=ALU.subtract), ["params"], ["dl"])
    P.add("act", lambda e: e.activation(out=c.lbs[:, 1, :], in_=dl[:, :], func=AF.Sigmoid), ["dl"], ["const"])
    P.add("dve", lambda e: e.tensor_scalar(out=c.oml[:, 1, :], in0=c.lbs[:, 1, :], scalar1=-1.0, scalar2=1.0,
                                           op0=ALU.mult, op1=ALU.add), ["const"], ["const"])
    pr = alloc(c, es, "pr", [128, 4], F32)
    psl = palloc(c, es, "psl", [128, 4])
    el = alloc(c, es, "el", [128, 4], F32)
    for l in range(2):
        b = l * PL + PC["lam"]
        P.add("dve", lambda e, l=l, b=b: e.tensor_tensor(out=pr[:, 2 * l:2 * l + 1], in0=c.params[:, b:b + 1], in1=c.params[:, b + 1:b + 2],
                                                         op=ALU.mult), ["params"], ["pr"])
        P.add("dve", lambda e, l=l, b=b: e.tensor_tensor(out=pr[:, 2 * l + 1:2 * l + 2], in0=c.params[:, b + 2:b + 3], in1=c.params[:, b + 3:b + 4],
                                                         op=ALU.mult), ["params"], ["pr"])
    P.add("pe", lambda e: e.matmul(psl[:, :], lhsT=c.ones_f[:, :], rhs=pr[:, :], start=True, stop=True), ["pr", "const"], ["psl"])
    P.add("act", lambda e: e.activation(out=el[:, :], in_=psl[:, :], func=AF.Exp), ["psl"], ["el"])
    for l in range(2):
        lam_init = 0.8 - 0.6 * math.exp(-0.3 * l)
        P.add("dve", lambda e, l=l, li=lam_init: e.scalar_tensor_tensor(
            out=c.nlam[:, l:l + 1], in0=el[:, 2 * l + 1:2 * l + 2], scalar=-li, in1=el[:, 2 * l:2 * l + 1],
            op0=ALU.add, op1=ALU.subtract), ["el"], ["const"])
    P.flush()
    es.close()


_CACHE = {}


def host_inputs(inp, b, consts, params, w_dw_rep):
    x = inp["x"]
    xin = np.ascontiguousarray(np.concatenate([inp["meta_tokens"].T, x[b].T], axis=1), dtype=np.float32)
    m = {"xin": xin, "params": params, "relb": np.ascontiguousarray(inp["rel_bias"], dtype=np.float32),
         "ident": consts["ident"], "sel": consts["sel"], "e31": consts["e31"], "ones64": consts["ones64"],
         "fut": consts["fut"], "causT": consts["causT"], "w_dw_rep": w_dw_rep}
    for k in WNAMES:
        m[k] = np.ascontiguousarray(inp[k], dtype=np.float32)
    return m


def run(inp, phases=None, dbg=False, G=4, trace=False):
    inp = {k: np.asarray(v) for k, v in inp.items()}
    B, S, _ = inp["x"].shape
    topk = min(256, S // 4)
    nc, c = build(S, topk, G=G, phases=phases, dbg=dbg)
    consts = make_consts()
    params = pack_params(inp)
    w16 = inp["w_in"][:, :, OFF["d_w"]:OFF["d_w"] + 16]
    w_dw_rep = np.ascontiguousarray(np.repeat(w16, 64, axis=2), dtype=np.float32)
    in_maps = [host_inputs(inp, b, consts, params, w_dw_rep) for b in range(B)]
    res = run_bass_kernel_spmd(nc, in_maps, core_ids=list(range(B)), trace=trace)
    return res, c


def kernel(**inputs):
    res, c = run(inputs)
    outs = [r["yout"] for r in res.results]
    y = np.stack([np.ascontiguousarray(o[:, 16:].T) for o in outs], axis=0)
    return y.astype(np.float32)


def fm_chunks():
    skip = [(OFF["a_i"], OFF["a_g"]), (OFF["c_v"], OFF["d_q"]), (OFF["d_v"], OFF["d_qi"])]
    out = []
    c0 = 0
    while c0 < NFM:
        n = min(128, NFM - c0)
        if not any(a <= c0 < b for a, b in skip):
            out.append((c0, n))
        c0 += n
    return out


def proj_phase(c, l):
    nc, P = c.nc, c.P
    es = ExitStack()
    groups = make_groups(c.L, c.G_proj)
    NG = max(g[1] for g in groups)
    pb = l * PL
    nb = norm_bufs(c, es)
    hbuf = alloc(c, es, "hbuf", [128, NCH, NG], BF16)
    wfm = Rot("wfm", [alloc(c, es, "wfm%d" % i, [128, NCH, 128], BF16) for i in range(3)])
    wtm = Rot("wtm", [alloc(c, es, "wtm%d" % i, [128, NCH, 512], BF16) for i in range(2)])
    evs = Rot("ev", [alloc(c, es, "ev%d" % i, [128, 512], F32) for i in range(2)])
    evb = Rot("evb", [alloc(c, es, "evb%d" % i, [128, 512], BF16) for i in range(2)])
    ps = Rot("ps", [palloc(c, es, "ps%d" % i, [128, 512]) for i in range(4)])
    win = fm(c.w["w_in"][l])
    wrep = fm(c.w["w_dw_rep"][l])
    gcol = c.params[:, pb + PC["mix"]:pb + PC["mix"] + 16]
    hbufs = [hbuf, alloc(c, es, "hbuf2", [128, NCH, NG], BF16)]
    tmc_ = [0]
    for gi_, (g0, ng, subs, tiles) in enumerate(groups):
        hbuf = hbufs[gi_ % 2]
        hk = ("hbuf", gi_ % 2)
        if gi_ == 0:
            norm_stage(c, nb, c.xres, gcol, hbuf, hk, g0, ng)
        if gi_ + 1 < len(groups):
            ngen = norm_gen(c, nb, c.xres, gcol, hbufs[(gi_ + 1) % 2], ("hbuf", (gi_ + 1) % 2), groups[gi_ + 1][0], groups[gi_ + 1][1])
        else:
            ngen = iter(())
        for (c0, mc) in fm_chunks():
            w, wk = wfm.next()
            P.dma("pool", w[:, :, 0:mc], win[:, :, c0:c0 + mc], [], [wk], wk)
            for (t0, n) in subs:
                off = t0 - g0
                p, pk = ps.next()
                for ch in range(NCH):
                    P.add("pe", lambda e, p=p, w=w, ch=ch, off=off, n=n, mc=mc, hbuf=hbuf: e.matmul(
                        p[0:mc, 0:n], lhsT=w[:, ch, 0:mc], rhs=hbuf[:, ch, off:off + n], start=(ch == 0), stop=(ch == NCH - 1)),
                        [wk, hk], [pk])
                ev, ek = evs.next()
                P.add("act", lambda e, ev=ev, p=p, n=n, mc=mc: e.activation(out=ev[0:mc, 0:n], in_=p[0:mc, 0:n], func=AF.Copy),
                      [pk], [ek])
                P.dma("sp", c.proj[c0:c0 + mc, t0:t0 + n], ev[0:mc, 0:n], [ek], [], ek)
        for r in range(8):
            w, wk = wfm.next()
            P.dma("pool", w[:, :, :], wrep[:, :, r * 128:(r + 1) * 128], [], [wk], wk)
            for (t0, n) in subs:
                off = t0 - g0
                p, pk = ps.next()
                for ch in range(NCH):
                    P.add("pe", lambda e, p=p, w=w, ch=ch, off=off, n=n, hbuf=hbuf: e.matmul(
                        p[:, 0:n], lhsT=w[:, ch, :], rhs=hbuf[:, ch, off:off + n], start=(ch == 0), stop=(ch == NCH - 1)),
                        [wk, hk], [pk])
                ev, ek = evb.next()
                P.add("act", lambda e, ev=ev, p=p, n=n: e.activation(out=ev[:, 0:n], in_=p[:, 0:n], func=AF.Abs), [pk], [ek])
                P.dma("sp", c.wabs[r * 128:(r + 1) * 128, t0:t0 + n], ev[:, 0:n], [ek], [], ek)
        for (c0, ncol, dst, isf32) in ((OFF["a_i"], 512, c.vA, False), (OFF["c_v"], 512, c.vC, False),
                                       (OFF["d_v"], 128, c.vD, False), (OFF["d_w"], 16, c.wT, True)):
            w, wk = wtm.next()
            P.dma("pool", w[:, :, 0:ncol], win[:, :, c0:c0 + ncol], [], [wk], wk)
            for ti in tiles:
                t0, nt = c.tiles[ti]
                off = t0 - g0
                p, pk = ps.next()
                for ch in range(NCH):
                    P.add("pe", lambda e, p=p, w=w, ch=ch, off=off, nt=nt, ncol=ncol, hbuf=hbuf: e.matmul(
                        p[0:nt, 0:ncol], lhsT=hbuf[:, ch, off:off + nt], rhs=w[:, ch, 0:ncol], start=(ch == 0), stop=(ch == NCH - 1)),
                        [wk, hk], [pk])
                if isf32:
                    ev, ek = evs.next()
                else:
                    ev, ek = evb.next()
                P.add("act", lambda e, ev=ev, p=p, nt=nt, ncol=ncol: e.activation(out=ev[0:nt, 0:ncol], in_=p[0:nt, 0:ncol], func=AF.Copy),
                      [pk], [ek])
                P.dma("sp", dst[t0:t0 + nt, 0:ncol], ev[0:nt, 0:ncol], [ek], [], ek)
                tmc_[0] += 1
                if tmc_[0] % 3 == 0:
                    next(ngen, None)
        for _ in ngen:
            pass
    P.flush()
    es.close()


def conv_phase(c, l):
    nc, P = c.nc, c.P
    es = ExitStack()
    L = c.L
    pb = l * PL + PC["conv"]
    bb = Rot("bb", [alloc(c, es, "bb%d" % i, [128, L], F32) for i in range(2)])
    bc = Rot("bc", [alloc(c, es, "bc%d" % i, [128, L], F32) for i in range(2)])
    bu = Rot("bu", [alloc(c, es, "bu%d" % i, [128, L], F32) for i in range(2)])
    zc = alloc(c, es, "zc", [128, L + 2], F32)
    yb = alloc(c, es, "yb", [128, L], F32)
    ob = Rot("ob", [alloc(c, es, "ob%d" % i, [128, L], BF16) for i in range(2)])
    P.add("dve", lambda e: e.memset(zc[:, 0:2], 0.0), [], ["zc0"])
    for ch in range(4):
        tb, tbk = bb.next()
        tc_, tck = bc.next()
        tu, tuk = bu.next()
        P.dma("sp", tb[:, :], c.proj[OFF["b_b"] + ch * 128:OFF["b_b"] + (ch + 1) * 128, :], [], [tbk], tbk)
        P.dma("sp", tc_[:, :], c.proj[OFF["b_c"] + ch * 128:OFF["b_c"] + (ch + 1) * 128, :], [], [tck], tck)
        P.dma("sp", tu[:, :], c.proj[OFF["b_u"] + ch * 128:OFF["b_u"] + (ch + 1) * 128, :], [], [tuk], tuk)
        P.add("pool", lambda e, tc_=tc_, tu=tu: e.tensor_tensor(out=zc[:, 2:L + 2], in0=tc_[:, :], in1=tu[:, :], op=ALU.mult),
              [tck, tuk], ["zc"])
        w0 = c.params[:, pb + ch:pb + ch + 1]
        w1 = c.params[:, pb + 4 + ch:pb + 4 + ch + 1]
        w2 = c.params[:, pb + 8 + ch:pb + 8 + ch + 1]
        P.add("dve", lambda e, w0=w0: e.tensor_scalar(out=yb[:, :], in0=zc[:, 2:L + 2], scalar1=w0, scalar2=None, op0=ALU.mult),
              ["zc", "zc0", "params"], ["yb"])
        P.add("dve", lambda e, w1=w1: e.scalar_tensor_tensor(out=yb[:, :], in0=zc[:, 1:L + 1], scalar=w1, in1=yb[:, :],
                                                             op0=ALU.mult, op1=ALU.add), ["zc", "zc0", "yb", "params"], ["yb"])
        P.add("dve", lambda e, w2=w2: e.scalar_tensor_tensor(out=yb[:, :], in0=zc[:, 0:L], scalar=w2, in1=yb[:, :],
                                                             op0=ALU.mult, op1=ALU.add), ["zc", "zc0", "yb", "params"], ["yb"])
        o, ok = ob.next()
        P.add("dve", lambda e, o=o, tb=tb: e.tensor_tensor(out=o[:, :], in0=yb[:, :], in1=tb[:, :], op=ALU.mult), ["yb", tbk], [ok])
        P.dma("sp", c.br[1][ch * 128:(ch + 1) * 128, :], o[:, :], [ok], [], ok)
    P.flush()
    es.close()


def merge_phase(c, l):
    nc, P = c.nc, c.P
    es = ExitStack()
    groups = make_groups(c.L, c.G_merge)
    NG = max(g[1] for g in groups)
    pb = l * PL
    nb = norm_bufs(c, es)
    hbuf = alloc(c, es, "hbuf", [128, NCH, NG], BF16)
    brb = alloc(c, es, "brb", [128, 16, NG], BF16)
    mrg = alloc(c, es, "mrg", [128, NCH, NG], BF16)
    wg = Rot("wg", [alloc(c, es, "wg%d" % i, [128, NCH, 128], BF16) for i in range(3)])
    wb = Rot("wb", [alloc(c, es, "wb%d" % i, [128, 4, 128], BF16) for i in range(3)])
    wo = Rot("wo", [alloc(c, es, "wo%d" % i, [128, NCH, 128], BF16) for i in range(2)])
    sgs = Rot("sg", [alloc(c, es, "sg%d" % i, [128, 512], F32) for i in range(2)])
    macc = Rot("macc", [alloc(c, es, "macc%d" % i, [128, 512], F32) for i in range(3)])
    tmps = Rot("tmp", [alloc(c, es, "tmp%d" % i, [128, 512], F32) for i in range(2)])
    xrs = Rot("xr", [alloc(c, es, "xr%d" % i, [128, 512], F32) for i in range(3)])
    xos = Rot("xo", [alloc(c, es, "xo%d" % i, [128, 512], F32) for i in range(3)])
    psg = Rot("psg", [palloc(c, es, "psg%d" % i, [128, 512]) for i in range(2)])
    psb = Rot("psb", [palloc(c, es, "psb%d" % i, [128, 512]) for i in range(2)])
    pso = Rot("pso", [palloc(c, es, "pso%d" % i, [128, 512]) for i in range(2)])
    win = fm(c.w["w_in"][l])
    wout = fm(c.w["w_out"][l])
    gcol = c.params[:, pb + PC["mix"]:pb + PC["mix"] + 16]
    xv = fm(c.xres)
    for gi_, (g0, ng, subs, tiles) in enumerate(groups):
        if gi_ == 0:
            norm_stage(c, nb, c.xres, gcol, hbuf, "hbuf", g0, ng)
        if gi_ + 1 < len(groups):
            ngen = norm_gen(c, nb, c.xres, gcol, hbuf, "hbuf", groups[gi_ + 1][0], groups[gi_ + 1][1])
        else:
            ngen = iter(())
        for br in range(4):
            P.dma("sp", brb[:, br * 4:(br + 1) * 4, 0:ng], fm(c.br[br])[:, :, g0:g0 + ng], [], [("brb", br)], ("brb", br))
        for fc in range(NCH):
            accs = {}
            for br in range(4):
                w, wk = wg.next()
                gc0 = OFF["gate"] + br * D + fc * 128
                P.dma("pool", w[:, :, :], win[:, :, gc0:gc0 + 128], [], [wk], wk)
                w2, w2k = wb.next()
                P.dma("pool", w2[:, :, :], c.w["w_branch"][l, br].rearrange("(kc p) m -> p kc m", p=128)[:, :, fc * 128:(fc + 1) * 128],
                      [], [w2k], w2k)
                for si, (t0, n) in enumerate(subs):
                    off = t0 - g0
                    pg, pgk = psg.next()
                    pbr, pbk = psb.next()
                    for ch in range(NCH):
                        P.add("pe", lambda e, pg=pg, w=w, ch=ch, off=off, n=n: e.matmul(
                            pg[:, 0:n], lhsT=w[:, ch, :], rhs=hbuf[:, ch, off:off + n], start=(ch == 0), stop=(ch == NCH - 1)),
                            [wk, "hbuf"], [pgk])
                    for kc in range(4):
                        P.add("pe", lambda e, pbr=pbr, w2=w2, kc=kc, br=br, off=off, n=n: e.matmul(
                            pbr[:, 0:n], lhsT=w2[:, kc, :], rhs=brb[:, br * 4 + kc, off:off + n], start=(kc == 0), stop=(kc == 3)),
                            [w2k, ("brb", br)], [pbk])
                    sg, sgk = sgs.next()
                    P.add("act", lambda e, sg=sg, pg=pg, n=n: e.activation(out=sg[:, 0:n], in_=pg[:, 0:n], func=AF.Sigmoid),
                          [pgk], [sgk])
                    if br == 0:
                        accs[si] = macc.next()
                        ma, mak = accs[si]
                        P.add("dve", lambda e, ma=ma, sg=sg, pbr=pbr, n=n: e.tensor_tensor(
                            out=ma[:, 0:n], in0=sg[:, 0:n], in1=pbr[:, 0:n], op=ALU.mult), [sgk, pbk], [mak])
                    else:
                        ma, mak = accs[si]
                        tm, tmk = tmps.next()
                        P.add("dve", lambda e, tm=tm, sg=sg, pbr=pbr, n=n: e.tensor_tensor(
                            out=tm[:, 0:n], in0=sg[:, 0:n], in1=pbr[:, 0:n], op=ALU.mult), [sgk, pbk], [tmk])
                        if br < 3:
                            P.add("dve", lambda e, ma=ma, tm=tm, n=n: e.tensor_tensor(
                                out=ma[:, 0:n], in0=ma[:, 0:n], in1=tm[:, 0:n], op=ALU.add), [mak, tmk], [mak])
                        else:
                            P.add("dve", lambda e, ma=ma, tm=tm, n=n, fc=fc, off=off: e.tensor_tensor(
                                out=mrg[:, fc, off:off + n], in0=ma[:, 0:n], in1=tm[:, 0:n], op=ALU.add), [mak, tmk], [("mrg", fc)])
        for m in range(NCH):
            w, wk = wo.next()
            P.dma("pool", w[:, :, :], wout[:, :, m * 128:(m + 1) * 128], [], [wk], wk)
            for (t0, n) in subs:
                off = t0 - g0
                xr, xrk = xrs.next()
                P.dma("sp", xr[:, 0:n], xv[:, m, t0:t0 + n], [], [xrk], xrk)
                po, pok = pso.next()
                for ch in range(NCH):
                    P.add("pe", lambda e, po=po, w=w, ch=ch, off=off, n=n: e.matmul(
                        po[:, 0:n], lhsT=w[:, ch, :], rhs=mrg[:, ch, off:off + n], start=(ch == 0), stop=(ch == NCH - 1)),
                        [wk, ("mrg", ch)], [pok])
                xo, xok = xos.next()
                P.add("dve", lambda e, xo=xo, po=po, xr=xr, n=n: e.tensor_tensor(
                    out=xo[:, 0:n], in0=po[:, 0:n], in1=xr[:, 0:n], op=ALU.add), [pok, xrk], [xok])
                P.dma("act", xv[:, m, t0:t0 + n], xo[:, 0:n], [xok], [], xok)
            if m >= 2:
                next(ngen, None)
        for _ in ngen:
            pass
    P.flush()
    es.close()


def load_rows_norm(c, bufs, row0, nrows, out_fn, gsc_col, ones_m, gsize, key_out, dup64=False):
    P = c.P
    st, sq, rs, psn = bufs
    for (t0, n) in split_cols(0, c.L, 512):
        s, sk = st.next()
        if dup64:
            P.dma("sp", s[0:64, 0:n], c.proj[row0:row0 + 64, t0:t0 + n], [], [sk], sk)
            P.dma("sp", s[64:128, 0:n], c.proj[row0:row0 + 64, t0:t0 + n], [], [sk], sk)
        else:
            P.dma("sp", s[0:nrows, 0:n], c.proj[row0:row0 + nrows, t0:t0 + n], [], [sk], sk)
        if gsize is None:
            P.add("act", lambda e, s=s, t0=t0, n=n: e.activation(out=out_fn(t0, n), in_=s[:, 0:n], func=AF.Copy), [sk], [key_out])
            continue
        q, qk = sq.next()
        P.add("act", lambda e, q=q, s=s, n=n: e.activation(out=q[:, 0:n], in_=s[:, 0:n], func=AF.Square), [sk], [qk])
        P.add("pe", lambda e, q=q, n=n: e.matmul(psn[:, 0:n], lhsT=ones_m[:, :], rhs=q[:, 0:n], start=True, stop=True),
              [qk, "const"], ["psn"])
        r, rk = rs.next()
        rsqrt_op(c, r[:, 0:n], psn[:, 0:n], 1.0 / gsize, ["psn"], rk)
        P.add("dve", lambda e, s=s, r=r, t0=t0, n=n: e.scalar_tensor_tensor(
            out=out_fn(t0, n), in0=s[:, 0:n], scalar=gsc_col, in1=r[:, 0:n], op0=ALU.mult, op1=ALU.mult),
            [sk, rk, "gsc"], [key_out])


def rows_bufs(c, es):
    st = Rot("st", [alloc(c, es, "st%d" % i, [128, 512], F32) for i in range(2)])
    sq = Rot("sqq", [alloc(c, es, "sqq%d" % i, [128, 512], BF16) for i in range(2)])
    rs = Rot("rs", [alloc(c, es, "rs%d" % i, [128, 512], F32) for i in range(2)])
    psn = palloc(c, es, "psn", [128, 512])
    return (st, sq, rs, psn)


def near_case(i, j):
    if j == i:
        return 0
    if j >= 1 and j == i - 1:
        return 1
    if j == 0 and i == 1:
        return 2
    return None


def load_vtm(c, vsb, src, col0, ncol, key, pitch_view):
    P = c.P
    nt_full = len(c.tiles) - 1
    P.dma("sp", pitch_view(0, 16), src[0:16, col0:col0 + ncol], [], [key], key)
    for ti in range(1, nt_full + 1):
        t0, nt = c.tiles[ti]
        P.dma("sp", pitch_view(ti, nt), src[t0:t0 + nt, col0:col0 + ncol], [], [key], key)


def diff_phase(c, l):
    nc, P = c.nc, c.P
    es = ExitStack()
    L = c.L
    NT = len(c.tiles)
    pb = l * PL
    lam_init = 0.8 - 0.6 * math.exp(-0.3 * l)
    rb = rows_bufs(c, es)
    qC = alloc(c, es, "qC", [128, 4, L], BF16)
    kC = alloc(c, es, "kC", [128, 4, L], BF16)
    vC = alloc(c, es, "vCs", [128, NT, 4, 129], BF16)
    ob = alloc(c, es, "obC", [128, 4, L], BF16)
    gs = alloc(c, es, "gsC", [128, 4], F32)
    zc = alloc(c, es, "zcol", [128, 1], F32)
    pTs = Rot("pT", [alloc(c, es, "pT%d" % i, [128, 512], BF16) for i in range(3)])
    rr = Rot("rr", [alloc(c, es, "rr%d" % i, [128, 4], F32) for i in range(2)])
    ods = Rot("od", [alloc(c, es, "od%d" % i, [128, 128], F32) for i in range(2)])
    junk = alloc(c, es, "junk", [128, 128], F32)
    ons = Rot("on", [alloc(c, es, "on%d" % i, [128, 128], BF16) for i in range(2)])
    pss = Rot("pss", [palloc(c, es, "pss%d" % i, [128, 512]) for i in range(4)])
    pso = Rot("pso", [palloc(c, es, "pso%d" % i, [128, 512]) for i in range(2)])
    pst = palloc(c, es, "pst", [128, 512])
    posb = Rot("posb", [alloc(c, es, "posb%d" % i, [128, 264], F32) for i in range(2)])
    P.add("dve", lambda e: e.tensor_scalar(out=gs[:, 0:1], in0=c.params[:, pb + PC["dqn"]:pb + PC["dqn"] + 1], scalar1=0.125,
                                           scalar2=None, op0=ALU.mult), ["params"], ["gsc"])
    P.add("dve", lambda e: e.tensor_copy(out=gs[:, 1:2], in_=c.params[:, pb + PC["dkn"]:pb + PC["dkn"] + 1]), ["params"], ["gsc"])
    P.add("dve", lambda e: e.tensor_scalar(out=gs[:, 2:3], in0=c.params[:, pb + PC["subln"]:pb + PC["subln"] + 1],
                                           scalar1=1.0 - lam_init, scalar2=None, op0=ALU.mult), ["params"], ["gsc"])
    P.add("dve", lambda e: e.memset(zc[:, :], 0.0), [], ["gsc"])
    P.add("dve", lambda e: e.memset(vC[:, :, :, 128:129], 1.0), [], ["vC1"])
    for h in range(4):
        load_rows_norm(c, rb, OFF["c_q"] + h * 128, 128, lambda t0, n, h=h: qC[:, h, t0:t0 + n], gs[:, 0:1], c.ones64_bf, 64, ("qC", h))
        load_rows_norm(c, rb, OFF["c_k"] + h * 128, 128, lambda t0, n, h=h: kC[:, h, t0:t0 + n], gs[:, 1:2], c.ones64_bf, 64, ("kC", h))
    for ti in range(NT):
        t0, nt = c.tiles[ti]
        P.dma("sp", vC[0:nt, ti, :, 0:128], c.vC[t0:t0 + nt, :].rearrange("t (h e) -> t h e", h=4), [], ["vC"], "vC")
    c.prev_tail = None
    for i_ in range(NT):
      for h_ in range(4):
        def do_block(i, h, q0, nq):
            pos = [pso.next(), pso.next()]
            groups_ = [(cc, jg) for cc in range(2) for jg in range(0, i + 1, 4)]
            psl = {}

            def emit_s(gi):
                cc, jg = groups_[gi]
                grp = list(range(jg, min(i + 1, jg + 4)))
                ps, psk = pss.next()
                psl[gi] = (ps, psk, grp)
                for sl, j in enumerate(grp):
                    k0, nk = c.tiles[j]
                    case = near_case(i, j)
                    P.add("pe", lambda e, ps=ps, cc=cc, k0=k0, nk=nk, case=case, sl=sl: e.matmul(
                        ps[0:nk, sl * 128:sl * 128 + nq], lhsT=kC[cc * 64:(cc + 1) * 64, h, k0:k0 + nk],
                        rhs=qC[cc * 64:(cc + 1) * 64, h, q0:q0 + nq], start=True, stop=(case is None)), [("kC", h), ("qC", h)], [psk])
                    if case is not None:
                        P.add("pe", lambda e, ps=ps, nk=nk, case=case, sl=sl: e.matmul(
                            ps[0:nk, sl * 128:sl * 128 + nq], lhsT=c.ident_bf[0:nk, 0:nk], rhs=c.BT[0:nk, h * 3 + case, 0:nq],
                            start=False, stop=True), ["const"], [psk])

            def emit_pv(gi):
                cc, jg = groups_[gi]
                ps, psk, grp = psl[gi]
                po, pok = pos[cc]
                W = (len(grp) - 1) * 128 + nq
                pT, pTk = pTs.next()
                P.add("act", lambda e, pT=pT, ps=ps, W=W: e.activation(out=pT[:, 0:W], in_=ps[:, 0:W], func=AF.Exp), [psk], [pTk])
                for sl, j in enumerate(grp):
                    k0, nk = c.tiles[j]
                    P.add("pe", lambda e, po=po, pT=pT, nk=nk, j=j, sl=sl: e.matmul(
                        po[0:nq, 0:129], lhsT=pT[0:nk, sl * 128:sl * 128 + nq], rhs=vC[0:nk, j, h, 0:129], start=(j == 0), stop=(j == i)),
                        [pTk, "vC", "vC1"], [pok])

            emit_s(0)
            for gi in range(len(groups_)):
                if gi + 1 < len(groups_):
                    emit_s(gi + 1)
                emit_pv(gi)
            ob2, ob2k = posb.next()
            P.add("act", lambda e, ob2=ob2: e.activation(out=ob2[0:nq, 0:129], in_=pos[0][0][0:nq, 0:129], func=AF.Copy), [pos[0][1]], [ob2k])
            P.add("act", lambda e, ob2=ob2: e.activation(out=ob2[0:nq, 132:261], in_=pos[1][0][0:nq, 0:129], func=AF.Copy), [pos[1][1]], [ob2k])
            p0, p0k = ob2[:, 0:132], ob2k
            p1, p1k = ob2[:, 132:264], ob2k
            r, rk = rr.next()
            P.add("dve", lambda e, r=r, p0=p0, nq=nq: e.reciprocal(out=r[0:nq, 0:1], in_=p0[0:nq, 128:129]), [p0k], [rk])
            P.add("dve", lambda e, r=r, p1=p1, nq=nq: e.reciprocal(out=r[0:nq, 1:2], in_=p1[0:nq, 128:129]), [p1k], [rk])
            P.add("dve", lambda e, r=r, nq=nq: e.tensor_tensor(out=r[0:nq, 2:3], in0=r[0:nq, 1:2], in1=c.nlam[0:nq, l:l + 1], op=ALU.mult),
                  [rk, "const"], [rk])
            od, odk = ods.next()
            P.add("dve", lambda e, od=od, p0=p0, r=r, nq=nq: e.tensor_scalar(out=od[0:nq, :], in0=p0[0:nq, 0:128], scalar1=r[0:nq, 0:1],
                                                                              scalar2=None, op0=ALU.mult), [p0k, rk], [odk])
            P.add("dve", lambda e, od=od, p1=p1, r=r, nq=nq: e.scalar_tensor_tensor(
                out=od[0:nq, :], in0=p1[0:nq, 0:128], scalar=r[0:nq, 2:3], in1=od[0:nq, :], op0=ALU.mult, op1=ALU.add),
                [p1k, rk, odk], [odk])
            P.add("dve", lambda e, od=od, nq=nq: e.tensor_tensor(out=junk[0:nq, :], in0=od[0:nq, :], in1=od[0:nq, :], op=ALU.mult),
                  [odk], ["junk"])
            P.add("dve", lambda e, r=r, nq=nq: e.tensor_reduce(out=r[0:nq, 3:4], in_=junk[0:nq, :], axis=AX.X, op=ALU.add),
                  ["junk", rk], [rk])
            rsqrt_op(c, r[0:nq, 3:4], r[0:nq, 3:4], 1.0 / 128, [rk], rk, np_=nq)
            on, onk = ons.next()
            P.add("dve", lambda e, on=on, od=od, r=r, nq=nq: e.tensor_scalar(out=on[0:nq, :], in0=od[0:nq, :], scalar1=r[0:nq, 3:4],
                                                                              scalar2=None, op0=ALU.mult), [odk, rk], [onk])
            def tail():
                P.add("pe", lambda e, on=on, nq=nq: e.matmul(pst[:, 0:nq], lhsT=on[0:nq, :], rhs=c.ident_bf[0:nq, 0:nq], start=True, stop=True),
                      [onk, "const"], ["pst"])
                P.add("act", lambda e, h=h, q0=q0, nq=nq: e.activation(out=ob[:, h, q0:q0 + nq], in_=pst[:, 0:nq], func=AF.Copy,
                                                                        scale=gs[:, 2:3]), ["pst", "gsc"], ["obC"])
            return tail
        tl_ = do_block(i_, h_, c.tiles[i_][0], c.tiles[i_][1])
        if c.prev_tail is not None:
            c.prev_tail()
        c.prev_tail = tl_
    c.prev_tail()
    P.dma("sp", fm(c.br[2]), ob[:, :, :], ["obC"], [], "obC")
    P.flush()
    es.close()


def hgrn_phase(c, l):
    nc, P = c.nc, c.P
    es = ExitStack()
    L = c.L
    NT = len(c.tiles)
    pb = l * PL
    zf = alloc(c, es, "zf", [128, L], F32)
    bb = alloc(c, es, "bb", [128, L], F32)
    kf = alloc(c, es, "kf", [128, L], F32)
    qf = alloc(c, es, "qf", [128, L], F32)
    tmp = alloc(c, es, "tmpE", [128, L], F32)
    qt = alloc(c, es, "qt", [128, L], BF16)
    qh = alloc(c, es, "qh", [128, L], BF16)
    kh = alloc(c, es, "kh", [128, L], BF16)
    khT = alloc(c, es, "khT", [128, NT, 128], BF16)
    vA = alloc(c, es, "vAs", [128, NT, 128], BF16)
    osb = alloc(c, es, "osb", [128, L], BF16)
    obr = alloc(c, es, "obr", [128, L], BF16)
    e1 = alloc(c, es, "e1", [128, NT], F32)
    e2 = alloc(c, es, "e2", [128, NT], F32)
    Sf = alloc(c, es, "Sf", [128, 128], F32)
    St = alloc(c, es, "St", [128, 128], F32)
    Sb = Rot("Sb", [alloc(c, es, "Sb%d" % i, [128, 128], BF16) for i in range(2)])
    Pm = Rot("Pm", [alloc(c, es, "Pm%d" % i, [128, 128], BF16) for i in range(2)])
    sq = Rot("sqh", [alloc(c, es, "sqh%d" % i, [128, 512], BF16) for i in range(2)])
    rs = Rot("rsh", [alloc(c, es, "rsh%d" % i, [128, 512], F32) for i in range(2)])
    pss = Rot("pss", [palloc(c, es, "pss%d" % i, [128, 512]) for i in range(2)])
    pso = Rot("pso", [palloc(c, es, "pso%d" % i, [128, 512]) for i in range(2)])
    psd = Rot("psd", [palloc(c, es, "psd%d" % i, [128, 512]) for i in range(2)])
    pstb = palloc(c, es, "pstb", [128, 128], BF16)
    psn = palloc(c, es, "psnh", [128, 512])
    for k in range(2):
        P.add("pool", lambda e, k=k: e.memset(Pm.t[k][:, :], 0.0), [], [("Pm", k)])
    for h in range(4):
        lbc = c.lbs[:, l, h:h + 1]
        omc = c.oml[:, l, h:h + 1]
        P.dma("sp", zf[:, :], c.proj[OFF["a_f"] + h * 128:OFF["a_f"] + (h + 1) * 128, :], [], ["zf"], "zf")
        P.dma("sp", qf[:, :], c.proj[OFF["a_q"] + h * 128:OFF["a_q"] + (h + 1) * 128, :], [], ["qf"], "qf")
        load_vtm(c, vA, c.vA, h * 128, 128, "vA", lambda ti, nt: vA[0:nt, ti, :])
        P.add("act", lambda e: e.activation(out=zf[:, :], in_=zf[:, :], func=AF.Sigmoid), ["zf"], ["zf"])
        P.add("dve", lambda e, lbc=lbc, omc=omc: e.tensor_scalar(out=zf[:, :], in0=zf[:, :], scalar1=omc, scalar2=lbc,
                                                                 op0=ALU.mult, op1=ALU.add), ["zf", "const"], ["zf"])
        P.add("dve", lambda e: e.tensor_scalar(out=kf[:, :], in0=zf[:, :], scalar1=-1.0, scalar2=1.0, op0=ALU.mult, op1=ALU.add),
              ["zf"], ["kf"])
        P.add("act", lambda e: e.activation(out=zf[:, :], in_=zf[:, :], func=AF.Ln), ["zf", "kf"], ["zf"])
        for ti in range(NT):
            t0, nt = c.tiles[ti]
            P.add("dve", lambda e, t0=t0, nt=nt: e.tensor_tensor_scan(out=bb[:, t0:t0 + nt], data0=c.ones_f[:, 0:nt], data1=zf[:, t0:t0 + nt],
                                                                      initial=0.0, op0=ALU.mult, op1=ALU.add), ["zf", "const"], ["bb"])
        for ti in range(NT):
            t0, nt = c.tiles[ti]
            mid = t0 + nt // 2
            P.add("dve", lambda e, t0=t0, nt=nt, mid=mid: e.tensor_scalar(out=zf[:, t0:t0 + nt], in0=bb[:, t0:t0 + nt], scalar1=bb[:, mid:mid + 1],
                                                                          scalar2=None, op0=ALU.subtract), ["bb", "zf"], ["zf"])
        for ti in range(NT):
            t0, nt = c.tiles[ti]
            P.add("act", lambda e, ti=ti, t0=t0, nt=nt: e.activation(out=e1[:, ti:ti + 1], in_=bb[:, t0 + nt - 1:t0 + nt], func=AF.Exp),
                  ["bb"], ["e1"])
            P.add("act", lambda e, ti=ti, t0=t0, nt=nt: e.activation(out=e2[:, ti:ti + 1], in_=zf[:, t0 + nt - 1:t0 + nt], func=AF.Exp),
                  ["zf"], ["e2"])
        P.add("act", lambda e: e.activation(out=qf[:, :], in_=qf[:, :], func=AF.Silu), ["qf"], ["qf"])
        P.add("act", lambda e: e.activation(out=tmp[:, :], in_=bb[:, :], func=AF.Exp), ["bb"], ["tmp"])
        P.add("dve", lambda e: e.tensor_tensor(out=qt[:, :], in0=qf[:, :], in1=tmp[:, :], op=ALU.mult), ["qf", "tmp"], ["qt"])
        P.add("act", lambda e: e.activation(out=tmp[:, :], in_=zf[:, :], func=AF.Exp), ["zf", "qt"], ["tmp"])
        P.add("dve", lambda e: e.tensor_tensor(out=qh[:, :], in0=qf[:, :], in1=tmp[:, :], op=ALU.mult), ["qf", "tmp"], ["qh"])
        P.add("act", lambda e: e.activation(out=tmp[:, :], in_=zf[:, :], func=AF.Exp, scale=-1.0), ["zf", "qh"], ["tmp"])
        P.add("dve", lambda e: e.tensor_tensor(out=kh[:, :], in0=kf[:, :], in1=tmp[:, :], op=ALU.mult), ["kf", "tmp"], ["kh"])
        for ti in range(NT):
            t0, nt = c.tiles[ti]
            P.add("pe", lambda e, t0=t0, nt=nt: e.transpose(out=pstb[0:nt, :], in_=kh[:, t0:t0 + nt], identity=c.ident_bf[:, :]),
                  ["kh", "const"], ["pstb"])
            P.add("act", lambda e, ti=ti, nt=nt: e.activation(out=khT[0:nt, ti, :], in_=pstb[0:nt, :], func=AF.Copy), ["pstb"], ["khT"])
        P.add("dve", lambda e: e.memset(Sf[:, :], 0.0), [], ["Sf"])
        sb, sbk = Sb.next()
        P.add("pool", lambda e, sb=sb: e.memset(sb[:, :], 0.0), [], [sbk])
        for ti in range(NT):
            t0, nt = c.tiles[ti]
            ps, psk = pss.next()
            P.add("pe", lambda e, ps=ps, t0=t0, nt=nt: e.matmul(ps[0:nt, 0:nt], lhsT=kh[:, t0:t0 + nt], rhs=qh[:, t0:t0 + nt], start=True, stop=True),
                  ["kh", "qh"], [psk])
            pm, pmk = Pm.next()
            P.add("dve", lambda e, pm=pm, ps=ps, nt=nt: e.copy_predicated(out=pm[0:nt, 0:nt], mask=c.causT[0:nt, 0:nt], data=ps[0:nt, 0:nt]),
                  [psk, "const_c"], [pmk])
            po, pok = pso.next()
            P.add("pe", lambda e, po=po, pm=pm, ti=ti, nt=nt: e.matmul(po[:, 0:nt], lhsT=vA[0:nt, ti, :], rhs=pm[0:nt, 0:nt], start=True, stop=False),
                  ["vA", pmk], [pok])
            P.add("pe", lambda e, po=po, sb=sb, t0=t0, nt=nt: e.matmul(po[:, 0:nt], lhsT=sb[:, :], rhs=qt[:, t0:t0 + nt], start=False, stop=True),
                  [sbk, "qt"], [pok])
            P.add("act", lambda e, po=po, t0=t0, nt=nt: e.activation(out=osb[:, t0:t0 + nt], in_=po[:, 0:nt], func=AF.Copy), [pok], ["osb"])
            pd, pdk = psd.next()
            P.add("pe", lambda e, pd=pd, ti=ti, nt=nt: e.matmul(pd[:, 0:128], lhsT=khT[0:nt, ti, :], rhs=vA[0:nt, ti, :], start=True, stop=True),
                  ["khT", "vA"], [pdk])
            P.add("dve", lambda e, ti=ti: e.tensor_scalar(out=St[:, :], in0=Sf[:, :], scalar1=e1[:, ti:ti + 1], scalar2=None, op0=ALU.mult),
                  ["Sf", "e1"], ["St"])
            P.add("dve", lambda e, pd=pd, ti=ti: e.scalar_tensor_tensor(out=Sf[:, :], in0=pd[:, 0:128], scalar=e2[:, ti:ti + 1], in1=St[:, :],
                                                                        op0=ALU.mult, op1=ALU.add), [pdk, "St", "e2"], ["Sf"])
            sb, sbk = Sb.next()
            P.add("act", lambda e, sb=sb: e.activation(out=sb[:, :], in_=Sf[:, :], func=AF.Copy), ["Sf"], [sbk])
        P.dma("sp", qf[:, :], c.proj[OFF["a_g"] + h * 128:OFF["a_g"] + (h + 1) * 128, :], [], ["qf"], "qf")
        P.add("act", lambda e: e.activation(out=qf[:, :], in_=qf[:, :], func=AF.Silu), ["qf"], ["qf"])
        gcol = c.params[:, pb + PC["gnorm"] + h:pb + PC["gnorm"] + h + 1]
        for (t0, n) in split_cols(0, L, 512):
            q, qk = sq.next()
            P.add("act", lambda e, q=q, t0=t0, n=n: e.activation(out=q[:, 0:n], in_=osb[:, t0:t0 + n], func=AF.Square), ["osb"], [qk])
            P.add("pe", lambda e, q=q, n=n: e.matmul(psn[:, 0:n], lhsT=c.ones_bf[:, :], rhs=q[:, 0:n], start=True, stop=True),
                  [qk, "const"], ["psn"])
            r, rk = rs.next()
            rsqrt_op(c, r[:, 0:n], psn[:, 0:n], 1.0 / 128, ["psn"], rk)
            P.add("dve", lambda e, r=r, t0=t0, n=n, gcol=gcol: e.scalar_tensor_tensor(
                out=r[:, 0:n], in0=osb[:, t0:t0 + n], scalar=gcol, in1=r[:, 0:n], op0=ALU.mult, op1=ALU.mult),
                ["osb", rk, "params"], [rk])
            P.add("dve", lambda e, r=r, t0=t0, n=n: e.tensor_tensor(out=obr[:, t0:t0 + n], in0=r[:, 0:n], in1=qf[:, t0:t0 + n], op=ALU.mult),
                  [rk, "qf"], ["obr"])
        P.dma("sp", c.br[0][h * 128:(h + 1) * 128, :], obr[:, :], ["obr"], [], "obr")
    P.flush()
    es.close()


def dsa_phase(c, l):
    nc, P = c.nc, c.P
    es = ExitStack()
    L = c.L
    NT = len(c.tiles)
    pb = l * PL
    topk = c.topk
    rb = rows_bufs(c, es)
    st, _sq, _rs, _psn = rb
    qD = alloc(c, es, "qD", [128, 4, L], BF16)
    kD = alloc(c, es, "kD", [128, L], BF16)
    vD = alloc(c, es, "vDs", [128, NT, 129], BF16)
    kiD = alloc(c, es, "kiD", [128, L], BF16)
    sgn = alloc(c, es, "sgn", [128, NT, 16], F32)
    idx = alloc(c, es, "idx", [128, L], F32)
    work = alloc(c, es, "work", [128, L], F32)
    maskb = alloc(c, es, "maskb", [128, L], BF16)
    mT = alloc(c, es, "mT", [128, NT, 128], BF16)
    gs = alloc(c, es, "gsD", [128, 2], F32)
    zc = alloc(c, es, "zcolD", [128, 1], F32)
    m8 = alloc(c, es, "m8", [128, 8], F32)
    th = alloc(c, es, "th", [128, 1], F32)
    rls = Rot("rl", [alloc(c, es, "rl%d" % i, [128, 512], BF16) for i in range(3)])
    idxs = [idx, alloc(c, es, "idx2", [128, L], F32)]
    dgs = Rot("dg", [alloc(c, es, "dg%d" % i, [128, 16, 128], BF16) for i in range(2)])
    eTs = Rot("eT", [alloc(c, es, "eT%d" % i, [128, 512], BF16) for i in range(2)])
    pTs = Rot("pTD", [alloc(c, es, "pTD%d" % i, [128, 512], BF16) for i in range(2)])
    rr = Rot("rrD", [alloc(c, es, "rrD%d" % i, [128, 1], F32) for i in range(2)])
    ons = Rot("onD", [alloc(c, es, "onD%d" % i, [128, 128], BF16) for i in range(2)])
    ocs = Rot("ocD", [alloc(c, es, "ocD%d" % i, [128, 128], BF16) for i in range(3)])
    psi = Rot("psi", [palloc(c, es, "psi%d" % i, [128, 512]) for i in range(2)])
    pss = Rot("pssD", [palloc(c, es, "pssD%d" % i, [128, 512]) for i in range(2)])
    pso = Rot("psoD", [palloc(c, es, "psoD%d" % i, [128, 512]) for i in range(2)])
    pstb = palloc(c, es, "pstbD", [128, 128], BF16)
    P.add("dve", lambda e: e.tensor_scalar(out=gs[:, 0:1], in0=c.params[:, pb + PC["dsaq"]:pb + PC["dsaq"] + 1], scalar1=128 ** -0.5,
                                           scalar2=None, op0=ALU.mult), ["params"], ["gsc"])
    P.add("dve", lambda e: e.tensor_copy(out=gs[:, 1:2], in_=c.params[:, pb + PC["dsak"]:pb + PC["dsak"] + 1]), ["params"], ["gsc"])
    P.add("dve", lambda e: e.memset(zc[:, :], 0.0), [], ["gsc"])
    P.add("dve", lambda e: e.memset(vD[:, :, 128:129], 1.0), [], ["vD1"])
    for h in range(4):
        load_rows_norm(c, rb, OFF["d_q"] + h * 128, 128, lambda t0, n, h=h: qD[:, h, t0:t0 + n], gs[:, 0:1], c.ones_bf, 128, ("qD", h))
    load_rows_norm(c, rb, OFF["d_k"], 128, lambda t0, n: kD[:, t0:t0 + n], gs[:, 1:2], c.ones_bf, 128, "kD")
    load_rows_norm(c, rb, OFF["d_ki"], 64, lambda t0, n: kiD[:, t0:t0 + n], None, None, None, "kiD", dup64=True)
    load_vtm(c, vD, c.vD, 0, 128, "vD", lambda ti, nt: vD[0:nt, ti, 0:128])
    load_vtm(c, sgn, c.wT, 0, 16, "sgn", lambda ti, nt: sgn[0:nt, ti, :])
    P.add("act", lambda e: e.activation(out=sgn[:, :, :], in_=sgn[:, :, :], func=AF.Sign), ["sgn"], ["sgn"])
    qsts = Rot("qst", [alloc(c, es, "qst%d" % i, [128, 8, 128], F32) for i in range(2)])
    wsts = Rot("wst", [alloc(c, es, "wst%d" % i, [128, 8, 128], BF16) for i in range(2)])
    qits = Rot("qit", [alloc(c, es, "qit%d" % i, [128, 8, 128], BF16) for i in range(2)])
    mTs = [mT, alloc(c, es, "mT2", [128, NT, 128], BF16)]
    osb = [alloc(c, es, "osbD%d" % i, [128, 132], F32) for i in range(4)]
    qiv = c.proj[OFF["d_qi"]:OFF["d_qi"] + 1024, :].rearrange("(r p) t -> p r t", p=128)
    wav = c.wabs.rearrange("(r p) t -> p r t", p=128)

    def stage_a1(i):
        q0, nq = c.tiles[i]
        K = q0 + nq
        qs, qsk = qsts.next()
        ws, wsk = wsts.next()
        P.dma("sp", qs[:, :, 0:nq], qiv[:, :, q0:q0 + nq], [], [qsk], qsk)
        P.dma("sp", ws[:, :, 0:nq], wav[:, :, q0:q0 + nq], [], [wsk], wsk)
        qit, qitk = qits.next()
        P.add("pool", lambda e: e.tensor_tensor(out=qit[:, :, 0:nq], in0=qs[:, :, 0:nq], in1=ws[:, :, 0:nq], op=ALU.mult),
              [qsk, wsk], [qitk])
        idxc, idxk = idxs[i % 2], ("idx", i % 2)
        dg, dgk = dgs.next()
        for hi in range(16):
            P.add("pool", lambda e, hi=hi: e.tensor_scalar(out=dg[0:nq, hi, 0:nq], in0=c.ident_bf[0:nq, 0:nq], scalar1=sgn[0:nq, i, hi:hi + 1],
                                                           scalar2=None, op0=ALU.mult), ["const", "sgn"], [dgk])
        for (kb0, kn) in split_cols(0, K, 512):
            dots = {}

            def emit_dot(hi):
                r, half = hi // 2, hi % 2
                p, pk = psi.next()
                dots[hi] = (p, pk)
                P.add("pe", lambda e, p=p, r=r, half=half, kb0=kb0, kn=kn: e.matmul(
                    p[0:nq, 0:kn], lhsT=qit[half * 64:(half + 1) * 64, r, 0:nq], rhs=kiD[half * 64:(half + 1) * 64, kb0:kb0 + kn],
                    start=True, stop=True), [qitk, "kiD"], [pk])

            def emit_acc(hi):
                p, pk = dots[hi]
                rl, rlk = rls.next()
                P.add("act", lambda e, rl=rl, p=p, kn=kn: e.activation(out=rl[0:nq, 0:kn], in_=p[0:nq, 0:kn], func=AF.Relu), [pk], [rlk])
                P.add("pe", lambda e, rl=rl, hi=hi, kn=kn: e.matmul(_psn[0:nq, 0:kn], lhsT=dg[0:nq, hi, 0:nq], rhs=rl[0:nq, 0:kn],
                                                                   start=(hi == 0), stop=(hi == 15)), [rlk, dgk], ["psn"])

            emit_dot(0)
            for hi in range(16):
                if hi + 1 < 16:
                    emit_dot(hi + 1)
                emit_acc(hi)
            P.add("act", lambda e, kb0=kb0, kn=kn: e.activation(out=idxc[0:nq, kb0:kb0 + kn], in_=_psn[0:nq, 0:kn], func=AF.Copy),
                  ["psn"], [idxk])

    def stage_a2(i):
        q0, nq = c.tiles[i]
        K = q0 + nq
        mTc = mTs[i % 2]
        idxc, idxk = idxs[i % 2], ("idx", i % 2)
        P.add("dve", lambda e: e.tensor_tensor(out=idxc[0:nq, q0:q0 + nq], in0=idxc[0:nq, q0:q0 + nq], in1=c.fut[0:nq, 0:nq], op=ALU.add),
              [idxk, "const_f"], [idxk])
        if K > topk:
            for rd in range(topk // 8):
                src = idxc if rd == 0 else work
                P.add("dve", lambda e, src=src: e.max(out=m8[0:nq, :], in_=src[0:nq, 0:K]), [idxk, "work"], ["m8"])
                if rd < topk // 8 - 1:
                    P.add("dve", lambda e, src=src: e.match_replace(out=work[0:nq, 0:K], in_to_replace=m8[0:nq, :],
                                                                    in_values=src[0:nq, 0:K], imm_value=-1e30),
                          [idxk, "work", "m8"], ["work"])
            P.add("dve", lambda e: e.tensor_copy(out=th[0:nq, :], in_=m8[0:nq, 7:8]), ["m8"], ["th"])
        else:
            P.add("dve", lambda e: e.memset(th[0:nq, :], -1e29), ["m8"], ["th"])
        P.add("dve", lambda e: e.tensor_scalar(out=maskb[0:nq, 0:K], in0=idxc[0:nq, 0:K], scalar1=th[0:nq, 0:1], scalar2=None,
                                               op0=ALU.is_ge), [idxk, "th"], ["maskb"])
        for j in range(i + 1):
            k0, nk = c.tiles[j]
            P.add("pe", lambda e, k0=k0, nk=nk: e.transpose(out=pstb[0:nk, 0:nq], in_=maskb[0:nq, k0:k0 + nk], identity=c.ident_bf[0:nq, 0:nq]),
                  ["maskb", "const"], ["pstbD"])
            P.add("act", lambda e, j=j, nk=nk: e.activation(out=mTc[0:nk, j, 0:nq], in_=pstb[0:nk, 0:nq], func=AF.Copy), ["pstbD"], [("mT", i % 2, j)])

    def stage_b_main(i):
        q0, nq = c.tiles[i]
        mTc = mTs[i % 2]
        for h in range(4):
            po, pok = pso.next()
            for jg in range(0, i + 1, 4):
                grp = list(range(jg, min(i + 1, jg + 4)))
                ps, psk = pss.next()
                for sl, j in enumerate(grp):
                    k0, nk = c.tiles[j]
                    case = near_case(i, j)
                    P.add("pe", lambda e, ps=ps, h=h, k0=k0, nk=nk, case=case, sl=sl: e.matmul(
                        ps[0:nk, sl * 128:sl * 128 + nq], lhsT=kD[:, k0:k0 + nk], rhs=qD[:, h, q0:q0 + nq], start=True, stop=(case is None)),
                        ["kD", ("qD", h)], [psk])
                    if case is not None:
                        P.add("pe", lambda e, ps=ps, nk=nk, h=h, case=case, sl=sl: e.matmul(
                            ps[0:nk, sl * 128:sl * 128 + nq], lhsT=c.ident_bf[0:nk, 0:nk], rhs=c.BT[0:nk, (4 + h) * 3 + case, 0:nq],
                            start=False, stop=True), ["const"], [psk])
                W = (len(grp) - 1) * 128 + nq
                eT, eTk = eTs.next()
                P.add("act", lambda e, eT=eT, ps=ps, W=W: e.activation(out=eT[:, 0:W], in_=ps[:, 0:W], func=AF.Exp), [psk], [eTk])
                pT, pTk = pTs.next()
                if nq == 128:
                    ng_ = len(grp)
                    P.add("pool", lambda e, pT=pT, eT=eT, jg=jg, ng_=ng_: e.tensor_tensor(
                        out=pT[:, 0:ng_ * 128].rearrange("p (s q) -> p s q", q=128), in0=eT[:, 0:ng_ * 128].rearrange("p (s q) -> p s q", q=128),
                        in1=mTc[:, jg:jg + ng_, :], op=ALU.mult), [eTk] + [("mT", i % 2, j) for j in grp], [pTk])
                else:
                    P.add("pool", lambda e, pT=pT, eT=eT: e.tensor_tensor(
                        out=pT[0:16, 0:nq], in0=eT[0:16, 0:nq], in1=mTc[0:16, 0, 0:nq], op=ALU.mult), [eTk, ("mT", i % 2, 0)], [pTk])
                for sl, j in enumerate(grp):
                    k0, nk = c.tiles[j]
                    P.add("pe", lambda e, po=po, pT=pT, nk=nk, j=j, sl=sl: e.matmul(
                        po[0:nq, 0:129], lhsT=pT[0:nk, sl * 128:sl * 128 + nq], rhs=vD[0:nk, j, 0:129], start=(j == 0), stop=(j == i)),
                        [pTk, "vD", "vD1"], [pok])
            P.add("act", lambda e, po=po, h=h: e.activation(out=osb[h][0:nq, 0:129], in_=po[0:nq, 0:129], func=AF.Copy), [pok], [("osbD", h)])

    def stage_b_fin(i):
        q0, nq = c.tiles[i]
        for h in range(4):
            r, rk = rr.next()
            P.add("dve", lambda e, r=r, h=h: e.reciprocal(out=r[0:nq, 0:1], in_=osb[h][0:nq, 128:129]), [("osbD", h)], [rk])
            on, onk = ons.next()
            P.add("dve", lambda e, on=on, r=r, h=h: e.tensor_scalar(out=on[0:nq, :], in0=osb[h][0:nq, 0:128], scalar1=r[0:nq, 0:1],
                                                                     scalar2=None, op0=ALU.mult), [("osbD", h), rk], [onk])
            ps2, ps2k = pss.next()
            P.add("pe", lambda e, ps2=ps2, on=on: e.matmul(ps2[:, 0:nq], lhsT=on[0:nq, :], rhs=c.ident_bf[0:nq, 0:nq], start=True, stop=True),
                  [onk, "const"], [ps2k])
            oc, ock = ocs.next()
            P.add("act", lambda e, oc=oc, ps2=ps2: e.activation(out=oc[:, 0:nq], in_=ps2[:, 0:nq], func=AF.Copy), [ps2k], [ock])
            P.dma("sp", c.br[3][h * 128:(h + 1) * 128, q0:q0 + nq], oc[:, 0:nq], [ock], [], ock)

    stage_a1(0)
    if NT > 1:
        stage_a1(1)
    stage_a2(0)
    for i in range(NT):
        if i + 2 < NT:
            stage_a1(i + 2)
        stage_b_main(i)
        if i + 1 < NT:
            stage_a2(i + 1)
        stage_b_fin(i)
    P.flush()
    es.close()
```

```python
import numpy as np
from contextlib import ExitStack
import concourse.bass as bass
import concourse.mybir as mybir
from concourse.bass_utils import run_bass_kernel_spmd

F32 = mybir.dt.float32
BF16 = mybir.dt.bfloat16
I32 = mybir.dt.int32
ALU = mybir.AluOpType
AF = mybir.ActivationFunctionType
AX = mybir.AxisListType

ENGS = ("pe", "act", "dve", "pool", "sp")


class Prog:
    def __init__(self, nc):
        self.nc = nc
        self.es = ExitStack()
        self.eng_h = {"pe": nc.tensor, "act": nc.scalar, "dve": nc.vector,
                      "pool": nc.gpsimd, "sp": nc.sync}
        self.sem = {e: self.es.enter_context(nc.semaphore("s_" + e)) for e in ENGS}
        self.sig_cnt = {e: 0 for e in ENGS}
        self.waited = {e: {} for e in ENGS}
        self.dma_sems = {}
        self.semobj = {("eng", e): self.sem[e] for e in ENGS}
        self.ops = []
        self.lastw = {}
        self.readers = {}
        self.n_inst = 0
        self.phase_es = None

    def add(self, eng, fn, reads=(), writes=(), dma_key=None):
        idx = len(self.ops)
        is_dma = dma_key is not None
        deps = {}
        for r in reads:
            w = self.lastw.get(r)
            if w is not None:
                deps[w] = "raw"
        for wk in writes:
            w = self.lastw.get(wk)
            if w is not None and w not in deps:
                deps[w] = "waw"
            for r in self.readers.get(wk, {}).values():
                if r not in deps:
                    deps[r] = "war"
        op = dict(eng=eng, fn=fn, deps=deps, dma_key=dma_key, sig=False, dma_val=None)
        if is_dma:
            if dma_key not in self.dma_sems:
                s = self.es.enter_context(self.nc.semaphore("d_%d" % len(self.dma_sems)))
                self.dma_sems[dma_key] = [s, 0]
            self.dma_sems[dma_key][1] += 16
            op["dma_val"] = self.dma_sems[dma_key][1]
        self.ops.append(op)
        rk = ("dma", dma_key) if is_dma else eng
        for r in reads:
            self.readers.setdefault(r, {})[rk] = idx
        for wk in writes:
            self.lastw[wk] = idx
            self.readers[wk] = {}
        return idx

    def dma(self, q, out, in_, reads, writes, key):
        self.add(q, lambda e: e.dma_start(out=out, in_=in_), reads, writes, dma_key=key)

    def flush(self):
        ops = self.ops
        if not ops:
            return
        for i, op in enumerate(ops):
            waits = []
            for d, kind in op["deps"].items():
                od = ops[d]
                if od["dma_key"] is not None:
                    waits.append(("dma", od["dma_key"], od["dma_val"]))
                    continue
                if od["eng"] == op["eng"] and op["dma_key"] is None:
                    if op["eng"] == "pe" or kind != "raw":
                        continue
                od["sig"] = True
                waits.append(("eng", od["eng"], d))
            op["waits"] = waits
        last = {}
        for i, op in enumerate(ops):
            if op["dma_key"] is None:
                last[op["eng"]] = i
        for e, i in last.items():
            ops[i]["sig"] = True
        cnt = dict(self.sig_cnt)
        for op in ops:
            if op["dma_key"] is None and op["sig"]:
                cnt[op["eng"]] += 1
                op["sig_val"] = cnt[op["eng"]]
        end_cnt = cnt

        def emit_engine(ename):
            def body(eh):
                waited = self.waited[ename]
                for op in ops:
                    if op["eng"] != ename:
                        continue
                    for w in op["waits"]:
                        if w[0] == "dma":
                            semk = ("dma", w[1]); val = w[2]
                            s = self.dma_sems[w[1]][0]
                        else:
                            semk = ("eng", w[1]); val = ops[w[2]]["sig_val"]
                            s = self.sem[w[1]]
                        if waited.get(semk, 0) >= val:
                            continue
                        waited[semk] = val
                        eh.wait_ge(s, val)
                        self.n_inst += 1
                    ins = op["fn"](eh)
                    self.n_inst += 1
                    if op["dma_key"] is not None:
                        ins.then_inc(self.dma_sems[op["dma_key"]][0], 16)
                    elif op["sig"]:
                        ins.then_inc(self.sem[ename], 1)
                for e2 in ENGS:
                    if e2 == ename:
                        continue
                    v = end_cnt[e2]
                    if v > 0 and waited.get(("eng", e2), 0) < v:
                        waited[("eng", e2)] = v
                        eh.wait_ge(self.sem[e2], v)
                for k, (s, c) in self.dma_sems.items():
                    if c > 0 and waited.get(("dma", k), 0) < c:
                        waited[("dma", k)] = c
                        eh.wait_ge(s, c)
            return body

        with self.nc.Block() as block:
            block.tensor(emit_engine("pe"))
            block.scalar(emit_engine("act"))
            block.vector(emit_engine("dve"))
            block.gpsimd(emit_engine("pool"))
            block.sync(emit_engine("sp"))
        self.sig_cnt = end_cnt
        self.ops = []
        self.lastw = {}
        self.readers = {}

    def close(self):
        self.flush()
        self.es.close()
import math


D = 2048
NCH = 16
DFF = 5632
NJ = 44
EPS = 1e-6
OFF = dict(a_q=0, a_f=512, a_i=1024, a_g=1536, b_b=2048, b_c=2560, b_u=3072, c_q=3584, c_k=4096,
           c_v=4608, d_q=5120, d_k=5632, d_v=5760, d_qi=5888, d_ki=6912, d_w=6976, gate=6992)
NFM = 6976
NEG = -30000.0


class Rot:
    def __init__(self, name, tensors):
        self.name = name
        self.t = tensors
        self.i = 0

    def next(self):
        k = self.i % len(self.t)
        self.i += 1
        return self.t[k], (self.name, k)


def split_cols(t0, n, step):
    out = []
    o = 0
    while o < n:
        m = min(step, n - o)
        out.append((t0 + o, m))
        o += m
    return out


class Ctx:
    pass


def token_tiles(L):
    return [(0, 16)] + [(16 + 128 * i, 128) for i in range((L - 16) // 128)]


def make_groups(L, G):
    tl = token_tiles(L)
    nt = len(tl) - 1
    G = min(G, nt)
    sizes = [nt // G + (1 if k < nt % G else 0) for k in range(G)]
    groups = []
    i = 1
    first = True
    while i <= nt:
        j = min(nt, i + sizes[len(groups)] - 1)
        g0 = 0 if first else tl[i][0]
        g1 = tl[j][0] + tl[j][1]
        subs = []
        if first:
            subs.append((0, 16))
            subs += split_cols(16, g1 - 16, 512)
        else:
            subs += split_cols(g0, g1 - g0, 512)
        tiles = ([0] if first else []) + list(range(i, j + 1))
        groups.append((g0, g1 - g0, subs, tiles))
        first = False
        i = j + 1
    return groups


def fm(ap2d):
    return ap2d.rearrange("(c p) t -> p c t", p=128)


_UID = [0]


def alloc(c, es, name, shape, dt):
    _UID[0] += 1
    return es.enter_context(c.nc.sbuf_tensor("s%d_%s" % (_UID[0], name), list(shape), dt))


def palloc(c, es, name, shape, dt=None):
    _UID[0] += 1
    return es.enter_context(c.nc.psum_tensor("p%d_%s" % (_UID[0], name), list(shape), dt or F32))


def rsqrt_op(c, out, in_, scale, reads, wkey, np_=128):
    P = c.P
    P.add("act", lambda e: e.activation(out=out, in_=in_, func=AF.Sqrt, bias=c.epsc[0:np_, 0:1], scale=scale),
          list(reads) + ["const"], [wkey])
    P.add("dve", lambda e: e.reciprocal(out=out, in_=out), [wkey], [wkey])


def norm_stage(c, es_bufs, src, gcol, hbuf, hkey, g0, ng):
    for _ in norm_gen(c, es_bufs, src, gcol, hbuf, hkey, g0, ng):
        pass


def norm_gen(c, es_bufs, src, gcol, hbuf, hkey, g0, ng):
    P = c.P
    xts, sqs, rstd, psn = es_bufs
    srcv = fm(src)
    for (t0, n) in split_cols(g0, ng, 128):
        off = t0 - g0
        xt, xk = xts.next()
        P.dma("sp", xt[:, :, 0:n], srcv[:, :, t0:t0 + n], [], [xk], xk)
        for ch in range(NCH):
            sq, sk = sqs.next()
            P.add("act", lambda e, sq=sq, xt=xt, ch=ch, n=n: e.activation(out=sq[:, 0:n], in_=xt[:, ch, 0:n], func=AF.Square),
                  [xk], [sk])
            P.add("pe", lambda e, sq=sq, ch=ch, n=n: e.matmul(psn[:, 0:n], lhsT=c.ones_bf[:, :], rhs=sq[:, 0:n],
                                                              start=(ch == 0), stop=(ch == NCH - 1)),
                  [sk, "const"], ["psn"])
        rsqrt_op(c, rstd[:, 0:n], psn[:, 0:n], 1.0 / D, ["psn"], "rstd")
        for ch in range(NCH):
            P.add("dve", lambda e, xt=xt, ch=ch, n=n, off=off: e.scalar_tensor_tensor(
                out=hbuf[:, ch, off:off + n], in0=xt[:, ch, 0:n], scalar=gcol[:, ch:ch + 1], in1=rstd[:, 0:n],
                op0=ALU.mult, op1=ALU.mult), [xk, "rstd", "params"], [hkey])
        yield


def norm_bufs(c, es):
    xts = Rot("xt", [alloc(c, es, "xt%d" % i, [128, NCH, 128], F32) for i in range(2)])
    sqs = Rot("sq", [alloc(c, es, "sq%d" % i, [128, 256], BF16) for i in range(3)])
    rstd = alloc(c, es, "rstd", [128, 256], F32)
    psn = palloc(c, es, "psn", [128, 512])
    return (xts, sqs, rstd, psn)


def ffn_phase(c, src, dst, gcol, w_gu, w_down):
    nc, P = c.nc, c.P
    es = ExitStack()
    groups = make_groups(c.L, c.G_ffn)
    NG = max(g[1] for g in groups)
    nb = norm_bufs(c, es)
    hbuf = alloc(c, es, "hbuf", [128, NCH, NG], BF16)
    act = alloc(c, es, "actb", [128, NJ, NG], BF16)
    wgu = Rot("wgu", [alloc(c, es, "wgu%d" % i, [128, NCH, 256], BF16) for i in range(2)])
    wd = Rot("wd", [alloc(c, es, "wd%d" % i, [128, NJ, 128], BF16) for i in range(2)])
    sgs = Rot("sg", [alloc(c, es, "sg%d" % i, [128, 512], F32) for i in range(2)])
    xrs = Rot("xr", [alloc(c, es, "xr%d" % i, [128, 512], F32) for i in range(2)])
    xos = Rot("xo", [alloc(c, es, "xo%d" % i, [128, 512], F32) for i in range(2)])
    psg = Rot("psg", [palloc(c, es, "psg%d" % i, [128, 512]) for i in range(2)])
    psu = Rot("psu", [palloc(c, es, "psu%d" % i, [128, 512]) for i in range(2)])
    pso = Rot("pso", [palloc(c, es, "pso%d" % i, [128, 512]) for i in range(2)])
    srcv, dstv = fm(src), fm(dst)
    for gi_, (g0, ng, subs, _tiles) in enumerate(groups):
        if gi_ == 0:
            norm_stage(c, nb, src, gcol, hbuf, "hbuf", g0, ng)
        if gi_ + 1 < len(groups):
            ngen = norm_gen(c, nb, src, gcol, hbuf, "hbuf", groups[gi_ + 1][0], groups[gi_ + 1][1])
        else:
            ngen = iter(())
        for j in range(NJ):
            w, wk = wgu.next()
            P.dma("pool", w[:, :, 0:128], w_gu[j], [], [wk], wk)
            P.dma("pool", w[:, :, 128:256], w_gu[NJ + j], [], [wk], wk)
            for (t0, n) in subs:
                off = t0 - g0
                pg, pgk = psg.next()
                pu, puk = psu.next()
                for ch in range(NCH):
                    P.add("pe", lambda e, pg=pg, w=w, ch=ch, off=off, n=n: e.matmul(
                        pg[:, 0:n], lhsT=w[:, ch, 0:128], rhs=hbuf[:, ch, off:off + n], start=(ch == 0), stop=(ch == NCH - 1)),
                        [wk, "hbuf"], [pgk])
                for ch in range(NCH):
                    P.add("pe", lambda e, pu=pu, w=w, ch=ch, off=off, n=n: e.matmul(
                        pu[:, 0:n], lhsT=w[:, ch, 128:256], rhs=hbuf[:, ch, off:off + n], start=(ch == 0), stop=(ch == NCH - 1)),
                        [wk, "hbuf"], [puk])
                sg, sgk = sgs.next()
                P.add("act", lambda e, sg=sg, pg=pg, n=n: e.activation(out=sg[:, 0:n], in_=pg[:, 0:n], func=AF.Silu),
                      [pgk], [sgk])
                P.add("dve", lambda e, sg=sg, pu=pu, j=j, off=off, n=n: e.tensor_tensor(
                    out=act[:, j, off:off + n], in0=sg[:, 0:n], in1=pu[:, 0:n], op=ALU.mult), [sgk, puk], [("act", j)])
        for m in range(NCH):
            w, wk = wd.next()
            P.dma("pool", w[:, :, :], w_down[m], [], [wk], wk)
            for (t0, n) in subs:
                off = t0 - g0
                xr, xrk = xrs.next()
                P.dma("sp", xr[:, 0:n], srcv[:, m, t0:t0 + n], [], [xrk], xrk)
                po, pok = pso.next()
                for j in range(NJ):
                    P.add("pe", lambda e, po=po, w=w, j=j, off=off, n=n: e.matmul(
                        po[:, 0:n], lhsT=w[:, j, :], rhs=act[:, j, off:off + n], start=(j == 0), stop=(j == NJ - 1)),
                        [wk, ("act", j)], [pok])
                xo, xok = xos.next()
                P.add("dve", lambda e, xo=xo, po=po, xr=xr, n=n: e.scalar_tensor_tensor(
                    out=xo[:, 0:n], in0=po[:, 0:n], scalar=0.5, in1=xr[:, 0:n], op0=ALU.mult, op1=ALU.add),
                    [pok, xrk], [xok])
                P.dma("act", dstv[:, m, t0:t0 + n], xo[:, 0:n], [xok], [], xok)
            if m >= 2:
                next(ngen, None)
        for _ in ngen:
            pass
    P.flush()
    es.close()


PL = 80
PC = dict(ffn1=0, mix=16, ffn2=32, gnorm=48, conv=52, dqn=64, dkn=65, subln=66, dsaq=67, dsak=68, lb=69, lam=73)


def pack_params(inp):
    p = np.zeros((128, 2 * PL), np.float32)
    for l in range(2):
        b = l * PL
        p[:, b + PC["ffn1"]:b + PC["ffn1"] + 16] = inp["ffn1_norm"][l].reshape(16, 128).T
        p[:, b + PC["mix"]:b + PC["mix"] + 16] = inp["mix_norm"][l].reshape(16, 128).T
        p[:, b + PC["ffn2"]:b + PC["ffn2"] + 16] = inp["ffn2_norm"][l].reshape(16, 128).T
        p[:, b + PC["gnorm"]:b + PC["gnorm"] + 4] = inp["hgrn_gnorm"][l].reshape(4, 128).T
        for j in range(3):
            p[:, b + PC["conv"] + j * 4:b + PC["conv"] + j * 4 + 4] = inp["conv_w"][l, j].reshape(4, 128).T
        p[:, b + PC["dqn"]] = np.tile(inp["diff_q_norm"][l], 2)
        p[:, b + PC["dkn"]] = np.tile(inp["diff_k_norm"][l], 2)
        p[:, b + PC["subln"]] = inp["diff_subln"][l]
        p[:, b + PC["dsaq"]] = inp["dsa_q_norm"][l]
        p[:, b + PC["dsak"]] = inp["dsa_k_norm"][l]
        p[:, b + PC["lb"]:b + PC["lb"] + 4] = inp["hgrn_lb"][l].reshape(4, 128).T
        p[0:64, b + PC["lam"]:b + PC["lam"] + 4] = inp["diff_lambda"][l].T
    return p


def rel_bucket_np(n):
    n = np.maximum(n, 0)
    nf = np.maximum(n, 1).astype(np.float32)
    large = 16 + (np.log(nf / np.float32(16)) / np.float32(math.log(128 / 16)) * np.float32(16)).astype(np.int32)
    large = np.minimum(large, 31)
    return np.where(n < 16, n, large)


def make_consts():
    cst = {}
    cst["ident"] = np.eye(128, dtype=np.float32)
    sel = np.zeros((33, 3, 256), np.float32)
    for ci, delta in enumerate((0, 128, 16)):
        m = np.arange(255)
        n = m - 127 + delta
        bk = rel_bucket_np(n)
        for mm in range(255):
            if n[mm] < 0:
                sel[32, ci, mm] = 1.0
            else:
                sel[bk[mm], ci, mm] = 1.0
    cst["sel"] = sel
    e31 = np.zeros((32, 128), np.float32)
    e31[31, :] = 1.0
    cst["e31"] = e31
    o64 = np.zeros((128, 128), np.float32)
    o64[:64, :64] = 1.0
    o64[64:, 64:] = 1.0
    cst["ones64"] = o64
    q = np.arange(128)[:, None]
    k = np.arange(128)[None, :]
    cst["fut"] = np.where(k > q, -1e30, 0.0).astype(np.float32)
    cst["causT"] = (q <= k).astype(np.int32)
    return cst


WNAMES = ("ffn1_w_gu", "ffn1_w_down", "w_in", "w_branch", "w_out", "ffn2_w_gu", "ffn2_w_down")


def build(S, topk, G=4, phases=None, dbg=False):
    L = S + 16
    nc = bass.Bass("TRN2", target_bir_lowering=False)
    c = Ctx()
    c.nc = nc
    c.L, c.S, c.topk = L, S, topk
    c.tiles = token_tiles(L)
    c.G_ffn, c.G_proj, c.G_merge = (4, 2, 4) if S >= 2048 else (2, 2, 2)
    c.dbg = dbg

    def din(name, shape, dt=F32):
        return nc.dram_tensor(name, list(shape), dt, kind="ExternalInput").ap()

    def dscr(name, shape, dt=F32):
        kind = "ExternalOutput" if dbg else "Internal"
        return nc.dram_tensor(name, list(shape), dt, kind=kind).ap()

    c.xin = din("xin", [D, L])
    c.params_d = din("params", [128, 2 * PL])
    c.relb_d = din("relb", [32, 8])
    c.ident_d = din("ident", [128, 128])
    c.sel_d = din("sel", [33, 3, 256])
    c.e31_d = din("e31", [32, 128])
    c.ones64_d = din("ones64", [128, 128])
    c.fut_d = din("fut", [128, 128])
    c.causT_d = din("causT", [128, 128], I32)
    c.w = {}
    c.w["ffn1_w_gu"] = din("ffn1_w_gu", [2, 2 * NJ, 128, NCH, 128])
    c.w["ffn1_w_down"] = din("ffn1_w_down", [2, NCH, 128, NJ, 128])
    c.w["ffn2_w_gu"] = din("ffn2_w_gu", [2, 2 * NJ, 128, NCH, 128])
    c.w["ffn2_w_down"] = din("ffn2_w_down", [2, NCH, 128, NJ, 128])
    c.w["w_in"] = din("w_in", [2, D, 15184])
    c.w["w_dw_rep"] = din("w_dw_rep", [2, D, 1024])
    c.w["w_branch"] = din("w_branch", [2, 4, 512, D])
    c.w["w_out"] = din("w_out", [2, D, D])
    c.yout = nc.dram_tensor("yout", [D, L], F32, kind="ExternalOutput").ap()
    c.xres = dscr("xres", [D, L])
    c.proj = dscr("proj", [NFM, L])
    c.wabs = dscr("wabs", [1024, L], BF16)
    c.vA = dscr("vA", [L, 512], BF16)
    c.vC = dscr("vC", [L, 512], BF16)
    c.vD = dscr("vD", [L, 128], BF16)
    c.wT = dscr("wT", [L, 16])
    c.br = [dscr("br%d" % i, [512, L], BF16) for i in range(4)]
    c.gsc = dscr("gsc", [24, 128, 255])

    es = ExitStack()
    P = Prog(nc)
    c.P = P
    c.ident_f = alloc(c, es, "ident_f", [128, 128], F32)
    c.ident_bf = alloc(c, es, "ident_bf", [128, 128], BF16)
    c.ones_bf = alloc(c, es, "ones_bf", [128, 128], BF16)
    c.ones_f = alloc(c, es, "ones_f", [128, 128], F32)
    c.epsc = alloc(c, es, "epsc", [128, 1], F32)
    c.ones64_bf = alloc(c, es, "ones64_bf", [128, 128], BF16)
    c.params = alloc(c, es, "params_sb", [128, 2 * PL], F32)
    c.BT = alloc(c, es, "BT", [128, 24, 128], BF16)
    c.cbias = alloc(c, es, "cbias", [128, 8], F32)
    c.fut = alloc(c, es, "fut", [128, 128], F32)
    c.causT = alloc(c, es, "causT", [128, 128], I32)
    c.lbs = alloc(c, es, "lbs", [128, 2, 4], F32)
    c.oml = alloc(c, es, "oml", [128, 2, 4], F32)
    c.nlam = alloc(c, es, "nlam", [128, 2], F32)
    setup_phase(c)

    ph = phases
    for l in range(2):
        pb = l * PL
        first = (l == 0)
        if ph is None or ("ffn1_%d" % l) in ph:
            ffn_phase(c, c.xin if first else c.xres, c.xres, c.params[:, pb + PC["ffn1"]:pb + PC["ffn1"] + 16],
                      c.w["ffn1_w_gu"][l], c.w["ffn1_w_down"][l])
        if ph is None or ("proj_%d" % l) in ph:
            proj_phase(c, l)
        if ph is None or ("conv_%d" % l) in ph:
            conv_phase(c, l)
        if ph is None or ("hgrn_%d" % l) in ph:
            hgrn_phase(c, l)
        if ph is None or ("diff_%d" % l) in ph:
            diff_phase(c, l)
        if ph is None or ("dsa_%d" % l) in ph:
            dsa_phase(c, l)
        if ph is None or ("merge_%d" % l) in ph:
            merge_phase(c, l)
        if ph is None or ("ffn2_%d" % l) in ph:
            ffn_phase(c, c.xres, c.yout if l == 1 else c.xres, c.params[:, pb + PC["ffn2"]:pb + PC["ffn2"] + 16],
                      c.w["ffn2_w_gu"][l], c.w["ffn2_w_down"][l])
    P.close()
    es.close()
    return nc, c


def setup_phase(c):
    nc, P = c.nc, c.P
    es = ExitStack()
    P.dma("sp", c.ident_f[:], c.ident_d, [], ["const_i"], "ident_f")
    P.dma("sp", c.params[:], c.params_d, [], ["params"], "params")
    P.dma("sp", c.fut[:], c.fut_d, [], ["const_f"], "fut")
    P.dma("sp", c.causT[:], c.causT_d, [], ["const_c"], "causT")
    o64 = alloc(c, es, "o64f", [128, 128], F32)
    P.dma("sp", o64[:], c.ones64_d, [], ["o64f"], "o64f")
    P.add("dve", lambda e: e.tensor_copy(out=c.ident_bf[:], in_=c.ident_f[:]), ["const_i"], ["const"])
    P.add("dve", lambda e: e.tensor_copy(out=c.ones64_bf[:], in_=o64[:]), ["o64f"], ["const"])
    P.add("dve", lambda e: e.memset(c.ones_bf[:], 1.0), [], ["const"])
    P.add("dve", lambda e: e.memset(c.ones_f[:], 1.0), [], ["const"])
    P.add("dve", lambda e: e.memset(c.epsc[:], EPS), [], ["const"])
    tab = alloc(c, es, "tab", [32, 8], F32)
    sel = alloc(c, es, "selsb", [33, 3, 256], F32)
    e31 = alloc(c, es, "e31sb", [32, 128], F32)
    P.dma("sp", tab[:], c.relb_d, [], ["tab"], "tab")
    P.dma("sp", sel[:], c.sel_d, [], ["sel"], "sel")
    P.dma("sp", e31[:], c.e31_d, [], ["e31"], "e31")
    tabB = Rot("tabB", [alloc(c, es, "tabB%d" % i, [33, 128], F32) for i in range(2)])
    gsb = Rot("gsb", [alloc(c, es, "gsb%d" % i, [128, 255], F32) for i in range(2)])
    btf = Rot("btf", [alloc(c, es, "btf%d" % i, [128, 128], F32) for i in range(2)])
    psG = Rot("psG", [palloc(c, es, "psG%d" % i, [128, 256]) for i in range(2)])
    psc = palloc(c, es, "psc", [128, 8])
    P.add("pe", lambda e: e.matmul(psc[:, :], lhsT=e31[:, :], rhs=tab[:, :], start=True, stop=True), ["e31", "tab"], ["psc"])
    P.add("dve", lambda e: e.tensor_copy(out=c.cbias[:], in_=psc[:]), ["psc"], ["const"])
    for hh in range(8):
        tb, tbk = tabB.next()
        P.add("dve", lambda e, tb=tb: e.memset(tb[:, :], NEG), [], [tbk])
        P.add("dve", lambda e, tb=tb, hh=hh: e.tensor_scalar(out=tb[0:32, :], in0=c.ones_f[0:32, :], scalar1=tab[0:32, hh:hh + 1],
                                                             scalar2=None, op0=ALU.mult), ["tab", "const", tbk], [tbk])
        for ci in range(3):
            pg, pgk = psG.next()
            P.add("pe", lambda e, pg=pg, tb=tb, ci=ci: e.matmul(pg[:, 0:256], lhsT=tb[:, :], rhs=sel[:, ci, :], start=True, stop=True),
                  [tbk, "sel"], [pgk])
            gs, gsk = gsb.next()
            P.add("act", lambda e, gs=gs, pg=pg: e.activation(out=gs[:, :], in_=pg[:, 0:255], func=AF.Copy), [pgk], [gsk])
            idx = hh * 3 + ci
            P.dma("sp", c.gsc[idx], gs[:, :], [gsk], [("gsc", idx)], gsk)
            bt, btk = btf.next()
            skew = bass.AP(tensor=c.gsc.tensor, offset=idx * 128 * 255 + 127, ap=[[254, 128], [1, 128]])
            P.dma("sp", bt[:, :], skew, [("gsc", idx)], [btk], btk)
            P.add("dve", lambda e, bt=bt, idx=idx, hh=hh: e.tensor_scalar(out=c.BT[:, idx, :], in0=bt[:, :], scalar1=c.cbias[:, hh:hh + 1],
                                                                          scalar2=None, op0=ALU.subtract), [btk, "const"], ["const"])
    P.add("dve", lambda e: e.memset(c.lbs[:, 0, :], 0.0), [], ["const"])
    P.add("dve", lambda e: e.memset(c.oml[:, 0, :], 1.0), [], ["const"])
    dl = alloc(c, es, "dl", [128, 4], F32)
    P.add("dve", lambda e: e.tensor_tensor(out=dl[:, :], in0=c.params[:, PL + PC["lb"]:PL + PC["lb"] + 4],
                                           in1=c.params[:, PC["lb"]:PC["lb"] + 4], op=ALU.subtract), ["params"], ["dl"])
    P.add("act", lambda e: e.activation(out=c.lbs[:, 1, :], in_=dl[:, :], func=AF.Sigmoid), ["dl"], ["const"])
    P.add("dve", lambda e: e.tensor_scalar(out=c.oml[:, 1, :], in0=c.lbs[:, 1, :], scalar1=-1.0, scalar2=1.0,
                                           op0=ALU.mult, op1=ALU.add), ["const"], ["const"])
    pr = alloc(c, es, "pr", [128, 4], F32)
    psl = palloc(c, es, "psl", [128, 4])
    el = alloc(c, es, "el", [128, 4], F32)
    for l in range(2):
        b = l * PL + PC["lam"]
        P.add("dve", lambda e, l=l, b=b: e.tensor_tensor(out=pr[:, 2 * l:2 * l + 1], in0=c.params[:, b:b + 1], in1=c.params[:, b + 1:b + 2],
                                                         op=ALU.mult), ["params"], ["pr"])
        P.add("dve", lambda e, l=l, b=b: e.tensor_tensor(out=pr[:, 2 * l + 1:2 * l + 2], in0=c.params[:, b + 2:b + 3], in1=c.params[:, b + 3:b + 4],
                                                         op=ALU.mult), ["params"], ["pr"])
    P.add("pe", lambda e: e.matmul(psl[:, :], lhsT=c.ones_f[:, :], rhs=pr[:, :], start=True, stop=True), ["pr", "const"], ["psl"])
    P.add("act", lambda e: e.activation(out=el[:, :], in_=psl[:, :], func=AF.Exp), ["psl"], ["el"])
    for l in range(2):
        lam_init = 0.8 - 0.6 * math.exp(-0.3 * l)
        P.add("dve", lambda e, l=l, li=lam_init: e.scalar_tensor_tensor(
            out=c.nlam[:, l:l + 1], in0=el[:, 2 * l + 1:2 * l + 2], scalar=-li, in1=el[:, 2 * l:2 * l + 1],
            op0=ALU.add, op1=ALU.subtract), ["el"], ["const"])
    P.flush()
    es.close()


_CACHE = {}


def layout_weights(inp):
    wl = {}
    for k in WNAMES:
        a = np.asarray(inp[k], dtype=np.float32)
        if k.endswith("w_gu"):
            a = a.reshape(2, NCH, 128, 2 * NJ, 128).transpose(0, 3, 2, 1, 4)
        elif k.endswith("w_down"):
            a = a.reshape(2, NJ, 128, NCH, 128).transpose(0, 3, 2, 1, 4)
        wl[k] = np.ascontiguousarray(a)
    return wl


def host_inputs(inp, b, consts, params, w_dw_rep, wl):
    x = inp["x"]
    xin = np.ascontiguousarray(np.concatenate([inp["meta_tokens"].T, x[b].T], axis=1), dtype=np.float32)
    m = {"xin": xin, "params": params, "relb": np.ascontiguousarray(inp["rel_bias"], dtype=np.float32),
         "ident": consts["ident"], "sel": consts["sel"], "e31": consts["e31"], "ones64": consts["ones64"],
         "fut": consts["fut"], "causT": consts["causT"], "w_dw_rep": w_dw_rep}
    for k in WNAMES:
        m[k] = wl[k]
    return m


def run(inp, phases=None, dbg=False, G=4, trace=False):
    inp = {k: np.asarray(v) for k, v in inp.items()}
    B, S, _ = inp["x"].shape
    topk = min(256, S // 4)
    nc, c = build(S, topk, G=G, phases=phases, dbg=dbg)
    consts = make_consts()
    params = pack_params(inp)
    w16 = inp["w_in"][:, :, OFF["d_w"]:OFF["d_w"] + 16]
    w_dw_rep = np.ascontiguousarray(np.repeat(w16, 64, axis=2), dtype=np.float32)
    wl = layout_weights(inp)
    in_maps = [host_inputs(inp, b, consts, params, w_dw_rep, wl) for b in range(B)]
    res = run_bass_kernel_spmd(nc, in_maps, core_ids=list(range(B)), trace=trace)
    return res, c


def kernel(**inputs):
    res, c = run(inputs)
    outs = [r["yout"] for r in res.results]
    y = np.stack([np.ascontiguousarray(o[:, 16:].T) for o in outs], axis=0)
    return y.astype(np.float32)


def fm_chunks():
    skip = [(OFF["a_i"], OFF["a_g"]), (OFF["c_v"], OFF["d_q"]), (OFF["d_v"], OFF["d_qi"])]
    out = []
    c0 = 0
    while c0 < NFM:
        n = min(128, NFM - c0)
        if not any(a <= c0 < b for a, b in skip):
            out.append((c0, n))
        c0 += n
    return out


def proj_phase(c, l):
    nc, P = c.nc, c.P
    es = ExitStack()
    groups = make_groups(c.L, c.G_proj)
    NG = max(g[1] for g in groups)
    pb = l * PL
    nb = norm_bufs(c, es)
    hbuf = alloc(c, es, "hbuf", [128, NCH, NG], BF16)
    wfm = Rot("wfm", [alloc(c, es, "wfm%d" % i, [128, NCH, 128], BF16) for i in range(3)])
    wtm = Rot("wtm", [alloc(c, es, "wtm%d" % i, [128, NCH, 512], BF16) for i in range(2)])
    evs = Rot("ev", [alloc(c, es, "ev%d" % i, [128, 512], F32) for i in range(2)])
    evb = Rot("evb", [alloc(c, es, "evb%d" % i, [128, 512], BF16) for i in range(2)])
    ps = Rot("ps", [palloc(c, es, "ps%d" % i, [128, 512]) for i in range(4)])
    win = fm(c.w["w_in"][l])
    wrep = fm(c.w["w_dw_rep"][l])
    gcol = c.params[:, pb + PC["mix"]:pb + PC["mix"] + 16]
    hbufs = [hbuf, alloc(c, es, "hbuf2", [128, NCH, NG], BF16)]
    tmc_ = [0]
    for gi_, (g0, ng, subs, tiles) in enumerate(groups):
        hbuf = hbufs[gi_ % 2]
        hk = ("hbuf", gi_ % 2)
        if gi_ == 0:
            norm_stage(c, nb, c.xres, gcol, hbuf, hk, g0, ng)
        if gi_ + 1 < len(groups):
            ngen = norm_gen(c, nb, c.xres, gcol, hbufs[(gi_ + 1) % 2], ("hbuf", (gi_ + 1) % 2), groups[gi_ + 1][0], groups[gi_ + 1][1])
        else:
            ngen = iter(())
        for (c0, mc) in fm_chunks():
            w, wk = wfm.next()
            P.dma("pool", w[:, :, 0:mc], win[:, :, c0:c0 + mc], [], [wk], wk)
            for (t0, n) in subs:
                off = t0 - g0
                p, pk = ps.next()
                for ch in range(NCH):
                    P.add("pe", lambda e, p=p, w=w, ch=ch, off=off, n=n, mc=mc, hbuf=hbuf: e.matmul(
                        p[0:mc, 0:n], lhsT=w[:, ch, 0:mc], rhs=hbuf[:, ch, off:off + n], start=(ch == 0), stop=(ch == NCH - 1)),
                        [wk, hk], [pk])
                ev, ek = evs.next()
                P.add("act", lambda e, ev=ev, p=p, n=n, mc=mc: e.activation(out=ev[0:mc, 0:n], in_=p[0:mc, 0:n], func=AF.Copy),
                      [pk], [ek])
                P.dma("sp", c.proj[c0:c0 + mc, t0:t0 + n], ev[0:mc, 0:n], [ek], [], ek)
        for r in range(8):
            w, wk = wfm.next()
            P.dma("pool", w[:, :, :], wrep[:, :, r * 128:(r + 1) * 128], [], [wk], wk)
            for (t0, n) in subs:
                off = t0 - g0
                p, pk = ps.next()
                for ch in range(NCH):
                    P.add("pe", lambda e, p=p, w=w, ch=ch, off=off, n=n, hbuf=hbuf: e.matmul(
                        p[:, 0:n], lhsT=w[:, ch, :], rhs=hbuf[:, ch, off:off + n], start=(ch == 0), stop=(ch == NCH - 1)),
                        [wk, hk], [pk])
                ev, ek = evb.next()
                P.add("act", lambda e, ev=ev, p=p, n=n: e.activation(out=ev[:, 0:n], in_=p[:, 0:n], func=AF.Abs), [pk], [ek])
                P.dma("sp", c.wabs[r * 128:(r + 1) * 128, t0:t0 + n], ev[:, 0:n], [ek], [], ek)
        for (c0, ncol, dst, isf32) in ((OFF["a_i"], 512, c.vA, False), (OFF["c_v"], 512, c.vC, False),
                                       (OFF["d_v"], 128, c.vD, False), (OFF["d_w"], 16, c.wT, True)):
            w, wk = wtm.next()
            P.dma("pool", w[:, :, 0:ncol], win[:, :, c0:c0 + ncol], [], [wk], wk)
            for ti in tiles:
                t0, nt = c.tiles[ti]
                off = t0 - g0
                p, pk = ps.next()
                for ch in range(NCH):
                    P.add("pe", lambda e, p=p, w=w, ch=ch, off=off, nt=nt, ncol=ncol, hbuf=hbuf: e.matmul(
                        p[0:nt, 0:ncol], lhsT=hbuf[:, ch, off:off + nt], rhs=w[:, ch, 0:ncol], start=(ch == 0), stop=(ch == NCH - 1)),
                        [wk, hk], [pk])
                if isf32:
                    ev, ek = evs.next()
                else:
                    ev, ek = evb.next()
                P.add("act", lambda e, ev=ev, p=p, nt=nt, ncol=ncol: e.activation(out=ev[0:nt, 0:ncol], in_=p[0:nt, 0:ncol], func=AF.Copy),
                      [pk], [ek])
                P.dma("sp", dst[t0:t0 + nt, 0:ncol], ev[0:nt, 0:ncol], [ek], [], ek)
                tmc_[0] += 1
                if tmc_[0] % 3 == 0:
                    next(ngen, None)
        for _ in ngen:
            pass
    P.flush()
    es.close()


def conv_phase(c, l):
    nc, P = c.nc, c.P
    es = ExitStack()
    L = c.L
    pb = l * PL + PC["conv"]
    bb = Rot("bb", [alloc(c, es, "bb%d" % i, [128, L], F32) for i in range(2)])
    bc = Rot("bc", [alloc(c, es, "bc%d" % i, [128, L], F32) for i in range(2)])
    bu = Rot("bu", [alloc(c, es, "bu%d" % i, [128, L], F32) for i in range(2)])
    zc = alloc(c, es, "zc", [128, L + 2], F32)
    yb = alloc(c, es, "yb", [128, L], F32)
    ob = Rot("ob", [alloc(c, es, "ob%d" % i, [128, L], BF16) for i in range(2)])
    P.add("dve", lambda e: e.memset(zc[:, 0:2], 0.0), [], ["zc0"])
    for ch in range(4):
        tb, tbk = bb.next()
        tc_, tck = bc.next()
        tu, tuk = bu.next()
        P.dma("sp", tb[:, :], c.proj[OFF["b_b"] + ch * 128:OFF["b_b"] + (ch + 1) * 128, :], [], [tbk], tbk)
        P.dma("sp", tc_[:, :], c.proj[OFF["b_c"] + ch * 128:OFF["b_c"] + (ch + 1) * 128, :], [], [tck], tck)
        P.dma("sp", tu[:, :], c.proj[OFF["b_u"] + ch * 128:OFF["b_u"] + (ch + 1) * 128, :], [], [tuk], tuk)
        P.add("pool", lambda e, tc_=tc_, tu=tu: e.tensor_tensor(out=zc[:, 2:L + 2], in0=tc_[:, :], in1=tu[:, :], op=ALU.mult),
              [tck, tuk], ["zc"])
        w0 = c.params[:, pb + ch:pb + ch + 1]
        w1 = c.params[:, pb + 4 + ch:pb + 4 + ch + 1]
        w2 = c.params[:, pb + 8 + ch:pb + 8 + ch + 1]
        P.add("dve", lambda e, w0=w0: e.tensor_scalar(out=yb[:, :], in0=zc[:, 2:L + 2], scalar1=w0, scalar2=None, op0=ALU.mult),
              ["zc", "zc0", "params"], ["yb"])
        P.add("dve", lambda e, w1=w1: e.scalar_tensor_tensor(out=yb[:, :], in0=zc[:, 1:L + 1], scalar=w1, in1=yb[:, :],
                                                             op0=ALU.mult, op1=ALU.add), ["zc", "zc0", "yb", "params"], ["yb"])
        P.add("dve", lambda e, w2=w2: e.scalar_tensor_tensor(out=yb[:, :], in0=zc[:, 0:L], scalar=w2, in1=yb[:, :],
                                                             op0=ALU.mult, op1=ALU.add), ["zc", "zc0", "yb", "params"], ["yb"])
        o, ok = ob.next()
        P.add("dve", lambda e, o=o, tb=tb: e.tensor_tensor(out=o[:, :], in0=yb[:, :], in1=tb[:, :], op=ALU.mult), ["yb", tbk], [ok])
        P.dma("sp", c.br[1][ch * 128:(ch + 1) * 128, :], o[:, :], [ok], [], ok)
    P.flush()
    es.close()


def merge_phase(c, l):
    nc, P = c.nc, c.P
    es = ExitStack()
    groups = make_groups(c.L, c.G_merge)
    NG = max(g[1] for g in groups)
    pb = l * PL
    nb = norm_bufs(c, es)
    hbuf = alloc(c, es, "hbuf", [128, NCH, NG], BF16)
    brb = alloc(c, es, "brb", [128, 16, NG], BF16)
    mrg = alloc(c, es, "mrg", [128, NCH, NG], BF16)
    wg = Rot("wg", [alloc(c, es, "wg%d" % i, [128, NCH, 128], BF16) for i in range(3)])
    wb = Rot("wb", [alloc(c, es, "wb%d" % i, [128, 4, 128], BF16) for i in range(3)])
    wo = Rot("wo", [alloc(c, es, "wo%d" % i, [128, NCH, 128], BF16) for i in range(2)])
    sgs = Rot("sg", [alloc(c, es, "sg%d" % i, [128, 512], F32) for i in range(2)])
    macc = Rot("macc", [alloc(c, es, "macc%d" % i, [128, 512], F32) for i in range(3)])
    tmps = Rot("tmp", [alloc(c, es, "tmp%d" % i, [128, 512], F32) for i in range(2)])
    xrs = Rot("xr", [alloc(c, es, "xr%d" % i, [128, 512], F32) for i in range(3)])
    xos = Rot("xo", [alloc(c, es, "xo%d" % i, [128, 512], F32) for i in range(3)])
    psg = Rot("psg", [palloc(c, es, "psg%d" % i, [128, 512]) for i in range(2)])
    psb = Rot("psb", [palloc(c, es, "psb%d" % i, [128, 512]) for i in range(2)])
    pso = Rot("pso", [palloc(c, es, "pso%d" % i, [128, 512]) for i in range(2)])
    win = fm(c.w["w_in"][l])
    wout = fm(c.w["w_out"][l])
    gcol = c.params[:, pb + PC["mix"]:pb + PC["mix"] + 16]
    xv = fm(c.xres)
    for gi_, (g0, ng, subs, tiles) in enumerate(groups):
        if gi_ == 0:
            norm_stage(c, nb, c.xres, gcol, hbuf, "hbuf", g0, ng)
        if gi_ + 1 < len(groups):
            ngen = norm_gen(c, nb, c.xres, gcol, hbuf, "hbuf", groups[gi_ + 1][0], groups[gi_ + 1][1])
        else:
            ngen = iter(())
        for br in range(4):
            P.dma("sp", brb[:, br * 4:(br + 1) * 4, 0:ng], fm(c.br[br])[:, :, g0:g0 + ng], [], [("brb", br)], ("brb", br))
        for fc in range(NCH):
            accs = {}
            for br in range(4):
                w, wk = wg.next()
                gc0 = OFF["gate"] + br * D + fc * 128
                P.dma("pool", w[:, :, :], win[:, :, gc0:gc0 + 128], [], [wk], wk)
                w2, w2k = wb.next()
                P.dma("pool", w2[:, :, :], c.w["w_branch"][l, br].rearrange("(kc p) m -> p kc m", p=128)[:, :, fc * 128:(fc + 1) * 128],
                      [], [w2k], w2k)
                for si, (t0, n) in enumerate(subs):
                    off = t0 - g0
                    pg, pgk = psg.next()
                    pbr, pbk = psb.next()
                    for ch in range(NCH):
                        P.add("pe", lambda e, pg=pg, w=w, ch=ch, off=off, n=n: e.matmul(
                            pg[:, 0:n], lhsT=w[:, ch, :], rhs=hbuf[:, ch, off:off + n], start=(ch == 0), stop=(ch == NCH - 1)),
                            [wk, "hbuf"], [pgk])
                    for kc in range(4):
                        P.add("pe", lambda e, pbr=pbr, w2=w2, kc=kc, br=br, off=off, n=n: e.matmul(
                            pbr[:, 0:n], lhsT=w2[:, kc, :], rhs=brb[:, br * 4 + kc, off:off + n], start=(kc == 0), stop=(kc == 3)),
                            [w2k, ("brb", br)], [pbk])
                    sg, sgk = sgs.next()
                    P.add("act", lambda e, sg=sg, pg=pg, n=n: e.activation(out=sg[:, 0:n], in_=pg[:, 0:n], func=AF.Sigmoid),
                          [pgk], [sgk])
                    if br == 0:
                        accs[si] = macc.next()
                        ma, mak = accs[si]
                        P.add("dve", lambda e, ma=ma, sg=sg, pbr=pbr, n=n: e.tensor_tensor(
                            out=ma[:, 0:n], in0=sg[:, 0:n], in1=pbr[:, 0:n], op=ALU.mult), [sgk, pbk], [mak])
                    else:
                        ma, mak = accs[si]
                        tm, tmk = tmps.next()
                        P.add("dve", lambda e, tm=tm, sg=sg, pbr=pbr, n=n: e.tensor_tensor(
                            out=tm[:, 0:n], in0=sg[:, 0:n], in1=pbr[:, 0:n], op=ALU.mult), [sgk, pbk], [tmk])
                        if br < 3:
                            P.add("dve", lambda e, ma=ma, tm=tm, n=n: e.tensor_tensor(
                                out=ma[:, 0:n], in0=ma[:, 0:n], in1=tm[:, 0:n], op=ALU.add), [mak, tmk], [mak])
                        else:
                            P.add("dve", lambda e, ma=ma, tm=tm, n=n, fc=fc, off=off: e.tensor_tensor(
                                out=mrg[:, fc, off:off + n], in0=ma[:, 0:n], in1=tm[:, 0:n], op=ALU.add), [mak, tmk], [("mrg", fc)])
        for m in range(NCH):
            w, wk = wo.next()
            P.dma("pool", w[:, :, :], wout[:, :, m * 128:(m + 1) * 128], [], [wk], wk)
            for (t0, n) in subs:
                off = t0 - g0
                xr, xrk = xrs.next()
                P.dma("sp", xr[:, 0:n], xv[:, m, t0:t0 + n], [], [xrk], xrk)
                po, pok = pso.next()
                for ch in range(NCH):
                    P.add("pe", lambda e, po=po, w=w, ch=ch, off=off, n=n: e.matmul(
                        po[:, 0:n], lhsT=w[:, ch, :], rhs=mrg[:, ch, off:off + n], start=(ch == 0), stop=(ch == NCH - 1)),
                        [wk, ("mrg", ch)], [pok])
                xo, xok = xos.next()
                P.add("dve", lambda e, xo=xo, po=po, xr=xr, n=n: e.tensor_tensor(
                    out=xo[:, 0:n], in0=po[:, 0:n], in1=xr[:, 0:n], op=ALU.add), [pok, xrk], [xok])
                P.dma("act", xv[:, m, t0:t0 + n], xo[:, 0:n], [xok], [], xok)
            if m >= 2:
                next(ngen, None)
        for _ in ngen:
            pass
    P.flush()
    es.close()


def load_rows_norm(c, bufs, row0, nrows, out_fn, gsc_col, ones_m, gsize, key_out, dup64=False):
    P = c.P
    st, sq, rs, psn = bufs
    for (t0, n) in split_cols(0, c.L, 512):
        s, sk = st.next()
        if dup64:
            P.dma("sp", s[0:64, 0:n], c.proj[row0:row0 + 64, t0:t0 + n], [], [sk], sk)
            P.dma("sp", s[64:128, 0:n], c.proj[row0:row0 + 64, t0:t0 + n], [], [sk], sk)
        else:
            P.dma("sp", s[0:nrows, 0:n], c.proj[row0:row0 + nrows, t0:t0 + n], [], [sk], sk)
        if gsize is None:
            P.add("act", lambda e, s=s, t0=t0, n=n: e.activation(out=out_fn(t0, n), in_=s[:, 0:n], func=AF.Copy), [sk], [key_out])
            continue
        q, qk = sq.next()
        P.add("act", lambda e, q=q, s=s, n=n: e.activation(out=q[:, 0:n], in_=s[:, 0:n], func=AF.Square), [sk], [qk])
        P.add("pe", lambda e, q=q, n=n: e.matmul(psn[:, 0:n], lhsT=ones_m[:, :], rhs=q[:, 0:n], start=True, stop=True),
              [qk, "const"], ["psn"])
        r, rk = rs.next()
        rsqrt_op(c, r[:, 0:n], psn[:, 0:n], 1.0 / gsize, ["psn"], rk)
        P.add("dve", lambda e, s=s, r=r, t0=t0, n=n: e.scalar_tensor_tensor(
            out=out_fn(t0, n), in0=s[:, 0:n], scalar=gsc_col, in1=r[:, 0:n], op0=ALU.mult, op1=ALU.mult),
            [sk, rk, "gsc"], [key_out])


def rows_bufs(c, es):
    st = Rot("st", [alloc(c, es, "st%d" % i, [128, 512], F32) for i in range(2)])
    sq = Rot("sqq", [alloc(c, es, "sqq%d" % i, [128, 512], BF16) for i in range(2)])
    rs = Rot("rs", [alloc(c, es, "rs%d" % i, [128, 512], F32) for i in range(2)])
    psn = palloc(c, es, "psn", [128, 512])
    return (st, sq, rs, psn)


def near_case(i, j):
    if j == i:
        return 0
    if j >= 1 and j == i - 1:
        return 1
    if j == 0 and i == 1:
        return 2
    return None


def load_vtm(c, vsb, src, col0, ncol, key, pitch_view):
    P = c.P
    nt_full = len(c.tiles) - 1
    P.dma("sp", pitch_view(0, 16), src[0:16, col0:col0 + ncol], [], [key], key)
    for ti in range(1, nt_full + 1):
        t0, nt = c.tiles[ti]
        P.dma("sp", pitch_view(ti, nt), src[t0:t0 + nt, col0:col0 + ncol], [], [key], key)


def diff_phase(c, l):
    nc, P = c.nc, c.P
    es = ExitStack()
    L = c.L
    NT = len(c.tiles)
    pb = l * PL
    lam_init = 0.8 - 0.6 * math.exp(-0.3 * l)
    rb = rows_bufs(c, es)
    qC = alloc(c, es, "qC", [128, 4, L], BF16)
    kC = alloc(c, es, "kC", [128, 4, L], BF16)
    vC = alloc(c, es, "vCs", [128, NT, 4, 129], BF16)
    ob = alloc(c, es, "obC", [128, 4, L], BF16)
    gs = alloc(c, es, "gsC", [128, 4], F32)
    zc = alloc(c, es, "zcol", [128, 1], F32)
    pTs = Rot("pT", [alloc(c, es, "pT%d" % i, [128, 512], BF16) for i in range(3)])
    rr = Rot("rr", [alloc(c, es, "rr%d" % i, [128, 4], F32) for i in range(2)])
    ods = Rot("od", [alloc(c, es, "od%d" % i, [128, 128], F32) for i in range(2)])
    junk = alloc(c, es, "junk", [128, 128], F32)
    ons = Rot("on", [alloc(c, es, "on%d" % i, [128, 128], BF16) for i in range(2)])
    pss = Rot("pss", [palloc(c, es, "pss%d" % i, [128, 512]) for i in range(4)])
    pso = Rot("pso", [palloc(c, es, "pso%d" % i, [128, 512]) for i in range(2)])
    pst = palloc(c, es, "pst", [128, 512])
    posb = Rot("posb", [alloc(c, es, "posb%d" % i, [128, 264], F32) for i in range(2)])
    P.add("dve", lambda e: e.tensor_scalar(out=gs[:, 0:1], in0=c.params[:, pb + PC["dqn"]:pb + PC["dqn"] + 1], scalar1=0.125,
                                           scalar2=None, op0=ALU.mult), ["params"], ["gsc"])
    P.add("dve", lambda e: e.tensor_copy(out=gs[:, 1:2], in_=c.params[:, pb + PC["dkn"]:pb + PC["dkn"] + 1]), ["params"], ["gsc"])
    P.add("dve", lambda e: e.tensor_scalar(out=gs[:, 2:3], in0=c.params[:, pb + PC["subln"]:pb + PC["subln"] + 1],
                                           scalar1=1.0 - lam_init, scalar2=None, op0=ALU.mult), ["params"], ["gsc"])
    P.add("dve", lambda e: e.memset(zc[:, :], 0.0), [], ["gsc"])
    P.add("dve", lambda e: e.memset(vC[:, :, :, 128:129], 1.0), [], ["vC1"])
    for h in range(4):
        load_rows_norm(c, rb, OFF["c_q"] + h * 128, 128, lambda t0, n, h=h: qC[:, h, t0:t0 + n], gs[:, 0:1], c.ones64_bf, 64, ("qC", h))
        load_rows_norm(c, rb, OFF["c_k"] + h * 128, 128, lambda t0, n, h=h: kC[:, h, t0:t0 + n], gs[:, 1:2], c.ones64_bf, 64, ("kC", h))
    for ti in range(NT):
        t0, nt = c.tiles[ti]
        P.dma("sp", vC[0:nt, ti, :, 0:128], c.vC[t0:t0 + nt, :].rearrange("t (h e) -> t h e", h=4), [], ["vC"], "vC")
    c.prev_tail = None
    for i_ in range(NT):
      for h_ in range(4):
        def do_block(i, h, q0, nq):
            pos = [pso.next(), pso.next()]
            groups_ = [(cc, jg) for cc in range(2) for jg in range(0, i + 1, 4)]
            psl = {}

            def emit_s(gi):
                cc, jg = groups_[gi]
                grp = list(range(jg, min(i + 1, jg + 4)))
                ps, psk = pss.next()
                psl[gi] = (ps, psk, grp)
                for sl, j in enumerate(grp):
                    k0, nk = c.tiles[j]
                    case = near_case(i, j)
                    P.add("pe", lambda e, ps=ps, cc=cc, k0=k0, nk=nk, case=case, sl=sl: e.matmul(
                        ps[0:nk, sl * 128:sl * 128 + nq], lhsT=kC[cc * 64:(cc + 1) * 64, h, k0:k0 + nk],
                        rhs=qC[cc * 64:(cc + 1) * 64, h, q0:q0 + nq], start=True, stop=(case is None)), [("kC", h), ("qC", h)], [psk])
                    if case is not None:
                        P.add("pe", lambda e, ps=ps, nk=nk, case=case, sl=sl: e.matmul(
                            ps[0:nk, sl * 128:sl * 128 + nq], lhsT=c.ident_bf[0:nk, 0:nk], rhs=c.BT[0:nk, h * 3 + case, 0:nq],
                            start=False, stop=True), ["const"], [psk])

            def emit_pv(gi):
                cc, jg = groups_[gi]
                ps, psk, grp = psl[gi]
                po, pok = pos[cc]
                W = (len(grp) - 1) * 128 + nq
                pT, pTk = pTs.next()
                P.add("act", lambda e, pT=pT, ps=ps, W=W: e.activation(out=pT[:, 0:W], in_=ps[:, 0:W], func=AF.Exp), [psk], [pTk])
                for sl, j in enumerate(grp):
                    k0, nk = c.tiles[j]
                    P.add("pe", lambda e, po=po, pT=pT, nk=nk, j=j, sl=sl: e.matmul(
                        po[0:nq, 0:129], lhsT=pT[0:nk, sl * 128:sl * 128 + nq], rhs=vC[0:nk, j, h, 0:129], start=(j == 0), stop=(j == i)),
                        [pTk, "vC", "vC1"], [pok])

            emit_s(0)
            for gi in range(len(groups_)):
                if gi + 1 < len(groups_):
                    emit_s(gi + 1)
                emit_pv(gi)
            ob2, ob2k = posb.next()
            P.add("act", lambda e, ob2=ob2: e.activation(out=ob2[0:nq, 0:129], in_=pos[0][0][0:nq, 0:129], func=AF.Copy), [pos[0][1]], [ob2k])
            P.add("act", lambda e, ob2=ob2: e.activation(out=ob2[0:nq, 132:261], in_=pos[1][0][0:nq, 0:129], func=AF.Copy), [pos[1][1]], [ob2k])
            p0, p0k = ob2[:, 0:132], ob2k
            p1, p1k = ob2[:, 132:264], ob2k
            r, rk = rr.next()
            P.add("dve", lambda e, r=r, p0=p0, nq=nq: e.reciprocal(out=r[0:nq, 0:1], in_=p0[0:nq, 128:129]), [p0k], [rk])
            P.add("dve", lambda e, r=r, p1=p1, nq=nq: e.reciprocal(out=r[0:nq, 1:2], in_=p1[0:nq, 128:129]), [p1k], [rk])
            P.add("dve", lambda e, r=r, nq=nq: e.tensor_tensor(out=r[0:nq, 2:3], in0=r[0:nq, 1:2], in1=c.nlam[0:nq, l:l + 1], op=ALU.mult),
                  [rk, "const"], [rk])
            od, odk = ods.next()
            P.add("dve", lambda e, od=od, p0=p0, r=r, nq=nq: e.tensor_scalar(out=od[0:nq, :], in0=p0[0:nq, 0:128], scalar1=r[0:nq, 0:1],
                                                                              scalar2=None, op0=ALU.mult), [p0k, rk], [odk])
            P.add("dve", lambda e, od=od, p1=p1, r=r, nq=nq: e.scalar_tensor_tensor(
                out=od[0:nq, :], in0=p1[0:nq, 0:128], scalar=r[0:nq, 2:3], in1=od[0:nq, :], op0=ALU.mult, op1=ALU.add),
                [p1k, rk, odk], [odk])
            P.add("dve", lambda e, od=od, nq=nq: e.tensor_tensor(out=junk[0:nq, :], in0=od[0:nq, :], in1=od[0:nq, :], op=ALU.mult),
                  [odk], ["junk"])
            P.add("dve", lambda e, r=r, nq=nq: e.tensor_reduce(out=r[0:nq, 3:4], in_=junk[0:nq, :], axis=AX.X, op=ALU.add),
                  ["junk", rk], [rk])
            rsqrt_op(c, r[0:nq, 3:4], r[0:nq, 3:4], 1.0 / 128, [rk], rk, np_=nq)
            on, onk = ons.next()
            P.add("dve", lambda e, on=on, od=od, r=r, nq=nq: e.tensor_scalar(out=on[0:nq, :], in0=od[0:nq, :], scalar1=r[0:nq, 3:4],
                                                                              scalar2=None, op0=ALU.mult), [odk, rk], [onk])
            def tail():
                P.add("pe", lambda e, on=on, nq=nq: e.matmul(pst[:, 0:nq], lhsT=on[0:nq, :], rhs=c.ident_bf[0:nq, 0:nq], start=True, stop=True),
                      [onk, "const"], ["pst"])
                P.add("act", lambda e, h=h, q0=q0, nq=nq: e.activation(out=ob[:, h, q0:q0 + nq], in_=pst[:, 0:nq], func=AF.Copy,
                                                                        scale=gs[:, 2:3]), ["pst", "gsc"], ["obC"])
            return tail
        tl_ = do_block(i_, h_, c.tiles[i_][0], c.tiles[i_][1])
        if c.prev_tail is not None:
            c.prev_tail()
        c.prev_tail = tl_
    c.prev_tail()
    P.dma("sp", fm(c.br[2]), ob[:, :, :], ["obC"], [], "obC")
    P.flush()
    es.close()


def hgrn_phase(c, l):
    nc, P = c.nc, c.P
    es = ExitStack()
    L = c.L
    NT = len(c.tiles)
    pb = l * PL
    zf = alloc(c, es, "zf", [128, L], F32)
    bb = alloc(c, es, "bb", [128, L], F32)
    kf = alloc(c, es, "kf", [128, L], F32)
    qf = alloc(c, es, "qf", [128, L], F32)
    tmp = alloc(c, es, "tmpE", [128, L], F32)
    qt = alloc(c, es, "qt", [128, L], BF16)
    qh = alloc(c, es, "qh", [128, L], BF16)
    kh = alloc(c, es, "kh", [128, L], BF16)
    khT = alloc(c, es, "khT", [128, NT, 128], BF16)
    vA = alloc(c, es, "vAs", [128, NT, 128], BF16)
    osb = alloc(c, es, "osb", [128, L], BF16)
    obr = alloc(c, es, "obr", [128, L], BF16)
    e1 = alloc(c, es, "e1", [128, NT], F32)
    e2 = alloc(c, es, "e2", [128, NT], F32)
    Sf = alloc(c, es, "Sf", [128, 128], F32)
    St = alloc(c, es, "St", [128, 128], F32)
    Sb = Rot("Sb", [alloc(c, es, "Sb%d" % i, [128, 128], BF16) for i in range(2)])
    Pm = Rot("Pm", [alloc(c, es, "Pm%d" % i, [128, 128], BF16) for i in range(2)])
    sq = Rot("sqh", [alloc(c, es, "sqh%d" % i, [128, 512], BF16) for i in range(2)])
    rs = Rot("rsh", [alloc(c, es, "rsh%d" % i, [128, 512], F32) for i in range(2)])
    pss = Rot("pss", [palloc(c, es, "pss%d" % i, [128, 512]) for i in range(2)])
    pso = Rot("pso", [palloc(c, es, "pso%d" % i, [128, 512]) for i in range(2)])
    psd = Rot("psd", [palloc(c, es, "psd%d" % i, [128, 512]) for i in range(2)])
    pstb = palloc(c, es, "pstb", [128, 128], BF16)
    psn = palloc(c, es, "psnh", [128, 512])
    for k in range(2):
        P.add("pool", lambda e, k=k: e.memset(Pm.t[k][:, :], 0.0), [], [("Pm", k)])
    for h in range(4):
        lbc = c.lbs[:, l, h:h + 1]
        omc = c.oml[:, l, h:h + 1]
        P.dma("sp", zf[:, :], c.proj[OFF["a_f"] + h * 128:OFF["a_f"] + (h + 1) * 128, :], [], ["zf"], "zf")
        P.dma("sp", qf[:, :], c.proj[OFF["a_q"] + h * 128:OFF["a_q"] + (h + 1) * 128, :], [], ["qf"], "qf")
        load_vtm(c, vA, c.vA, h * 128, 128, "vA", lambda ti, nt: vA[0:nt, ti, :])
        P.add("act", lambda e: e.activation(out=zf[:, :], in_=zf[:, :], func=AF.Sigmoid), ["zf"], ["zf"])
        P.add("dve", lambda e, lbc=lbc, omc=omc: e.tensor_scalar(out=zf[:, :], in0=zf[:, :], scalar1=omc, scalar2=lbc,
                                                                 op0=ALU.mult, op1=ALU.add), ["zf", "const"], ["zf"])
        P.add("dve", lambda e: e.tensor_scalar(out=kf[:, :], in0=zf[:, :], scalar1=-1.0, scalar2=1.0, op0=ALU.mult, op1=ALU.add),
              ["zf"], ["kf"])
        P.add("act", lambda e: e.activation(out=zf[:, :], in_=zf[:, :], func=AF.Ln), ["zf", "kf"], ["zf"])
        for ti in range(NT):
            t0, nt = c.tiles[ti]
            P.add("dve", lambda e, t0=t0, nt=nt: e.tensor_tensor_scan(out=bb[:, t0:t0 + nt], data0=c.ones_f[:, 0:nt], data1=zf[:, t0:t0 + nt],
                                                                      initial=0.0, op0=ALU.mult, op1=ALU.add), ["zf", "const"], ["bb"])
        for ti in range(NT):
            t0, nt = c.tiles[ti]
            mid = t0 + nt // 2
            P.add("dve", lambda e, t0=t0, nt=nt, mid=mid: e.tensor_scalar(out=zf[:, t0:t0 + nt], in0=bb[:, t0:t0 + nt], scalar1=bb[:, mid:mid + 1],
                                                                          scalar2=None, op0=ALU.subtract), ["bb", "zf"], ["zf"])
        for ti in range(NT):
            t0, nt = c.tiles[ti]
            P.add("act", lambda e, ti=ti, t0=t0, nt=nt: e.activation(out=e1[:, ti:ti + 1], in_=bb[:, t0 + nt - 1:t0 + nt], func=AF.Exp),
                  ["bb"], ["e1"])
            P.add("act", lambda e, ti=ti, t0=t0, nt=nt: e.activation(out=e2[:, ti:ti + 1], in_=zf[:, t0 + nt - 1:t0 + nt], func=AF.Exp),
                  ["zf"], ["e2"])
        P.add("act", lambda e: e.activation(out=qf[:, :], in_=qf[:, :], func=AF.Silu), ["qf"], ["qf"])
        P.add("act", lambda e: e.activation(out=tmp[:, :], in_=bb[:, :], func=AF.Exp), ["bb"], ["tmp"])
        P.add("dve", lambda e: e.tensor_tensor(out=qt[:, :], in0=qf[:, :], in1=tmp[:, :], op=ALU.mult), ["qf", "tmp"], ["qt"])
        P.add("act", lambda e: e.activation(out=tmp[:, :], in_=zf[:, :], func=AF.Exp), ["zf", "qt"], ["tmp"])
        P.add("dve", lambda e: e.tensor_tensor(out=qh[:, :], in0=qf[:, :], in1=tmp[:, :], op=ALU.mult), ["qf", "tmp"], ["qh"])
        P.add("act", lambda e: e.activation(out=tmp[:, :], in_=zf[:, :], func=AF.Exp, scale=-1.0), ["zf", "qh"], ["tmp"])
        P.add("dve", lambda e: e.tensor_tensor(out=kh[:, :], in0=kf[:, :], in1=tmp[:, :], op=ALU.mult), ["kf", "tmp"], ["kh"])
        for ti in range(NT):
            t0, nt = c.tiles[ti]
            P.add("pe", lambda e, t0=t0, nt=nt: e.transpose(out=pstb[0:nt, :], in_=kh[:, t0:t0 + nt], identity=c.ident_bf[:, :]),
                  ["kh", "const"], ["pstb"])
            P.add("act", lambda e, ti=ti, nt=nt: e.activation(out=khT[0:nt, ti, :], in_=pstb[0:nt, :], func=AF.Copy), ["pstb"], ["khT"])
        P.add("dve", lambda e: e.memset(Sf[:, :], 0.0), [], ["Sf"])
        sb, sbk = Sb.next()
        P.add("pool", lambda e, sb=sb: e.memset(sb[:, :], 0.0), [], [sbk])
        for ti in range(NT):
            t0, nt = c.tiles[ti]
            ps, psk = pss.next()
            P.add("pe", lambda e, ps=ps, t0=t0, nt=nt: e.matmul(ps[0:nt, 0:nt], lhsT=kh[:, t0:t0 + nt], rhs=qh[:, t0:t0 + nt], start=True, stop=True),
                  ["kh", "qh"], [psk])
            pm, pmk = Pm.next()
            P.add("dve", lambda e, pm=pm, ps=ps, nt=nt: e.copy_predicated(out=pm[0:nt, 0:nt], mask=c.causT[0:nt, 0:nt], data=ps[0:nt, 0:nt]),
                  [psk, "const_c"], [pmk])
            po, pok = pso.next()
            P.add("pe", lambda e, po=po, pm=pm, ti=ti, nt=nt: e.matmul(po[:, 0:nt], lhsT=vA[0:nt, ti, :], rhs=pm[0:nt, 0:nt], start=True, stop=False),
                  ["vA", pmk], [pok])
            P.add("pe", lambda e, po=po, sb=sb, t0=t0, nt=nt: e.matmul(po[:, 0:nt], lhsT=sb[:, :], rhs=qt[:, t0:t0 + nt], start=False, stop=True),
                  [sbk, "qt"], [pok])
            P.add("act", lambda e, po=po, t0=t0, nt=nt: e.activation(out=osb[:, t0:t0 + nt], in_=po[:, 0:nt], func=AF.Copy), [pok], ["osb"])
            pd, pdk = psd.next()
            P.add("pe", lambda e, pd=pd, ti=ti, nt=nt: e.matmul(pd[:, 0:128], lhsT=khT[0:nt, ti, :], rhs=vA[0:nt, ti, :], start=True, stop=True),
                  ["khT", "vA"], [pdk])
            P.add("dve", lambda e, ti=ti: e.tensor_scalar(out=St[:, :], in0=Sf[:, :], scalar1=e1[:, ti:ti + 1], scalar2=None, op0=ALU.mult),
                  ["Sf", "e1"], ["St"])
            P.add("dve", lambda e, pd=pd, ti=ti: e.scalar_tensor_tensor(out=Sf[:, :], in0=pd[:, 0:128], scalar=e2[:, ti:ti + 1], in1=St[:, :],
                                                                        op0=ALU.mult, op1=ALU.add), [pdk, "St", "e2"], ["Sf"])
            sb, sbk = Sb.next()
            P.add("act", lambda e, sb=sb: e.activation(out=sb[:, :], in_=Sf[:, :], func=AF.Copy), ["Sf"], [sbk])
        P.dma("sp", qf[:, :], c.proj[OFF["a_g"] + h * 128:OFF["a_g"] + (h + 1) * 128, :], [], ["qf"], "qf")
        P.add("act", lambda e: e.activation(out=qf[:, :], in_=qf[:, :], func=AF.Silu), ["qf"], ["qf"])
        gcol = c.params[:, pb + PC["gnorm"] + h:pb + PC["gnorm"] + h + 1]
        for (t0, n) in split_cols(0, L, 512):
            q, qk = sq.next()
            P.add("act", lambda e, q=q, t0=t0, n=n: e.activation(out=q[:, 0:n], in_=osb[:, t0:t0 + n], func=AF.Square), ["osb"], [qk])
            P.add("pe", lambda e, q=q, n=n: e.matmul(psn[:, 0:n], lhsT=c.ones_bf[:, :], rhs=q[:, 0:n], start=True, stop=True),
                  [qk, "const"], ["psn"])
            r, rk = rs.next()
            rsqrt_op(c, r[:, 0:n], psn[:, 0:n], 1.0 / 128, ["psn"], rk)
            P.add("dve", lambda e, r=r, t0=t0, n=n, gcol=gcol: e.scalar_tensor_tensor(
                out=r[:, 0:n], in0=osb[:, t0:t0 + n], scalar=gcol, in1=r[:, 0:n], op0=ALU.mult, op1=ALU.mult),
                ["osb", rk, "params"], [rk])
            P.add("dve", lambda e, r=r, t0=t0, n=n: e.tensor_tensor(out=obr[:, t0:t0 + n], in0=r[:, 0:n], in1=qf[:, t0:t0 + n], op=ALU.mult),
                  [rk, "qf"], ["obr"])
        P.dma("sp", c.br[0][h * 128:(h + 1) * 128, :], obr[:, :], ["obr"], [], "obr")
    P.flush()
    es.close()


def dsa_phase(c, l):
    nc, P = c.nc, c.P
    es = ExitStack()
    L = c.L
    NT = len(c.tiles)
    pb = l * PL
    topk = c.topk
    rb = rows_bufs(c, es)
    st, _sq, _rs, _psn = rb
    qD = alloc(c, es, "qD", [128, 4, L], BF16)
    kD = alloc(c, es, "kD", [128, L], BF16)
    vD = alloc(c, es, "vDs", [128, NT, 129], BF16)
    kiD = alloc(c, es, "kiD", [128, L], BF16)
    sgn = alloc(c, es, "sgn", [128, NT, 16], F32)
    idx = alloc(c, es, "idx", [128, L], F32)
    work = alloc(c, es, "work", [128, L], F32)
    maskb = alloc(c, es, "maskb", [128, L], BF16)
    mT = alloc(c, es, "mT", [128, NT, 128], BF16)
    gs = alloc(c, es, "gsD", [128, 2], F32)
    zc = alloc(c, es, "zcolD", [128, 1], F32)
    m8 = alloc(c, es, "m8", [128, 8], F32)
    th = alloc(c, es, "th", [128, 1], F32)
    rls = Rot("rl", [alloc(c, es, "rl%d" % i, [128, 512], BF16) for i in range(3)])
    idxs = [idx, alloc(c, es, "idx2", [128, L], F32)]
    dgs = Rot("dg", [alloc(c, es, "dg%d" % i, [128, 16, 128], BF16) for i in range(2)])
    eTs = Rot("eT", [alloc(c, es, "eT%d" % i, [128, 512], BF16) for i in range(2)])
    pTs = Rot("pTD", [alloc(c, es, "pTD%d" % i, [128, 512], BF16) for i in range(2)])
    rr = Rot("rrD", [alloc(c, es, "rrD%d" % i, [128, 1], F32) for i in range(2)])
    ons = Rot("onD", [alloc(c, es, "onD%d" % i, [128, 128], BF16) for i in range(2)])
    ocs = Rot("ocD", [alloc(c, es, "ocD%d" % i, [128, 128], BF16) for i in range(3)])
    psi = Rot("psi", [palloc(c, es, "psi%d" % i, [128, 512]) for i in range(2)])
    pss = Rot("pssD", [palloc(c, es, "pssD%d" % i, [128, 512]) for i in range(2)])
    pso = Rot("psoD", [palloc(c, es, "psoD%d" % i, [128, 512]) for i in range(2)])
    pstb = palloc(c, es, "pstbD", [128, 128], BF16)
    P.add("dve", lambda e: e.tensor_scalar(out=gs[:, 0:1], in0=c.params[:, pb + PC["dsaq"]:pb + PC["dsaq"] + 1], scalar1=128 ** -0.5,
                                           scalar2=None, op0=ALU.mult), ["params"], ["gsc"])
    P.add("dve", lambda e: e.tensor_copy(out=gs[:, 1:2], in_=c.params[:, pb + PC["dsak"]:pb + PC["dsak"] + 1]), ["params"], ["gsc"])
    P.add("dve", lambda e: e.memset(zc[:, :], 0.0), [], ["gsc"])
    P.add("dve", lambda e: e.memset(vD[:, :, 128:129], 1.0), [], ["vD1"])
    for h in range(4):
        load_rows_norm(c, rb, OFF["d_q"] + h * 128, 128, lambda t0, n, h=h: qD[:, h, t0:t0 + n], gs[:, 0:1], c.ones_bf, 128, ("qD", h))
    load_rows_norm(c, rb, OFF["d_k"], 128, lambda t0, n: kD[:, t0:t0 + n], gs[:, 1:2], c.ones_bf, 128, "kD")
    load_rows_norm(c, rb, OFF["d_ki"], 64, lambda t0, n: kiD[:, t0:t0 + n], None, None, None, "kiD", dup64=True)
    load_vtm(c, vD, c.vD, 0, 128, "vD", lambda ti, nt: vD[0:nt, ti, 0:128])
    load_vtm(c, sgn, c.wT, 0, 16, "sgn", lambda ti, nt: sgn[0:nt, ti, :])
    P.add("act", lambda e: e.activation(out=sgn[:, :, :], in_=sgn[:, :, :], func=AF.Sign), ["sgn"], ["sgn"])
    qsts = Rot("qst", [alloc(c, es, "qst%d" % i, [128, 8, 128], F32) for i in range(2)])
    wsts = Rot("wst", [alloc(c, es, "wst%d" % i, [128, 8, 128], BF16) for i in range(2)])
    qits = Rot("qit", [alloc(c, es, "qit%d" % i, [128, 8, 128], BF16) for i in range(2)])
    mTs = [mT, alloc(c, es, "mT2", [128, NT, 128], BF16)]
    osb = [alloc(c, es, "osbD%d" % i, [128, 132], F32) for i in range(4)]
    qiv = c.proj[OFF["d_qi"]:OFF["d_qi"] + 1024, :].rearrange("(r p) t -> p r t", p=128)
    wav = c.wabs.rearrange("(r p) t -> p r t", p=128)

    def stage_a1(i):
        q0, nq = c.tiles[i]
        K = q0 + nq
        qs, qsk = qsts.next()
        ws, wsk = wsts.next()
        P.dma("sp", qs[:, :, 0:nq], qiv[:, :, q0:q0 + nq], [], [qsk], qsk)
        P.dma("sp", ws[:, :, 0:nq], wav[:, :, q0:q0 + nq], [], [wsk], wsk)
        qit, qitk = qits.next()
        P.add("pool", lambda e: e.tensor_tensor(out=qit[:, :, 0:nq], in0=qs[:, :, 0:nq], in1=ws[:, :, 0:nq], op=ALU.mult),
              [qsk, wsk], [qitk])
        idxc, idxk = idxs[i % 2], ("idx", i % 2)
        dg, dgk = dgs.next()
        for hi in range(16):
            P.add("pool", lambda e, hi=hi: e.tensor_scalar(out=dg[0:nq, hi, 0:nq], in0=c.ident_bf[0:nq, 0:nq], scalar1=sgn[0:nq, i, hi:hi + 1],
                                                           scalar2=None, op0=ALU.mult), ["const", "sgn"], [dgk])
        for (kb0, kn) in split_cols(0, K, 512):
            dots = {}

            def emit_dot(hi):
                r, half = hi // 2, hi % 2
                p, pk = psi.next()
                dots[hi] = (p, pk)
                P.add("pe", lambda e, p=p, r=r, half=half, kb0=kb0, kn=kn: e.matmul(
                    p[0:nq, 0:kn], lhsT=qit[half * 64:(half + 1) * 64, r, 0:nq], rhs=kiD[half * 64:(half + 1) * 64, kb0:kb0 + kn],
                    start=True, stop=True), [qitk, "kiD"], [pk])

            def emit_acc(hi):
                p, pk = dots[hi]
                rl, rlk = rls.next()
                P.add("act", lambda e, rl=rl, p=p, kn=kn: e.activation(out=rl[0:nq, 0:kn], in_=p[0:nq, 0:kn], func=AF.Relu), [pk], [rlk])
                P.add("pe", lambda e, rl=rl, hi=hi, kn=kn: e.matmul(_psn[0:nq, 0:kn], lhsT=dg[0:nq, hi, 0:nq], rhs=rl[0:nq, 0:kn],
                                                                   start=(hi == 0), stop=(hi == 15)), [rlk, dgk], ["psn"])

            emit_dot(0)
            for hi in range(16):
                if hi + 1 < 16:
                    emit_dot(hi + 1)
                emit_acc(hi)
            P.add("act", lambda e, kb0=kb0, kn=kn: e.activation(out=idxc[0:nq, kb0:kb0 + kn], in_=_psn[0:nq, 0:kn], func=AF.Copy),
                  ["psn"], [idxk])

    def stage_a2(i):
        q0, nq = c.tiles[i]
        K = q0 + nq
        mTc = mTs[i % 2]
        idxc, idxk = idxs[i % 2], ("idx", i % 2)
        P.add("dve", lambda e: e.tensor_tensor(out=idxc[0:nq, q0:q0 + nq], in0=idxc[0:nq, q0:q0 + nq], in1=c.fut[0:nq, 0:nq], op=ALU.add),
              [idxk, "const_f"], [idxk])
        if K > topk:
            for rd in range(topk // 8):
                src = idxc if rd == 0 else work
                P.add("dve", lambda e, src=src: e.max(out=m8[0:nq, :], in_=src[0:nq, 0:K]), [idxk, "work"], ["m8"])
                if rd < topk // 8 - 1:
                    P.add("dve", lambda e, src=src: e.match_replace(out=work[0:nq, 0:K], in_to_replace=m8[0:nq, :],
                                                                    in_values=src[0:nq, 0:K], imm_value=-1e30),
                          [idxk, "work", "m8"], ["work"])
            P.add("dve", lambda e: e.tensor_copy(out=th[0:nq, :], in_=m8[0:nq, 7:8]), ["m8"], ["th"])
        else:
            P.add("dve", lambda e: e.memset(th[0:nq, :], -1e29), ["m8"], ["th"])
        P.add("dve", lambda e: e.tensor_scalar(out=maskb[0:nq, 0:K], in0=idxc[0:nq, 0:K], scalar1=th[0:nq, 0:1], scalar2=None,
                                               op0=ALU.is_ge), [idxk, "th"], ["maskb"])
        for j in range(i + 1):
            k0, nk = c.tiles[j]
            P.add("pe", lambda e, k0=k0, nk=nk: e.transpose(out=pstb[0:nk, 0:nq], in_=maskb[0:nq, k0:k0 + nk], identity=c.ident_bf[0:nq, 0:nq]),
                  ["maskb", "const"], ["pstbD"])
            P.add("act", lambda e, j=j, nk=nk: e.activation(out=mTc[0:nk, j, 0:nq], in_=pstb[0:nk, 0:nq], func=AF.Copy), ["pstbD"], [("mT", i % 2, j)])

    def stage_b_main(i):
        q0, nq = c.tiles[i]
        mTc = mTs[i % 2]
        for h in range(4):
            po, pok = pso.next()
            for jg in range(0, i + 1, 4):
                grp = list(range(jg, min(i + 1, jg + 4)))
                ps, psk = pss.next()
                for sl, j in enumerate(grp):
                    k0, nk = c.tiles[j]
                    case = near_case(i, j)
                    P.add("pe", lambda e, ps=ps, h=h, k0=k0, nk=nk, case=case, sl=sl: e.matmul(
                        ps[0:nk, sl * 128:sl * 128 + nq], lhsT=kD[:, k0:k0 + nk], rhs=qD[:, h, q0:q0 + nq], start=True, stop=(case is None)),
                        ["kD", ("qD", h)], [psk])
                    if case is not None:
                        P.add("pe", lambda e, ps=ps, nk=nk, h=h, case=case, sl=sl: e.matmul(
                            ps[0:nk, sl * 128:sl * 128 + nq], lhsT=c.ident_bf[0:nk, 0:nk], rhs=c.BT[0:nk, (4 + h) * 3 + case, 0:nq],
                            start=False, stop=True), ["const"], [psk])
                W = (len(grp) - 1) * 128 + nq
                eT, eTk = eTs.next()
                P.add("act", lambda e, eT=eT, ps=ps, W=W: e.activation(out=eT[:, 0:W], in_=ps[:, 0:W], func=AF.Exp), [psk], [eTk])
                pT, pTk = pTs.next()
                if nq == 128:
                    ng_ = len(grp)
                    P.add("pool", lambda e, pT=pT, eT=eT, jg=jg, ng_=ng_: e.tensor_tensor(
                        out=pT[:, 0:ng_ * 128].rearrange("p (s q) -> p s q", q=128), in0=eT[:, 0:ng_ * 128].rearrange("p (s q) -> p s q", q=128),
                        in1=mTc[:, jg:jg + ng_, :], op=ALU.mult), [eTk] + [("mT", i % 2, j) for j in grp], [pTk])
                else:
                    P.add("pool", lambda e, pT=pT, eT=eT: e.tensor_tensor(
                        out=pT[0:16, 0:nq], in0=eT[0:16, 0:nq], in1=mTc[0:16, 0, 0:nq], op=ALU.mult), [eTk, ("mT", i % 2, 0)], [pTk])
                for sl, j in enumerate(grp):
                    k0, nk = c.tiles[j]
                    P.add("pe", lambda e, po=po, pT=pT, nk=nk, j=j, sl=sl: e.matmul(
                        po[0:nq, 0:129], lhsT=pT[0:nk, sl * 128:sl * 128 + nq], rhs=vD[0:nk, j, 0:129], start=(j == 0), stop=(j == i)),
                        [pTk, "vD", "vD1"], [pok])
            P.add("act", lambda e, po=po, h=h: e.activation(out=osb[h][0:nq, 0:129], in_=po[0:nq, 0:129], func=AF.Copy), [pok], [("osbD", h)])

    def stage_b_fin(i):
        q0, nq = c.tiles[i]
        for h in range(4):
            r, rk = rr.next()
            P.add("dve", lambda e, r=r, h=h: e.reciprocal(out=r[0:nq, 0:1], in_=osb[h][0:nq, 128:129]), [("osbD", h)], [rk])
            on, onk = ons.next()
            P.add("dve", lambda e, on=on, r=r, h=h: e.tensor_scalar(out=on[0:nq, :], in0=osb[h][0:nq, 0:128], scalar1=r[0:nq, 0:1],
                                                                     scalar2=None, op0=ALU.mult), [("osbD", h), rk], [onk])
            ps2, ps2k = pss.next()
            P.add("pe", lambda e, ps2=ps2, on=on: e.matmul(ps2[:, 0:nq], lhsT=on[0:nq, :], rhs=c.ident_bf[0:nq, 0:nq], start=True, stop=True),
                  [onk, "const"], [ps2k])
            oc, ock = ocs.next()
            P.add("act", lambda e, oc=oc, ps2=ps2: e.activation(out=oc[:, 0:nq], in_=ps2[:, 0:nq], func=AF.Copy), [ps2k], [ock])
            P.dma("sp", c.br[3][h * 128:(h + 1) * 128, q0:q0 + nq], oc[:, 0:nq], [ock], [], ock)

    stage_a1(0)
    if NT > 1:
        stage_a1(1)
    stage_a2(0)
    for i in range(NT):
        if i + 2 < NT:
            stage_a1(i + 2)
        stage_b_main(i)
        if i + 1 < NT:
            stage_a2(i + 1)
        stage_b_fin(i)
    P.flush()
    es.close()
```

```python
import numpy as np
from contextlib import ExitStack
import concourse.bass as bass
import concourse.mybir as mybir
from concourse.bass_utils import run_bass_kernel_spmd

F32 = mybir.dt.float32
BF16 = mybir.dt.bfloat16
I32 = mybir.dt.int32
ALU = mybir.AluOpType
AF = mybir.ActivationFunctionType
AX = mybir.AxisListType

ENGS = ("pe", "act", "dve", "pool", "sp")


class Prog:
    def __init__(self, nc):
        self.nc = nc
        self.es = ExitStack()
        self.eng_h = {"pe": nc.tensor, "act": nc.scalar, "dve": nc.vector,
                      "pool": nc.gpsimd, "sp": nc.sync}
        self.sem = {e: self.es.enter_context(nc.semaphore("s_" + e)) for e in ENGS}
        self.sig_cnt = {e: 0 for e in ENGS}
        self.waited = {e: {} for e in ENGS}
        self.dma_sems = {}
        self.semobj = {("eng", e): self.sem[e] for e in ENGS}
        self.ops = []
        self.lastw = {}
        self.readers = {}
        self.n_inst = 0
        self.phase_es = None

    def add(self, eng, fn, reads=(), writes=(), dma_key=None):
        idx = len(self.ops)
        is_dma = dma_key is not None
        deps = {}
        for r in reads:
            w = self.lastw.get(r)
            if w is not None:
                deps[w] = "raw"
        for wk in writes:
            w = self.lastw.get(wk)
            if w is not None and w not in deps:
                deps[w] = "waw"
            for r in self.readers.get(wk, {}).values():
                if r not in deps:
                    deps[r] = "war"
        op = dict(eng=eng, fn=fn, deps=deps, dma_key=dma_key, sig=False, dma_val=None)
        if is_dma:
            if dma_key not in self.dma_sems:
                s = self.es.enter_context(self.nc.semaphore("d_%d" % len(self.dma_sems)))
                self.dma_sems[dma_key] = [s, 0]
            self.dma_sems[dma_key][1] += 16
            op["dma_val"] = self.dma_sems[dma_key][1]
        self.ops.append(op)
        rk = ("dma", dma_key) if is_dma else eng
        for r in reads:
            self.readers.setdefault(r, {})[rk] = idx
        for wk in writes:
            self.lastw[wk] = idx
            self.readers[wk] = {}
        return idx

    def dma(self, q, out, in_, reads, writes, key):
        self.add(q, lambda e: e.dma_start(out=out, in_=in_), reads, writes, dma_key=key)

    def flush(self):
        ops = self.ops
        if not ops:
            return
        for i, op in enumerate(ops):
            waits = []
            for d, kind in op["deps"].items():
                od = ops[d]
                if od["dma_key"] is not None:
                    waits.append(("dma", od["dma_key"], od["dma_val"]))
                    continue
                if od["eng"] == op["eng"] and op["dma_key"] is None:
                    if op["eng"] == "pe" or kind != "raw":
                        continue
                od["sig"] = True
                waits.append(("eng", od["eng"], d))
            op["waits"] = waits
        last = {}
        for i, op in enumerate(ops):
            if op["dma_key"] is None:
                last[op["eng"]] = i
        for e, i in last.items():
            ops[i]["sig"] = True
        cnt = dict(self.sig_cnt)
        for op in ops:
            if op["dma_key"] is None and op["sig"]:
                cnt[op["eng"]] += 1
                op["sig_val"] = cnt[op["eng"]]
        end_cnt = cnt

        def emit_engine(ename):
            def body(eh):
                waited = self.waited[ename]
                for op in ops:
                    if op["eng"] != ename:
                        continue
                    for w in op["waits"]:
                        if w[0] == "dma":
                            semk = ("dma", w[1]); val = w[2]
                            s = self.dma_sems[w[1]][0]
                        else:
                            semk = ("eng", w[1]); val = ops[w[2]]["sig_val"]
                            s = self.sem[w[1]]
                        if waited.get(semk, 0) >= val:
                            continue
                        waited[semk] = val
                        eh.wait_ge(s, val)
                        self.n_inst += 1
                    ins = op["fn"](eh)
                    self.n_inst += 1
                    if op["dma_key"] is not None:
                        ins.then_inc(self.dma_sems[op["dma_key"]][0], 16)
                    elif op["sig"]:
                        ins.then_inc(self.sem[ename], 1)
                for e2 in ENGS:
                    if e2 == ename:
                        continue
                    v = end_cnt[e2]
                    if v > 0 and waited.get(("eng", e2), 0) < v:
                        waited[("eng", e2)] = v
                        eh.wait_ge(self.sem[e2], v)
                for k, (s, c) in self.dma_sems.items():
                    if c > 0 and waited.get(("dma", k), 0) < c:
                        waited[("dma", k)] = c
                        eh.wait_ge(s, c)
            return body

        with self.nc.Block() as block:
            block.tensor(emit_engine("pe"))
            block.scalar(emit_engine("act"))
            block.vector(emit_engine("dve"))
            block.gpsimd(emit_engine("pool"))
            block.sync(emit_engine("sp"))
        self.sig_cnt = end_cnt
        self.ops = []
        self.lastw = {}
        self.readers = {}

    def close(self):
        self.flush()
        self.es.close()
import math


D = 2048
NCH = 16
DFF = 5632
NJ = 44
EPS = 1e-6
OFF = dict(a_q=0, a_f=512, a_i=1024, a_g=1536, b_b=2048, b_c=2560, b_u=3072, c_q=3584, c_k=4096,
           c_v=4608, d_q=5120, d_k=5632, d_v=5760, d_qi=5888, d_ki=6912, d_w=6976, gate=6992)
NFM = 6976
NEG = -30000.0


class Rot:
    def __init__(self, name, tensors):
        self.name = name
        self.t = tensors
        self.i = 0

    def next(self):
        k = self.i % len(self.t)
        self.i += 1
        return self.t[k], (self.name, k)


def split_cols(t0, n, step):
    out = []
    o = 0
    while o < n:
        m = min(step, n - o)
        out.append((t0 + o, m))
        o += m
    return out


class Ctx:
    pass


def token_tiles(L):
    return [(0, 16)] + [(16 + 128 * i, 128) for i in range((L - 16) // 128)]


def make_groups(L, G):
    tl = token_tiles(L)
    nt = len(tl) - 1
    G = min(G, nt)
    sizes = [nt // G + (1 if k < nt % G else 0) for k in range(G)]
    groups = []
    i = 1
    first = True
    while i <= nt:
        j = min(nt, i + sizes[len(groups)] - 1)
        g0 = 0 if first else tl[i][0]
        g1 = tl[j][0] + tl[j][1]
        subs = []
        if first:
            subs.append((0, 16))
            subs += split_cols(16, g1 - 16, 512)
        else:
            subs += split_cols(g0, g1 - g0, 512)
        tiles = ([0] if first else []) + list(range(i, j + 1))
        groups.append((g0, g1 - g0, subs, tiles))
        first = False
        i = j + 1
    return groups


def fm(ap2d):
    return ap2d.rearrange("(c p) t -> p c t", p=128)


_UID = [0]


def alloc(c, es, name, shape, dt):
    _UID[0] += 1
    return es.enter_context(c.nc.sbuf_tensor("s%d_%s" % (_UID[0], name), list(shape), dt))


def palloc(c, es, name, shape, dt=None):
    _UID[0] += 1
    return es.enter_context(c.nc.psum_tensor("p%d_%s" % (_UID[0], name), list(shape), dt or F32))


def rsqrt_op(c, out, in_, scale, reads, wkey, np_=128):
    P = c.P
    P.add("act", lambda e: e.activation(out=out, in_=in_, func=AF.Sqrt, bias=c.epsc[0:np_, 0:1], scale=scale),
          list(reads) + ["const"], [wkey])
    P.add("dve", lambda e: e.reciprocal(out=out, in_=out), [wkey], [wkey])


def norm_stage(c, es_bufs, src, gcol, hbuf, hkey, g0, ng):
    for _ in norm_gen(c, es_bufs, src, gcol, hbuf, hkey, g0, ng):
        pass


def norm_gen(c, es_bufs, src, gcol, hbuf, hkey, g0, ng):
    P = c.P
    xts, sqs, rstd, psn = es_bufs
    srcv = fm(src)
    for (t0, n) in split_cols(g0, ng, 128):
        off = t0 - g0
        xt, xk = xts.next()
        P.dma("sp", xt[:, :, 0:n], srcv[:, :, t0:t0 + n], [], [xk], xk)
        for ch in range(NCH):
            sq, sk = sqs.next()
            P.add("act", lambda e, sq=sq, xt=xt, ch=ch, n=n: e.activation(out=sq[:, 0:n], in_=xt[:, ch, 0:n], func=AF.Square),
                  [xk], [sk])
            P.add("pe", lambda e, sq=sq, ch=ch, n=n: e.matmul(psn[:, 0:n], lhsT=c.ones_bf[:, :], rhs=sq[:, 0:n],
                                                              start=(ch == 0), stop=(ch == NCH - 1)),
                  [sk, "const"], ["psn"])
        rsqrt_op(c, rstd[:, 0:n], psn[:, 0:n], 1.0 / D, ["psn"], "rstd")
        for ch in range(NCH):
            P.add("dve", lambda e, xt=xt, ch=ch, n=n, off=off: e.scalar_tensor_tensor(
                out=hbuf[:, ch, off:off + n], in0=xt[:, ch, 0:n], scalar=gcol[:, ch:ch + 1], in1=rstd[:, 0:n],
                op0=ALU.mult, op1=ALU.mult), [xk, "rstd", "params"], [hkey])
        yield


def norm_bufs(c, es):
    xts = Rot("xt", [alloc(c, es, "xt%d" % i, [128, NCH, 128], F32) for i in range(2)])
    sqs = Rot("sq", [alloc(c, es, "sq%d" % i, [128, 256], BF16) for i in range(3)])
    rstd = alloc(c, es, "rstd", [128, 256], F32)
    psn = palloc(c, es, "psn", [128, 512])
    return (xts, sqs, rstd, psn)


def ffn_phase(c, src, dst, gcol, w_gu, w_down):
    nc, P = c.nc, c.P
    es = ExitStack()
    groups = make_groups(c.L, c.G_ffn)
    NG = max(g[1] for g in groups)
    nb = norm_bufs(c, es)
    hbuf = alloc(c, es, "hbuf", [128, NCH, NG], BF16)
    act = alloc(c, es, "actb", [128, NJ, NG], BF16)
    wgu = Rot("wgu", [alloc(c, es, "wgu%d" % i, [128, NCH, 256], BF16) for i in range(2)])
    wd = Rot("wd", [alloc(c, es, "wd%d" % i, [128, NJ, 128], BF16) for i in range(2)])
    sgs = Rot("sg", [alloc(c, es, "sg%d" % i, [128, 512], F32) for i in range(2)])
    xrs = Rot("xr", [alloc(c, es, "xr%d" % i, [128, 512], F32) for i in range(2)])
    xos = Rot("xo", [alloc(c, es, "xo%d" % i, [128, 512], F32) for i in range(2)])
    psg = Rot("psg", [palloc(c, es, "psg%d" % i, [128, 512]) for i in range(2)])
    psu = Rot("psu", [palloc(c, es, "psu%d" % i, [128, 512]) for i in range(2)])
    pso = Rot("pso", [palloc(c, es, "pso%d" % i, [128, 512]) for i in range(2)])
    wguv = fm(w_gu)
    wdv = fm(w_down)
    srcv, dstv = fm(src), fm(dst)
    for gi_, (g0, ng, subs, _tiles) in enumerate(groups):
        if gi_ == 0:
            norm_stage(c, nb, src, gcol, hbuf, "hbuf", g0, ng)
        if gi_ + 1 < len(groups):
            ngen = norm_gen(c, nb, src, gcol, hbuf, "hbuf", groups[gi_ + 1][0], groups[gi_ + 1][1])
        else:
            ngen = iter(())
        for j in range(NJ):
            w, wk = wgu.next()
            P.dma("pool", w[:, :, 0:128], wguv[:, :, j * 128:(j + 1) * 128], [], [wk], wk)
            P.dma("pool", w[:, :, 128:256], wguv[:, :, DFF + j * 128:DFF + (j + 1) * 128], [], [wk], wk)
            for (t0, n) in subs:
                off = t0 - g0
                pg, pgk = psg.next()
                pu, puk = psu.next()
                for ch in range(NCH):
                    P.add("pe", lambda e, pg=pg, w=w, ch=ch, off=off, n=n: e.matmul(
                        pg[:, 0:n], lhsT=w[:, ch, 0:128], rhs=hbuf[:, ch, off:off + n], start=(ch == 0), stop=(ch == NCH - 1)),
                        [wk, "hbuf"], [pgk])
                for ch in range(NCH):
                    P.add("pe", lambda e, pu=pu, w=w, ch=ch, off=off, n=n: e.matmul(
                        pu[:, 0:n], lhsT=w[:, ch, 128:256], rhs=hbuf[:, ch, off:off + n], start=(ch == 0), stop=(ch == NCH - 1)),
                        [wk, "hbuf"], [puk])
                sg, sgk = sgs.next()
                P.add("act", lambda e, sg=sg, pg=pg, n=n: e.activation(out=sg[:, 0:n], in_=pg[:, 0:n], func=AF.Silu),
                      [pgk], [sgk])
                P.add("dve", lambda e, sg=sg, pu=pu, j=j, off=off, n=n: e.tensor_tensor(
                    out=act[:, j, off:off + n], in0=sg[:, 0:n], in1=pu[:, 0:n], op=ALU.mult), [sgk, puk], [("act", j)])
        for m in range(NCH):
            w, wk = wd.next()
            P.dma("pool", w[:, :, :], wdv[:, :, m * 128:(m + 1) * 128], [], [wk], wk)
            for (t0, n) in subs:
                off = t0 - g0
                xr, xrk = xrs.next()
                P.dma("sp", xr[:, 0:n], srcv[:, m, t0:t0 + n], [], [xrk], xrk)
                po, pok = pso.next()
                for j in range(NJ):
                    P.add("pe", lambda e, po=po, w=w, j=j, off=off, n=n: e.matmul(
                        po[:, 0:n], lhsT=w[:, j, :], rhs=act[:, j, off:off + n], start=(j == 0), stop=(j == NJ - 1)),
                        [wk, ("act", j)], [pok])
                xo, xok = xos.next()
                P.add("dve", lambda e, xo=xo, po=po, xr=xr, n=n: e.scalar_tensor_tensor(
                    out=xo[:, 0:n], in0=po[:, 0:n], scalar=0.5, in1=xr[:, 0:n], op0=ALU.mult, op1=ALU.add),
                    [pok, xrk], [xok])
                P.dma("act", dstv[:, m, t0:t0 + n], xo[:, 0:n], [xok], [], xok)
            if m >= 2:
                next(ngen, None)
        for _ in ngen:
            pass
    P.flush()
    es.close()


PL = 80
PC = dict(ffn1=0, mix=16, ffn2=32, gnorm=48, conv=52, dqn=64, dkn=65, subln=66, dsaq=67, dsak=68, lb=69, lam=73)


def pack_params(inp):
    p = np.zeros((128, 2 * PL), np.float32)
    for l in range(2):
        b = l * PL
        p[:, b + PC["ffn1"]:b + PC["ffn1"] + 16] = inp["ffn1_norm"][l].reshape(16, 128).T
        p[:, b + PC["mix"]:b + PC["mix"] + 16] = inp["mix_norm"][l].reshape(16, 128).T
        p[:, b + PC["ffn2"]:b + PC["ffn2"] + 16] = inp["ffn2_norm"][l].reshape(16, 128).T
        p[:, b + PC["gnorm"]:b + PC["gnorm"] + 4] = inp["hgrn_gnorm"][l].reshape(4, 128).T
        for j in range(3):
            p[:, b + PC["conv"] + j * 4:b + PC["conv"] + j * 4 + 4] = inp["conv_w"][l, j].reshape(4, 128).T
        p[:, b + PC["dqn"]] = np.tile(inp["diff_q_norm"][l], 2)
        p[:, b + PC["dkn"]] = np.tile(inp["diff_k_norm"][l], 2)
        p[:, b + PC["subln"]] = inp["diff_subln"][l]
        p[:, b + PC["dsaq"]] = inp["dsa_q_norm"][l]
        p[:, b + PC["dsak"]] = inp["dsa_k_norm"][l]
        p[:, b + PC["lb"]:b + PC["lb"] + 4] = inp["hgrn_lb"][l].reshape(4, 128).T
        p[0:64, b + PC["lam"]:b + PC["lam"] + 4] = inp["diff_lambda"][l].T
    return p


def rel_bucket_np(n):
    n = np.maximum(n, 0)
    nf = np.maximum(n, 1).astype(np.float32)
    large = 16 + (np.log(nf / np.float32(16)) / np.float32(math.log(128 / 16)) * np.float32(16)).astype(np.int32)
    large = np.minimum(large, 31)
    return np.where(n < 16, n, large)


def make_consts():
    cst = {}
    cst["ident"] = np.eye(128, dtype=np.float32)
    sel = np.zeros((33, 3, 256), np.float32)
    for ci, delta in enumerate((0, 128, 16)):
        m = np.arange(255)
        n = m - 127 + delta
        bk = rel_bucket_np(n)
        for mm in range(255):
            if n[mm] < 0:
                sel[32, ci, mm] = 1.0
            else:
                sel[bk[mm], ci, mm] = 1.0
    cst["sel"] = sel
    e31 = np.zeros((32, 128), np.float32)
    e31[31, :] = 1.0
    cst["e31"] = e31
    o64 = np.zeros((128, 128), np.float32)
    o64[:64, :64] = 1.0
    o64[64:, 64:] = 1.0
    cst["ones64"] = o64
    q = np.arange(128)[:, None]
    k = np.arange(128)[None, :]
    cst["fut"] = np.where(k > q, -1e30, 0.0).astype(np.float32)
    cst["causT"] = (q <= k).astype(np.int32)
    return cst


WNAMES = ("ffn1_w_gu", "ffn1_w_down", "w_in", "w_branch", "w_out", "ffn2_w_gu", "ffn2_w_down")


def build(S, topk, G=4, phases=None, dbg=False):
    L = S + 16
    nc = bass.Bass("TRN2", target_bir_lowering=False)
    c = Ctx()
    c.nc = nc
    c.L, c.S, c.topk = L, S, topk
    c.tiles = token_tiles(L)
    c.G_ffn, c.G_proj, c.G_merge = (4, 2, 4) if S >= 2048 else (2, 2, 2)
    c.dbg = dbg

    def din(name, shape, dt=F32):
        return nc.dram_tensor(name, list(shape), dt, kind="ExternalInput").ap()

    def dscr(name, shape, dt=F32):
        kind = "ExternalOutput" if dbg else "Internal"
        return nc.dram_tensor(name, list(shape), dt, kind=kind).ap()

    c.xin = din("xin", [D, L])
    c.params_d = din("params", [128, 2 * PL])
    c.relb_d = din("relb", [32, 8])
    c.ident_d = din("ident", [128, 128])
    c.sel_d = din("sel", [33, 3, 256])
    c.e31_d = din("e31", [32, 128])
    c.ones64_d = din("ones64", [128, 128])
    c.fut_d = din("fut", [128, 128])
    c.causT_d = din("causT", [128, 128], I32)
    c.w = {}
    c.w["ffn1_w_gu"] = din("ffn1_w_gu", [2, D, 2 * DFF])
    c.w["ffn1_w_down"] = din("ffn1_w_down", [2, DFF, D])
    c.w["ffn2_w_gu"] = din("ffn2_w_gu", [2, D, 2 * DFF])
    c.w["ffn2_w_down"] = din("ffn2_w_down", [2, DFF, D])
    c.w["w_in"] = din("w_in", [2, D, 15184])
    c.w["w_dw_rep"] = din("w_dw_rep", [2, D, 1024])
    c.w["w_branch"] = din("w_branch", [2, 4, 512, D])
    c.w["w_out"] = din("w_out", [2, D, D])
    c.yout = nc.dram_tensor("yout", [D, L], F32, kind="ExternalOutput").ap()
    c.xres = dscr("xres", [D, L])
    c.proj = dscr("proj", [NFM, L])
    c.wabs = dscr("wabs", [1024, L], BF16)
    c.vA = dscr("vA", [L, 512], BF16)
    c.vC = dscr("vC", [L, 512], BF16)
    c.vD = dscr("vD", [L, 128], BF16)
    c.wT = dscr("wT", [L, 16])
    c.br = [dscr("br%d" % i, [512, L], BF16) for i in range(4)]
    c.gsc = dscr("gsc", [24, 128, 255])

    es = ExitStack()
    P = Prog(nc)
    c.P = P
    c.ident_f = alloc(c, es, "ident_f", [128, 128], F32)
    c.ident_bf = alloc(c, es, "ident_bf", [128, 128], BF16)
    c.ones_bf = alloc(c, es, "ones_bf", [128, 128], BF16)
    c.ones_f = alloc(c, es, "ones_f", [128, 128], F32)
    c.epsc = alloc(c, es, "epsc", [128, 1], F32)
    c.ones64_bf = alloc(c, es, "ones64_bf", [128, 128], BF16)
    c.params = alloc(c, es, "params_sb", [128, 2 * PL], F32)
    c.BT = alloc(c, es, "BT", [128, 24, 128], BF16)
    c.cbias = alloc(c, es, "cbias", [128, 8], F32)
    c.fut = alloc(c, es, "fut", [128, 128], F32)
    c.causT = alloc(c, es, "causT", [128, 128], I32)
    c.lbs = alloc(c, es, "lbs", [128, 2, 4], F32)
    c.oml = alloc(c, es, "oml", [128, 2, 4], F32)
    c.nlam = alloc(c, es, "nlam", [128, 2], F32)
    setup_phase(c)

    ph = phases
    for l in range(2):
        pb = l * PL
        first = (l == 0)
        if ph is None or ("ffn1_%d" % l) in ph:
            ffn_phase(c, c.xin if first else c.xres, c.xres, c.params[:, pb + PC["ffn1"]:pb + PC["ffn1"] + 16],
                      c.w["ffn1_w_gu"][l], c.w["ffn1_w_down"][l])
        if ph is None or ("proj_%d" % l) in ph:
            proj_phase(c, l)
        if ph is None or ("conv_%d" % l) in ph:
            conv_phase(c, l)
        if ph is None or ("hgrn_%d" % l) in ph:
            hgrn_phase(c, l)
        if ph is None or ("diff_%d" % l) in ph:
            diff_phase(c, l)
        if ph is None or ("dsa_%d" % l) in ph:
            dsa_phase(c, l)
        if ph is None or ("merge_%d" % l) in ph:
            merge_phase(c, l)
        if ph is None or ("ffn2_%d" % l) in ph:
            ffn_phase(c, c.xres, c.yout if l == 1 else c.xres, c.params[:, pb + PC["ffn2"]:pb + PC["ffn2"] + 16],
                      c.w["ffn2_w_gu"][l], c.w["ffn2_w_down"][l])
    P.close()
    es.close()
    return nc, c


def setup_phase(c):
    nc, P = c.nc, c.P
    es = ExitStack()
    P.dma("sp", c.ident_f[:], c.ident_d, [], ["const_i"], "ident_f")
    P.dma("sp", c.params[:], c.params_d, [], ["params"], "params")
    P.dma("sp", c.fut[:], c.fut_d, [], ["const_f"], "fut")
    P.dma("sp", c.causT[:], c.causT_d, [], ["const_c"], "causT")
    o64 = alloc(c, es, "o64f", [128, 128], F32)
    P.dma("sp", o64[:], c.ones64_d, [], ["o64f"], "o64f")
    P.add("dve", lambda e: e.tensor_copy(out=c.ident_bf[:], in_=c.ident_f[:]), ["const_i"], ["const"])
    P.add("dve", lambda e: e.tensor_copy(out=c.ones64_bf[:], in_=o64[:]), ["o64f"], ["const"])
    P.add("dve", lambda e: e.memset(c.ones_bf[:], 1.0), [], ["const"])
    P.add("dve", lambda e: e.memset(c.ones_f[:], 1.0), [], ["const"])
    P.add("dve", lambda e: e.memset(c.epsc[:], EPS), [], ["const"])
    tab = alloc(c, es, "tab", [32, 8], F32)
    sel = alloc(c, es, "selsb", [33, 3, 256], F32)
    e31 = alloc(c, es, "e31sb", [32, 128], F32)
    P.dma("sp", tab[:], c.relb_d, [], ["tab"], "tab")
    P.dma("sp", sel[:], c.sel_d, [], ["sel"], "sel")
    P.dma("sp", e31[:], c.e31_d, [], ["e31"], "e31")
    tabB = Rot("tabB", [alloc(c, es, "tabB%d" % i, [33, 128], F32) for i in range(2)])
    gsb = Rot("gsb", [alloc(c, es, "gsb%d" % i, [128, 255], F32) for i in range(2)])
    btf = Rot("btf", [alloc(c, es, "btf%d" % i, [128, 128], F32) for i in range(2)])
    psG = Rot("psG", [palloc(c, es, "psG%d" % i, [128, 256]) for i in range(2)])
    psc = palloc(c, es, "psc", [128, 8])
    P.add("pe", lambda e: e.matmul(psc[:, :], lhsT=e31[:, :], rhs=tab[:, :], start=True, stop=True), ["e31", "tab"], ["psc"])
    P.add("dve", lambda e: e.tensor_copy(out=c.cbias[:], in_=psc[:]), ["psc"], ["const"])
    for hh in range(8):
        tb, tbk = tabB.next()
        P.add("dve", lambda e, tb=tb: e.memset(tb[:, :], NEG), [], [tbk])
        P.add("dve", lambda e, tb=tb, hh=hh: e.tensor_scalar(out=tb[0:32, :], in0=c.ones_f[0:32, :], scalar1=tab[0:32, hh:hh + 1],
                                                             scalar2=None, op0=ALU.mult), ["tab", "const", tbk], [tbk])
        for ci in range(3):
            pg, pgk = psG.next()
            P.add("pe", lambda e, pg=pg, tb=tb, ci=ci: e.matmul(pg[:, 0:256], lhsT=tb[:, :], rhs=sel[:, ci, :], start=True, stop=True),
                  [tbk, "sel"], [pgk])
            gs, gsk = gsb.next()
            P.add("act", lambda e, gs=gs, pg=pg: e.activation(out=gs[:, :], in_=pg[:, 0:255], func=AF.Copy), [pgk], [gsk])
            idx = hh * 3 + ci
            P.dma("sp", c.gsc[idx], gs[:, :], [gsk], [("gsc", idx)], gsk)
            bt, btk = btf.next()
            skew = bass.AP(tensor=c.gsc.tensor, offset=idx * 128 * 255 + 127, ap=[[254, 128], [1, 128]])
            P.dma("sp", bt[:, :], skew, [("gsc", idx)], [btk], btk)
            P.add("dve", lambda e, bt=bt, idx=idx, hh=hh: e.tensor_scalar(out=c.BT[:, idx, :], in0=bt[:, :], scalar1=c.cbias[:, hh:hh + 1],
                                                                          scalar2=None, op0=ALU.subtract), [btk, "const"], ["const"])
    P.add("dve", lambda e: e.memset(c.lbs[:, 0, :], 0.0), [], ["const"])
    P.add("dve", lambda e: e.memset(c.oml[:, 0, :], 1.0), [], ["const"])
    dl = alloc(c, es, "dl", [128, 4], F32)
    P.add("dve", lambda e: e.tensor_tensor(out=dl[:, :], in0=c.params[:, PL + PC["lb"]:PL + PC["lb"] + 4],
                                           in1=c.params[:, PC["lb"]:PC["lb"] + 4], op=ALU.subtract), ["params"], ["dl"])
    P.add("act", lambda e: e.activation(out=c.lbs[:, 1, :], in_=dl[:, :], func=AF.Sigmoid), ["dl"], ["const"])
    P.add("dve", lambda e: e.tensor_scalar(out=c.oml[:, 1, :], in0=c.lbs[:, 1, :], scalar1=-1.0, scalar2=1.0,
                                           op0=ALU.mult, op1=ALU.add), ["const"], ["const"])
    pr = alloc(c, es, "pr", [128, 4], F32)
    psl = palloc(c, es, "psl", [128, 4])
    el = alloc(c, es, "el", [128, 4], F32)
    for l in range(2):
        b = l * PL + PC["lam"]
        P.add("dve", lambda e, l=l, b=b: e.tensor_tensor(out=pr[:, 2 * l:2 * l + 1], in0=c.params[:, b:b + 1], in1=c.params[:, b + 1:b + 2],
                                                         op=ALU.mult), ["params"], ["pr"])
        P.add("dve", lambda e, l=l, b=b: e.tensor_tensor(out=pr[:, 2 * l + 1:2 * l + 2], in0=c.params[:, b + 2:b + 3], in1=c.params[:, b + 3:b + 4],
                                                         op=ALU.mult), ["params"], ["pr"])
    P.add("pe", lambda e: e.matmul(psl[:, :], lhsT=c.ones_f[:, :], rhs=pr[:, :], start=True, stop=True), ["pr", "const"], ["psl"])
    P.add("act", lambda e: e.activation(out=el[:, :], in_=psl[:, :], func=AF.Exp), ["psl"], ["el"])
    for l in range(2):
        lam_init = 0.8 - 0.6 * math.exp(-0.3 * l)
        P.add("dve", lambda e, l=l, li=lam_init: e.scalar_tensor_tensor(
            out=c.nlam[:, l:l + 1], in0=el[:, 2 * l + 1:2 * l + 2], scalar=-li, in1=el[:, 2 * l:2 * l + 1],
            op0=ALU.add, op1=ALU.subtract), ["el"], ["const"])
    P.flush()
    es.close()


_CACHE = {}


def host_inputs(inp, b, consts, params, w_dw_rep):
    x = inp["x"]
    xin = np.ascontiguousarray(np.concatenate([inp["meta_tokens"].T, x[b].T], axis=1), dtype=np.float32)
    m = {"xin": xin, "params": params, "relb": np.ascontiguousarray(inp["rel_bias"], dtype=np.float32),
         "ident": consts["ident"], "sel": consts["sel"], "e31": consts["e31"], "ones64": consts["ones64"],
         "fut": consts["fut"], "causT": consts["causT"], "w_dw_rep": w_dw_rep}
    for k in WNAMES:
        m[k] = np.ascontiguousarray(inp[k], dtype=np.float32)
    return m


def run(inp, phases=None, dbg=False, G=4, trace=False):
    inp = {k: np.asarray(v) for k, v in inp.items()}
    B, S, _ = inp["x"].shape
    topk = min(256, S // 4)
    nc, c = build(S, topk, G=G, phases=phases, dbg=dbg)
    consts = make_consts()
    params = pack_params(inp)
    w16 = inp["w_in"][:, :, OFF["d_w"]:OFF["d_w"] + 16]
    w_dw_rep = np.ascontiguousarray(np.repeat(w16, 64, axis=2), dtype=np.float32)
    in_maps = [host_inputs(inp, b, consts, params, w_dw_rep) for b in range(B)]
    res = run_bass_kernel_spmd(nc, in_maps, core_ids=list(range(B)), trace=trace)
    return res, c


def kernel(**inputs):
    res, c = run(inputs)
    outs = [r["yout"] for r in res.results]
    y = np.stack([np.ascontiguousarray(o[:, 16:].T) for o in outs], axis=0)
    return y.astype(np.float32)


def fm_chunks():
    skip = [(OFF["a_i"], OFF["a_g"]), (OFF["c_v"], OFF["d_q"]), (OFF["d_v"], OFF["d_qi"])]
    out = []
    c0 = 0
    while c0 < NFM:
        n = min(128, NFM - c0)
        if not any(a <= c0 < b for a, b in skip):
            out.append((c0, n))
        c0 += n
    return out


def proj_phase(c, l):
    nc, P = c.nc, c.P
    es = ExitStack()
    groups = make_groups(c.L, c.G_proj)
    NG = max(g[1] for g in groups)
    pb = l * PL
    nb = norm_bufs(c, es)
    hbuf = alloc(c, es, "hbuf", [128, NCH, NG], BF16)
    wfm = Rot("wfm", [alloc(c, es, "wfm%d" % i, [128, NCH, 128], BF16) for i in range(3)])
    wtm = Rot("wtm", [alloc(c, es, "wtm%d" % i, [128, NCH, 512], BF16) for i in range(2)])
    evs = Rot("ev", [alloc(c, es, "ev%d" % i, [128, 512], F32) for i in range(2)])
    evb = Rot("evb", [alloc(c, es, "evb%d" % i, [128, 512], BF16) for i in range(2)])
    ps = Rot("ps", [palloc(c, es, "ps%d" % i, [128, 512]) for i in range(4)])
    win = fm(c.w["w_in"][l])
    wrep = fm(c.w["w_dw_rep"][l])
    gcol = c.params[:, pb + PC["mix"]:pb + PC["mix"] + 16]
    hbufs = [hbuf, alloc(c, es, "hbuf2", [128, NCH, NG], BF16)]
    tmc_ = [0]
    for gi_, (g0, ng, subs, tiles) in enumerate(groups):
        hbuf = hbufs[gi_ % 2]
        hk = ("hbuf", gi_ % 2)
        if gi_ == 0:
            norm_stage(c, nb, c.xres, gcol, hbuf, hk, g0, ng)
        if gi_ + 1 < len(groups):
            ngen = norm_gen(c, nb, c.xres, gcol, hbufs[(gi_ + 1) % 2], ("hbuf", (gi_ + 1) % 2), groups[gi_ + 1][0], groups[gi_ + 1][1])
        else:
            ngen = iter(())
        for (c0, mc) in fm_chunks():
            w, wk = wfm.next()
            P.dma("pool", w[:, :, 0:mc], win[:, :, c0:c0 + mc], [], [wk], wk)
            for (t0, n) in subs:
                off = t0 - g0
                p, pk = ps.next()
                for ch in range(NCH):
                    P.add("pe", lambda e, p=p, w=w, ch=ch, off=off, n=n, mc=mc, hbuf=hbuf: e.matmul(
                        p[0:mc, 0:n], lhsT=w[:, ch, 0:mc], rhs=hbuf[:, ch, off:off + n], start=(ch == 0), stop=(ch == NCH - 1)),
                        [wk, hk], [pk])
                ev, ek = evs.next()
                P.add("act", lambda e, ev=ev, p=p, n=n, mc=mc: e.activation(out=ev[0:mc, 0:n], in_=p[0:mc, 0:n], func=AF.Copy),
                      [pk], [ek])
                P.dma("sp", c.proj[c0:c0 + mc, t0:t0 + n], ev[0:mc, 0:n], [ek], [], ek)
        for r in range(8):
            w, wk = wfm.next()
            P.dma("pool", w[:, :, :], wrep[:, :, r * 128:(r + 1) * 128], [], [wk], wk)
            for (t0, n) in subs:
                off = t0 - g0
                p, pk = ps.next()
                for ch in range(NCH):
                    P.add("pe", lambda e, p=p, w=w, ch=ch, off=off, n=n, hbuf=hbuf: e.matmul(
                        p[:, 0:n], lhsT=w[:, ch, :], rhs=hbuf[:, ch, off:off + n], start=(ch == 0), stop=(ch == NCH - 1)),
                        [wk, hk], [pk])
                ev, ek = evb.next()
                P.add("act", lambda e, ev=ev, p=p, n=n: e.activation(out=ev[:, 0:n], in_=p[:, 0:n], func=AF.Abs), [pk], [ek])
                P.dma("sp", c.wabs[r * 128:(r + 1) * 128, t0:t0 + n], ev[:, 0:n], [ek], [], ek)
        for (c0, ncol, dst, isf32) in ((OFF["a_i"], 512, c.vA, False), (OFF["c_v"], 512, c.vC, False),
                                       (OFF["d_v"], 128, c.vD, False), (OFF["d_w"], 16, c.wT, True)):
            w, wk = wtm.next()
            P.dma("pool", w[:, :, 0:ncol], win[:, :, c0:c0 + ncol], [], [wk], wk)
            for ti in tiles:
                t0, nt = c.tiles[ti]
                off = t0 - g0
                p, pk = ps.next()
                for ch in range(NCH):
                    P.add("pe", lambda e, p=p, w=w, ch=ch, off=off, nt=nt, ncol=ncol, hbuf=hbuf: e.matmul(
                        p[0:nt, 0:ncol], lhsT=hbuf[:, ch, off:off + nt], rhs=w[:, ch, 0:ncol], start=(ch == 0), stop=(ch == NCH - 1)),
                        [wk, hk], [pk])
                if isf32:
                    ev, ek = evs.next()
                else:
                    ev, ek = evb.next()
                P.add("act", lambda e, ev=ev, p=p, nt=nt, ncol=ncol: e.activation(out=ev[0:nt, 0:ncol], in_=p[0:nt, 0:ncol], func=AF.Copy),
                      [pk], [ek])
                P.dma("sp", dst[t0:t0 + nt, 0:ncol], ev[0:nt, 0:ncol], [ek], [], ek)
                tmc_[0] += 1
                if tmc_[0] % 3 == 0:
                    next(ngen, None)
        for _ in ngen:
            pass
    P.flush()
    es.close()


def conv_phase(c, l):
    nc, P = c.nc, c.P
    es = ExitStack()
    L = c.L
    pb = l * PL + PC["conv"]
    bb = Rot("bb", [alloc(c, es, "bb%d" % i, [128, L], F32) for i in range(2)])
    bc = Rot("bc", [alloc(c, es, "bc%d" % i, [128, L], F32) for i in range(2)])
    bu = Rot("bu", [alloc(c, es, "bu%d" % i, [128, L], F32) for i in range(2)])
    zc = alloc(c, es, "zc", [128, L + 2], F32)
    yb = alloc(c, es, "yb", [128, L], F32)
    ob = Rot("ob", [alloc(c, es, "ob%d" % i, [128, L], BF16) for i in range(2)])
    P.add("dve", lambda e: e.memset(zc[:, 0:2], 0.0), [], ["zc0"])
    for ch in range(4):
        tb, tbk = bb.next()
        tc_, tck = bc.next()
        tu, tuk = bu.next()
        P.dma("sp", tb[:, :], c.proj[OFF["b_b"] + ch * 128:OFF["b_b"] + (ch + 1) * 128, :], [], [tbk], tbk)
        P.dma("sp", tc_[:, :], c.proj[OFF["b_c"] + ch * 128:OFF["b_c"] + (ch + 1) * 128, :], [], [tck], tck)
        P.dma("sp", tu[:, :], c.proj[OFF["b_u"] + ch * 128:OFF["b_u"] + (ch + 1) * 128, :], [], [tuk], tuk)
        P.add("pool", lambda e, tc_=tc_, tu=tu: e.tensor_tensor(out=zc[:, 2:L + 2], in0=tc_[:, :], in1=tu[:, :], op=ALU.mult),
              [tck, tuk], ["zc"])
        w0 = c.params[:, pb + ch:pb + ch + 1]
        w1 = c.params[:, pb + 4 + ch:pb + 4 + ch + 1]
        w2 = c.params[:, pb + 8 + ch:pb + 8 + ch + 1]
        P.add("dve", lambda e, w0=w0: e.tensor_scalar(out=yb[:, :], in0=zc[:, 2:L + 2], scalar1=w0, scalar2=None, op0=ALU.mult),
              ["zc", "zc0", "params"], ["yb"])
        P.add("dve", lambda e, w1=w1: e.scalar_tensor_tensor(out=yb[:, :], in0=zc[:, 1:L + 1], scalar=w1, in1=yb[:, :],
                                                             op0=ALU.mult, op1=ALU.add), ["zc", "zc0", "yb", "params"], ["yb"])
        P.add("dve", lambda e, w2=w2: e.scalar_tensor_tensor(out=yb[:, :], in0=zc[:, 0:L], scalar=w2, in1=yb[:, :],
                                                             op0=ALU.mult, op1=ALU.add), ["zc", "zc0", "yb", "params"], ["yb"])
        o, ok = ob.next()
        P.add("dve", lambda e, o=o, tb=tb: e.tensor_tensor(out=o[:, :], in0=yb[:, :], in1=tb[:, :], op=ALU.mult), ["yb", tbk], [ok])
        P.dma("sp", c.br[1][ch * 128:(ch + 1) * 128, :], o[:, :], [ok], [], ok)
    P.flush()
    es.close()


def merge_phase(c, l):
    nc, P = c.nc, c.P
    es = ExitStack()
    groups = make_groups(c.L, c.G_merge)
    NG = max(g[1] for g in groups)
    pb = l * PL
    nb = norm_bufs(c, es)
    hbuf = alloc(c, es, "hbuf", [128, NCH, NG], BF16)
    brb = alloc(c, es, "brb", [128, 16, NG], BF16)
    mrg = alloc(c, es, "mrg", [128, NCH, NG], BF16)
    wg = Rot("wg", [alloc(c, es, "wg%d" % i, [128, NCH, 128], BF16) for i in range(3)])
    wb = Rot("wb", [alloc(c, es, "wb%d" % i, [128, 4, 128], BF16) for i in range(3)])
    wo = Rot("wo", [alloc(c, es, "wo%d" % i, [128, NCH, 128], BF16) for i in range(2)])
    sgs = Rot("sg", [alloc(c, es, "sg%d" % i, [128, 512], F32) for i in range(2)])
    macc = Rot("macc", [alloc(c, es, "macc%d" % i, [128, 512], F32) for i in range(3)])
    tmps = Rot("tmp", [alloc(c, es, "tmp%d" % i, [128, 512], F32) for i in range(2)])
    xrs = Rot("xr", [alloc(c, es, "xr%d" % i, [128, 512], F32) for i in range(3)])
    xos = Rot("xo", [alloc(c, es, "xo%d" % i, [128, 512], F32) for i in range(3)])
    psg = Rot("psg", [palloc(c, es, "psg%d" % i, [128, 512]) for i in range(2)])
    psb = Rot("psb", [palloc(c, es, "psb%d" % i, [128, 512]) for i in range(2)])
    pso = Rot("pso", [palloc(c, es, "pso%d" % i, [128, 512]) for i in range(2)])
    win = fm(c.w["w_in"][l])
    wout = fm(c.w["w_out"][l])
    gcol = c.params[:, pb + PC["mix"]:pb + PC["mix"] + 16]
    xv = fm(c.xres)
    for gi_, (g0, ng, subs, tiles) in enumerate(groups):
        if gi_ == 0:
            norm_stage(c, nb, c.xres, gcol, hbuf, "hbuf", g0, ng)
        if gi_ + 1 < len(groups):
            ngen = norm_gen(c, nb, c.xres, gcol, hbuf, "hbuf", groups[gi_ + 1][0], groups[gi_ + 1][1])
        else:
            ngen = iter(())
        for br in range(4):
            P.dma("sp", brb[:, br * 4:(br + 1) * 4, 0:ng], fm(c.br[br])[:, :, g0:g0 + ng], [], [("brb", br)], ("brb", br))
        for fc in range(NCH):
            accs = {}
            for br in range(4):
                w, wk = wg.next()
                gc0 = OFF["gate"] + br * D + fc * 128
                P.dma("pool", w[:, :, :], win[:, :, gc0:gc0 + 128], [], [wk], wk)
                w2, w2k = wb.next()
                P.dma("pool", w2[:, :, :], c.w["w_branch"][l, br].rearrange("(kc p) m -> p kc m", p=128)[:, :, fc * 128:(fc + 1) * 128],
                      [], [w2k], w2k)
                for si, (t0, n) in enumerate(subs):
                    off = t0 - g0
                    pg, pgk = psg.next()
                    pbr, pbk = psb.next()
                    for ch in range(NCH):
                        P.add("pe", lambda e, pg=pg, w=w, ch=ch, off=off, n=n: e.matmul(
                            pg[:, 0:n], lhsT=w[:, ch, :], rhs=hbuf[:, ch, off:off + n], start=(ch == 0), stop=(ch == NCH - 1)),
                            [wk, "hbuf"], [pgk])
                    for kc in range(4):
                        P.add("pe", lambda e, pbr=pbr, w2=w2, kc=kc, br=br, off=off, n=n: e.matmul(
                            pbr[:, 0:n], lhsT=w2[:, kc, :], rhs=brb[:, br * 4 + kc, off:off + n], start=(kc == 0), stop=(kc == 3)),
                            [w2k, ("brb", br)], [pbk])
                    sg, sgk = sgs.next()
                    P.add("act", lambda e, sg=sg, pg=pg, n=n: e.activation(out=sg[:, 0:n], in_=pg[:, 0:n], func=AF.Sigmoid),
                          [pgk], [sgk])
                    if br == 0:
                        accs[si] = macc.next()
                        ma, mak = accs[si]
                        P.add("dve", lambda e, ma=ma, sg=sg, pbr=pbr, n=n: e.tensor_tensor(
                            out=ma[:, 0:n], in0=sg[:, 0:n], in1=pbr[:, 0:n], op=ALU.mult), [sgk, pbk], [mak])
                    else:
                        ma, mak = accs[si]
                        tm, tmk = tmps.next()
                        P.add("dve", lambda e, tm=tm, sg=sg, pbr=pbr, n=n: e.tensor_tensor(
                            out=tm[:, 0:n], in0=sg[:, 0:n], in1=pbr[:, 0:n], op=ALU.mult), [sgk, pbk], [tmk])
                        if br < 3:
                            P.add("dve", lambda e, ma=ma, tm=tm, n=n: e.tensor_tensor(
                                out=ma[:, 0:n], in0=ma[:, 0:n], in1=tm[:, 0:n], op=ALU.add), [mak, tmk], [mak])
                        else:
                            P.add("dve", lambda e, ma=ma, tm=tm, n=n, fc=fc, off=off: e.tensor_tensor(
                                out=mrg[:, fc, off:off + n], in0=ma[:, 0:n], in1=tm[:, 0:n], op=ALU.add), [mak, tmk], [("mrg", fc)])
        for m in range(NCH):
            w, wk = wo.next()
            P.dma("pool", w[:, :, :], wout[:, :, m * 128:(m + 1) * 128], [], [wk], wk)
            for (t0, n) in subs:
                off = t0 - g0
                xr, xrk = xrs.next()
                P.dma("sp", xr[:, 0:n], xv[:, m, t0:t0 + n], [], [xrk], xrk)
                po, pok = pso.next()
                for ch in range(NCH):
                    P.add("pe", lambda e, po=po, w=w, ch=ch, off=off, n=n: e.matmul(
                        po[:, 0:n], lhsT=w[:, ch, :], rhs=mrg[:, ch, off:off + n], start=(ch == 0), stop=(ch == NCH - 1)),
                        [wk, ("mrg", ch)], [pok])
                xo, xok = xos.next()
                P.add("dve", lambda e, xo=xo, po=po, xr=xr, n=n: e.tensor_tensor(
                    out=xo[:, 0:n], in0=po[:, 0:n], in1=xr[:, 0:n], op=ALU.add), [pok, xrk], [xok])
                P.dma("act", xv[:, m, t0:t0 + n], xo[:, 0:n], [xok], [], xok)
            if m >= 2:
                next(ngen, None)
        for _ in ngen:
            pass
    P.flush()
    es.close()


def load_rows_norm(c, bufs, row0, nrows, out_fn, gsc_col, ones_m, gsize, key_out, dup64=False):
    P = c.P
    st, sq, rs, psn = bufs
    for (t0, n) in split_cols(0, c.L, 512):
        s, sk = st.next()
        if dup64:
            P.dma("sp", s[0:64, 0:n], c.proj[row0:row0 + 64, t0:t0 + n], [], [sk], sk)
            P.dma("sp", s[64:128, 0:n], c.proj[row0:row0 + 64, t0:t0 + n], [], [sk], sk)
        else:
            P.dma("sp", s[0:nrows, 0:n], c.proj[row0:row0 + nrows, t0:t0 + n], [], [sk], sk)
        if gsize is None:
            P.add("act", lambda e, s=s, t0=t0, n=n: e.activation(out=out_fn(t0, n), in_=s[:, 0:n], func=AF.Copy), [sk], [key_out])
            continue
        q, qk = sq.next()
        P.add("act", lambda e, q=q, s=s, n=n: e.activation(out=q[:, 0:n], in_=s[:, 0:n], func=AF.Square), [sk], [qk])
        P.add("pe", lambda e, q=q, n=n: e.matmul(psn[:, 0:n], lhsT=ones_m[:, :], rhs=q[:, 0:n], start=True, stop=True),
              [qk, "const"], ["psn"])
        r, rk = rs.next()
        rsqrt_op(c, r[:, 0:n], psn[:, 0:n], 1.0 / gsize, ["psn"], rk)
        P.add("dve", lambda e, s=s, r=r, t0=t0, n=n: e.scalar_tensor_tensor(
            out=out_fn(t0, n), in0=s[:, 0:n], scalar=gsc_col, in1=r[:, 0:n], op0=ALU.mult, op1=ALU.mult),
            [sk, rk, "gsc"], [key_out])


def rows_bufs(c, es):
    st = Rot("st", [alloc(c, es, "st%d" % i, [128, 512], F32) for i in range(2)])
    sq = Rot("sqq", [alloc(c, es, "sqq%d" % i, [128, 512], BF16) for i in range(2)])
    rs = Rot("rs", [alloc(c, es, "rs%d" % i, [128, 512], F32) for i in range(2)])
    psn = palloc(c, es, "psn", [128, 512])
    return (st, sq, rs, psn)


def near_case(i, j):
    if j == i:
        return 0
    if j >= 1 and j == i - 1:
        return 1
    if j == 0 and i == 1:
        return 2
    return None


def load_vtm(c, vsb, src, col0, ncol, key, pitch_view):
    P = c.P
    nt_full = len(c.tiles) - 1
    P.dma("sp", pitch_view(0, 16), src[0:16, col0:col0 + ncol], [], [key], key)
    for ti in range(1, nt_full + 1):
        t0, nt = c.tiles[ti]
        P.dma("sp", pitch_view(ti, nt), src[t0:t0 + nt, col0:col0 + ncol], [], [key], key)


def diff_phase(c, l):
    nc, P = c.nc, c.P
    es = ExitStack()
    L = c.L
    NT = len(c.tiles)
    pb = l * PL
    lam_init = 0.8 - 0.6 * math.exp(-0.3 * l)
    rb = rows_bufs(c, es)
    qC = alloc(c, es, "qC", [128, 4, L], BF16)
    kC = alloc(c, es, "kC", [128, 4, L], BF16)
    vC = alloc(c, es, "vCs", [128, NT, 4, 129], BF16)
    ob = alloc(c, es, "obC", [128, 4, L], BF16)
    gs = alloc(c, es, "gsC", [128, 4], F32)
    zc = alloc(c, es, "zcol", [128, 1], F32)
    pTs = Rot("pT", [alloc(c, es, "pT%d" % i, [128, 512], BF16) for i in range(3)])
    rr = Rot("rr", [alloc(c, es, "rr%d" % i, [128, 4], F32) for i in range(2)])
    ods = Rot("od", [alloc(c, es, "od%d" % i, [128, 128], F32) for i in range(2)])
    junk = alloc(c, es, "junk", [128, 128], F32)
    ons = Rot("on", [alloc(c, es, "on%d" % i, [128, 128], BF16) for i in range(2)])
    pss = Rot("pss", [palloc(c, es, "pss%d" % i, [128, 512]) for i in range(4)])
    pso = Rot("pso", [palloc(c, es, "pso%d" % i, [128, 512]) for i in range(2)])
    pst = palloc(c, es, "pst", [128, 512])
    posb = Rot("posb", [alloc(c, es, "posb%d" % i, [128, 264], F32) for i in range(2)])
    qzs = Rot("qz", [alloc(c, es, "qz%d" % i, [128, 256], BF16) for i in range(2)])
    for k_ in range(2):
        P.add("pool", lambda e, k_=k_: e.memset(qzs.t[k_][:, :], 0.0), [], [("qz", k_)])
    P.add("dve", lambda e: e.tensor_scalar(out=gs[:, 0:1], in0=c.params[:, pb + PC["dqn"]:pb + PC["dqn"] + 1], scalar1=0.125,
                                           scalar2=None, op0=ALU.mult), ["params"], ["gsc"])
    P.add("dve", lambda e: e.tensor_copy(out=gs[:, 1:2], in_=c.params[:, pb + PC["dkn"]:pb + PC["dkn"] + 1]), ["params"], ["gsc"])
    P.add("dve", lambda e: e.tensor_scalar(out=gs[:, 2:3], in0=c.params[:, pb + PC["subln"]:pb + PC["subln"] + 1],
                                           scalar1=1.0 - lam_init, scalar2=None, op0=ALU.mult), ["params"], ["gsc"])
    P.add("dve", lambda e: e.memset(zc[:, :], 0.0), [], ["gsc"])
    P.add("dve", lambda e: e.memset(vC[:, :, :, 128:129], 1.0), [], ["vC1"])
    for h in range(4):
        load_rows_norm(c, rb, OFF["c_q"] + h * 128, 128, lambda t0, n, h=h: qC[:, h, t0:t0 + n], gs[:, 0:1], c.ones64_bf, 64, ("qC", h))
        load_rows_norm(c, rb, OFF["c_k"] + h * 128, 128, lambda t0, n, h=h: kC[:, h, t0:t0 + n], gs[:, 1:2], c.ones64_bf, 64, ("kC", h))
    for ti in range(NT):
        t0, nt = c.tiles[ti]
        P.dma("sp", vC[0:nt, ti, :, 0:128], c.vC[t0:t0 + nt, :].rearrange("t (h e) -> t h e", h=4), [], ["vC"], "vC")
    c.prev_tail = None
    for i_ in range(NT):
      for h_ in range(4):
        def do_block(i, h, q0, nq):
            pos = [pso.next(), pso.next()]
            groups_ = [jg for jg in range(0, i + 1, 2)]
            psl = {}
            qz, qzk = qzs.next()
            P.add("pool", lambda e: e.tensor_copy(out=qz[0:64, 0:nq], in_=qC[0:64, h, q0:q0 + nq]), [("qC", h)], [qzk])
            P.add("pool", lambda e: e.tensor_copy(out=qz[64:128, 128:128 + nq], in_=qC[64:128, h, q0:q0 + nq]), [("qC", h)], [qzk])

            def emit_s(gi):
                jg = groups_[gi]
                grp = list(range(jg, min(i + 1, jg + 2)))
                ps, psk = pss.next()
                psl[gi] = (ps, psk, grp)
                for sl, j in enumerate(grp):
                    k0, nk = c.tiles[j]
                    case = near_case(i, j)
                    P.add("pe", lambda e, ps=ps, k0=k0, nk=nk, case=case, sl=sl: e.matmul(
                        ps[0:nk, sl * 256:sl * 256 + 128 + nq], lhsT=kC[:, h, k0:k0 + nk],
                        rhs=qz[:, 0:128 + nq], start=True, stop=(case is None)), [("kC", h), qzk], [psk])
                    if case is not None:
                        for cc in range(2):
                            P.add("pe", lambda e, ps=ps, nk=nk, case=case, sl=sl, cc=cc: e.matmul(
                                ps[0:nk, sl * 256 + cc * 128:sl * 256 + cc * 128 + nq], lhsT=c.ident_bf[0:nk, 0:nk],
                                rhs=c.BT[0:nk, h * 3 + case, 0:nq], start=False, stop=(cc == 1)), ["const"], [psk])

            def emit_pv(gi):
                ps, psk, grp = psl[gi]
                W = (len(grp) - 1) * 256 + 128 + nq
                pT, pTk = pTs.next()
                P.add("act", lambda e, pT=pT, ps=ps, W=W: e.activation(out=pT[:, 0:W], in_=ps[:, 0:W], func=AF.Exp), [psk], [pTk])
                for sl, j in enumerate(grp):
                    k0, nk = c.tiles[j]
                    for cc in range(2):
                        po, pok = pos[cc]
                        P.add("pe", lambda e, po=po, pT=pT, nk=nk, j=j, sl=sl, cc=cc: e.matmul(
                            po[0:nq, 0:129], lhsT=pT[0:nk, sl * 256 + cc * 128:sl * 256 + cc * 128 + nq], rhs=vC[0:nk, j, h, 0:129],
                            start=(j == 0), stop=(j == i)), [pTk, "vC", "vC1"], [pok])

            emit_s(0)
            for gi in range(len(groups_)):
                if gi + 1 < len(groups_):
                    emit_s(gi + 1)
                emit_pv(gi)
            ob2, ob2k = posb.next()
            P.add("act", lambda e, ob2=ob2: e.activation(out=ob2[0:nq, 0:129], in_=pos[0][0][0:nq, 0:129], func=AF.Copy), [pos[0][1]], [ob2k])
            P.add("act", lambda e, ob2=ob2: e.activation(out=ob2[0:nq, 132:261], in_=pos[1][0][0:nq, 0:129], func=AF.Copy), [pos[1][1]], [ob2k])
            p0, p0k = ob2[:, 0:132], ob2k
            p1, p1k = ob2[:, 132:264], ob2k
            r, rk = rr.next()
            P.add("dve", lambda e, r=r, p0=p0, nq=nq: e.reciprocal(out=r[0:nq, 0:1], in_=p0[0:nq, 128:129]), [p0k], [rk])
            P.add("dve", lambda e, r=r, p1=p1, nq=nq: e.reciprocal(out=r[0:nq, 1:2], in_=p1[0:nq, 128:129]), [p1k], [rk])
            P.add("dve", lambda e, r=r, nq=nq: e.tensor_tensor(out=r[0:nq, 2:3], in0=r[0:nq, 1:2], in1=c.nlam[0:nq, l:l + 1], op=ALU.mult),
                  [rk, "const"], [rk])
            od, odk = ods.next()
            P.add("dve", lambda e, od=od, p0=p0, r=r, nq=nq: e.tensor_scalar(out=od[0:nq, :], in0=p0[0:nq, 0:128], scalar1=r[0:nq, 0:1],
                                                                              scalar2=None, op0=ALU.mult), [p0k, rk], [odk])
            P.add("dve", lambda e, od=od, p1=p1, r=r, nq=nq: e.scalar_tensor_tensor(
                out=od[0:nq, :], in0=p1[0:nq, 0:128], scalar=r[0:nq, 2:3], in1=od[0:nq, :], op0=ALU.mult, op1=ALU.add),
                [p1k, rk, odk], [odk])
            P.add("dve", lambda e, od=od, nq=nq: e.tensor_tensor(out=junk[0:nq, :], in0=od[0:nq, :], in1=od[0:nq, :], op=ALU.mult),
                  [odk], ["junk"])
            P.add("dve", lambda e, r=r, nq=nq: e.tensor_reduce(out=r[0:nq, 3:4], in_=junk[0:nq, :], axis=AX.X, op=ALU.add),
                  ["junk", rk], [rk])
            rsqrt_op(c, r[0:nq, 3:4], r[0:nq, 3:4], 1.0 / 128, [rk], rk, np_=nq)
            on, onk = ons.next()
            P.add("dve", lambda e, on=on, od=od, r=r, nq=nq: e.tensor_scalar(out=on[0:nq, :], in0=od[0:nq, :], scalar1=r[0:nq, 3:4],
                                                                              scalar2=None, op0=ALU.mult), [odk, rk], [onk])
            def tail():
                P.add("pe", lambda e, on=on, nq=nq: e.matmul(pst[:, 0:nq], lhsT=on[0:nq, :], rhs=c.ident_bf[0:nq, 0:nq], start=True, stop=True),
                      [onk, "const"], ["pst"])
                P.add("act", lambda e, h=h, q0=q0, nq=nq: e.activation(out=ob[:, h, q0:q0 + nq], in_=pst[:, 0:nq], func=AF.Copy,
                                                                        scale=gs[:, 2:3]), ["pst", "gsc"], ["obC"])
            return tail
        tl_ = do_block(i_, h_, c.tiles[i_][0], c.tiles[i_][1])
        if c.prev_tail is not None:
            c.prev_tail()
        c.prev_tail = tl_
    c.prev_tail()
    P.dma("sp", fm(c.br[2]), ob[:, :, :], ["obC"], [], "obC")
    P.flush()
    es.close()


def hgrn_phase(c, l):
    nc, P = c.nc, c.P
    es = ExitStack()
    L = c.L
    NT = len(c.tiles)
    pb = l * PL
    zf = alloc(c, es, "zf", [128, L], F32)
    bb = alloc(c, es, "bb", [128, L], F32)
    kf = alloc(c, es, "kf", [128, L], F32)
    qf = alloc(c, es, "qf", [128, L], F32)
    tmp = alloc(c, es, "tmpE", [128, L], F32)
    qt = alloc(c, es, "qt", [128, L], BF16)
    qh = alloc(c, es, "qh", [128, L], BF16)
    kh = alloc(c, es, "kh", [128, L], BF16)
    khT = alloc(c, es, "khT", [128, NT, 128], BF16)
    vA = alloc(c, es, "vAs", [128, NT, 128], BF16)
    osb = alloc(c, es, "osb", [128, L], BF16)
    obr = alloc(c, es, "obr", [128, L], BF16)
    e1 = alloc(c, es, "e1", [128, NT], F32)
    e2 = alloc(c, es, "e2", [128, NT], F32)
    Sf = alloc(c, es, "Sf", [128, 128], F32)
    St = alloc(c, es, "St", [128, 128], F32)
    Sb = Rot("Sb", [alloc(c, es, "Sb%d" % i, [128, 128], BF16) for i in range(2)])
    Pm = Rot("Pm", [alloc(c, es, "Pm%d" % i, [128, 128], BF16) for i in range(2)])
    sq = Rot("sqh", [alloc(c, es, "sqh%d" % i, [128, 512], BF16) for i in range(2)])
    rs = Rot("rsh", [alloc(c, es, "rsh%d" % i, [128, 512], F32) for i in range(2)])
    pss = Rot("pss", [palloc(c, es, "pss%d" % i, [128, 512]) for i in range(2)])
    pso = Rot("pso", [palloc(c, es, "pso%d" % i, [128, 512]) for i in range(2)])
    psd = Rot("psd", [palloc(c, es, "psd%d" % i, [128, 512]) for i in range(2)])
    pstb = palloc(c, es, "pstb", [128, 128], BF16)
    psn = palloc(c, es, "psnh", [128, 512])
    for k in range(2):
        P.add("pool", lambda e, k=k: e.memset(Pm.t[k][:, :], 0.0), [], [("Pm", k)])
    for h in range(4):
        lbc = c.lbs[:, l, h:h + 1]
        omc = c.oml[:, l, h:h + 1]
        P.dma("sp", zf[:, :], c.proj[OFF["a_f"] + h * 128:OFF["a_f"] + (h + 1) * 128, :], [], ["zf"], "zf")
        P.dma("sp", qf[:, :], c.proj[OFF["a_q"] + h * 128:OFF["a_q"] + (h + 1) * 128, :], [], ["qf"], "qf")
        load_vtm(c, vA, c.vA, h * 128, 128, "vA", lambda ti, nt: vA[0:nt, ti, :])
        P.add("act", lambda e: e.activation(out=zf[:, :], in_=zf[:, :], func=AF.Sigmoid), ["zf"], ["zf"])
        P.add("dve", lambda e, lbc=lbc, omc=omc: e.tensor_scalar(out=zf[:, :], in0=zf[:, :], scalar1=omc, scalar2=lbc,
                                                                 op0=ALU.mult, op1=ALU.add), ["zf", "const"], ["zf"])
        P.add("dve", lambda e: e.tensor_scalar(out=kf[:, :], in0=zf[:, :], scalar1=-1.0, scalar2=1.0, op0=ALU.mult, op1=ALU.add),
              ["zf"], ["kf"])
        P.add("act", lambda e: e.activation(out=zf[:, :], in_=zf[:, :], func=AF.Ln), ["zf", "kf"], ["zf"])
        for ti in range(NT):
            t0, nt = c.tiles[ti]
            P.add("dve", lambda e, t0=t0, nt=nt: e.tensor_tensor_scan(out=bb[:, t0:t0 + nt], data0=c.ones_f[:, 0:nt], data1=zf[:, t0:t0 + nt],
                                                                      initial=0.0, op0=ALU.mult, op1=ALU.add), ["zf", "const"], ["bb"])
        for ti in range(NT):
            t0, nt = c.tiles[ti]
            mid = t0 + nt // 2
            P.add("dve", lambda e, t0=t0, nt=nt, mid=mid: e.tensor_scalar(out=zf[:, t0:t0 + nt], in0=bb[:, t0:t0 + nt], scalar1=bb[:, mid:mid + 1],
                                                                          scalar2=None, op0=ALU.subtract), ["bb", "zf"], ["zf"])
        for ti in range(NT):
            t0, nt = c.tiles[ti]
            P.add("act", lambda e, ti=ti, t0=t0, nt=nt: e.activation(out=e1[:, ti:ti + 1], in_=bb[:, t0 + nt - 1:t0 + nt], func=AF.Exp),
                  ["bb"], ["e1"])
            P.add("act", lambda e, ti=ti, t0=t0, nt=nt: e.activation(out=e2[:, ti:ti + 1], in_=zf[:, t0 + nt - 1:t0 + nt], func=AF.Exp),
                  ["zf"], ["e2"])
        P.add("act", lambda e: e.activation(out=qf[:, :], in_=qf[:, :], func=AF.Silu), ["qf"], ["qf"])
        P.add("act", lambda e: e.activation(out=tmp[:, :], in_=bb[:, :], func=AF.Exp), ["bb"], ["tmp"])
        P.add("dve", lambda e: e.tensor_tensor(out=qt[:, :], in0=qf[:, :], in1=tmp[:, :], op=ALU.mult), ["qf", "tmp"], ["qt"])
        P.add("act", lambda e: e.activation(out=tmp[:, :], in_=zf[:, :], func=AF.Exp), ["zf", "qt"], ["tmp"])
        P.add("dve", lambda e: e.tensor_tensor(out=qh[:, :], in0=qf[:, :], in1=tmp[:, :], op=ALU.mult), ["qf", "tmp"], ["qh"])
        P.add("act", lambda e: e.activation(out=tmp[:, :], in_=zf[:, :], func=AF.Exp, scale=-1.0), ["zf", "qh"], ["tmp"])
        P.add("dve", lambda e: e.tensor_tensor(out=kh[:, :], in0=kf[:, :], in1=tmp[:, :], op=ALU.mult), ["kf", "tmp"], ["kh"])
        for ti in range(NT):
            t0, nt = c.tiles[ti]
            P.add("pe", lambda e, t0=t0, nt=nt: e.transpose(out=pstb[0:nt, :], in_=kh[:, t0:t0 + nt], identity=c.ident_bf[:, :]),
                  ["kh", "const"], ["pstb"])
            P.add("act", lambda e, ti=ti, nt=nt: e.activation(out=khT[0:nt, ti, :], in_=pstb[0:nt, :], func=AF.Copy), ["pstb"], ["khT"])
        P.add("dve", lambda e: e.memset(Sf[:, :], 0.0), [], ["Sf"])
        sb, sbk = Sb.next()
        P.add("pool", lambda e, sb=sb: e.memset(sb[:, :], 0.0), [], [sbk])
        for ti in range(NT):
            t0, nt = c.tiles[ti]
            ps, psk = pss.next()
            P.add("pe", lambda e, ps=ps, t0=t0, nt=nt: e.matmul(ps[0:nt, 0:nt], lhsT=kh[:, t0:t0 + nt], rhs=qh[:, t0:t0 + nt], start=True, stop=True),
                  ["kh", "qh"], [psk])
            pm, pmk = Pm.next()
            P.add("dve", lambda e, pm=pm, ps=ps, nt=nt: e.copy_predicated(out=pm[0:nt, 0:nt], mask=c.causT[0:nt, 0:nt], data=ps[0:nt, 0:nt]),
                  [psk, "const_c"], [pmk])
            po, pok = pso.next()
            P.add("pe", lambda e, po=po, pm=pm, ti=ti, nt=nt: e.matmul(po[:, 0:nt], lhsT=vA[0:nt, ti, :], rhs=pm[0:nt, 0:nt], start=True, stop=False),
                  ["vA", pmk], [pok])
            P.add("pe", lambda e, po=po, sb=sb, t0=t0, nt=nt: e.matmul(po[:, 0:nt], lhsT=sb[:, :], rhs=qt[:, t0:t0 + nt], start=False, stop=True),
                  [sbk, "qt"], [pok])
            P.add("act", lambda e, po=po, t0=t0, nt=nt: e.activation(out=osb[:, t0:t0 + nt], in_=po[:, 0:nt], func=AF.Copy), [pok], ["osb"])
            pd, pdk = psd.next()
            P.add("pe", lambda e, pd=pd, ti=ti, nt=nt: e.matmul(pd[:, 0:128], lhsT=khT[0:nt, ti, :], rhs=vA[0:nt, ti, :], start=True, stop=True),
                  ["khT", "vA"], [pdk])
            P.add("dve", lambda e, ti=ti: e.tensor_scalar(out=St[:, :], in0=Sf[:, :], scalar1=e1[:, ti:ti + 1], scalar2=None, op0=ALU.mult),
                  ["Sf", "e1"], ["St"])
            P.add("dve", lambda e, pd=pd, ti=ti: e.scalar_tensor_tensor(out=Sf[:, :], in0=pd[:, 0:128], scalar=e2[:, ti:ti + 1], in1=St[:, :],
                                                                        op0=ALU.mult, op1=ALU.add), [pdk, "St", "e2"], ["Sf"])
            sb, sbk = Sb.next()
            P.add("act", lambda e, sb=sb: e.activation(out=sb[:, :], in_=Sf[:, :], func=AF.Copy), ["Sf"], [sbk])
        P.dma("sp", qf[:, :], c.proj[OFF["a_g"] + h * 128:OFF["a_g"] + (h + 1) * 128, :], [], ["qf"], "qf")
        P.add("act", lambda e: e.activation(out=qf[:, :], in_=qf[:, :], func=AF.Silu), ["qf"], ["qf"])
        gcol = c.params[:, pb + PC["gnorm"] + h:pb + PC["gnorm"] + h + 1]
        for (t0, n) in split_cols(0, L, 512):
            q, qk = sq.next()
            P.add("act", lambda e, q=q, t0=t0, n=n: e.activation(out=q[:, 0:n], in_=osb[:, t0:t0 + n], func=AF.Square), ["osb"], [qk])
            P.add("pe", lambda e, q=q, n=n: e.matmul(psn[:, 0:n], lhsT=c.ones_bf[:, :], rhs=q[:, 0:n], start=True, stop=True),
                  [qk, "const"], ["psn"])
            r, rk = rs.next()
            rsqrt_op(c, r[:, 0:n], psn[:, 0:n], 1.0 / 128, ["psn"], rk)
            P.add("dve", lambda e, r=r, t0=t0, n=n, gcol=gcol: e.scalar_tensor_tensor(
                out=r[:, 0:n], in0=osb[:, t0:t0 + n], scalar=gcol, in1=r[:, 0:n], op0=ALU.mult, op1=ALU.mult),
                ["osb", rk, "params"], [rk])
            P.add("dve", lambda e, r=r, t0=t0, n=n: e.tensor_tensor(out=obr[:, t0:t0 + n], in0=r[:, 0:n], in1=qf[:, t0:t0 + n], op=ALU.mult),
                  [rk, "qf"], ["obr"])
        P.dma("sp", c.br[0][h * 128:(h + 1) * 128, :], obr[:, :], ["obr"], [], "obr")
    P.flush()
    es.close()


def dsa_phase(c, l):
    nc, P = c.nc, c.P
    es = ExitStack()
    L = c.L
    NT = len(c.tiles)
    pb = l * PL
    topk = c.topk
    rb = rows_bufs(c, es)
    st, _sq, _rs, _psn = rb
    qD = alloc(c, es, "qD", [128, 4, L], BF16)
    kD = alloc(c, es, "kD", [128, L], BF16)
    vD = alloc(c, es, "vDs", [128, NT, 129], BF16)
    kiD = alloc(c, es, "kiD", [128, L], BF16)
    sgn = alloc(c, es, "sgn", [128, NT, 16], F32)
    idx = alloc(c, es, "idx", [128, L], F32)
    work = alloc(c, es, "work", [128, L], F32)
    maskb = alloc(c, es, "maskb", [128, L], BF16)
    mT = alloc(c, es, "mT", [128, NT, 128], BF16)
    gs = alloc(c, es, "gsD", [128, 2], F32)
    zc = alloc(c, es, "zcolD", [128, 1], F32)
    m8 = alloc(c, es, "m8", [128, 8], F32)
    th = alloc(c, es, "th", [128, 1], F32)
    rls = Rot("rl", [alloc(c, es, "rl%d" % i, [128, 512], BF16) for i in range(3)])
    idxs = [idx, alloc(c, es, "idx2", [128, L], F32)]
    dgs = Rot("dg", [alloc(c, es, "dg%d" % i, [128, 16, 128], BF16) for i in range(2)])
    eTs = Rot("eT", [alloc(c, es, "eT%d" % i, [128, 512], BF16) for i in range(2)])
    pTs = Rot("pTD", [alloc(c, es, "pTD%d" % i, [128, 512], BF16) for i in range(2)])
    rr = Rot("rrD", [alloc(c, es, "rrD%d" % i, [128, 1], F32) for i in range(2)])
    ons = Rot("onD", [alloc(c, es, "onD%d" % i, [128, 128], BF16) for i in range(2)])
    ocs = Rot("ocD", [alloc(c, es, "ocD%d" % i, [128, 128], BF16) for i in range(3)])
    psi = Rot("psi", [palloc(c, es, "psi%d" % i, [128, 512]) for i in range(2)])
    pss = Rot("pssD", [palloc(c, es, "pssD%d" % i, [128, 512]) for i in range(2)])
    pso = Rot("psoD", [palloc(c, es, "psoD%d" % i, [128, 512]) for i in range(2)])
    pstb = palloc(c, es, "pstbD", [128, 128], BF16)
    P.add("dve", lambda e: e.tensor_scalar(out=gs[:, 0:1], in0=c.params[:, pb + PC["dsaq"]:pb + PC["dsaq"] + 1], scalar1=128 ** -0.5,
                                           scalar2=None, op0=ALU.mult), ["params"], ["gsc"])
    P.add("dve", lambda e: e.tensor_copy(out=gs[:, 1:2], in_=c.params[:, pb + PC["dsak"]:pb + PC["dsak"] + 1]), ["params"], ["gsc"])
    P.add("dve", lambda e: e.memset(zc[:, :], 0.0), [], ["gsc"])
    P.add("dve", lambda e: e.memset(vD[:, :, 128:129], 1.0), [], ["vD1"])
    for h in range(4):
        load_rows_norm(c, rb, OFF["d_q"] + h * 128, 128, lambda t0, n, h=h: qD[:, h, t0:t0 + n], gs[:, 0:1], c.ones_bf, 128, ("qD", h))
    load_rows_norm(c, rb, OFF["d_k"], 128, lambda t0, n: kD[:, t0:t0 + n], gs[:, 1:2], c.ones_bf, 128, "kD")
    load_rows_norm(c, rb, OFF["d_ki"], 64, lambda t0, n: kiD[:, t0:t0 + n], None, None, None, "kiD", dup64=True)
    load_vtm(c, vD, c.vD, 0, 128, "vD", lambda ti, nt: vD[0:nt, ti, 0:128])
    load_vtm(c, sgn, c.wT, 0, 16, "sgn", lambda ti, nt: sgn[0:nt, ti, :])
    P.add("act", lambda e: e.activation(out=sgn[:, :, :], in_=sgn[:, :, :], func=AF.Sign), ["sgn"], ["sgn"])
    qsts = Rot("qst", [alloc(c, es, "qst%d" % i, [128, 8, 128], F32) for i in range(2)])
    wsts = Rot("wst", [alloc(c, es, "wst%d" % i, [128, 8, 128], BF16) for i in range(2)])
    qits = Rot("qit", [alloc(c, es, "qit%d" % i, [128, 8, 128], BF16) for i in range(2)])
    mTs = [mT, alloc(c, es, "mT2", [128, NT, 128], BF16)]
    osb = [alloc(c, es, "osbD%d" % i, [128, 132], F32) for i in range(4)]
    qiv = c.proj[OFF["d_qi"]:OFF["d_qi"] + 1024, :].rearrange("(r p) t -> p r t", p=128)
    wav = c.wabs.rearrange("(r p) t -> p r t", p=128)

    def stage_a1(i):
        q0, nq = c.tiles[i]
        K = q0 + nq
        qs, qsk = qsts.next()
        ws, wsk = wsts.next()
        P.dma("sp", qs[:, :, 0:nq], qiv[:, :, q0:q0 + nq], [], [qsk], qsk)
        P.dma("sp", ws[:, :, 0:nq], wav[:, :, q0:q0 + nq], [], [wsk], wsk)
        qit, qitk = qits.next()
        P.add("pool", lambda e: e.tensor_tensor(out=qit[:, :, 0:nq], in0=qs[:, :, 0:nq], in1=ws[:, :, 0:nq], op=ALU.mult),
              [qsk, wsk], [qitk])
        idxc, idxk = idxs[i % 2], ("idx", i % 2)
        dg, dgk = dgs.next()
        for hi in range(16):
            P.add("pool", lambda e, hi=hi: e.tensor_scalar(out=dg[0:nq, hi, 0:nq], in0=c.ident_bf[0:nq, 0:nq], scalar1=sgn[0:nq, i, hi:hi + 1],
                                                           scalar2=None, op0=ALU.mult), ["const", "sgn"], [dgk])
        for (kb0, kn) in split_cols(0, K, 512):
            dots = {}

            def emit_dot(hi):
                r, half = hi // 2, hi % 2
                p, pk = psi.next()
                dots[hi] = (p, pk)
                P.add("pe", lambda e, p=p, r=r, half=half, kb0=kb0, kn=kn: e.matmul(
                    p[0:nq, 0:kn], lhsT=qit[half * 64:(half + 1) * 64, r, 0:nq], rhs=kiD[half * 64:(half + 1) * 64, kb0:kb0 + kn],
                    start=True, stop=True), [qitk, "kiD"], [pk])

            def emit_acc(hi):
                p, pk = dots[hi]
                rl, rlk = rls.next()
                P.add("act", lambda e, rl=rl, p=p, kn=kn: e.activation(out=rl[0:nq, 0:kn], in_=p[0:nq, 0:kn], func=AF.Relu), [pk], [rlk])
                P.add("pe", lambda e, rl=rl, hi=hi, kn=kn: e.matmul(_psn[0:nq, 0:kn], lhsT=dg[0:nq, hi, 0:nq], rhs=rl[0:nq, 0:kn],
                                                                   start=(hi == 0), stop=(hi == 15)), [rlk, dgk], ["psn"])

            emit_dot(0)
            for hi in range(16):
                if hi + 1 < 16:
                    emit_dot(hi + 1)
                emit_acc(hi)
            P.add("act", lambda e, kb0=kb0, kn=kn: e.activation(out=idxc[0:nq, kb0:kb0 + kn], in_=_psn[0:nq, 0:kn], func=AF.Copy),
                  ["psn"], [idxk])

    def stage_a2(i):
        q0, nq = c.tiles[i]
        K = q0 + nq
        mTc = mTs[i % 2]
        idxc, idxk = idxs[i % 2], ("idx", i % 2)
        P.add("dve", lambda e: e.tensor_tensor(out=idxc[0:nq, q0:q0 + nq], in0=idxc[0:nq, q0:q0 + nq], in1=c.fut[0:nq, 0:nq], op=ALU.add),
              [idxk, "const_f"], [idxk])
        if K > topk:
            for rd in range(topk // 8):
                src = idxc if rd == 0 else work
                P.add("dve", lambda e, src=src: e.max(out=m8[0:nq, :], in_=src[0:nq, 0:K]), [idxk, "work"], ["m8"])
                if rd < topk // 8 - 1:
                    P.add("dve", lambda e, src=src: e.match_replace(out=work[0:nq, 0:K], in_to_replace=m8[0:nq, :],
                                                                    in_values=src[0:nq, 0:K], imm_value=-1e30),
                          [idxk, "work", "m8"], ["work"])
            P.add("dve", lambda e: e.tensor_copy(out=th[0:nq, :], in_=m8[0:nq, 7:8]), ["m8"], ["th"])
        else:
            P.add("dve", lambda e: e.memset(th[0:nq, :], -1e29), ["m8"], ["th"])
        P.add("dve", lambda e: e.tensor_scalar(out=maskb[0:nq, 0:K], in0=idxc[0:nq, 0:K], scalar1=th[0:nq, 0:1], scalar2=None,
                                               op0=ALU.is_ge), [idxk, "th"], ["maskb"])
        for j in range(i + 1):
            k0, nk = c.tiles[j]
            P.add("pe", lambda e, k0=k0, nk=nk: e.transpose(out=pstb[0:nk, 0:nq], in_=maskb[0:nq, k0:k0 + nk], identity=c.ident_bf[0:nq, 0:nq]),
                  ["maskb", "const"], ["pstbD"])
            P.add("act", lambda e, j=j, nk=nk: e.activation(out=mTc[0:nk, j, 0:nq], in_=pstb[0:nk, 0:nq], func=AF.Copy), ["pstbD"], [("mT", i % 2, j)])

    def stage_b_main(i):
        q0, nq = c.tiles[i]
        mTc = mTs[i % 2]
        for h in range(4):
            po, pok = pso.next()
            for jg in range(0, i + 1, 4):
                grp = list(range(jg, min(i + 1, jg + 4)))
                ps, psk = pss.next()
                for sl, j in enumerate(grp):
                    k0, nk = c.tiles[j]
                    case = near_case(i, j)
                    P.add("pe", lambda e, ps=ps, h=h, k0=k0, nk=nk, case=case, sl=sl: e.matmul(
                        ps[0:nk, sl * 128:sl * 128 + nq], lhsT=kD[:, k0:k0 + nk], rhs=qD[:, h, q0:q0 + nq], start=True, stop=(case is None)),
                        ["kD", ("qD", h)], [psk])
                    if case is not None:
                        P.add("pe", lambda e, ps=ps, nk=nk, h=h, case=case, sl=sl: e.matmul(
                            ps[0:nk, sl * 128:sl * 128 + nq], lhsT=c.ident_bf[0:nk, 0:nk], rhs=c.BT[0:nk, (4 + h) * 3 + case, 0:nq],
                            start=False, stop=True), ["const"], [psk])
                W = (len(grp) - 1) * 128 + nq
                eT, eTk = eTs.next()
                P.add("act", lambda e, eT=eT, ps=ps, W=W: e.activation(out=eT[:, 0:W], in_=ps[:, 0:W], func=AF.Exp), [psk], [eTk])
                pT, pTk = pTs.next()
                if nq == 128:
                    ng_ = len(grp)
                    P.add("pool", lambda e, pT=pT, eT=eT, jg=jg, ng_=ng_: e.tensor_tensor(
                        out=pT[:, 0:ng_ * 128].rearrange("p (s q) -> p s q", q=128), in0=eT[:, 0:ng_ * 128].rearrange("p (s q) -> p s q", q=128),
                        in1=mTc[:, jg:jg + ng_, :], op=ALU.mult), [eTk] + [("mT", i % 2, j) for j in grp], [pTk])
                else:
                    P.add("pool", lambda e, pT=pT, eT=eT: e.tensor_tensor(
                        out=pT[0:16, 0:nq], in0=eT[0:16, 0:nq], in1=mTc[0:16, 0, 0:nq], op=ALU.mult), [eTk, ("mT", i % 2, 0)], [pTk])
                for sl, j in enumerate(grp):
                    k0, nk = c.tiles[j]
                    P.add("pe", lambda e, po=po, pT=pT, nk=nk, j=j, sl=sl: e.matmul(
                        po[0:nq, 0:129], lhsT=pT[0:nk, sl * 128:sl * 128 + nq], rhs=vD[0:nk, j, 0:129], start=(j == 0), stop=(j == i)),
                        [pTk, "vD", "vD1"], [pok])
            P.add("act", lambda e, po=po, h=h: e.activation(out=osb[h][0:nq, 0:129], in_=po[0:nq, 0:129], func=AF.Copy), [pok], [("osbD", h)])

    def stage_b_fin(i):
        q0, nq = c.tiles[i]
        for h in range(4):
            r, rk = rr.next()
            P.add("dve", lambda e, r=r, h=h: e.reciprocal(out=r[0:nq, 0:1], in_=osb[h][0:nq, 128:129]), [("osbD", h)], [rk])
            on, onk = ons.next()
            P.add("dve", lambda e, on=on, r=r, h=h: e.tensor_scalar(out=on[0:nq, :], in0=osb[h][0:nq, 0:128], scalar1=r[0:nq, 0:1],
                                                                     scalar2=None, op0=ALU.mult), [("osbD", h), rk], [onk])
            ps2, ps2k = pss.next()
            P.add("pe", lambda e, ps2=ps2, on=on: e.matmul(ps2[:, 0:nq], lhsT=on[0:nq, :], rhs=c.ident_bf[0:nq, 0:nq], start=True, stop=True),
                  [onk, "const"], [ps2k])
            oc, ock = ocs.next()
            P.add("act", lambda e, oc=oc, ps2=ps2: e.activation(out=oc[:, 0:nq], in_=ps2[:, 0:nq], func=AF.Copy), [ps2k], [ock])
            P.dma("sp", c.br[3][h * 128:(h + 1) * 128, q0:q0 + nq], oc[:, 0:nq], [ock], [], ock)

    stage_a1(0)
    if NT > 1:
        stage_a1(1)
    stage_a2(0)
    for i in range(NT):
        if i + 2 < NT:
            stage_a1(i + 2)
        stage_b_main(i)
        if i + 1 < NT:
            stage_a2(i + 1)
        stage_b_fin(i)
    P.flush()
    es.close()
```
